# Optimizing a Trainium2 kernel written in Bass

```python
import math
import jax
import jax.numpy as jnp
from jax import lax
import numpy as np

D_MODEL = 2048
BATCH = 1
SEQ = 8192
DEPTH = 1

GRID_W = 64
CTX_LEN = 256
N_MOD = 9
EPS = 1e-6
CONV_W = 5
D_FF = 256 * ((8 * D_MODEL // 3 + 255) // 256)

ML_HEADS = 8
ML_DV = D_MODEL // ML_HEADS
ML_DQK = ML_DV // 2
ML_CHUNK = 64
ML_QK = ML_HEADS * ML_DQK
ML_V = ML_HEADS * ML_DV
ML_COLS = 2 * ML_QK + 2 * ML_V + 4 * ML_HEADS

SSM_DINNER = 2 * D_MODEL
SSM_HEADDIM = 64
SSM_HEADS = SSM_DINNER // SSM_HEADDIM
SSM_GROUPS = 8
SSM_HPG = SSM_HEADS // SSM_GROUPS
SSM_DSTATE = 128
SSM_CHUNK = 128
SSM_XBC = SSM_DINNER + 2 * SSM_GROUPS * SSM_DSTATE
SSM_COLS = SSM_DINNER + SSM_XBC + 2 * SSM_HEADS

IN_COLS = ML_COLS + SSM_COLS

kernel_name = 'hybrid_mlstm_mamba2_macaron_dit_block'


def rmsnorm(x, w):
    xf = x.astype(jnp.float32)
    y = xf * lax.rsqrt(jnp.mean(jnp.square(xf), axis=-1, keepdims=True) + EPS)
    return y.astype(x.dtype) * w


def modulate(x, w, shift, scale):
    return rmsnorm(x, w) * (1 + scale) + shift


def swiglu(h, w_gate, w_up, w_down):
    return (jax.nn.silu(h @ w_gate) * (h @ w_up)) @ w_down


def dwconv_centred(x, w, b):
    pad = CONV_W // 2
    y = lax.conv_general_dilated(x, w[:, None, :].astype(x.dtype), window_strides=(1,),
                                 padding=[(pad, pad)], dimension_numbers=('NWC', 'WIO', 'NWC'),
                                 feature_group_count=x.shape[-1])
    return y + b


def flip(t):
    return jnp.flip(t, axis=1)


def to_colmajor(t, rows):
    bsz, n, ch = t.shape
    return t.reshape(bsz, rows, GRID_W, ch).transpose(0, 2, 1, 3).reshape(bsz, n, ch)


def from_colmajor(t, rows):
    bsz, n, ch = t.shape
    return t.reshape(bsz, GRID_W, rows, ch).transpose(0, 2, 1, 3).reshape(bsz, n, ch)


def mlstm_zero_state(bsz):
    one = (jnp.zeros((bsz, ML_HEADS, ML_DV, ML_DQK), jnp.float32),
           jnp.zeros((bsz, ML_HEADS, ML_DQK), jnp.float32),
           jnp.zeros((bsz, ML_HEADS), jnp.float32))
    return (one, one)


def ssd_zero_state(bsz):
    one = jnp.zeros((bsz, SSM_GROUPS, SSM_HPG, SSM_HEADDIM, SSM_DSTATE), jnp.float32)
    return (one, one)


def mlstm_dir(q, k, v, ig, lf, state0, with_output):
    f32 = jnp.float32
    bsz, n_tok, nh, dk = q.shape
    dv = v.shape[-1]
    L = ML_CHUNK
    nc = n_tok // L
    qc = q.astype(f32).reshape(bsz, nc, L, nh, dk)
    kc = k.astype(f32).reshape(bsz, nc, L, nh, dk)
    vc = v.astype(f32).reshape(bsz, nc, L, nh, dv)
    igc = ig.astype(f32).reshape(bsz, nc, L, nh)
    b = jnp.cumsum(lf.astype(f32).reshape(bsz, nc, L, nh), axis=2)
    b_end = b[:, :, -1]
    a = b_end[:, :, None] + igc - b
    m_loc = jnp.max(a, axis=2)
    kw = kc * jnp.exp(a - m_loc[:, :, None])[..., None]
    c_loc = jnp.einsum('bclhv,bclhk->bchvk', vc, kw)
    n_loc = jnp.sum(kw, axis=2)

    def step(carry, xs):
        c_st, n_st, m_st = carry
        be, cl, nl, ml = xs
        m_new = jnp.maximum(be + m_st, ml)
        s_old = jnp.exp(be + m_st - m_new)
        s_new = jnp.exp(ml - m_new)
        c_new = s_old[..., None, None] * c_st + s_new[..., None, None] * cl
        n_new = s_old[..., None] * n_st + s_new[..., None] * nl
        return (c_new, n_new, m_new), (c_st, n_st, m_st)

    xs = (jnp.moveaxis(b_end, 1, 0), jnp.moveaxis(c_loc, 1, 0),
          jnp.moveaxis(n_loc, 1, 0), jnp.moveaxis(m_loc, 1, 0))
    init = (state0[0].astype(f32), state0[1].astype(f32), state0[2].astype(f32))
    final, starts = lax.scan(step, init, xs)
    if not with_output:
        return None, final
    c0 = jnp.moveaxis(starts[0], 0, 1)
    n0 = jnp.moveaxis(starts[1], 0, 1)
    m0 = jnp.moveaxis(starts[2], 0, 1)
    causal = jnp.tril(jnp.ones((L, L), dtype=bool))
    logw = b[:, :, :, None, :] - b[:, :, None, :, :] + igc[:, :, None, :, :]
    logw = jnp.where(causal[None, None, :, :, None], logw, -jnp.inf)
    m_inter = b + m0[:, :, None, :]
    m_out = jnp.maximum(m_inter, jnp.max(logw, axis=3))
    s = jnp.einsum('bcihk,bcjhk->bcijh', qc, kc) * jnp.exp(logw - m_out[:, :, :, None, :])
    s_inter = jnp.exp(m_inter - m_out)
    num = (jnp.einsum('bcijh,bcjhv->bcihv', s, vc)
           + s_inter[..., None] * jnp.einsum('bcihk,bchvk->bcihv', qc, c0))
    den = jnp.sum(s, axis=3) + s_inter * jnp.einsum('bcihk,bchk->bcih', qc, n0)
    h = num / jnp.maximum(jnp.abs(den), jnp.exp(-m_out))[..., None]
    return h.reshape(bsz, n_tok, nh, dv).astype(v.dtype), final


def ssd_dir(x, dt, A, bm, cm, state0, with_output):
    f32 = jnp.float32
    bsz, n_tok, ng, ne, hp = x.shape
    ns = bm.shape[-1]
    L = SSM_CHUNK
    nc = n_tok // L
    xc = x.astype(f32).reshape(bsz, nc, L, ng, ne, hp)
    dtc = dt.astype(f32).reshape(bsz, nc, L, ng, ne)
    bc = bm.astype(f32).reshape(bsz, nc, L, ng, ns)
    cc = cm.astype(f32).reshape(bsz, nc, L, ng, ns)
    b = jnp.cumsum(dtc * A.astype(f32), axis=2)
    b_end = b[:, :, -1]
    xdt = xc * dtc[..., None]
    s_loc = jnp.einsum('bclgep,bclgn->bcgepn',
                       xdt * jnp.exp(b_end[:, :, None] - b)[..., None], bc)

    def step(s_st, xs):
        be, sl = xs
        return jnp.exp(be)[..., None, None] * s_st + sl, s_st

    final, starts = lax.scan(step, state0.astype(f32),
                             (jnp.moveaxis(b_end, 1, 0), jnp.moveaxis(s_loc, 1, 0)))
    if not with_output:
        return None, final
    s0 = jnp.moveaxis(starts, 0, 1)
    causal = jnp.tril(jnp.ones((L, L), dtype=bool))
    seg = b[:, :, :, None] - b[:, :, None, :]
    decay = jnp.exp(jnp.where(causal[None, None, :, :, None, None], seg, -jnp.inf))
    cb = jnp.einsum('bcign,bcjgn->bcijg', cc, bc)
    y = jnp.einsum('bcijge,bcjgep->bcigep', cb[..., None] * decay, xdt)
    y = y + jnp.exp(b)[..., None] * jnp.einsum('bcign,bcgepn->bcigep', cc, s0)
    return y.reshape(bsz, n_tok, ng, ne, hp).astype(x.dtype), final


def mlstm_branch(u, init, conv_w, conv_b, gate_b, norm_w, w_proj, with_output):
    bsz, n_tok, _ = u.shape
    qk, v, o, g = jnp.split(u, [2 * ML_QK, 2 * ML_QK + ML_V, 2 * ML_QK + 2 * ML_V], axis=-1)
    qk = jax.nn.silu(dwconv_centred(qk, conv_w, conv_b))
    q, k = jnp.split(qk, 2, axis=-1)
    q = q.reshape(bsz, n_tok, ML_HEADS, ML_DQK) * (ML_DQK ** -0.5)
    k = k.reshape(bsz, n_tok, ML_HEADS, ML_DQK)
    v = v.reshape(bsz, n_tok, ML_HEADS, ML_DV)
    g = g.astype(jnp.float32).reshape(bsz, n_tok, 4, ML_HEADS) + gate_b.astype(jnp.float32)
    ig_f, lf_f = g[:, :, 0], jax.nn.log_sigmoid(g[:, :, 1])
    ig_b, lf_b = g[:, :, 2], jax.nn.log_sigmoid(g[:, :, 3])
    h_f, st_f = mlstm_dir(q, k, v, ig_f, lf_f, init[0], with_output)
    h_b, st_b = mlstm_dir(flip(q), flip(k), flip(v), flip(ig_b), flip(lf_b), init[1], with_output)
    if not with_output:
        return None, (st_f, st_b)
    h = h_f + flip(h_b)
    h = rmsnorm(h, norm_w.reshape(ML_HEADS, ML_DV)).reshape(bsz, n_tok, ML_V)
    y = (jax.nn.sigmoid(o) * h) @ w_proj
    return y, (st_f, st_b)


def ssm_branch(u, init, conv_w, conv_b, dt_bias, a_log, d_skip, norm_w, w_proj, with_output):
    bsz, n_tok, _ = u.shape
    z, xbc, dt = jnp.split(u, [SSM_DINNER, SSM_DINNER + SSM_XBC], axis=-1)
    xbc = jax.nn.silu(dwconv_centred(xbc, conv_w, conv_b))
    xs, bm, cm = jnp.split(xbc, [SSM_DINNER, SSM_DINNER + SSM_GROUPS * SSM_DSTATE], axis=-1)
    xs = xs.reshape(bsz, n_tok, SSM_GROUPS, SSM_HPG, SSM_HEADDIM)
    bm = bm.reshape(bsz, n_tok, SSM_GROUPS, SSM_DSTATE)
    cm = cm.reshape(bsz, n_tok, SSM_GROUPS, SSM_DSTATE)
    dt = jax.nn.softplus(dt.astype(jnp.float32).reshape(bsz, n_tok, 2, SSM_GROUPS, SSM_HPG)
                         + dt_bias.astype(jnp.float32).reshape(2, SSM_GROUPS, SSM_HPG))
    A = -jnp.exp(a_log.astype(jnp.float32)).reshape(2, SSM_GROUPS, SSM_HPG)
    y_f, st_f = ssd_dir(xs, dt[:, :, 0], A[0], bm, cm, init[0], with_output)
    y_b, st_b = ssd_dir(flip(xs), flip(dt[:, :, 1]), A[1], flip(bm), flip(cm), init[1], with_output)
    if not with_output:
        return None, (st_f, st_b)
    y = y_f + flip(y_b) + d_skip.reshape(SSM_GROUPS, SSM_HPG)[..., None] * xs
    y = y.reshape(bsz, n_tok, SSM_DINNER) * jax.nn.silu(z)
    y = rmsnorm(y.reshape(bsz, n_tok, SSM_GROUPS, SSM_DINNER // SSM_GROUPS),
                norm_w.reshape(SSM_GROUPS, SSM_DINNER // SSM_GROUPS)).reshape(bsz, n_tok, SSM_DINNER)
    return y @ w_proj, (st_f, st_b)


def hybrid_layer(x, ctx, mod, mod_c, last, norm_w, ffn_w_gate, ffn_w_up, ffn_w_down, w_in,
                 ml_conv_w, ml_conv_b, ml_gate_b, ml_norm_w, w_proj_ml,
                 ssm_conv_w, ssm_conv_b, ssm_dt_bias, ssm_a_log, ssm_d, ssm_norm_w, w_proj_ssm,
                 w_gate, b_gate, w_out):
    m = jnp.split(mod, N_MOD, axis=-1)
    mc = jnp.split(mod_c, N_MOD, axis=-1)
    bsz, n_tok, _ = x.shape
    rows = n_tok // GRID_W

    def half_ffn(s, mm, sub, j):
        h = modulate(s, norm_w[sub], mm[3 * sub], mm[3 * sub + 1])
        return s + 0.5 * mm[3 * sub + 2] * swiglu(h, ffn_w_gate[j], ffn_w_up[j], ffn_w_down[j])

    def merge(h, y_ml, y_ssm):
        gml, gssm = jnp.split(jax.nn.sigmoid(h @ w_gate + b_gate), 2, axis=-1)
        return (gml * y_ml + gssm * y_ssm) @ w_out

    ml_args = (ml_conv_w, ml_conv_b, ml_gate_b, ml_norm_w, w_proj_ml)
    ssm_args = (ssm_conv_w, ssm_conv_b, ssm_dt_bias, ssm_a_log, ssm_d, ssm_norm_w, w_proj_ssm)

    x = half_ffn(x, m, 0, 0)
    ctx = half_ffn(ctx, mc, 0, 0)

    h = modulate(x, norm_w[1], m[3], m[4])
    hc = modulate(ctx, norm_w[1], mc[3], mc[4])
    u = h @ w_in
    uc = hc @ w_in
    yc_ml, st_ml = mlstm_branch(uc[..., :ML_COLS], mlstm_zero_state(bsz), *ml_args, not last)
    yc_ssm, st_ssm = ssm_branch(uc[..., ML_COLS:], ssd_zero_state(bsz), *ssm_args, not last)
    y_ml, _ = mlstm_branch(to_colmajor(u[..., :ML_COLS], rows), st_ml, *ml_args, True)
    y_ml = from_colmajor(y_ml, rows)
    y_ssm, _ = ssm_branch(u[..., ML_COLS:], st_ssm, *ssm_args, True)
    x = x + m[5] * merge(h, y_ml, y_ssm)

    x = half_ffn(x, m, 2, 1)
    if not last:
        ctx = ctx + mc[5] * merge(hc, yc_ml, yc_ssm)
        ctx = half_ffn(ctx, mc, 2, 1)
    return x, ctx


def setup_inputs(seed: int = 0) -> dict:
    key = jax.random.key(seed)
    ks = jax.random.split(key, 32)
    f32 = jnp.float32
    D = D_MODEL

    def nrm(k, shape, scale=1.0):
        return scale * jax.random.normal(k, shape, f32)

    def dense(k, shape, fan_in, gain=1.0):
        return nrm(k, shape, gain * fan_in ** -0.5)

    x = nrm(ks[0], (BATCH, SEQ, D))
    c = nrm(ks[1], (BATCH, D))
    ctx = nrm(ks[2], (BATCH, CTX_LEN, D))
    c_ctx = nrm(ks[3], (D,))
    w_ada = dense(ks[4], (DEPTH, D, N_MOD * D), D, 0.3)
    b_ada = nrm(ks[5], (DEPTH, N_MOD * D), 0.02)
    norm_w = 1.0 + nrm(ks[6], (DEPTH, 3, D), 0.05)
    ffn_w_gate = dense(ks[7], (DEPTH, 2, D, D_FF), D)
    ffn_w_up = dense(ks[8], (DEPTH, 2, D, D_FF), D)
    ffn_w_down = dense(ks[9], (DEPTH, 2, D_FF, D), D_FF)
    w_in = dense(ks[10], (DEPTH, D, IN_COLS), D)
    ml_conv_w = dense(ks[11], (DEPTH, CONV_W, 2 * ML_QK), CONV_W)
    ml_conv_b = nrm(ks[12], (DEPTH, 2 * ML_QK), 0.02)
    ig_b = nrm(ks[13], (DEPTH, 2, ML_HEADS), 0.1)
    fg_b = 3.0 + 3.0 * jax.random.uniform(ks[14], (DEPTH, 2, ML_HEADS), f32)
    ml_gate_b = jnp.stack([ig_b[:, 0], fg_b[:, 0], ig_b[:, 1], fg_b[:, 1]], axis=1)
    ml_norm_w = 1.0 + nrm(ks[15], (DEPTH, ML_V), 0.05)
    w_proj_ml = dense(ks[16], (DEPTH, ML_V, D), ML_V)
    ssm_conv_w = dense(ks[17], (DEPTH, CONV_W, SSM_XBC), CONV_W)
    ssm_conv_b = nrm(ks[18], (DEPTH, SSM_XBC), 0.02)
    dt0 = jnp.exp(jax.random.uniform(ks[19], (DEPTH, 2, SSM_HEADS), f32,
                                     math.log(1e-3), math.log(1e-1)))
    ssm_dt_bias = dt0 + jnp.log(-jnp.expm1(-dt0))
    ssm_a_log = jnp.log(jax.random.uniform(ks[20], (DEPTH, 2, SSM_HEADS), f32, 1.0, 16.0))
    ssm_d = 1.0 + nrm(ks[21], (DEPTH, SSM_HEADS), 0.1)
    ssm_norm_w = 1.0 + nrm(ks[22], (DEPTH, SSM_DINNER), 0.05)
    w_proj_ssm = dense(ks[23], (DEPTH, SSM_DINNER, D), SSM_DINNER)
    w_gate = dense(ks[24], (DEPTH, D, 2 * D), D)
    b_gate = nrm(ks[25], (DEPTH, 2 * D), 0.1)
    w_out = dense(ks[26], (DEPTH, D, D), D)
    final_norm_w = 1.0 + nrm(ks[27], (D,), 0.05)
    return {'x': x, 'c': c, 'ctx': ctx, 'c_ctx': c_ctx, 'w_ada': w_ada, 'b_ada': b_ada,
            'norm_w': norm_w, 'ffn_w_gate': ffn_w_gate, 'ffn_w_up': ffn_w_up,
            'ffn_w_down': ffn_w_down, 'w_in': w_in, 'ml_conv_w': ml_conv_w,
            'ml_conv_b': ml_conv_b, 'ml_gate_b': ml_gate_b, 'ml_norm_w': ml_norm_w,
            'w_proj_ml': w_proj_ml, 'ssm_conv_w': ssm_conv_w, 'ssm_conv_b': ssm_conv_b,
            'ssm_dt_bias': ssm_dt_bias, 'ssm_a_log': ssm_a_log, 'ssm_d': ssm_d,
            'ssm_norm_w': ssm_norm_w, 'w_proj_ssm': w_proj_ssm, 'w_gate': w_gate,
            'b_gate': b_gate, 'w_out': w_out, 'final_norm_w': final_norm_w}


def reference(x, c, ctx, c_ctx, w_ada, b_ada, norm_w, ffn_w_gate, ffn_w_up, ffn_w_down, w_in,
              ml_conv_w, ml_conv_b, ml_gate_b, ml_norm_w, w_proj_ml, ssm_conv_w, ssm_conv_b,
              ssm_dt_bias, ssm_a_log, ssm_d, ssm_norm_w, w_proj_ssm, w_gate, b_gate, w_out,
              final_norm_w):
    for l in range(DEPTH):
        last = l == DEPTH - 1
        mod = (jax.nn.silu(c) @ w_ada[l] + b_ada[l])[:, None, :]
        mod_c = jax.nn.silu(c_ctx) @ w_ada[l] + b_ada[l]
        x, ctx = hybrid_layer(x, ctx, mod, mod_c, last, norm_w[l], ffn_w_gate[l], ffn_w_up[l],
                              ffn_w_down[l], w_in[l], ml_conv_w[l], ml_conv_b[l], ml_gate_b[l],
                              ml_norm_w[l], w_proj_ml[l], ssm_conv_w[l], ssm_conv_b[l],
                              ssm_dt_bias[l], ssm_a_log[l], ssm_d[l], ssm_norm_w[l],
                              w_proj_ssm[l], w_gate[l], b_gate[l], w_out[l])
    return rmsnorm(x, final_norm_w)
```

```python
import contextlib
import numpy as np
import ml_dtypes
import concourse.bass as bass
import concourse.mybir as mybir
from concourse.bass_utils import run_bass_kernel_spmd

F32 = mybir.dt.float32
BF16 = mybir.dt.bfloat16
AF = mybir.ActivationFunctionType
ALU = mybir.AluOpType
AX = mybir.AxisListType

D = 2048
KC = 16
DFF = 5632
FC = 44
EPS = 1e-6
NCORES = 8

ENGS = ("pe", "act", "dve", "pool", "sp")


class Op:
    __slots__ = ("eng", "fn", "deps", "isdma", "sem", "val", "inc")

    def __init__(self, eng, fn, isdma):
        self.eng = eng
        self.fn = fn
        self.deps = []
        self.isdma = isdma
        self.sem = None
        self.val = None
        self.inc = False


class Prog:
    def __init__(self, nc):
        self.nc = nc
        self.ops = {e: [] for e in ENGS}
        self.keys = {}
        self.stack = contextlib.ExitStack()
        self.dram_n = 0
        self.extra_r = []
        self.nbar = 0
        self.prefix = ""
        self.bar_t = self.sb("bar_t", [128, 8], F32)

    def sb(self, name, shape, dtype, stack=None):
        return (stack or self.stack).enter_context(self.nc.sbuf_tensor(self.prefix + name, list(shape), dtype))

    def ps(self, name, shape, dtype=F32, stack=None):
        return (stack or self.stack).enter_context(self.nc.psum_tensor(self.prefix + name, list(shape), dtype))

    @contextlib.contextmanager
    def scope(self, prefix):
        old_stack, old_prefix = self.stack, self.prefix
        sub = contextlib.ExitStack()
        self.stack, self.prefix = sub, prefix
        try:
            yield
        finally:
            self.barrier()
            sub.close()
            self.stack, self.prefix = old_stack, old_prefix

    def _track(self, op, r, w, after=()):
        deps = set(after)
        for k in r:
            st = self.keys.get(k)
            if st is None:
                st = self.keys[k] = [None, {}, []]
            if st[0] is not None:
                deps.add(st[0])
            if op.isdma:
                st[2].append(op)
            else:
                st[1][op.eng] = op
        for k in w:
            st = self.keys.get(k)
            if st is None:
                st = self.keys[k] = [None, {}, []]
            if st[0] is not None:
                deps.add(st[0])
            for rd in st[1].values():
                deps.add(rd)
            for rd in st[2]:
                deps.add(rd)
            st[0] = op
            st[1] = {}
            st[2] = []
        deps.discard(op)
        for d in deps:
            if d.eng == "pe" and op.eng == "pe" and not d.isdma and not op.isdma:
                continue
            op.deps.append(d)
            d.inc = True

    def add(self, eng, fn, r=(), w=(), after=()):
        op = Op(eng, fn, False)
        r = list(r) + self.extra_r
        self._track(op, r, w, after)
        self.ops[eng].append(op)
        return op

    def dma(self, eng, out, in_, r=(), w=(), sem=None, after=(), **kw):
        op = Op(eng, (lambda e, out=out, in_=in_, kw=kw: e.dma_start(out=out, in_=in_, **kw)), True)
        op.sem = ("dma", sem)
        op.inc = True
        r = list(r) + self.extra_r
        self._track(op, r, w, after)
        self.ops[eng].append(op)
        return op

    def coll(self, kind, src, dst, r, w, sem):
        groups = [list(range(NCORES))]
        op = Op("pool", (lambda e: e.collective_compute(kind, ALU.bypass, replica_groups=groups, ins=[src], outs=[dst])), True)
        op.sem = ("dma", sem)
        op.inc = True
        self._track(op, list(r) + self.extra_r, w)
        self.ops["pool"].append(op)
        return op

    def mm(self, out, lhsT, rhs, start, stop, r, w):
        return self.add("pe", lambda e: e.matmul(out, lhsT, rhs, start=start, stop=stop), r=r, w=w)

    def act(self, out, in_, func, r, w, bias=0.0, scale=1.0, eng="act"):
        return self.add(eng, lambda e: e.activation(out=out, in_=in_, func=func, bias=bias, scale=scale), r=r, w=w)

    def tt(self, eng, out, in0, in1, op, r, w):
        return self.add(eng, lambda e: e.tensor_tensor(out=out, in0=in0, in1=in1, op=op), r=r, w=w)

    def ts(self, eng, out, in0, s1, s2, op0, op1, r, w):
        if s2 is None:
            return self.add(eng, lambda e: e.tensor_scalar(out, in0, s1, None, op0), r=r, w=w)
        return self.add(eng, lambda e: e.tensor_scalar(out, in0, s1, s2, op0, op1), r=r, w=w)

    def stt(self, eng, out, in0, scalar, in1, op0, op1, r, w):
        return self.add(eng, lambda e: e.scalar_tensor_tensor(out=out, in0=in0, scalar=scalar, in1=in1, op0=op0, op1=op1), r=r, w=w)

    def copy(self, eng, out, in_, r, w):
        if eng == "act":
            return self.add(eng, lambda e: e.activation(out=out, in_=in_, func=AF.Identity), r=r, w=w)
        return self.add(eng, lambda e: e.tensor_copy(out=out, in_=in_), r=r, w=w)

    def memset(self, eng, ap, val, w):
        return self.add(eng, lambda e: e.memset(ap, val), w=w)

    def barrier(self):
        allkeys = list(self.keys.keys())
        self.extra_r = []
        tok = ("bar", self.nbar)
        i = self.nbar
        self.nbar += 1
        self.add("dve", lambda e: e.memset(self.bar_t[:, i % 8:i % 8 + 1], 0.0), r=allkeys, w=allkeys + [tok])
        self.extra_r = [tok]

    def emit(self):
        nc = self.nc
        counts = {}
        for e in ENGS:
            for op in self.ops[e]:
                if op.isdma:
                    k = op.sem
                    counts[k] = counts.get(k, 0) + 16
                    op.val = counts[k]
                elif op.inc:
                    k = ("eng", e)
                    op.sem = k
                    counts[k] = counts.get(k, 0) + 1
                    op.val = counts[k]
        sems = {}
        st = contextlib.ExitStack()
        for i, k in enumerate(counts.keys()):
            sems[k] = st.enter_context(nc.semaphore("s%d" % i))
        self.maxcount = max(counts.values()) if counts else 0
        self.nsems = len(counts)
        engobj = {"pe": "tensor", "act": "scalar", "dve": "vector", "pool": "gpsimd", "sp": "sync"}

        def run(e, eo):
            waited = {}
            for op in self.ops[e]:
                need = {}
                for d in op.deps:
                    if need.get(d.sem, 0) < d.val:
                        need[d.sem] = d.val
                for k, v in need.items():
                    if waited.get(k, 0) < v:
                        eo.wait_ge(sems[k], v)
                        waited[k] = v
                ins = op.fn(eo)
                if op.inc:
                    ins.then_inc(sems[op.sem], 16 if op.isdma else 1)

        with nc.Block() as block:
            for e in ENGS:
                if not self.ops[e]:
                    continue
                getattr(block, engobj[e])(lambda eo, e=e: run(e, eo))
        st.close()
        self.stack.close()


def dram(nc, name, shape, dtype, kind):
    return nc.dram_tensor(name, list(shape), dtype, kind=kind).ap()


class Ctx:
    pass


def setup_common(p, T, nslot=3, ffn=True, bank=None):
    c = Ctx()
    c.T = T
    c.bank = bank if bank is not None else [p.ps("bank%d" % b, [128, 512]) for b in range(8)]
    c.NSLOT = nslot
    c.slot = [p.sb("slot%d" % s, [128, 8192], BF16) for s in range(c.NSLOT)]
    c.slot_i = 0
    c.ones = p.sb("ones", [128, 128], F32)
    p.memset("dve", c.ones[:], 1.0, w=["ones"])
    if ffn:
        c.xb = p.sb("xb", [128, 2, T], F32)
        c.xb_i = 0
        c.rstd = p.sb("rstd", [128, T], F32)
        c.sg = p.sb("sg", [128, 2, 512], F32)
        c.sq = p.sb("sq", [128, 2, 512], F32)
        c.sq_i = 0
        c.hT = p.sb("hT", [128, KC, T], BF16)
        c.aT = p.sb("aT", [128, FC, T], BF16)
    return c


def next_slot(c):
    s = c.slot_i % c.NSLOT
    c.slot_i += 1
    return s


def ffn_stage(p, c, name, tiles, segs, x_in, A_in, B_in, HG, wg, wu, wd, A_out, B_out,
              x_out, h_out, h_out_sb, interleave=None, hout_rot=0):
    T = c.T
    nt = len(tiles)
    ssb = [5, 6, 7]

    def segs_in(n0, n1):
        out = []
        for (c0, c1, j) in segs:
            a, b = max(c0, n0), min(c1, n1)
            if a < b:
                out.append((a, b, j))
        return out

    def load_x(src, k, tag):
        b = c.xb_i % 2
        c.xb_i += 1
        p.dma("sp", c.xb[:, b, :], src[k * 128:(k + 1) * 128, :], r=[(tag, k)], w=[("xb", b)], sem="xb%d" % b)
        return b

    def sumsq_accum(src_ap_fn, srckeys, k, first, last):
        for n, (n0, n1) in enumerate(tiles):
            q = c.sq_i % 2
            c.sq_i += 1
            w_ = n1 - n0
            p.act(c.sq[:, q, 0:w_], src_ap_fn(n0, n1), AF.Square, r=srckeys, w=[("sq", q)])
            p.mm(c.bank[ssb[n]][:, 0:w_], c.ones[:], c.sq[:, q, 0:w_], first, last,
                 r=["ones", ("sq", q)], w=[("bank", ssb[n])])

    def make_rstd():
        for n, (n0, n1) in enumerate(tiles):
            w_ = n1 - n0
            p.act(c.rstd[:, n0:n1], c.bank[ssb[n]][:, 0:w_], AF.Sqrt, r=[("bank", ssb[n])], w=[("rstd", n)],
                  bias=c.epsb[:, 0:1], scale=1.0 / D)
            p.add("dve", lambda e, n0=n0, n1=n1: e.reciprocal(out=c.rstd[:, n0:n1], in_=c.rstd[:, n0:n1]),
                  r=[("rstd", n)], w=[("rstd", n)])

    rstd_keys = [("rstd", n) for n in range(nt)]

    def apply_norm(b, k, A, B, dst, dstkey, kk=None):
        if kk is None:
            kk = k
        p.tt("dve", c.xb[:, b, :], c.xb[:, b, :], c.rstd[:, :], ALU.mult, r=[("xb", b)] + rstd_keys, w=[("xb", b)])
        for (c0, c1, j) in segs:
            bias = B[:, k, j:j + 1] if B is not None else 0.0
            p.act(dst[:, kk, c0:c1], c.xb[:, b, c0:c1], AF.Identity, r=[("xb", b), "modAB"], w=[(dstkey, kk)],
                  bias=bias, scale=A[:, k, j:j + 1])

    for k in range(KC):
        b = load_x(x_in, k, name + "_xin")
        sumsq_accum(lambda n0, n1, b=b: c.xb[:, b, n0:n1], [("xb", b)], k, k == 0, k == KC - 1)
    make_rstd()
    for k in range(KC):
        b = load_x(x_in, k, name + "_xin")
        apply_norm(b, k, A_in, B_in, c.hT, "hT")

    wg_r = wg.rearrange("(k p) n -> p k n", p=128)
    wu_r = wu.rearrange("(k p) n -> p k n", p=128)
    gu_i = 0
    for fb in range(FC // 2):
        s = next_slot(c)
        sl = c.slot[s]
        wgs = sl[:, 0:KC * 256].rearrange("p (k n) -> p k n", k=KC)
        wus = sl[:, KC * 256:2 * KC * 256].rearrange("p (k n) -> p k n", k=KC)
        p.dma("pool", wgs, wg_r[:, :, fb * 256:(fb + 1) * 256], w=[("slot", s)], sem="slot%d" % s)
        p.dma("pool", wus, wu_r[:, :, fb * 256:(fb + 1) * 256], w=[("slot", s)], sem="slot%d" % s)
        for ff in range(2):
            f = fb * 2 + ff
            for n, (n0, n1) in enumerate(tiles):
                w_ = n1 - n0
                par = gu_i % 2
                gu_i += 1
                gb, ub = par * 2, par * 2 + 1
                for k in range(KC):
                    p.mm(c.bank[gb][:, 0:w_], wgs[:, k, ff * 128:(ff + 1) * 128], c.hT[:, k, n0:n1], k == 0, k == KC - 1,
                         r=[("slot", s), ("hT", k)], w=[("bank", gb)])
                for k in range(KC):
                    p.mm(c.bank[ub][:, 0:w_], wus[:, k, ff * 128:(ff + 1) * 128], c.hT[:, k, n0:n1], k == 0, k == KC - 1,
                         r=[("slot", s), ("hT", k)], w=[("bank", ub)])
                p.act(c.sg[:, par, 0:w_], c.bank[gb][:, 0:w_], AF.Silu, r=[("bank", gb)], w=[("sg", par)])
                p.tt("dve", c.aT[:, f, n0:n1], c.sg[:, par, 0:w_], c.bank[ub][:, 0:w_], ALU.mult,
                     r=[("sg", par), ("bank", ub)], w=[("aT", f)])
        if interleave is not None:
            interleave(fb)

    wd_r = wd.rearrange("(f p) n -> p f n", p=128)
    HF = FC // 2
    dn_i = 0
    for db in range(KC // 2):
        ss_ = []
        for half in range(2):
            s = next_slot(c)
            ws = c.slot[s][:, 0:HF * 256].rearrange("p (f n) -> p f n", f=HF)
            p.dma("pool", ws, wd_r[:, half * HF:(half + 1) * HF, db * 256:(db + 1) * 256], w=[("slot", s)], sem="slot%d" % s)
            ss_.append((s, ws))
        for dd in range(2):
            d = db * 2 + dd
            b = load_x(x_in, d, name + "_xin")
            for n, (n0, n1) in enumerate(tiles):
                w_ = n1 - n0
                ob = dn_i % 2
                dn_i += 1
                for f in range(FC):
                    s, ws = ss_[f // HF]
                    p.mm(c.bank[ob][:, 0:w_], ws[:, f % HF, dd * 128:(dd + 1) * 128], c.aT[:, f, n0:n1], f == 0, f == FC - 1,
                         r=[("slot", s), ("aT", f)], w=[("bank", ob)])
                for (a0, a1, j) in segs_in(n0, n1):
                    p.stt("dve", c.xb[:, b, a0:a1], c.bank[ob][:, a0 - n0:a1 - n0], HG[:, d, j:j + 1], c.xb[:, b, a0:a1],
                          ALU.mult, ALU.add, r=[("bank", ob), ("xb", b), "modAB"], w=[("xb", b)])
            sumsq_accum(lambda n0, n1, b=b: c.xb[:, b, n0:n1], [("xb", b)], d, d == 0, d == KC - 1)
            p.dma("sp", x_out[d * 128:(d + 1) * 128, :], c.xb[:, b, :], r=[("xb", b)], w=[(name + "_xout", d)], sem=name + "_xo")
    make_rstd()
    for k in range(KC):
        b = load_x(x_out, k, name + "_xout")
        kk = k % hout_rot if hout_rot else k
        apply_norm(b, k, A_out, B_out, h_out_sb, name + "_hout", kk)
        p.dma("sp", h_out[k * 128:(k + 1) * 128, :], h_out_sb[:, kk, :], r=[(name + "_hout", kk)], w=[(name + "_hdram", k)],
              sem=name + "_ho")


def make_AB(p, c, modfm, nw, sub, want_out=None):
    A = p.sb("A%d" % sub, [128, KC, 2], F32)
    B = p.sb("B%d" % sub, [128, KC, 2], F32)
    G = p.sb("G%d" % sub, [128, KC, 2], F32)
    m = modfm[:, :].rearrange("p (c j) -> p c j", j=2)
    sh = m[:, (3 * sub) * KC:(3 * sub + 1) * KC, :]
    sc = m[:, (3 * sub + 1) * KC:(3 * sub + 2) * KC, :]
    gt = m[:, (3 * sub + 2) * KC:(3 * sub + 3) * KC, :]
    for j in range(2):
        p.stt("dve", A[:, :, j], sc[:, :, j], 1.0, nw[:, sub, :], ALU.add, ALU.mult, r=["modfm", "nw"], w=["modAB"])
    p.copy("dve", B[:], sh, r=["modfm"], w=["modAB"])
    return A, B, G, gt


def build_p1():
    nc = bass.Bass("TRN2", target_bir_lowering=False)
    T = 1056
    xT = dram(nc, "xT", [D, T], F32, "ExternalInput")
    c2 = dram(nc, "c2", [128, KC * 2], F32, "ExternalInput")
    w_ada = dram(nc, "w_ada", [D, 9 * D], F32, "ExternalInput")
    b_fm = dram(nc, "b_fm", [128, 144], F32, "ExternalInput")
    nwd = dram(nc, "nw", [128, 3 * KC], F32, "ExternalInput")
    wg = dram(nc, "wg", [D, DFF], F32, "ExternalInput")
    wu = dram(nc, "wu", [D, DFF], F32, "ExternalInput")
    wd = dram(nc, "wd", [DFF, D], F32, "ExternalInput")
    x1T = dram(nc, "x1T", [D, T], F32, "ExternalOutput")
    hT2 = dram(nc, "hT2", [D, T], BF16, "ExternalOutput")
    modo = dram(nc, "modfm", [128, 288], F32, "ExternalOutput")

    p = Prog(nc)
    c = setup_common(p, T)
    c.epsb = p.sb("epsb", [128, 1], F32)
    p.memset("dve", c.epsb[:], EPS, w=["epsb"])
    c2t = p.sb("c2t", [128, KC * 2], F32)
    sc2 = p.sb("sc2", [128, KC, 2], BF16)
    bfm = p.sb("bfm", [128, 144], F32)
    nw = p.sb("nwt", [128, 3, KC], F32)
    modfm = p.sb("modfm_sb", [128, 288], F32)
    p.dma("sp", c2t[:], c2, w=["c2t"], sem="c0")
    p.dma("sp", bfm[:], b_fm, w=["bfm"], sem="c1")
    p.dma("sp", nw[:].rearrange("p s k -> p (s k)"), nwd, w=["nw"], sem="c2")
    p.act(sc2[:].rearrange("p k j -> p (k j)"), c2t[:], AF.Silu, r=["c2t"], w=["sc2"])

    wa_r = w_ada.rearrange("(k p) n -> p k n", p=128)
    modps = c.bank[4]

    def mod_block(blk):
        s = next_slot(c)
        ws = c.slot[s][:, :].rearrange("p (k n) -> p k n", k=KC)
        p.dma("pool", ws, wa_r[:, :, blk * 512:(blk + 1) * 512], w=[("slot", s)], sem="slot%d" % s)
        for cc in range(4):
            ch = blk * 4 + cc
            for k in range(KC):
                p.mm(modps[:, 2 * ch:2 * ch + 2], ws[:, k, cc * 128:(cc + 1) * 128], sc2[:, k, :], k == 0, k == KC - 1,
                     r=[("slot", s), "sc2"], w=[("bank", 4)])

    def mod_finish(c0, c1):
        for j in range(2):
            src = modps[:, 2 * c0:2 * c1].rearrange("p (c j) -> p c j", j=2)[:, :, j]
            dst = modfm[:, 2 * c0:2 * c1].rearrange("p (c j) -> p c j", j=2)[:, :, j]
            p.tt("dve", dst, src, bfm[:, c0:c1], ALU.add, r=[("bank", 4), "bfm"], w=["modfm"])

    for blk in range(8):
        mod_block(blk)
    mod_finish(0, 32)
    A1, B1, G1, gt1 = make_AB(p, c, modfm, nw, 0)

    tiles = [(0, 352), (352, 704), (704, 1056)]
    segs = [(0, 1024, 0), (1024, 1056, 1)]
    A2 = p.sb("A2x", [128, KC, 2], F32)
    B2 = p.sb("B2x", [128, KC, 2], F32)

    pending = list(range(8, 36))

    def inter2(fb):
        if not pending:
            return
        for _ in range(2):
            if pending:
                mod_block(pending.pop(0))
        if not pending:
            mod_finish(32, 144)
            p.ts("dve", G1[:], gt1, 0.5, None, ALU.mult, None, r=["modfm"], w=["modAB"])
            m = modfm[:, :].rearrange("p (c j) -> p c j", j=2)
            for j in range(2):
                p.stt("dve", A2[:, :, j], m[:, 4 * KC:5 * KC, j], 1.0, nw[:, 1, :], ALU.add, ALU.mult,
                      r=["modfm", "nw"], w=["modAB"])
            p.copy("dve", B2[:], m[:, 3 * KC:4 * KC, :], r=["modfm"], w=["modAB"])
            p.dma("sp", modo, modfm[:], r=["modfm"], w=["modo"], sem="modo")

    ffn_stage(p, c, "f1", tiles, segs, xT, A1, B1, G1, wg, wu, wd, A2, B2, x1T, hT2, c.hT, interleave=inter2)
    p.add("sp", lambda e: e.nop(), r=["modo"] + [("f1_hdram", k) for k in range(KC)] + [("f1_xout", k) for k in range(KC)])
    p.emit()
    return nc, p


def _fm(v):
    return np.ascontiguousarray(v.reshape(-1, 128).T)


def run_p1(inputs):
    x = inputs["x"][0]
    ctx = inputs["ctx"][0]
    nc, p = build_p1()
    c2 = np.stack([_fm(inputs["c"][0]), _fm(inputs["c_ctx"])], axis=-1).reshape(128, 32)
    nw = np.stack([_fm(inputs["norm_w"][0, s]) for s in range(3)], axis=1).reshape(128, 48)
    common = {
        "c2": np.ascontiguousarray(c2, dtype=np.float32),
        "w_ada": np.ascontiguousarray(inputs["w_ada"][0]),
        "b_fm": _fm(inputs["b_ada"][0]),
        "nw": np.ascontiguousarray(nw),
        "wg": np.ascontiguousarray(inputs["ffn_w_gate"][0, 0]),
        "wu": np.ascontiguousarray(inputs["ffn_w_up"][0, 0]),
        "wd": np.ascontiguousarray(inputs["ffn_w_down"][0, 0]),
    }
    in_maps = []
    for i in range(NCORES):
        xt = np.concatenate([x[1024 * i:1024 * (i + 1)], ctx[32 * i:32 * (i + 1)]], axis=0).T
        m = dict(common)
        m["xT"] = np.ascontiguousarray(xt)
        in_maps.append(m)
    res = run_bass_kernel_spmd(nc, in_maps, core_ids=list(range(NCORES)))
    return res.results


NTOK = 8448
NCH = 66
UW = 8456


def upos(t):
    return 2 + t if t < 256 else 6 + t


UW2 = 8576


def upos2(t):
    return 32 + t if t < 256 else 96 + t


def emit_p2a(nc, p, bank, pre=""):
    hT = dram(nc, pre + "hT", [D, NTOK], BF16, "ExternalInput")
    wml = dram(nc, pre + "wml", [D, 772], F32, "ExternalInput")
    cst = dram(nc, pre + "cst", [128, 3 * 128], F32, "ExternalInput")
    cw = dram(nc, pre + "cw", [128, 2 * 5 + 2], F32, "ExternalInput")
    gbn = dram(nc, pre + "gbn", [128, 4 + 256], F32, "ExternalInput")
    yo = dram(nc, "yml", [8192, 256], BF16, "ExternalOutput")
    cs = p.sb("cs", [128, 3, 128], F32)
    ident, triU, triL = cs[:, 0, :], cs[:, 1, :], cs[:, 2, :]
    identb = p.sb("identb", [128, 128], BF16)
    cwt = p.sb("cwt", [128, 12], F32)
    gb = p.sb("gb", [128, 260], F32)
    ones = p.sb("ones", [128, 128], F32)
    epsb = p.sb("epsb", [128, 1], F32)
    QT = p.sb("QT", [128, NTOK], BF16)
    KT = p.sb("KT", [128, NTOK], BF16)
    Ktm = p.sb("Ktm", [128, NCH, 128], BF16)
    Va = p.sb("Va", [128, NCH, 257], BF16)
    SO = p.sb("SO", [128, NCH, 256], BF16)
    gat = p.sb("gat", [128, NCH, 4], F32)
    p.dma("sp", cs[:].rearrange("p a b -> p (a b)"), cst, w=["cs"], sem="c0")
    p.dma("sp", cwt[:], cw, w=["cwt"], sem="c1")
    p.dma("sp", gb[:], gbn, w=["gb"], sem="c2")
    p.memset("dve", ones[:], 1.0, w=["ones"])
    p.memset("dve", epsb[:], EPS, w=["epsb"])
    p.copy("dve", identb[:], ident, r=["cs"], w=["identb"])
    p.memset("dve", Va[:, :, 256:257], 1.0, w=["Va"])

    stA = contextlib.ExitStack()
    W = p.sb("W", [128, KC, 772], BF16, stack=stA)
    ht = p.sb("ht", [128, KC, 512], BF16, stack=stA)
    U = p.sb("U", [128, 2, UW], BF16, stack=stA)
    dg = p.sb("dg", [128, 2, 5, 128], BF16, stack=stA)
    wr = wml.rearrange("(k p) n -> p k n", p=128)
    for k0 in range(0, KC, 4):
        p.dma("pool", W[:, k0:k0 + 4, :], wr[:, k0:k0 + 4, :], w=["W"], sem="W")
    p.memset("dve", U[:], 0.0, w=["U"])
    for f in range(2):
        for tap in range(5):
            p.ts("dve", dg[:, f, tap, :], ident, cwt[:, f * 5 + tap:f * 5 + tap + 1], None, ALU.mult, None,
                 r=["cs", "cwt"], w=["dg"])
    hr = hT.rearrange("(k p) n -> p k n", p=128)
    bi = 0
    for t0 in range(0, NTOK, 512):
        n = min(512, NTOK - t0)
        p.dma("sp", ht[:, :, 0:n], hr[:, :, t0:t0 + n], w=["ht"], sem="ht")
        for f in range(2):
            b = bi % 4
            bi += 1
            for k in range(KC):
                p.mm(bank[b][:, 0:n], W[:, k, f * 128:(f + 1) * 128], ht[:, k, 0:n], k == 0, k == KC - 1,
                     r=["W", "ht"], w=[("bank", b)])
            for (a0, a1) in ([(0, 256), (256, 512)] if t0 == 0 else [(0, n)]):
                p0 = upos(t0 + a0)
                p.copy("act", U[:, f, p0:p0 + a1 - a0], bank[b][:, a0:a1], r=[("bank", b)], w=["U"])
        for sub in range(n // 128):
            ch = t0 // 128 + sub
            b = bi % 4
            bi += 1
            b2 = 4 + b
            for k in range(KC):
                p.mm(bank[b][:, 0:260], ht[:, k, sub * 128:(sub + 1) * 128], W[:, k, 256:516], k == 0, k == KC - 1,
                     r=["W", "ht"], w=[("bank", b)])
            for k in range(KC):
                p.mm(bank[b2][:, 0:256], ht[:, k, sub * 128:(sub + 1) * 128], W[:, k, 516:772], k == 0, k == KC - 1,
                     r=["W", "ht"], w=[("bank", b2)])
            p.copy("dve", Va[:, ch, 0:256], bank[b][:, 0:256], r=[("bank", b)], w=["Va"])
            p.tt("dve", gat[:, ch, :], bank[b][:, 256:260], gb[:, 0:4], ALU.add, r=[("bank", b), "gb"], w=["gat"])
            p.act(SO[:, ch, :], bank[b2][:, 0:256], AF.Sigmoid, r=[("bank", b2)], w=["SO"])
    for f, dst in ((0, QT), (1, KT)):
        for (s0, s1) in [(0, 256)] + [(a, a + 512) for a in range(256, NTOK, 512)]:
            b = bi % 4
            bi += 1
            n = s1 - s0
            p0 = upos(s0)
            for tap in range(5):
                p.mm(bank[b][:, 0:n], dg[:, f, tap, :], U[:, f, p0 + tap - 2:p0 + tap - 2 + n], tap == 0, tap == 4,
                     r=["dg", "U"], w=[("bank", b)])
            p.act(dst[:, s0:s1], bank[b][:, 0:n], AF.Silu, r=[("bank", b), "cwt"], w=["QK"], bias=cwt[:, 10 + f:11 + f])
    for ch in range(NCH):
        b = bi % 4
        bi += 1
        p.mm(bank[b][:, 0:128], KT[:, ch * 128:(ch + 1) * 128], identb[:], True, True, r=["QK", "identb"], w=[("bank", b)])
        p.copy("act", Ktm[:, ch, :], bank[b][:, 0:128], r=[("bank", b)], w=["Ktm"])
    p.barrier()
    stA.close()

    lf = p.sb("lf", [128, 2, NCH], F32)
    sc = p.sb("sc", [128, 8, NCH], F32)
    tmp = p.sb("tmpg", [128, 2, NCH], F32)
    for d in range(2):
        p.act(tmp[:, d, :], gat[:, :, 2 * d + 1], AF.Exp, r=["gat"], w=["tmpg"], scale=-1.0)
        p.act(tmp[:, d, :], tmp[:, d, :], AF.Ln, r=["tmpg"], w=["tmpg"], bias=1.0)
        p.ts("dve", lf[:, d, :], tmp[:, d, :], -1.0, None, ALU.mult, None, r=["tmpg"], w=["lf"])
    cum = bank[7]
    p.mm(cum[:, 0:NCH], triU, lf[:, 0, :], True, True, r=["cs", "lf"], w=[("bank", 7)])
    p.mm(cum[:, NCH:2 * NCH], triL, lf[:, 1, :], True, True, r=["cs", "lf"], w=[("bank", 7)])
    p.mm(cum[:, 2 * NCH:4 * NCH], ones[:], lf[:].rearrange("p d c -> p (d c)"), True, True, r=["ones", "lf"], w=[("bank", 7)])
    for d in range(2):
        bcol = cum[:, d * NCH:(d + 1) * NCH]
        tot = cum[:, (2 + d) * NCH:(3 + d) * NCH]
        ig = gat[:, :, 2 * d]
        o = 4 * d
        p.tt("dve", tmp[:, d, :], ig, bcol, ALU.subtract, r=["gat", ("bank", 7)], w=["tmpg"])
        p.act(sc[:, o + 0, :], tmp[:, d, :], AF.Exp, r=["tmpg"], w=["sc"])
        p.act(sc[:, o + 1, :], bcol, AF.Exp, r=[("bank", 7)], w=["sc"])
        p.ts("dve", sc[:, o + 1, :], sc[:, o + 1, :], 128.0 ** -0.5, None, ALU.mult, None, r=["sc"], w=["sc"])
        p.act(sc[:, o + 3, :], tot, AF.Exp, r=[("bank", 7)], w=["sc"])
        p.tt("dve", sc[:, o + 2, :], sc[:, o + 0, :], sc[:, o + 3, :], ALU.mult, r=["sc"], w=["sc"])

    C32 = p.sb("C32", [128, 257], F32)
    Cb = p.sb("Cb", [128, 257], BF16)
    PT = p.sb("PT", [128, 2, 128], BF16)
    kw = p.sb("kw", [128, 2, 128], BF16)
    sm = p.sb("sm", [128, 2, 4], F32)
    hf = p.sb("hf", [128, 64, 256], BF16)
    hs = p.sb("hs", [128, 2, 256], F32)
    yb = p.sb("yb", [128, 2, 256], BF16)
    def pre(i, d, ch):
        q = i % 2
        o = 4 * d
        mask = triU if d == 0 else triL
        sb_, ub_ = 0 + q, 4 + q
        tk = slice(ch * 128, (ch + 1) * 128)
        if ch >= 2:
            p.mm(bank[sb_][:, 0:128], KT[:, tk], QT[:, tk], True, True, r=["QK"], w=[("bank", sb_)])
            p.stt("dve", PT[:, q, :], bank[sb_][:, 0:128], sc[:, o + 0, ch:ch + 1], mask, ALU.mult, ALU.mult,
                  r=[("bank", sb_), "sc", "cs"], w=[("PT", q)])
        p.ts("dve", kw[:, q, :], Ktm[:, ch, :], sc[:, o + 2, ch:ch + 1], None, ALU.mult, None, r=["Ktm", "sc"], w=[("kw", q)])
        p.mm(bank[ub_][:, 0:257], kw[:, q, :], Va[:, ch, :], True, True, r=[("kw", q), "Va"], w=[("bank", ub_)])

    def post(i, d, ch):
        q = i % 2
        o = 4 * d
        ab_, ub_ = 2 + q, 4 + q
        tk = slice(ch * 128, (ch + 1) * 128)
        if ch >= 2:
            p.mm(bank[ab_][:, 0:257], PT[:, q, :], Va[:, ch, :], True, False, r=[("PT", q), "Va"], w=[("bank", ab_)])
            p.mm(bank[ab_][:, 0:257], QT[:, tk], Cb[:], False, True, r=["QK", "Cb"], w=[("bank", ab_)])
        p.stt("dve", C32[:], C32[:], sc[:, o + 3, ch:ch + 1], bank[ub_][:, 0:257], ALU.mult, ALU.add,
              r=["C32", "sc", ("bank", ub_)], w=["C32"])
        p.copy("act", Cb[:], C32[:], r=["C32"], w=["Cb"])
        if ch >= 2:
            p.act(sm[:, q, 0:1], bank[ab_][:, 256:257], AF.Abs, r=[("bank", ab_), "sc"], w=[("sm", q)],
                  scale=sc[:, o + 1, ch:ch + 1])
            p.ts("dve", sm[:, q, 1:2], sm[:, q, 0:1], 1.0, None, ALU.max, None, r=[("sm", q)], w=[("sm", q)])
            p.add("dve", lambda e, q=q: e.reciprocal(out=sm[:, q, 2:3], in_=sm[:, q, 1:2]), r=[("sm", q)], w=[("sm", q)])
            p.tt("dve", sm[:, q, 3:4], sm[:, q, 2:3], sc[:, o + 1, ch:ch + 1], ALU.mult, r=[("sm", q), "sc"], w=[("sm", q)])
            if d == 0:
                p.ts("dve", hf[:, ch - 2, :], bank[ab_][:, 0:256], sm[:, q, 3:4], None, ALU.mult, None,
                     r=[("bank", ab_), ("sm", q)], w=[("hf", ch)])
            else:
                p.stt("dve", hs[:, q, :], bank[ab_][:, 0:256], sm[:, q, 3:4], hf[:, ch - 2, :], ALU.mult, ALU.add,
                      r=[("bank", ab_), ("sm", q), ("hf", ch)], w=[("hs", q)])
                p.add("act", lambda e, q=q: e.activation(out=yb[:, q, :], in_=hs[:, q, :], func=AF.Square,
                                                         accum_out=sm[:, q, 0:1]), r=[("hs", q)], w=[("sm", q), ("yb", q)])
                p.act(sm[:, q, 1:2], sm[:, q, 0:1], AF.Sqrt, r=[("sm", q)], w=[("sm", q)], bias=epsb[:, 0:1], scale=1.0 / 256)
                p.add("dve", lambda e, q=q: e.reciprocal(out=sm[:, q, 2:3], in_=sm[:, q, 1:2]), r=[("sm", q)], w=[("sm", q)])
                p.stt("dve", hs[:, q, :], hs[:, q, :], sm[:, q, 2:3], gb[:, 4:260], ALU.mult, ALU.mult,
                      r=[("hs", q), ("sm", q), "gb"], w=[("hs", q)])
                p.tt("dve", yb[:, q, :], hs[:, q, :], SO[:, ch, :], ALU.mult, r=[("hs", q), "SO"], w=[("yb", q)])
                p.dma("sp", yo[(ch - 2) * 128:(ch - 1) * 128, :], yb[:, q, :], r=[("yb", q)], w=[("yo", ch)], sem="yo%d" % q)

    seq = []
    for d in range(2):
        order = list(range(NCH)) if d == 0 else ([1, 0] + list(range(NCH - 1, 1, -1)))
        seq += [(d, ch) for ch in order]
    pre(0, *seq[0])
    for i, (d, ch) in enumerate(seq):
        if i == 0 or seq[i - 1][0] != d:
            p.memset("dve", C32[:], 0.0, w=["C32"])
            p.memset("dve", Cb[:], 0.0, w=["Cb"])
        if i + 1 < len(seq):
            pre(i + 1, *seq[i + 1])
        post(i, d, ch)
    return [("yo", ch) for ch in range(2, NCH)]


def build_p2a():
    nc = bass.Bass("TRN2", target_bir_lowering=False)
    p = Prog(nc)
    bank = [p.ps("bank%d" % b, [128, 512]) for b in range(8)]
    keys = emit_p2a(nc, p, bank)
    p.add("sp", lambda e: e.nop(), r=keys)
    p.emit()
    return nc, p


def to_cm(a):
    return a.reshape(128, 64, -1).transpose(1, 0, 2).reshape(8192, -1)


def from_cm(a):
    return a.reshape(64, 128, -1).transpose(1, 0, 2).reshape(8192, -1)


def consts_tri():
    ident = np.eye(128, dtype=np.float32)
    triU = np.triu(np.ones((128, 128), np.float32))
    triL = np.tril(np.ones((128, 128), np.float32))
    return np.ascontiguousarray(np.concatenate([ident, triU, triL], axis=1))


def p2a_inputs(inputs, h_all, hc_all, j):
    w_in = inputs["w_in"][0]
    cols = np.concatenate([np.arange(j * 128, (j + 1) * 128), 1024 + np.arange(j * 128, (j + 1) * 128),
                           2048 + np.arange(j * 256, (j + 1) * 256), 6144 + np.arange(4) * 8 + j,
                           4096 + np.arange(j * 256, (j + 1) * 256)])
    cwm = inputs["ml_conv_w"][0]
    cb = inputs["ml_conv_b"][0]
    cw = np.concatenate([cwm[:, j * 128:(j + 1) * 128].T, cwm[:, 1024 + j * 128:1024 + (j + 1) * 128].T,
                         cb[j * 128:(j + 1) * 128, None], cb[1024 + j * 128:1024 + (j + 1) * 128, None]], axis=1)
    gbn = np.concatenate([inputs["ml_gate_b"][0][:, j], inputs["ml_norm_w"][0][j * 256:(j + 1) * 256]])
    hcm = np.concatenate([hc_all, to_cm(h_all)], axis=0).T
    return {"hT": np.ascontiguousarray(hcm), "wml": np.ascontiguousarray(w_in[:, cols]), "cst": consts_tri(),
            "cw": np.ascontiguousarray(cw, dtype=np.float32),
            "gbn": np.ascontiguousarray(np.broadcast_to(gbn[None, :], (128, 260)), dtype=np.float32)}


def emit_p2b(nc, p, bank, pre=""):
    hT = dram(nc, pre + "hT", [D, NTOK], BF16, "ExternalInput")
    wss = dram(nc, pre + "wss", [D, 1296], F32, "ExternalInput")
    cst = dram(nc, pre + "cst", [128, 3 * 128], F32, "ExternalInput")
    cw = dram(nc, pre + "cw", [128, 36], F32, "ExternalInput")
    vecd = dram(nc, pre + "vec", [128, 552], F32, "ExternalInput")
    yo = dram(nc, "yssm", [8192, 512], BF16, "ExternalOutput")
    Ud = dram(nc, pre + "Ud", [768, UW2], BF16, "ExternalOutput")
    Zd = dram(nc, pre + "Zd", [8192, 512], BF16, "ExternalOutput")
    Yf = dram(nc, pre + "Yf", [8192, 512], F32, "ExternalOutput")
    cs = p.sb("cs", [128, 3, 128], F32)
    ident, triU, triL = cs[:, 0, :], cs[:, 1, :], cs[:, 2, :]
    identb = p.sb("identb", [128, 128], BF16)
    cwt = p.sb("cwt", [128, 36], F32)
    vec = p.sb("vec_sb", [128, 552], F32)
    ones = p.sb("ones", [128, 128], F32)
    epsb = p.sb("epsb", [128, 1], F32)
    Xtm = p.sb("Xtm", [128, NCH, 512], BF16)
    Btm = p.sb("Btm", [128, NCH, 128], BF16)
    BT = p.sb("BT", [128, NTOK], BF16)
    CT = p.sb("CT", [128, NTOK], BF16)
    dt = p.sb("dt", [128, NCH, 16], F32)
    p.dma("sp", cs[:].rearrange("p a b -> p (a b)"), cst, w=["cs"], sem="c0")
    p.dma("sp", cwt[:], cw, w=["cwt"], sem="c1")
    p.dma("sp", vec[:], vecd, w=["vec"], sem="c2")
    p.memset("dve", ones[:], 1.0, w=["ones"])
    p.memset("dve", epsb[:], EPS, w=["epsb"])
    p.copy("dve", identb[:], ident, r=["cs"], w=["identb"])

    stA = contextlib.ExitStack()
    W = p.sb("W", [128, KC, 1296], BF16, stack=stA)
    ht = p.sb("ht", [128, KC, 512], BF16, stack=stA)
    ev = p.sb("ev", [128, 2, 512], BF16, stack=stA)
    zb = p.sb("zb", [128, 2, 512], BF16, stack=stA)
    zt = p.sb("zt", [128, 64], BF16, stack=stA)
    dg = p.sb("dg", [128, 6, 5, 128], BF16, stack=stA)
    ut = p.sb("ut", [128, 2, 576], BF16, stack=stA)
    post = p.sb("post", [128, 2, 512], BF16, stack=stA)
    wr = wss.rearrange("(k p) n -> p k n", p=128)
    for k0 in range(0, KC, 4):
        p.dma("pool", W[:, k0:k0 + 4, :], wr[:, k0:k0 + 4, :], w=["W"], sem="W")
    p.memset("dve", zt[:], 0.0, w=["zt"])
    for f in range(6):
        for (a, n_) in ((0, 32), (288, 64), (8544, 32)):
            p.dma("sp", Ud[f * 128:(f + 1) * 128, a:a + n_], zt[:, 0:n_], r=["zt"], w=[("Ud", f)], sem="udz")
        for tap in range(5):
            p.ts("dve", dg[:, f, tap, :], ident, cwt[:, f * 5 + tap:f * 5 + tap + 1], None, ALU.mult, None,
                 r=["cs", "cwt"], w=["dg"])
    hr = hT.rearrange("(k p) n -> p k n", p=128)
    bi = 0
    ei = 0
    for t0 in range(0, NTOK, 512):
        n = min(512, NTOK - t0)
        p.dma("sp", ht[:, :, 0:n], hr[:, :, t0:t0 + n], w=["ht"], sem="ht")
        for f in range(6):
            b = bi % 4
            bi += 1
            q = ei % 2
            ei += 1
            for k in range(KC):
                p.mm(bank[b][:, 0:n], W[:, k, 512 + f * 128:512 + (f + 1) * 128], ht[:, k, 0:n], k == 0, k == KC - 1,
                     r=["W", "ht"], w=[("bank", b)])
            p.copy("act", ev[:, q, 0:n], bank[b][:, 0:n], r=[("bank", b)], w=[("ev", q)])
            for (a0, a1) in ([(0, 256), (256, 512)] if t0 == 0 else [(0, n)]):
                p0 = upos2(t0 + a0)
                p.dma("sp", Ud[f * 128:(f + 1) * 128, p0:p0 + a1 - a0], ev[:, q, a0:a1], r=[("ev", q)], w=[("Ud", f)], sem="ev%d" % q)
        for sub in range(n // 128):
            ch = t0 // 128 + sub
            b = bi % 4
            bi += 1
            b2 = 4 + b
            if ch >= 2:
                q = ei % 2
                ei += 1
                for k in range(KC):
                    p.mm(bank[b][:, 0:512], ht[:, k, sub * 128:(sub + 1) * 128], W[:, k, 0:512], k == 0, k == KC - 1,
                         r=["W", "ht"], w=[("bank", b)])
                p.act(zb[:, q, :], bank[b][:, 0:512], AF.Silu, r=[("bank", b)], w=[("zb", q)])
                p.dma("sp", Zd[(ch - 2) * 128:(ch - 1) * 128, :], zb[:, q, :], r=[("zb", q)], w=[("Zd", ch)], sem="zb%d" % q)
            for k in range(KC):
                p.mm(bank[b2][:, 0:16], ht[:, k, sub * 128:(sub + 1) * 128], W[:, k, 1280:1296], k == 0, k == KC - 1,
                     r=["W", "ht"], w=[("bank", b2)])
            p.tt("dve", dt[:, ch, :], bank[b2][:, 0:16], vec[:, 0:16], ALU.add, r=[("bank", b2), "vec"], w=["dt"])
    dtf = dt[:].rearrange("p c e -> p (c e)")
    p.act(dtf, dtf, AF.Exp, r=["dt"], w=["dt"])
    p.act(dtf, dtf, AF.Ln, r=["dt"], w=["dt"], bias=1.0)
    ui = 0
    for (s0, s1) in [(0, 256)] + [(a, a + 512) for a in range(256, NTOK, 512)]:
        n = s1 - s0
        p0 = upos2(s0)
        for f in range(6):
            q = ui % 2
            ui += 1
            b = bi % 4
            bi += 1
            p.dma("sp", ut[:, q, 0:n + 64], Ud[f * 128:(f + 1) * 128, p0 - 32:p0 + n + 32], r=[("Ud", f)], w=[("ut", q)], sem="ut%d" % q)
            for tap in range(5):
                p.mm(bank[b][:, 0:n], dg[:, f, tap, :], ut[:, q, 30 + tap:30 + tap + n], tap == 0, tap == 4, r=["dg", ("ut", q)], w=[("bank", b)])
            if f == 5:
                p.act(CT[:, s0:s1], bank[b][:, 0:n], AF.Silu, r=[("bank", b), "cwt"], w=["CT"], bias=cwt[:, 30 + f:31 + f])
                continue
            dst = BT[:, s0:s1] if f == 4 else post[:, q, 0:n]
            dkey = "BT" if f == 4 else ("post", q)
            p.act(dst, bank[b][:, 0:n], AF.Silu, r=[("bank", b), "cwt"], w=[dkey], bias=cwt[:, 30 + f:31 + f])
            for sub in range(n // 128):
                ch = s0 // 128 + sub
                b2 = 4 + (bi % 4)
                bi += 1
                p.mm(bank[b2][:, 0:128], dst[:, sub * 128:(sub + 1) * 128], identb[:], True, True, r=[dkey, "identb"], w=[("bank", b2)])
                if f == 4:
                    p.copy("dve", Btm[:, ch, :], bank[b2][:, 0:128], r=[("bank", b2)], w=["Btm"])
                else:
                    p.copy("dve", Xtm[:, ch, f * 128:(f + 1) * 128], bank[b2][:, 0:128], r=[("bank", b2)], w=["Xtm"])
    p.barrier()
    stA.close()

    Aneg = p.sb("Aneg", [128, 16], F32)
    dtA = p.sb("dtA", [128, NCH, 16], F32)
    bsb = p.sb("bsb", [128, 2, NCH, 8], F32)
    tot = p.sb("tot", [128, 2, NCH, 8], F32)
    eo = p.sb("eo", [128, 2, NCH, 8], F32)
    ws = p.sb("ws", [128, 2, NCH, 8], F32)
    dec = p.sb("dec", [128, 2, NCH, 8], F32)
    p.act(Aneg[:], vec[:, 16:32], AF.Exp, r=["vec"], w=["Aneg"])
    p.ts("dve", Aneg[:], Aneg[:], -1.0, None, ALU.mult, None, r=["Aneg"], w=["Aneg"])
    p.tt("dve", dtA[:], dt[:], Aneg[:].unsqueeze(1).broadcast_to([128, NCH, 16]), ALU.mult, r=["dt", "Aneg"], w=["dtA"])
    H = NCH // 2
    for d in range(2):
        tri = triU if d == 0 else triL
        for hf_ in range(2):
            src = dtA[:, hf_ * H:(hf_ + 1) * H, d * 8:(d + 1) * 8]
            b0, b1 = (d * 2 + hf_) % 4, 4 + (d * 2 + hf_) % 4
            p.mm(bank[b0][:, 0:H * 8].rearrange("p (c e) -> p c e", e=8), tri, src, True, True, r=["cs", "dtA"], w=[("bank", b0)])
            p.mm(bank[b1][:, 0:H * 8].rearrange("p (c e) -> p c e", e=8), ones[:], src, True, True, r=["ones", "dtA"], w=[("bank", b1)])
            p.copy("dve", bsb[:, d, hf_ * H:(hf_ + 1) * H, :], bank[b0][:, 0:H * 8].rearrange("p (c e) -> p c e", e=8),
                   r=[("bank", b0)], w=["bsb"])
            p.copy("dve", tot[:, d, hf_ * H:(hf_ + 1) * H, :], bank[b1][:, 0:H * 8].rearrange("p (c e) -> p c e", e=8),
                   r=[("bank", b1)], w=["tot"])
    fl = lambda t: t[:].rearrange("p d c e -> p (d c e)")
    p.act(fl(eo), fl(bsb), AF.Exp, r=["bsb"], w=["eo"])
    p.tt("dve", fl(ws), fl(tot), fl(bsb), ALU.subtract, r=["tot", "bsb"], w=["ws"])
    p.act(fl(ws), fl(ws), AF.Exp, r=["ws"], w=["ws"])
    p.act(fl(dec), fl(tot), AF.Exp, r=["tot"], w=["dec"])

    negb = p.sb("negb", [128, 2, NCH, 8], F32)
    p.ts("dve", fl(negb), fl(bsb), -1.0, None, ALU.mult, None, r=["bsb"], w=["negb"])
    S32 = p.sb("S32", [128, 512], F32)
    Sb = p.sb("Sb", [128, 512], BF16)
    CBm = p.sb("CBm", [128, 2, 128], F32)
    diagb = p.sb("diagb", [128, 2, 8, 128], F32)
    dm = p.sb("dm", [128, 2, 8, 128], BF16)
    G = p.sb("G", [128, 2, 8, 128], BF16)
    xdt = p.sb("xdt", [128, 3, 512], BF16)
    xw = p.sb("xw", [128, 3, 512], BF16)
    yis = p.sb("yis", [128, 2, 512], F32)
    sus = p.sb("sus", [128, 2, 512], F32)
    tmpd = p.sb("tmpd", [128, 512], F32)
    t2 = p.sb("t2", [128, 512], F32)
    t3 = p.sb("t3", [128, 512], BF16)
    ysb = p.sb("ysb", [128, 2, 512], F32)
    yfl = p.sb("yfl", [128, 3, 512], F32)
    zl = p.sb("zl", [128, 3, 512], BF16)
    ob = p.sb("ob", [128, 2, 512], BF16)
    sm = p.sb("sm", [128, 2, 4], F32)
    bc3 = lambda ap: ap.unsqueeze(2).broadcast_to([128, 8, 64])
    v3 = lambda ap: ap.rearrange("p (e c) -> p e c", e=8)

    def preA(i, d, ch):
        q, t = i % 2, i % 3
        lat = ch >= 2
        tk = slice(ch * 128, (ch + 1) * 128)
        dsl = slice(d * 8, (d + 1) * 8)
        p.tt("pool", v3(xdt[:, t, :]), v3(Xtm[:, ch, :]), bc3(dt[:, ch, dsl]), ALU.mult, r=["Xtm", "dt"], w=[("xdt", t)])
        p.tt("pool", v3(xw[:, t, :]), v3(xdt[:, t, :]), bc3(ws[:, d, ch, :]), ALU.mult, r=[("xdt", t), "ws"], w=[("xw", t)])
        if not lat:
            return
        rows = slice((ch - 2) * 128, (ch - 1) * 128)
        p.tt("pool", diagb[:, q], ident.unsqueeze(1).broadcast_to([128, 8, 128]),
             bsb[:, d, ch, :].unsqueeze(2).broadcast_to([128, 8, 128]), ALU.mult, r=["cs", "bsb"], w=[("diagb", q)])
        cbp = bank[7][:, q * 128:(q + 1) * 128]
        p.mm(cbp, BT[:, tk], CT[:, tk], True, True, r=["BT", "CT"], w=[("cbp", q)])
        for h2 in range(2):
            bb = 2 * q + h2
            p.mm(bank[bb][:, 0:512], ones[:], diagb[:, q, 4 * h2:4 * h2 + 4, :].rearrange("p e i -> p (e i)"), True, True,
                 r=["ones", ("diagb", q)], w=[("bank", bb)])
            for e4 in range(4):
                e = 4 * h2 + e4
                p.act(dm[:, q, e, :], bank[bb][:, e4 * 128:(e4 + 1) * 128], AF.Exp, r=[("bank", bb), "negb"], w=[("dm", q)],
                      bias=negb[:, d, ch, e:e + 1])
        if d == 1:
            p.dma("sp", yfl[:, t, :], Yf[rows, :], r=[("Yf", ch)], w=[("yfl", t)], sem="yl%d" % t)
            p.dma("sp", zl[:, t, :], Zd[rows, :], r=[("Zd", ch)], w=[("zl", t)], sem="zl%d" % t)
            p.tt("pool", v3(tmpd[:]), v3(Xtm[:, ch, :]), bc3(vec[:, 32:40]), ALU.mult, r=["Xtm", "vec"], w=["tmpd"])
            p.tt("pool", yfl[:, t, :], yfl[:, t, :], tmpd[:], ALU.add, r=["tmpd", ("yfl", t)], w=[("yfl", t)])

    def preB(i, d, ch):
        q, t = i % 2, i % 3
        tri = triU if d == 0 else triL
        lat = ch >= 2
        if lat:
            cbp = bank[7][:, q * 128:(q + 1) * 128]
            p.tt("dve", CBm[:, q, :], cbp, tri, ALU.mult, r=[("cbp", q), "cs"], w=[("CBm", q)])
            p.stt("dve", G[:, q], dm[:, q], 1.0, CBm[:, q, :].unsqueeze(1).broadcast_to([128, 8, 128]), ALU.min, ALU.mult,
                  r=[("dm", q), ("CBm", q)], w=[("G", q)])
            for e in range(8):
                p.mm(bank[4][:, e * 64:(e + 1) * 64], G[:, q, e, :], xdt[:, t, e * 64:(e + 1) * 64], True, True,
                     r=[("G", q), ("xdt", t)], w=[("bank", 4)])
            p.copy("act", yis[:, q, :], bank[4][:, 0:512], r=[("bank", 4)], w=[("yis", q)])
        p.mm(bank[6][:, 0:512], Btm[:, ch, :], xw[:, t, :], True, True, r=["Btm", ("xw", t)], w=[("bank", 6)])
        p.copy("act", sus[:, q, :], bank[6][:, 0:512], r=[("bank", 6)], w=[("sus", q)])

    def post(i, d, ch):
        q, t = i % 2, i % 3
        lat = ch >= 2
        tk = slice(ch * 128, (ch + 1) * 128)
        if lat:
            rows = slice((ch - 2) * 128, (ch - 1) * 128)
            p.mm(bank[5][:, 0:512], CT[:, tk], Sb[:], True, True, r=["CT", "Sb"], w=[("bank", 5)])
        p.tt("dve", v3(S32[:]), v3(S32[:]), bc3(dec[:, d, ch, :]), ALU.mult, r=["S32", "dec"], w=["S32"])
        p.tt("dve", S32[:], S32[:], sus[:, q, :], ALU.add, r=["S32", ("sus", q)], w=["S32"])
        p.copy("act", Sb[:], S32[:], r=["S32"], w=["Sb"])
        if lat:
            p.tt("dve", v3(t2[:]), v3(bank[5][:, 0:512]), bc3(eo[:, d, ch, :]), ALU.mult, r=[("bank", 5), "eo"], w=["t2"])
            p.tt("dve", ysb[:, q, :], t2[:], yis[:, q, :], ALU.add, r=["t2", ("yis", q)], w=[("ysb", q)])
            if d == 0:
                p.dma("sp", Yf[rows, :], ysb[:, q, :], r=[("ysb", q)], w=[("Yf", ch)], sem="yf%d" % q)
            else:
                p.tt("dve", ysb[:, q, :], ysb[:, q, :], yfl[:, t, :], ALU.add, r=[("ysb", q), ("yfl", t)], w=[("ysb", q)])
                p.tt("dve", ysb[:, q, :], ysb[:, q, :], zl[:, t, :], ALU.mult, r=[("ysb", q), ("zl", t)], w=[("ysb", q)])
                p.add("act", lambda e, q=q: e.activation(out=t3[:], in_=ysb[:, q, :], func=AF.Square, accum_out=sm[:, q, 0:1]),
                      r=[("ysb", q)], w=[("sm", q), "t3"])
                p.act(sm[:, q, 1:2], sm[:, q, 0:1], AF.Sqrt, r=[("sm", q)], w=[("sm", q)], bias=epsb[:, 0:1], scale=1.0 / 512)
                p.add("dve", lambda e, q=q: e.reciprocal(out=sm[:, q, 2:3], in_=sm[:, q, 1:2]), r=[("sm", q)], w=[("sm", q)])
                p.stt("dve", ob[:, q, :], ysb[:, q, :], sm[:, q, 2:3], vec[:, 40:552], ALU.mult, ALU.mult,
                      r=[("ysb", q), ("sm", q), "vec"], w=[("ob", q)])
                p.dma("sp", yo[rows, :], ob[:, q, :], r=[("ob", q)], w=[("yo", ch)], sem="yo%d" % q)

    seq = []
    for d in range(2):
        order = list(range(NCH)) if d == 0 else ([1, 0] + list(range(NCH - 1, 1, -1)))
        seq += [(d, ch) for ch in order]
    NS = len(seq)
    preA(0, *seq[0])
    preA(1, *seq[1])
    preB(0, *seq[0])
    for i, (d, ch) in enumerate(seq):
        if i == 0 or seq[i - 1][0] != d:
            p.memset("dve", S32[:], 0.0, w=["S32"])
            p.memset("dve", Sb[:], 0.0, w=["Sb"])
        if i + 2 < NS:
            preA(i + 2, *seq[i + 2])
        if i + 1 < NS:
            preB(i + 1, *seq[i + 1])
        post(i, d, ch)
    return [("yo", ch) for ch in range(2, NCH)]


def build_p2b():
    nc = bass.Bass("TRN2", target_bir_lowering=False)
    p = Prog(nc)
    bank = [p.ps("bank%d" % b, [128, 512]) for b in range(8)]
    keys = emit_p2b(nc, p, bank)
    p.add("sp", lambda e: e.nop(), r=keys)
    p.emit()
    return nc, p


def build_p2():
    nc = bass.Bass("TRN2", target_bir_lowering=False)
    p = Prog(nc)
    bank = [p.ps("bank%d" % b, [128, 512]) for b in range(8)]
    with p.scope("a_"):
        emit_p2a(nc, p, bank, "a_")
    with p.scope("b_"):
        keys = emit_p2b(nc, p, bank, "b_")
        p.add("sp", lambda e: e.nop(), r=keys)
    p.emit()
    return nc, p


def p2b_inputs(inputs, h_all, hc_all, g):
    w_in = inputs["w_in"][0]
    o = 6176
    cols = np.concatenate([o + np.arange(g * 512, (g + 1) * 512), o + 4096 + np.arange(g * 512, (g + 1) * 512),
                           o + 8192 + np.arange(g * 128, (g + 1) * 128), o + 9216 + np.arange(g * 128, (g + 1) * 128),
                           o + 10240 + np.arange(g * 8, (g + 1) * 8), o + 10304 + np.arange(g * 8, (g + 1) * 8)])
    cch = np.concatenate([np.arange(g * 512, (g + 1) * 512), 4096 + np.arange(g * 128, (g + 1) * 128),
                          5120 + np.arange(g * 128, (g + 1) * 128)])
    cwm = inputs["ssm_conv_w"][0][:, cch]
    cb = inputs["ssm_conv_b"][0][cch]
    cw = np.concatenate([cwm.reshape(5, 6, 128).transpose(2, 1, 0).reshape(128, 30), cb.reshape(6, 128).T], axis=1)
    hsl = slice(g * 8, (g + 1) * 8)
    vec = np.concatenate([inputs["ssm_dt_bias"][0][0, hsl], inputs["ssm_dt_bias"][0][1, hsl],
                          inputs["ssm_a_log"][0][0, hsl], inputs["ssm_a_log"][0][1, hsl],
                          inputs["ssm_d"][0][hsl], inputs["ssm_norm_w"][0][g * 512:(g + 1) * 512]])
    hrm = np.concatenate([hc_all, h_all], axis=0).T
    return {"hT": np.ascontiguousarray(hrm), "wss": np.ascontiguousarray(w_in[:, cols]), "cst": consts_tri(),
            "cw": np.ascontiguousarray(cw, dtype=np.float32),
            "vec": np.ascontiguousarray(np.broadcast_to(vec[None, :], (128, 552)), dtype=np.float32)}


def p2_inputs(inputs, h_all, hc_all, j):
    m = {"a_" + k: v for k, v in p2a_inputs(inputs, h_all, hc_all, j).items()}
    m.update({"b_" + k: v for k, v in p2b_inputs(inputs, h_all, hc_all, j).items()})
    return m


def emit_p3a(nc, p, bank, pre="", modd=None):
    T = 1024
    x1T = dram(nc, pre + "x1T", [D, T], F32, "ExternalInput")
    hTd = dram(nc, pre + "hT", [D, T], BF16, "ExternalInput")
    ymlT = dram(nc, pre + "ymlT", [D, T], BF16, "ExternalInput")
    yssT = dram(nc, pre + "yssT", [2 * D, T], BF16, "ExternalInput")
    if modd is None:
        modd = dram(nc, pre + "modfm", [128, 288], F32, "ExternalInput")
    bgd = dram(nc, pre + "bg", [128, 32], F32, "ExternalInput")
    wpm = dram(nc, pre + "wpm", [D, D], F32, "ExternalInput")
    wps = dram(nc, pre + "wps", [2 * D, D], F32, "ExternalInput")
    wgt = dram(nc, pre + "wgt", [D, 2 * D], F32, "ExternalInput")
    wo = dram(nc, pre + "wo", [D, D], F32, "ExternalInput")
    x2T = dram(nc, pre + "x2T", [D, T], F32, "ExternalOutput")
    c = setup_common(p, T, nslot=5, ffn=False, bank=bank)
    modfm = p.sb("modfm_sb", [128, 288], F32)
    bg = p.sb("bg_sb", [128, 32], F32)
    yml = p.sb("yml", [128, KC, 512], BF16)
    yss = p.sb("yss", [128, 2 * KC, 512], BF16)
    hT = p.sb("hTt", [128, KC, 512], BF16)
    mg = p.sb("mg", [128, KC, T], BF16)
    gs = p.sb("gs", [128, 2, 2, 512], F32)
    tu = p.sb("tu", [128, 2, 2, 512], F32)
    xt = p.sb("xt", [128, 2, 512], F32)
    p.dma("sp", modfm[:], modd, w=["modfm"], sem="c0")
    p.dma("sp", bg[:], bgd, w=["bg"], sem="c1")
    gmix = modfm[:, :].rearrange("p (c j) -> p c j", j=2)[:, 5 * KC:6 * KC, 0]
    r3 = lambda w_: w_.rearrange("(k p) n -> p k n", p=128)
    wpm_r, wps_r, wgt_r, wo_r = r3(wpm), r3(wps), r3(wgt), r3(wo)
    it = 0
    for n in range(2):
        ts_ = slice(n * 512, (n + 1) * 512)
        p.dma("sp", yml[:], r3(ymlT)[:, :, ts_], w=["yml"], sem="a0")
        p.dma("sp", yss[:], r3(yssT)[:, :, ts_], w=["yss"], sem="a1")
        p.dma("sp", hT[:], r3(hTd)[:, :, ts_], w=["hTt"], sem="a2")
        for dp in range(KC // 2):
            cs_ = slice(dp * 256, (dp + 1) * 256)
            s1, s2, s3 = next_slot(c), next_slot(c), next_slot(c)
            w1 = c.slot[s1][:, 0:4096].rearrange("p (k n) -> p k n", k=KC)
            w1g = c.slot[s1][:, 4096:8192].rearrange("p (k n) -> p k n", k=KC)
            w2 = c.slot[s2][:, :].rearrange("p (k n) -> p k n", k=2 * KC)
            w3 = c.slot[s3][:, 0:4096].rearrange("p (k n) -> p k n", k=KC)
            p.dma("pool", w1, wpm_r[:, :, cs_], w=[("slot", s1)], sem="slot%d" % s1)
            p.dma("pool", w1g, wgt_r[:, :, cs_], w=[("slot", s1)], sem="slot%d" % s1)
            p.dma("pool", w2, wps_r[:, :, cs_], w=[("slot", s2)], sem="slot%d" % s2)
            p.dma("pool", w3, wgt_r[:, :, D + dp * 256:D + (dp + 1) * 256], w=[("slot", s3)], sem="slot%d" % s3)
            for dd in range(2):
                d = dp * 2 + dd
                q = it % 2
                it += 1
                b0 = q * 4
                dsl = slice(dd * 128, (dd + 1) * 128)
                for k in range(KC):
                    p.mm(c.bank[b0][:, :], w1[:, k, dsl], yml[:, k, :], k == 0, k == KC - 1, r=[("slot", s1), "yml"], w=[("bank", b0)])
                for k in range(2 * KC):
                    p.mm(c.bank[b0 + 1][:, :], w2[:, k, dsl], yss[:, k, :], k == 0, k == 2 * KC - 1, r=[("slot", s2), "yss"], w=[("bank", b0 + 1)])
                for k in range(KC):
                    p.mm(c.bank[b0 + 2][:, :], w1g[:, k, dsl], hT[:, k, :], k == 0, k == KC - 1, r=[("slot", s1), "hTt"], w=[("bank", b0 + 2)])
                for k in range(KC):
                    p.mm(c.bank[b0 + 3][:, :], w3[:, k, dsl], hT[:, k, :], k == 0, k == KC - 1, r=[("slot", s3), "hTt"], w=[("bank", b0 + 3)])
                p.act(gs[:, q, 0, :], c.bank[b0 + 2][:, :], AF.Sigmoid, r=[("bank", b0 + 2), "bg"], w=[("gs", q)], bias=bg[:, d:d + 1])
                p.act(gs[:, q, 1, :], c.bank[b0 + 3][:, :], AF.Sigmoid, r=[("bank", b0 + 3), "bg"], w=[("gs", q)], bias=bg[:, KC + d:KC + d + 1])
                p.tt("dve", tu[:, q, 0, :], gs[:, q, 0, :], c.bank[b0][:, :], ALU.mult, r=[("gs", q), ("bank", b0)], w=[("tu", q)])
                p.tt("dve", tu[:, q, 1, :], gs[:, q, 1, :], c.bank[b0 + 1][:, :], ALU.mult, r=[("gs", q), ("bank", b0 + 1)], w=[("tu", q)])
                p.tt("dve", mg[:, d, ts_], tu[:, q, 0, :], tu[:, q, 1, :], ALU.add, r=[("tu", q)], w=[("mg", n)])
    for n in range(2):
        ts_ = slice(n * 512, (n + 1) * 512)
        for dp in range(KC // 2):
            s1 = next_slot(c)
            w1 = c.slot[s1][:, 0:4096].rearrange("p (k n) -> p k n", k=KC)
            p.dma("pool", w1, wo_r[:, :, dp * 256:(dp + 1) * 256], w=[("slot", s1)], sem="slot%d" % s1)
            for dd in range(2):
                d = dp * 2 + dd
                q = it % 2
                it += 1
                b0 = q
                p.dma("sp", xt[:, q, :], x1T[d * 128:(d + 1) * 128, ts_], w=[("xt", q)], sem="xt%d" % q)
                for k in range(KC):
                    p.mm(c.bank[b0][:, :], w1[:, k, dd * 128:(dd + 1) * 128], mg[:, k, ts_], k == 0, k == KC - 1,
                         r=[("slot", s1), ("mg", n)], w=[("bank", b0)])
                p.stt("dve", xt[:, q, :], c.bank[b0][:, :], gmix[:, d:d + 1], xt[:, q, :], ALU.mult, ALU.add,
                      r=[("bank", b0), ("xt", q), "modfm"], w=[("xt", q)])
                p.dma("sp", x2T[d * 128:(d + 1) * 128, ts_], xt[:, q, :], r=[("xt", q)], w=[("x2", d, n)], sem="xo%d" % q)
    return x2T, [("x2", d, n) for d in range(KC) for n in range(2)]


def build_p3a():
    nc = bass.Bass("TRN2", target_bir_lowering=False)
    p = Prog(nc)
    bank = [p.ps("bank%d" % b, [128, 512]) for b in range(8)]
    _, keys = emit_p3a(nc, p, bank)
    p.add("sp", lambda e: e.nop(), r=keys)
    p.emit()
    return nc, p


def emit_p3b(nc, p, bank, pre="", x2T=None, modd=None):
    T = 1024
    if x2T is None:
        x2T = dram(nc, pre + "x2T", [D, T], F32, "ExternalInput")
    if modd is None:
        modd = dram(nc, pre + "modfm", [128, 288], F32, "ExternalInput")
    nwd = dram(nc, pre + "nw", [128, 4 * KC], F32, "ExternalInput")
    wg = dram(nc, pre + "wg", [D, DFF], F32, "ExternalInput")
    wu = dram(nc, pre + "wu", [D, DFF], F32, "ExternalInput")
    wd = dram(nc, pre + "wd", [DFF, D], F32, "ExternalInput")
    x3T = dram(nc, pre + "x3T", [D, T], F32, "ExternalOutput")
    outT = dram(nc, "outT", [D, T], F32, "ExternalOutput")
    c = setup_common(p, T, bank=bank)
    c.epsb = p.sb("epsb", [128, 1], F32)
    p.memset("dve", c.epsb[:], EPS, w=["epsb"])
    modfm = p.sb("modfm_sb", [128, 288], F32)
    nw = p.sb("nwt", [128, 4, KC], F32)
    ob = p.sb("obuf", [128, 2, T], F32)
    p.dma("sp", modfm[:], modd, w=["modfm"], sem="c0")
    p.dma("sp", nw[:].rearrange("p s k -> p (s k)"), nwd, w=["nw"], sem="c1")
    A, B, G, gt = make_AB(p, c, modfm, nw, 2)
    p.ts("dve", G[:], gt, 0.5, None, ALU.mult, None, r=["modfm"], w=["modAB"])
    Ao = p.sb("Ao", [128, KC, 2], F32)
    for j in range(2):
        p.copy("dve", Ao[:, :, j], nw[:, 3, :], r=["nw"], w=["modAB"])
    tiles = [(0, 512), (512, 1024)]
    segs = [(0, 1024, 0)]
    ffn_stage(p, c, "f2", tiles, segs, x2T, A, B, G, wg, wu, wd, Ao, None, x3T, outT, ob, hout_rot=2)
    return [("f2_hdram", k) for k in range(KC)]


def build_p3b():
    nc = bass.Bass("TRN2", target_bir_lowering=False)
    p = Prog(nc)
    bank = [p.ps("bank%d" % b, [128, 512]) for b in range(8)]
    keys = emit_p3b(nc, p, bank)
    p.add("sp", lambda e: e.nop(), r=keys)
    p.emit()
    return nc, p


def build_p3():
    nc = bass.Bass("TRN2", target_bir_lowering=False)
    p = Prog(nc)
    bank = [p.ps("bank%d" % b, [128, 512]) for b in range(8)]
    modd = dram(nc, "modfm", [128, 288], F32, "ExternalInput")
    with p.scope("a_"):
        x2T, _ = emit_p3a(nc, p, bank, "a_", modd)
    with p.scope("b_"):
        keys = emit_p3b(nc, p, bank, "b_", x2T=x2T, modd=modd)
        p.add("sp", lambda e: e.nop(), r=keys)
    p.emit()
    return nc, p


def _run(nc, in_maps):
    return run_bass_kernel_spmd(nc, in_maps, core_ids=list(range(len(in_maps)))).results


def kernel(**inputs):
    bf = ml_dtypes.bfloat16
    r1 = run_p1(inputs)
    x1T = [np.asarray(r["x1T"]) for r in r1]
    hT2 = [np.asarray(r["hT2"]) for r in r1]
    modfm = np.asarray(r1[0]["modfm"])
    h_all = np.concatenate([h[:, :1024].T for h in hT2], axis=0)
    hc_all = np.concatenate([h[:, 1024:].T for h in hT2], axis=0)
    nc, _ = build_p2()
    r2 = _run(nc, [p2_inputs(inputs, h_all, hc_all, j) for j in range(NCORES)])
    yml = np.concatenate([from_cm(np.asarray(r["yml"])) for r in r2], axis=1)
    yss = np.concatenate([np.asarray(r["yssm"]) for r in r2], axis=1)
    nc, _ = build_p3()
    nw = np.concatenate([_fm(inputs["norm_w"][0, s]) for s in range(3)] + [_fm(inputs["final_norm_w"])], axis=1)
    common = {"modfm": modfm, "a_bg": _fm(inputs["b_gate"][0]),
              "a_wpm": np.ascontiguousarray(inputs["w_proj_ml"][0]), "a_wps": np.ascontiguousarray(inputs["w_proj_ssm"][0]),
              "a_wgt": np.ascontiguousarray(inputs["w_gate"][0]), "a_wo": np.ascontiguousarray(inputs["w_out"][0]),
              "b_nw": np.ascontiguousarray(nw),
              "b_wg": np.ascontiguousarray(inputs["ffn_w_gate"][0, 1]), "b_wu": np.ascontiguousarray(inputs["ffn_w_up"][0, 1]),
              "b_wd": np.ascontiguousarray(inputs["ffn_w_down"][0, 1])}
    maps = []
    for i in range(NCORES):
        sl = slice(1024 * i, 1024 * (i + 1))
        m = dict(common)
        m["a_x1T"] = np.ascontiguousarray(x1T[i][:, :1024])
        m["a_hT"] = np.ascontiguousarray(hT2[i][:, :1024])
        m["a_ymlT"] = np.ascontiguousarray(yml[sl].T)
        m["a_yssT"] = np.ascontiguousarray(yss[sl].T)
        maps.append(m)
    r3b = _run(nc, maps)
    out = np.concatenate([np.asarray(r["outT"]).T for r in r3b], axis=0)
    return np.ascontiguousarray(out[None].astype(np.float32))
```

```python
import contextlib
import numpy as np
import ml_dtypes
import concourse.bass as bass
import concourse.mybir as mybir
from concourse.bass_utils import run_bass_kernel_spmd

F32 = mybir.dt.float32
BF16 = mybir.dt.bfloat16
AF = mybir.ActivationFunctionType
ALU = mybir.AluOpType
AX = mybir.AxisListType

D = 2048
KC = 16
DFF = 5632
FC = 44
EPS = 1e-6
NCORES = 8

ENGS = ("pe", "act", "dve", "pool", "sp")


class Op:
    __slots__ = ("eng", "fn", "deps", "isdma", "sem", "val", "inc")

    def __init__(self, eng, fn, isdma):
        self.eng = eng
        self.fn = fn
        self.deps = []
        self.isdma = isdma
        self.sem = None
        self.val = None
        self.inc = False


class Prog:
    def __init__(self, nc):
        self.nc = nc
        self.ops = {e: [] for e in ENGS}
        self.keys = {}
        self.stack = contextlib.ExitStack()
        self.dram_n = 0
        self.extra_r = []
        self.nbar = 0
        self.prefix = ""
        self.bar_t = self.sb("bar_t", [128, 8], F32)

    def sb(self, name, shape, dtype, stack=None):
        return (stack or self.stack).enter_context(self.nc.sbuf_tensor(self.prefix + name, list(shape), dtype))

    def ps(self, name, shape, dtype=F32, stack=None):
        return (stack or self.stack).enter_context(self.nc.psum_tensor(self.prefix + name, list(shape), dtype))

    @contextlib.contextmanager
    def scope(self, prefix):
        old_stack, old_prefix = self.stack, self.prefix
        sub = contextlib.ExitStack()
        self.stack, self.prefix = sub, prefix
        try:
            yield
        finally:
            self.barrier()
            sub.close()
            self.stack, self.prefix = old_stack, old_prefix

    def _track(self, op, r, w, after=()):
        deps = set(after)
        for k in r:
            st = self.keys.get(k)
            if st is None:
                st = self.keys[k] = [None, {}, []]
            if st[0] is not None:
                deps.add(st[0])
            if op.isdma:
                st[2].append(op)
            else:
                st[1][op.eng] = op
        for k in w:
            st = self.keys.get(k)
            if st is None:
                st = self.keys[k] = [None, {}, []]
            if st[0] is not None:
                deps.add(st[0])
            for rd in st[1].values():
                deps.add(rd)
            for rd in st[2]:
                deps.add(rd)
            st[0] = op
            st[1] = {}
            st[2] = []
        deps.discard(op)
        for d in deps:
            if d.eng == "pe" and op.eng == "pe" and not d.isdma and not op.isdma:
                continue
            op.deps.append(d)
            d.inc = True

    def add(self, eng, fn, r=(), w=(), after=()):
        op = Op(eng, fn, False)
        r = list(r) + self.extra_r
        self._track(op, r, w, after)
        self.ops[eng].append(op)
        return op

    def dma(self, eng, out, in_, r=(), w=(), sem=None, after=(), **kw):
        op = Op(eng, (lambda e, out=out, in_=in_, kw=kw: e.dma_start(out=out, in_=in_, **kw)), True)
        op.sem = ("dma", sem)
        op.inc = True
        r = list(r) + self.extra_r
        self._track(op, r, w, after)
        self.ops[eng].append(op)
        return op

    def coll(self, kind, src, dst, r, w, sem):
        groups = [list(range(NCORES))]
        op = Op("pool", (lambda e: e.collective_compute(kind, ALU.bypass, replica_groups=groups, ins=[src], outs=[dst])), True)
        op.sem = ("dma", sem)
        op.inc = True
        self._track(op, list(r) + self.extra_r, w)
        self.ops["pool"].append(op)
        return op

    def mm(self, out, lhsT, rhs, start, stop, r, w):
        return self.add("pe", lambda e: e.matmul(out, lhsT, rhs, start=start, stop=stop), r=r, w=w)

    def act(self, out, in_, func, r, w, bias=0.0, scale=1.0, eng="act"):
        return self.add(eng, lambda e: e.activation(out=out, in_=in_, func=func, bias=bias, scale=scale), r=r, w=w)

    def tt(self, eng, out, in0, in1, op, r, w):
        return self.add(eng, lambda e: e.tensor_tensor(out=out, in0=in0, in1=in1, op=op), r=r, w=w)

    def ts(self, eng, out, in0, s1, s2, op0, op1, r, w):
        if s2 is None:
            return self.add(eng, lambda e: e.tensor_scalar(out, in0, s1, None, op0), r=r, w=w)
        return self.add(eng, lambda e: e.tensor_scalar(out, in0, s1, s2, op0, op1), r=r, w=w)

    def stt(self, eng, out, in0, scalar, in1, op0, op1, r, w):
        return self.add(eng, lambda e: e.scalar_tensor_tensor(out=out, in0=in0, scalar=scalar, in1=in1, op0=op0, op1=op1), r=r, w=w)

    def copy(self, eng, out, in_, r, w):
        if eng == "act":
            return self.add(eng, lambda e: e.activation(out=out, in_=in_, func=AF.Identity), r=r, w=w)
        return self.add(eng, lambda e: e.tensor_copy(out=out, in_=in_), r=r, w=w)

    def memset(self, eng, ap, val, w):
        return self.add(eng, lambda e: e.memset(ap, val), w=w)

    def barrier(self):
        allkeys = list(self.keys.keys())
        self.extra_r = []
        tok = ("bar", self.nbar)
        i = self.nbar
        self.nbar += 1
        self.add("dve", lambda e: e.memset(self.bar_t[:, i % 8:i % 8 + 1], 0.0), r=allkeys, w=allkeys + [tok])
        self.extra_r = [tok]

    def emit(self):
        nc = self.nc
        counts = {}
        for e in ENGS:
            for op in self.ops[e]:
                if op.isdma:
                    k = op.sem
                    counts[k] = counts.get(k, 0) + 16
                    op.val = counts[k]
                elif op.inc:
                    k = ("eng", e)
                    op.sem = k
                    counts[k] = counts.get(k, 0) + 1
                    op.val = counts[k]
        sems = {}
        st = contextlib.ExitStack()
        for i, k in enumerate(counts.keys()):
            sems[k] = st.enter_context(nc.semaphore("s%d" % i))
        self.maxcount = max(counts.values()) if counts else 0
        self.nsems = len(counts)
        engobj = {"pe": "tensor", "act": "scalar", "dve": "vector", "pool": "gpsimd", "sp": "sync"}

        def run(e, eo):
            waited = {}
            for op in self.ops[e]:
                need = {}
                for d in op.deps:
                    if need.get(d.sem, 0) < d.val:
                        need[d.sem] = d.val
                for k, v in need.items():
                    if waited.get(k, 0) < v:
                        eo.wait_ge(sems[k], v)
                        waited[k] = v
                ins = op.fn(eo)
                if op.inc:
                    ins.then_inc(sems[op.sem], 16 if op.isdma else 1)

        with nc.Block() as block:
            for e in ENGS:
                if not self.ops[e]:
                    continue
                getattr(block, engobj[e])(lambda eo, e=e: run(e, eo))
        st.close()
        self.stack.close()


def dram(nc, name, shape, dtype, kind):
    return nc.dram_tensor(name, list(shape), dtype, kind=kind).ap()


class Ctx:
    pass


def setup_common(p, T, nslot=3, ffn=True, bank=None):
    c = Ctx()
    c.T = T
    c.bank = bank if bank is not None else [p.ps("bank%d" % b, [128, 512]) for b in range(8)]
    c.NSLOT = nslot
    c.slot = [p.sb("slot%d" % s, [128, 8192], BF16) for s in range(c.NSLOT)]
    c.slot_i = 0
    c.ones = p.sb("ones", [128, 128], F32)
    p.memset("dve", c.ones[:], 1.0, w=["ones"])
    if ffn:
        c.xb = p.sb("xb", [128, 2, T], F32)
        c.xb_i = 0
        c.rstd = p.sb("rstd", [128, T], F32)
        c.sg = p.sb("sg", [128, 2, 512], F32)
        c.sq = p.sb("sq", [128, 2, 512], F32)
        c.sq_i = 0
        c.hT = p.sb("hT", [128, KC, T], BF16)
        c.aT = p.sb("aT", [128, FC, T], BF16)
    return c


def next_slot(c):
    s = c.slot_i % c.NSLOT
    c.slot_i += 1
    return s


def ffn_stage(p, c, name, tiles, segs, x_in, A_in, B_in, HG, wg, wu, wd, A_out, B_out,
              x_out, h_out, h_out_sb, interleave=None, hout_rot=0):
    T = c.T
    nt = len(tiles)
    ssb = [5, 6, 7]

    def segs_in(n0, n1):
        out = []
        for (c0, c1, j) in segs:
            a, b = max(c0, n0), min(c1, n1)
            if a < b:
                out.append((a, b, j))
        return out

    def load_x(src, k, tag):
        b = c.xb_i % 2
        c.xb_i += 1
        p.dma("sp", c.xb[:, b, :], src[k * 128:(k + 1) * 128, :], r=[(tag, k)], w=[("xb", b)], sem="xb%d" % b)
        return b

    def sumsq_accum(src_ap_fn, srckeys, k, first, last):
        for n, (n0, n1) in enumerate(tiles):
            q = c.sq_i % 2
            c.sq_i += 1
            w_ = n1 - n0
            p.act(c.sq[:, q, 0:w_], src_ap_fn(n0, n1), AF.Square, r=srckeys, w=[("sq", q)])
            p.mm(c.bank[ssb[n]][:, 0:w_], c.ones[:], c.sq[:, q, 0:w_], first, last,
                 r=["ones", ("sq", q)], w=[("bank", ssb[n])])

    def make_rstd():
        for n, (n0, n1) in enumerate(tiles):
            w_ = n1 - n0
            p.act(c.rstd[:, n0:n1], c.bank[ssb[n]][:, 0:w_], AF.Sqrt, r=[("bank", ssb[n])], w=[("rstd", n)],
                  bias=c.epsb[:, 0:1], scale=1.0 / D)
            p.add("dve", lambda e, n0=n0, n1=n1: e.reciprocal(out=c.rstd[:, n0:n1], in_=c.rstd[:, n0:n1]),
                  r=[("rstd", n)], w=[("rstd", n)])

    rstd_keys = [("rstd", n) for n in range(nt)]

    def apply_norm(b, k, A, B, dst, dstkey, kk=None):
        if kk is None:
            kk = k
        p.tt("dve", c.xb[:, b, :], c.xb[:, b, :], c.rstd[:, :], ALU.mult, r=[("xb", b)] + rstd_keys, w=[("xb", b)])
        for (c0, c1, j) in segs:
            bias = B[:, k, j:j + 1] if B is not None else 0.0
            p.act(dst[:, kk, c0:c1], c.xb[:, b, c0:c1], AF.Identity, r=[("xb", b), "modAB"], w=[(dstkey, kk)],
                  bias=bias, scale=A[:, k, j:j + 1])

    for k in range(KC):
        b = load_x(x_in, k, name + "_xin")
        sumsq_accum(lambda n0, n1, b=b: c.xb[:, b, n0:n1], [("xb", b)], k, k == 0, k == KC - 1)
    make_rstd()
    for k in range(KC):
        b = load_x(x_in, k, name + "_xin")
        apply_norm(b, k, A_in, B_in, c.hT, "hT")

    wg_r = wg.rearrange("(k p) n -> p k n", p=128)
    wu_r = wu.rearrange("(k p) n -> p k n", p=128)
    gu_i = 0
    for fb in range(FC // 2):
        s = next_slot(c)
        sl = c.slot[s]
        wgs = sl[:, 0:KC * 256].rearrange("p (k n) -> p k n", k=KC)
        wus = sl[:, KC * 256:2 * KC * 256].rearrange("p (k n) -> p k n", k=KC)
        p.dma("pool", wgs, wg_r[:, :, fb * 256:(fb + 1) * 256], w=[("slot", s)], sem="slot%d" % s)
        p.dma("pool", wus, wu_r[:, :, fb * 256:(fb + 1) * 256], w=[("slot", s)], sem="slot%d" % s)
        for ff in range(2):
            f = fb * 2 + ff
            for n, (n0, n1) in enumerate(tiles):
                w_ = n1 - n0
                par = gu_i % 2
                gu_i += 1
                gb, ub = par * 2, par * 2 + 1
                for k in range(KC):
                    p.mm(c.bank[gb][:, 0:w_], wgs[:, k, ff * 128:(ff + 1) * 128], c.hT[:, k, n0:n1], k == 0, k == KC - 1,
                         r=[("slot", s), ("hT", k)], w=[("bank", gb)])
                for k in range(KC):
                    p.mm(c.bank[ub][:, 0:w_], wus[:, k, ff * 128:(ff + 1) * 128], c.hT[:, k, n0:n1], k == 0, k == KC - 1,
                         r=[("slot", s), ("hT", k)], w=[("bank", ub)])
                p.act(c.sg[:, par, 0:w_], c.bank[gb][:, 0:w_], AF.Silu, r=[("bank", gb)], w=[("sg", par)])
                p.tt("dve", c.aT[:, f, n0:n1], c.sg[:, par, 0:w_], c.bank[ub][:, 0:w_], ALU.mult,
                     r=[("sg", par), ("bank", ub)], w=[("aT", f)])
        if interleave is not None:
            interleave(fb)

    wd_r = wd.rearrange("(f p) n -> p f n", p=128)
    HF = FC // 2
    dn_i = 0
    for db in range(KC // 2):
        ss_ = []
        for half in range(2):
            s = next_slot(c)
            ws = c.slot[s][:, 0:HF * 256].rearrange("p (f n) -> p f n", f=HF)
            p.dma("pool", ws, wd_r[:, half * HF:(half + 1) * HF, db * 256:(db + 1) * 256], w=[("slot", s)], sem="slot%d" % s)
            ss_.append((s, ws))
        for dd in range(2):
            d = db * 2 + dd
            b = load_x(x_in, d, name + "_xin")
            for n, (n0, n1) in enumerate(tiles):
                w_ = n1 - n0
                ob = dn_i % 2
                dn_i += 1
                for f in range(FC):
                    s, ws = ss_[f // HF]
                    p.mm(c.bank[ob][:, 0:w_], ws[:, f % HF, dd * 128:(dd + 1) * 128], c.aT[:, f, n0:n1], f == 0, f == FC - 1,
                         r=[("slot", s), ("aT", f)], w=[("bank", ob)])
                for (a0, a1, j) in segs_in(n0, n1):
                    p.stt("dve", c.xb[:, b, a0:a1], c.bank[ob][:, a0 - n0:a1 - n0], HG[:, d, j:j + 1], c.xb[:, b, a0:a1],
                          ALU.mult, ALU.add, r=[("bank", ob), ("xb", b), "modAB"], w=[("xb", b)])
            sumsq_accum(lambda n0, n1, b=b: c.xb[:, b, n0:n1], [("xb", b)], d, d == 0, d == KC - 1)
            p.dma("sp", x_out[d * 128:(d + 1) * 128, :], c.xb[:, b, :], r=[("xb", b)], w=[(name + "_xout", d)], sem=name + "_xo")
    make_rstd()
    for k in range(KC):
        b = load_x(x_out, k, name + "_xout")
        kk = k % hout_rot if hout_rot else k
        apply_norm(b, k, A_out, B_out, h_out_sb, name + "_hout", kk)
        p.dma("sp", h_out[k * 128:(k + 1) * 128, :], h_out_sb[:, kk, :], r=[(name + "_hout", kk)], w=[(name + "_hdram", k)],
              sem=name + "_ho")


def make_AB(p, c, modfm, nw, sub, want_out=None):
    A = p.sb("A%d" % sub, [128, KC, 2], F32)
    B = p.sb("B%d" % sub, [128, KC, 2], F32)
    G = p.sb("G%d" % sub, [128, KC, 2], F32)
    m = modfm[:, :].rearrange("p (c j) -> p c j", j=2)
    sh = m[:, (3 * sub) * KC:(3 * sub + 1) * KC, :]
    sc = m[:, (3 * sub + 1) * KC:(3 * sub + 2) * KC, :]
    gt = m[:, (3 * sub + 2) * KC:(3 * sub + 3) * KC, :]
    for j in range(2):
        p.stt("dve", A[:, :, j], sc[:, :, j], 1.0, nw[:, sub, :], ALU.add, ALU.mult, r=["modfm", "nw"], w=["modAB"])
    p.copy("dve", B[:], sh, r=["modfm"], w=["modAB"])
    return A, B, G, gt


def build_p1():
    nc = bass.Bass("TRN2", target_bir_lowering=False)
    T = 1056
    xT = dram(nc, "xT", [D, T], F32, "ExternalInput")
    c2 = dram(nc, "c2", [128, KC * 2], F32, "ExternalInput")
    w_ada = dram(nc, "w_ada", [D, 9 * D], F32, "ExternalInput")
    b_fm = dram(nc, "b_fm", [128, 144], F32, "ExternalInput")
    nwd = dram(nc, "nw", [128, 3 * KC], F32, "ExternalInput")
    wg = dram(nc, "wg", [D, DFF], F32, "ExternalInput")
    wu = dram(nc, "wu", [D, DFF], F32, "ExternalInput")
    wd = dram(nc, "wd", [DFF, D], F32, "ExternalInput")
    x1T = dram(nc, "x1T", [D, T], F32, "ExternalOutput")
    hT2 = dram(nc, "hT2", [D, T], BF16, "ExternalOutput")
    modo = dram(nc, "modfm", [128, 288], F32, "ExternalOutput")

    p = Prog(nc)
    c = setup_common(p, T)
    c.epsb = p.sb("epsb", [128, 1], F32)
    p.memset("dve", c.epsb[:], EPS, w=["epsb"])
    c2t = p.sb("c2t", [128, KC * 2], F32)
    sc2 = p.sb("sc2", [128, KC, 2], BF16)
    bfm = p.sb("bfm", [128, 144], F32)
    nw = p.sb("nwt", [128, 3, KC], F32)
    modfm = p.sb("modfm_sb", [128, 288], F32)
    p.dma("sp", c2t[:], c2, w=["c2t"], sem="c0")
    p.dma("sp", bfm[:], b_fm, w=["bfm"], sem="c1")
    p.dma("sp", nw[:].rearrange("p s k -> p (s k)"), nwd, w=["nw"], sem="c2")
    p.act(sc2[:].rearrange("p k j -> p (k j)"), c2t[:], AF.Silu, r=["c2t"], w=["sc2"])

    wa_r = w_ada.rearrange("(k p) n -> p k n", p=128)
    modps = c.bank[4]

    def mod_block(blk):
        s = next_slot(c)
        ws = c.slot[s][:, :].rearrange("p (k n) -> p k n", k=KC)
        p.dma("pool", ws, wa_r[:, :, blk * 512:(blk + 1) * 512], w=[("slot", s)], sem="slot%d" % s)
        for cc in range(4):
            ch = blk * 4 + cc
            for k in range(KC):
                p.mm(modps[:, 2 * ch:2 * ch + 2], ws[:, k, cc * 128:(cc + 1) * 128], sc2[:, k, :], k == 0, k == KC - 1,
                     r=[("slot", s), "sc2"], w=[("bank", 4)])

    def mod_finish(c0, c1):
        for j in range(2):
            src = modps[:, 2 * c0:2 * c1].rearrange("p (c j) -> p c j", j=2)[:, :, j]
            dst = modfm[:, 2 * c0:2 * c1].rearrange("p (c j) -> p c j", j=2)[:, :, j]
            p.tt("dve", dst, src, bfm[:, c0:c1], ALU.add, r=[("bank", 4), "bfm"], w=["modfm"])

    for blk in range(8):
        mod_block(blk)
    mod_finish(0, 32)
    A1, B1, G1, gt1 = make_AB(p, c, modfm, nw, 0)

    tiles = [(0, 352), (352, 704), (704, 1056)]
    segs = [(0, 1024, 0), (1024, 1056, 1)]
    A2 = p.sb("A2x", [128, KC, 2], F32)
    B2 = p.sb("B2x", [128, KC, 2], F32)

    pending = list(range(8, 36))

    def inter2(fb):
        if not pending:
            return
        for _ in range(2):
            if pending:
                mod_block(pending.pop(0))
        if not pending:
            mod_finish(32, 144)
            p.ts("dve", G1[:], gt1, 0.5, None, ALU.mult, None, r=["modfm"], w=["modAB"])
            m = modfm[:, :].rearrange("p (c j) -> p c j", j=2)
            for j in range(2):
                p.stt("dve", A2[:, :, j], m[:, 4 * KC:5 * KC, j], 1.0, nw[:, 1, :], ALU.add, ALU.mult,
                      r=["modfm", "nw"], w=["modAB"])
            p.copy("dve", B2[:], m[:, 3 * KC:4 * KC, :], r=["modfm"], w=["modAB"])
            p.dma("sp", modo, modfm[:], r=["modfm"], w=["modo"], sem="modo")

    ffn_stage(p, c, "f1", tiles, segs, xT, A1, B1, G1, wg, wu, wd, A2, B2, x1T, hT2, c.hT, interleave=inter2)
    p.add("sp", lambda e: e.nop(), r=["modo"] + [("f1_hdram", k) for k in range(KC)] + [("f1_xout", k) for k in range(KC)])
    p.emit()
    return nc, p


def _fm(v):
    return np.ascontiguousarray(v.reshape(-1, 128).T)


def run_p1(inputs):
    x = inputs["x"][0]
    ctx = inputs["ctx"][0]
    nc, p = build_p1()
    c2 = np.stack([_fm(inputs["c"][0]), _fm(inputs["c_ctx"])], axis=-1).reshape(128, 32)
    nw = np.stack([_fm(inputs["norm_w"][0, s]) for s in range(3)], axis=1).reshape(128, 48)
    common = {
        "c2": np.ascontiguousarray(c2, dtype=np.float32),
        "w_ada": np.ascontiguousarray(inputs["w_ada"][0]),
        "b_fm": _fm(inputs["b_ada"][0]),
        "nw": np.ascontiguousarray(nw),
        "wg": np.ascontiguousarray(inputs["ffn_w_gate"][0, 0]),
        "wu": np.ascontiguousarray(inputs["ffn_w_up"][0, 0]),
        "wd": np.ascontiguousarray(inputs["ffn_w_down"][0, 0]),
    }
    in_maps = []
    for i in range(NCORES):
        xt = np.concatenate([x[1024 * i:1024 * (i + 1)], ctx[32 * i:32 * (i + 1)]], axis=0).T
        m = dict(common)
        m["xT"] = np.ascontiguousarray(xt)
        in_maps.append(m)
    res = run_bass_kernel_spmd(nc, in_maps, core_ids=list(range(NCORES)))
    return res.results


NTOK = 8448
NCH = 66
UW = 8456


def upos(t):
    return 2 + t if t < 256 else 6 + t


UW2 = 8576


def upos2(t):
    return 32 + t if t < 256 else 96 + t


def emit_p2a(nc, p, bank, pre=""):
    hT = dram(nc, pre + "hT", [D, NTOK], BF16, "ExternalInput")
    wml = dram(nc, pre + "wml", [D, 772], F32, "ExternalInput")
    cst = dram(nc, pre + "cst", [128, 3 * 128], F32, "ExternalInput")
    cw = dram(nc, pre + "cw", [128, 2 * 5 + 2], F32, "ExternalInput")
    gbn = dram(nc, pre + "gbn", [128, 4 + 256], F32, "ExternalInput")
    yo = dram(nc, "yml", [8192, 256], BF16, "ExternalOutput")
    cs = p.sb("cs", [128, 3, 128], F32)
    ident, triU, triL = cs[:, 0, :], cs[:, 1, :], cs[:, 2, :]
    identb = p.sb("identb", [128, 128], BF16)
    cwt = p.sb("cwt", [128, 12], F32)
    gb = p.sb("gb", [128, 260], F32)
    ones = p.sb("ones", [128, 128], F32)
    epsb = p.sb("epsb", [128, 1], F32)
    QT = p.sb("QT", [128, NTOK], BF16)
    KT = p.sb("KT", [128, NTOK], BF16)
    Ktm = p.sb("Ktm", [128, NCH, 128], BF16)
    Va = p.sb("Va", [128, NCH, 257], BF16)
    SO = p.sb("SO", [128, NCH, 256], BF16)
    gat = p.sb("gat", [128, NCH, 4], F32)
    p.dma("sp", cs[:].rearrange("p a b -> p (a b)"), cst, w=["cs"], sem="c0")
    p.dma("sp", cwt[:], cw, w=["cwt"], sem="c1")
    p.dma("sp", gb[:], gbn, w=["gb"], sem="c2")
    p.memset("dve", ones[:], 1.0, w=["ones"])
    p.memset("dve", epsb[:], EPS, w=["epsb"])
    p.copy("dve", identb[:], ident, r=["cs"], w=["identb"])
    p.memset("dve", Va[:, :, 256:257], 1.0, w=["Va"])

    stA = contextlib.ExitStack()
    W = p.sb("W", [128, KC, 772], BF16, stack=stA)
    ht2 = p.sb("ht", [128, 2, KC, 256], BF16, stack=stA)
    U = p.sb("U", [128, 2, UW], BF16, stack=stA)
    dg = p.sb("dg", [128, 2, 5, 128], BF16, stack=stA)
    wr = wml.rearrange("(k p) n -> p k n", p=128)
    for k0 in range(0, KC, 4):
        p.dma("pool", W[:, k0:k0 + 4, :], wr[:, k0:k0 + 4, :], w=["W"], sem="W")
    p.memset("dve", U[:], 0.0, w=["U"])
    for f in range(2):
        for tap in range(5):
            p.ts("dve", dg[:, f, tap, :], ident, cwt[:, f * 5 + tap:f * 5 + tap + 1], None, ALU.mult, None,
                 r=["cs", "cwt"], w=["dg"])
    hr = hT.rearrange("(k p) n -> p k n", p=128)
    bi = 0
    for ti, t0 in enumerate(range(0, NTOK, 256)):
        n = 256
        hq = ti % 2
        ht = ht2[:, hq]
        p.dma("sp", ht[:, :, 0:n], hr[:, :, t0:t0 + n], w=[("ht", hq)], sem="ht%d" % hq)
        for f in range(2):
            b = bi % 4
            bi += 1
            for k in range(KC):
                p.mm(bank[b][:, 0:n], W[:, k, f * 128:(f + 1) * 128], ht[:, k, 0:n], k == 0, k == KC - 1,
                     r=["W", ("ht", hq)], w=[("bank", b)])
            for (a0, a1) in [(0, n)]:
                p0 = upos(t0 + a0)
                p.copy("act", U[:, f, p0:p0 + a1 - a0], bank[b][:, a0:a1], r=[("bank", b)], w=["U"])
        for sub in range(n // 128):
            ch = t0 // 128 + sub
            b = bi % 4
            bi += 1
            b2 = 4 + b
            for k in range(KC):
                p.mm(bank[b][:, 0:260], ht[:, k, sub * 128:(sub + 1) * 128], W[:, k, 256:516], k == 0, k == KC - 1,
                     r=["W", ("ht", hq)], w=[("bank", b)])
            for k in range(KC):
                p.mm(bank[b2][:, 0:256], ht[:, k, sub * 128:(sub + 1) * 128], W[:, k, 516:772], k == 0, k == KC - 1,
                     r=["W", ("ht", hq)], w=[("bank", b2)])
            p.copy("dve", Va[:, ch, 0:256], bank[b][:, 0:256], r=[("bank", b)], w=["Va"])
            p.tt("dve", gat[:, ch, :], bank[b][:, 256:260], gb[:, 0:4], ALU.add, r=[("bank", b), "gb"], w=["gat"])
            p.act(SO[:, ch, :], bank[b2][:, 0:256], AF.Sigmoid, r=[("bank", b2)], w=["SO"])
    for f, dst in ((0, QT), (1, KT)):
        for (s0, s1) in [(0, 256)] + [(a, a + 512) for a in range(256, NTOK, 512)]:
            b = bi % 4
            bi += 1
            n = s1 - s0
            p0 = upos(s0)
            for tap in range(5):
                p.mm(bank[b][:, 0:n], dg[:, f, tap, :], U[:, f, p0 + tap - 2:p0 + tap - 2 + n], tap == 0, tap == 4,
                     r=["dg", "U"], w=[("bank", b)])
            p.act(dst[:, s0:s1], bank[b][:, 0:n], AF.Silu, r=[("bank", b), "cwt"], w=["QK"], bias=cwt[:, 10 + f:11 + f])
    for ch in range(NCH):
        b = bi % 4
        bi += 1
        p.mm(bank[b][:, 0:128], KT[:, ch * 128:(ch + 1) * 128], identb[:], True, True, r=["QK", "identb"], w=[("bank", b)])
        p.copy("act", Ktm[:, ch, :], bank[b][:, 0:128], r=[("bank", b)], w=["Ktm"])
    p.barrier()
    stA.close()

    lf = p.sb("lf", [128, 2, NCH], F32)
    sc = p.sb("sc", [128, 8, NCH], F32)
    tmp = p.sb("tmpg", [128, 2, NCH], F32)
    for d in range(2):
        p.act(tmp[:, d, :], gat[:, :, 2 * d + 1], AF.Exp, r=["gat"], w=["tmpg"], scale=-1.0)
        p.act(tmp[:, d, :], tmp[:, d, :], AF.Ln, r=["tmpg"], w=["tmpg"], bias=1.0)
        p.ts("dve", lf[:, d, :], tmp[:, d, :], -1.0, None, ALU.mult, None, r=["tmpg"], w=["lf"])
    cum = bank[7]
    p.mm(cum[:, 0:NCH], triU, lf[:, 0, :], True, True, r=["cs", "lf"], w=[("bank", 7)])
    p.mm(cum[:, NCH:2 * NCH], triL, lf[:, 1, :], True, True, r=["cs", "lf"], w=[("bank", 7)])
    p.mm(cum[:, 2 * NCH:4 * NCH], ones[:], lf[:].rearrange("p d c -> p (d c)"), True, True, r=["ones", "lf"], w=[("bank", 7)])
    for d in range(2):
        bcol = cum[:, d * NCH:(d + 1) * NCH]
        tot = cum[:, (2 + d) * NCH:(3 + d) * NCH]
        ig = gat[:, :, 2 * d]
        o = 4 * d
        p.tt("dve", tmp[:, d, :], ig, bcol, ALU.subtract, r=["gat", ("bank", 7)], w=["tmpg"])
        p.act(sc[:, o + 0, :], tmp[:, d, :], AF.Exp, r=["tmpg"], w=["sc"])
        p.act(sc[:, o + 1, :], bcol, AF.Exp, r=[("bank", 7)], w=["sc"])
        p.ts("dve", sc[:, o + 1, :], sc[:, o + 1, :], 128.0 ** -0.5, None, ALU.mult, None, r=["sc"], w=["sc"])
        p.act(sc[:, o + 3, :], tot, AF.Exp, r=[("bank", 7)], w=["sc"])
        p.tt("dve", sc[:, o + 2, :], sc[:, o + 0, :], sc[:, o + 3, :], ALU.mult, r=["sc"], w=["sc"])

    C32 = p.sb("C32", [128, 257], F32)
    Cb = p.sb("Cb", [128, 257], BF16)
    PT = p.sb("PT", [128, 3, 128], BF16)
    kw = p.sb("kw", [128, 3, 128], BF16)
    sm = p.sb("sm", [128, 2, 4], F32)
    hraw = p.sb("hraw", [128, 2, 64, 256], BF16)
    den = p.sb("den", [128, 2, 64], F32)
    rr = p.sb("rr", [128, 2, 64], F32)
    ssq = p.sb("ssq", [128, 64], F32)
    rsd = p.sb("rsd", [128, 64], F32)
    hs = p.sb("hs", [128, 2, 256], F32)
    yb = p.sb("yb", [128, 2, 256], BF16)
    def pre(i, d, ch):
        q = i % 3
        o = 4 * d
        mask = triU if d == 0 else triL
        sb_, ub_ = 0 + i % 2, 4 + q
        tk = slice(ch * 128, (ch + 1) * 128)
        if ch >= 2:
            p.mm(bank[sb_][:, 0:128], KT[:, tk], QT[:, tk], True, True, r=["QK"], w=[("bank", sb_)])
            p.stt("dve", PT[:, q, :], bank[sb_][:, 0:128], sc[:, o + 0, ch:ch + 1], mask, ALU.mult, ALU.mult,
                  r=[("bank", sb_), "sc", "cs"], w=[("PT", q)])
        p.ts("dve", kw[:, q, :], Ktm[:, ch, :], sc[:, o + 2, ch:ch + 1], None, ALU.mult, None, r=["Ktm", "sc"], w=[("kw", q)])
        p.mm(bank[ub_][:, 0:257], kw[:, q, :], Va[:, ch, :], True, True, r=[("kw", q), "Va"], w=[("bank", ub_)])

    def post(i, d, ch):
        q3 = i % 3
        q = i % 2
        o = 4 * d
        ab_, ub_ = 2 + q, 4 + q3
        tk = slice(ch * 128, (ch + 1) * 128)
        if ch >= 2:
            p.mm(bank[ab_][:, 0:257], PT[:, q3, :], Va[:, ch, :], True, False, r=[("PT", q3), "Va"], w=[("bank", ab_)])
            p.mm(bank[ab_][:, 0:257], QT[:, tk], Cb[:], False, True, r=["QK", "Cb"], w=[("bank", ab_)])
        p.stt("dve", C32[:], C32[:], sc[:, o + 3, ch:ch + 1], bank[ub_][:, 0:257], ALU.mult, ALU.add,
              r=["C32", "sc", ("bank", ub_)], w=["C32"])
        p.copy("act", Cb[:], C32[:], r=["C32"], w=["Cb"])
        if ch >= 2:
            c_ = ch - 2
            p.act(den[:, d, c_:c_ + 1], bank[ab_][:, 256:257], AF.Abs, r=[("bank", ab_), "sc"], w=[("den", d, c_)],
                  scale=sc[:, o + 1, ch:ch + 1])
            p.copy("act", hraw[:, d, c_, :], bank[ab_][:, 0:256], r=[("bank", ab_)], w=[("hraw", d, c_)])

    seq = []
    for d in range(2):
        order = list(range(NCH)) if d == 0 else ([1, 0] + list(range(NCH - 1, 1, -1)))
        seq += [(d, ch) for ch in order]
    pre(0, *seq[0])
    pre(1, *seq[1])
    for i, (d, ch) in enumerate(seq):
        if i == 0 or seq[i - 1][0] != d:
            p.memset("dve", C32[:], 0.0, w=["C32"])
            p.memset("dve", Cb[:], 0.0, w=["Cb"])
        if i + 2 < len(seq):
            pre(i + 2, *seq[i + 2])
        post(i, d, ch)
    denk = [("den", d, c_) for d in range(2) for c_ in range(64)]
    for d in range(2):
        p.ts("dve", rr[:, d, :], den[:, d, :], 1.0, None, ALU.max, None, r=denk, w=["rr"])
        p.add("dve", lambda e, d=d: e.reciprocal(out=rr[:, d, :], in_=rr[:, d, :]), r=["rr"], w=["rr"])
        p.tt("dve", rr[:, d, :], rr[:, d, :], sc[:, 4 * d + 1, 2:66], ALU.mult, r=["rr", "sc"], w=["rr"])

    def comb(c_, q):
        p.ts("dve", hs[:, q, :], hraw[:, 0, c_, :], rr[:, 0, c_:c_ + 1], None, ALU.mult, None,
             r=[("hraw", 0, c_), "rr"], w=[("hs", q)])
        p.stt("dve", hs[:, q, :], hraw[:, 1, c_, :], rr[:, 1, c_:c_ + 1], hs[:, q, :], ALU.mult, ALU.add,
              r=[("hraw", 1, c_), "rr", ("hs", q)], w=[("hs", q)])

    for c_ in range(64):
        q = c_ % 2
        comb(c_, q)
        p.add("act", lambda e, q=q, c_=c_: e.activation(out=yb[:, q, :], in_=hs[:, q, :], func=AF.Square,
                                                        accum_out=ssq[:, c_:c_ + 1]), r=[("hs", q)], w=[("ssq", c_), ("yb", q)])
    p.act(rsd[:], ssq[:], AF.Sqrt, r=[("ssq", c_) for c_ in range(64)], w=["rsd"], bias=epsb[:, 0:1], scale=1.0 / 256)
    p.add("dve", lambda e: e.reciprocal(out=rsd[:], in_=rsd[:]), r=["rsd"], w=["rsd"])
    for c_ in range(64):
        q = c_ % 2
        comb(c_, q)
        p.stt("dve", hs[:, q, :], hs[:, q, :], rsd[:, c_:c_ + 1], gb[:, 4:260], ALU.mult, ALU.mult,
              r=[("hs", q), "rsd", "gb"], w=[("hs", q)])
        p.tt("dve", yb[:, q, :], hs[:, q, :], SO[:, c_ + 2, :], ALU.mult, r=[("hs", q), "SO"], w=[("yb", q)])
        p.dma("sp", yo[c_ * 128:(c_ + 1) * 128, :], yb[:, q, :], r=[("yb", q)], w=[("yo", c_ + 2)], sem="yo%d" % q)
    return [("yo", ch) for ch in range(2, NCH)]


def build_p2a():
    nc = bass.Bass("TRN2", target_bir_lowering=False)
    p = Prog(nc)
    bank = [p.ps("bank%d" % b, [128, 512]) for b in range(8)]
    keys = emit_p2a(nc, p, bank)
    p.add("sp", lambda e: e.nop(), r=keys)
    p.emit()
    return nc, p


def to_cm(a):
    return a.reshape(128, 64, -1).transpose(1, 0, 2).reshape(8192, -1)


def from_cm(a):
    return a.reshape(64, 128, -1).transpose(1, 0, 2).reshape(8192, -1)


def consts_tri():
    ident = np.eye(128, dtype=np.float32)
    triU = np.triu(np.ones((128, 128), np.float32))
    triL = np.tril(np.ones((128, 128), np.float32))
    return np.ascontiguousarray(np.concatenate([ident, triU, triL], axis=1))


def p2a_inputs(inputs, h_all, hc_all, j):
    w_in = inputs["w_in"][0]
    cols = np.concatenate([np.arange(j * 128, (j + 1) * 128), 1024 + np.arange(j * 128, (j + 1) * 128),
                           2048 + np.arange(j * 256, (j + 1) * 256), 6144 + np.arange(4) * 8 + j,
                           4096 + np.arange(j * 256, (j + 1) * 256)])
    cwm = inputs["ml_conv_w"][0]
    cb = inputs["ml_conv_b"][0]
    cw = np.concatenate([cwm[:, j * 128:(j + 1) * 128].T, cwm[:, 1024 + j * 128:1024 + (j + 1) * 128].T,
                         cb[j * 128:(j + 1) * 128, None], cb[1024 + j * 128:1024 + (j + 1) * 128, None]], axis=1)
    gbn = np.concatenate([inputs["ml_gate_b"][0][:, j], inputs["ml_norm_w"][0][j * 256:(j + 1) * 256]])
    hcm = np.concatenate([hc_all, to_cm(h_all)], axis=0).T
    return {"hT": np.ascontiguousarray(hcm), "wml": np.ascontiguousarray(w_in[:, cols]), "cst": consts_tri(),
            "cw": np.ascontiguousarray(cw, dtype=np.float32),
            "gbn": np.ascontiguousarray(np.broadcast_to(gbn[None, :], (128, 260)), dtype=np.float32)}


def emit_p2b(nc, p, bank, pre=""):
    hT = dram(nc, pre + "hT", [D, NTOK], BF16, "ExternalInput")
    wss = dram(nc, pre + "wss", [D, 1296], F32, "ExternalInput")
    cst = dram(nc, pre + "cst", [128, 3 * 128], F32, "ExternalInput")
    cw = dram(nc, pre + "cw", [128, 36], F32, "ExternalInput")
    vecd = dram(nc, pre + "vec", [128, 552], F32, "ExternalInput")
    yo = dram(nc, "yssm", [8192, 512], BF16, "ExternalOutput")
    Ud = dram(nc, pre + "Ud", [768, UW2], BF16, "ExternalOutput")
    Zd = dram(nc, pre + "Zd", [8192, 512], BF16, "ExternalOutput")
    Yf = dram(nc, pre + "Yf", [8192, 512], F32, "ExternalOutput")
    cs = p.sb("cs", [128, 3, 128], F32)
    ident, triU, triL = cs[:, 0, :], cs[:, 1, :], cs[:, 2, :]
    identb = p.sb("identb", [128, 128], BF16)
    cwt = p.sb("cwt", [128, 36], F32)
    vec = p.sb("vec_sb", [128, 552], F32)
    ones = p.sb("ones", [128, 128], F32)
    epsb = p.sb("epsb", [128, 1], F32)
    Xtm = p.sb("Xtm", [128, NCH, 512], BF16)
    Btm = p.sb("Btm", [128, NCH, 128], BF16)
    BT = p.sb("BT", [128, NTOK], BF16)
    CT = p.sb("CT", [128, NTOK], BF16)
    dt = p.sb("dt", [128, NCH, 16], F32)
    p.dma("sp", cs[:].rearrange("p a b -> p (a b)"), cst, w=["cs"], sem="c0")
    p.dma("sp", cwt[:], cw, w=["cwt"], sem="c1")
    p.dma("sp", vec[:], vecd, w=["vec"], sem="c2")
    p.memset("dve", ones[:], 1.0, w=["ones"])
    p.memset("dve", epsb[:], EPS, w=["epsb"])
    p.copy("dve", identb[:], ident, r=["cs"], w=["identb"])

    stA = contextlib.ExitStack()
    W = p.sb("W", [128, KC, 1296], BF16, stack=stA)
    ht2 = p.sb("ht", [128, 2, KC, 256], BF16, stack=stA)
    ev = p.sb("ev", [128, 2, 512], BF16, stack=stA)
    zb = p.sb("zb", [128, 2, 512], BF16, stack=stA)
    zt = p.sb("zt", [128, 64], BF16, stack=stA)
    dg = p.sb("dg", [128, 6, 5, 128], BF16, stack=stA)
    ut = p.sb("ut", [128, 2, 576], BF16, stack=stA)
    post = p.sb("post", [128, 2, 512], BF16, stack=stA)
    wr = wss.rearrange("(k p) n -> p k n", p=128)
    for k0 in range(0, KC, 4):
        p.dma("pool", W[:, k0:k0 + 4, :], wr[:, k0:k0 + 4, :], w=["W"], sem="W")
    p.memset("dve", zt[:], 0.0, w=["zt"])
    for f in range(6):
        for (a, n_) in ((0, 32), (288, 64), (8544, 32)):
            p.dma("sp", Ud[f * 128:(f + 1) * 128, a:a + n_], zt[:, 0:n_], r=["zt"], w=[("Ud", f)], sem="udz")
        for tap in range(5):
            p.ts("dve", dg[:, f, tap, :], ident, cwt[:, f * 5 + tap:f * 5 + tap + 1], None, ALU.mult, None,
                 r=["cs", "cwt"], w=["dg"])
    hr = hT.rearrange("(k p) n -> p k n", p=128)
    bi = 0
    ei = 0
    for ti, t0 in enumerate(range(0, NTOK, 256)):
        n = 256
        hq = ti % 2
        ht = ht2[:, hq]
        p.dma("sp", ht[:, :, 0:n], hr[:, :, t0:t0 + n], w=[("ht", hq)], sem="ht%d" % hq)
        for f in range(6):
            b = bi % 4
            bi += 1
            q = ei % 2
            ei += 1
            for k in range(KC):
                p.mm(bank[b][:, 0:n], W[:, k, 512 + f * 128:512 + (f + 1) * 128], ht[:, k, 0:n], k == 0, k == KC - 1,
                     r=["W", ("ht", hq)], w=[("bank", b)])
            p.copy("act", ev[:, q, 0:n], bank[b][:, 0:n], r=[("bank", b)], w=[("ev", q)])
            for (a0, a1) in [(0, n)]:
                p0 = upos2(t0 + a0)
                p.dma("sp", Ud[f * 128:(f + 1) * 128, p0:p0 + a1 - a0], ev[:, q, a0:a1], r=[("ev", q)], w=[("Ud", f)], sem="ev%d" % q)
        for sub in range(n // 128):
            ch = t0 // 128 + sub
            b = bi % 4
            bi += 1
            b2 = 4 + b
            if ch >= 2:
                q = ei % 2
                ei += 1
                for k in range(KC):
                    p.mm(bank[b][:, 0:512], ht[:, k, sub * 128:(sub + 1) * 128], W[:, k, 0:512], k == 0, k == KC - 1,
                         r=["W", ("ht", hq)], w=[("bank", b)])
                p.act(zb[:, q, :], bank[b][:, 0:512], AF.Silu, r=[("bank", b)], w=[("zb", q)])
                p.dma("sp", Zd[(ch - 2) * 128:(ch - 1) * 128, :], zb[:, q, :], r=[("zb", q)], w=[("Zd", ch)], sem="zb%d" % q)
            for k in range(KC):
                p.mm(bank[b2][:, 0:16], ht[:, k, sub * 128:(sub + 1) * 128], W[:, k, 1280:1296], k == 0, k == KC - 1,
                     r=["W", ("ht", hq)], w=[("bank", b2)])
            p.tt("dve", dt[:, ch, :], bank[b2][:, 0:16], vec[:, 0:16], ALU.add, r=[("bank", b2), "vec"], w=["dt"])
    dtf = dt[:].rearrange("p c e -> p (c e)")
    p.act(dtf, dtf, AF.Exp, r=["dt"], w=["dt"])
    p.act(dtf, dtf, AF.Ln, r=["dt"], w=["dt"], bias=1.0)
    ui = 0
    for (s0, s1) in [(0, 256)] + [(a, a + 512) for a in range(256, NTOK, 512)]:
        n = s1 - s0
        p0 = upos2(s0)
        for f in range(6):
            q = ui % 2
            ui += 1
            b = bi % 4
            bi += 1
            p.dma("sp", ut[:, q, 0:n + 64], Ud[f * 128:(f + 1) * 128, p0 - 32:p0 + n + 32], r=[("Ud", f)], w=[("ut", q)], sem="ut%d" % q)
            for tap in range(5):
                p.mm(bank[b][:, 0:n], dg[:, f, tap, :], ut[:, q, 30 + tap:30 + tap + n], tap == 0, tap == 4, r=["dg", ("ut", q)], w=[("bank", b)])
            if f == 5:
                p.act(CT[:, s0:s1], bank[b][:, 0:n], AF.Silu, r=[("bank", b), "cwt"], w=["CT"], bias=cwt[:, 30 + f:31 + f])
                continue
            dst = BT[:, s0:s1] if f == 4 else post[:, q, 0:n]
            dkey = "BT" if f == 4 else ("post", q)
            p.act(dst, bank[b][:, 0:n], AF.Silu, r=[("bank", b), "cwt"], w=[dkey], bias=cwt[:, 30 + f:31 + f])
            for sub in range(n // 128):
                ch = s0 // 128 + sub
                b2 = 4 + (bi % 4)
                bi += 1
                p.mm(bank[b2][:, 0:128], dst[:, sub * 128:(sub + 1) * 128], identb[:], True, True, r=[dkey, "identb"], w=[("bank", b2)])
                if f == 4:
                    p.copy("dve", Btm[:, ch, :], bank[b2][:, 0:128], r=[("bank", b2)], w=["Btm"])
                else:
                    p.copy("dve", Xtm[:, ch, f * 128:(f + 1) * 128], bank[b2][:, 0:128], r=[("bank", b2)], w=["Xtm"])
    p.barrier()
    stA.close()

    Aneg = p.sb("Aneg", [128, 16], F32)
    dtA = p.sb("dtA", [128, NCH, 16], F32)
    bsb = p.sb("bsb", [128, 2, NCH, 8], F32)
    tot = p.sb("tot", [128, 2, NCH, 8], F32)
    eo = p.sb("eo", [128, 2, NCH, 8], F32)
    ws = p.sb("ws", [128, 2, NCH, 8], F32)
    dec = p.sb("dec", [128, 2, NCH, 8], F32)
    p.act(Aneg[:], vec[:, 16:32], AF.Exp, r=["vec"], w=["Aneg"])
    p.ts("dve", Aneg[:], Aneg[:], -1.0, None, ALU.mult, None, r=["Aneg"], w=["Aneg"])
    p.tt("dve", dtA[:], dt[:], Aneg[:].unsqueeze(1).broadcast_to([128, NCH, 16]), ALU.mult, r=["dt", "Aneg"], w=["dtA"])
    H = NCH // 2
    for d in range(2):
        tri = triU if d == 0 else triL
        for hf_ in range(2):
            src = dtA[:, hf_ * H:(hf_ + 1) * H, d * 8:(d + 1) * 8]
            b0, b1 = (d * 2 + hf_) % 4, 4 + (d * 2 + hf_) % 4
            p.mm(bank[b0][:, 0:H * 8].rearrange("p (c e) -> p c e", e=8), tri, src, True, True, r=["cs", "dtA"], w=[("bank", b0)])
            p.mm(bank[b1][:, 0:H * 8].rearrange("p (c e) -> p c e", e=8), ones[:], src, True, True, r=["ones", "dtA"], w=[("bank", b1)])
            p.copy("dve", bsb[:, d, hf_ * H:(hf_ + 1) * H, :], bank[b0][:, 0:H * 8].rearrange("p (c e) -> p c e", e=8),
                   r=[("bank", b0)], w=["bsb"])
            p.copy("dve", tot[:, d, hf_ * H:(hf_ + 1) * H, :], bank[b1][:, 0:H * 8].rearrange("p (c e) -> p c e", e=8),
                   r=[("bank", b1)], w=["tot"])
    fl = lambda t: t[:].rearrange("p d c e -> p (d c e)")
    p.act(fl(eo), fl(bsb), AF.Exp, r=["bsb"], w=["eo"])
    p.tt("dve", fl(ws), fl(tot), fl(bsb), ALU.subtract, r=["tot", "bsb"], w=["ws"])
    p.act(fl(ws), fl(ws), AF.Exp, r=["ws"], w=["ws"])
    p.act(fl(dec), fl(tot), AF.Exp, r=["tot"], w=["dec"])

    negb = p.sb("negb", [128, 2, NCH, 8], F32)
    p.ts("dve", fl(negb), fl(bsb), -1.0, None, ALU.mult, None, r=["bsb"], w=["negb"])
    S32 = p.sb("S32", [128, 512], F32)
    Sb = p.sb("Sb", [128, 512], BF16)
    CBm = p.sb("CBm", [128, 2, 128], F32)
    diagb = p.sb("diagb", [128, 2, 8, 128], F32)
    dm = p.sb("dm", [128, 2, 8, 128], BF16)
    G = p.sb("G", [128, 2, 8, 128], BF16)
    xdt = p.sb("xdt", [128, 3, 512], BF16)
    xw = p.sb("xw", [128, 3, 512], BF16)
    yis = p.sb("yis", [128, 2, 512], F32)
    sus = p.sb("sus", [128, 2, 512], F32)
    tmpd = p.sb("tmpd", [128, 512], F32)
    t2 = p.sb("t2", [128, 512], F32)
    t3 = p.sb("t3", [128, 512], BF16)
    ysb = p.sb("ysb", [128, 2, 512], F32)
    yfl = p.sb("yfl", [128, 3, 512], F32)
    zl = p.sb("zl", [128, 3, 512], BF16)
    ob = p.sb("ob", [128, 2, 512], BF16)
    sm = p.sb("sm", [128, 2, 4], F32)
    bc3 = lambda ap: ap.unsqueeze(2).broadcast_to([128, 8, 64])
    v3 = lambda ap: ap.rearrange("p (e c) -> p e c", e=8)

    def preA(i, d, ch):
        q, t = i % 2, i % 3
        lat = ch >= 2
        tk = slice(ch * 128, (ch + 1) * 128)
        dsl = slice(d * 8, (d + 1) * 8)
        p.tt("pool", v3(xdt[:, t, :]), v3(Xtm[:, ch, :]), bc3(dt[:, ch, dsl]), ALU.mult, r=["Xtm", "dt"], w=[("xdt", t)])
        p.tt("pool", v3(xw[:, t, :]), v3(xdt[:, t, :]), bc3(ws[:, d, ch, :]), ALU.mult, r=[("xdt", t), "ws"], w=[("xw", t)])
        if not lat:
            return
        rows = slice((ch - 2) * 128, (ch - 1) * 128)
        p.tt("pool", diagb[:, q], ident.unsqueeze(1).broadcast_to([128, 8, 128]),
             bsb[:, d, ch, :].unsqueeze(2).broadcast_to([128, 8, 128]), ALU.mult, r=["cs", "bsb"], w=[("diagb", q)])
        cbp = bank[7][:, q * 128:(q + 1) * 128]
        p.mm(cbp, BT[:, tk], CT[:, tk], True, True, r=["BT", "CT"], w=[("cbp", q)])
        for h2 in range(2):
            bb = 2 * q + h2
            p.mm(bank[bb][:, 0:512], ones[:], diagb[:, q, 4 * h2:4 * h2 + 4, :].rearrange("p e i -> p (e i)"), True, True,
                 r=["ones", ("diagb", q)], w=[("bank", bb)])
            for e4 in range(4):
                e = 4 * h2 + e4
                p.act(dm[:, q, e, :], bank[bb][:, e4 * 128:(e4 + 1) * 128], AF.Exp, r=[("bank", bb), "negb"], w=[("dm", q)],
                      bias=negb[:, d, ch, e:e + 1])
        if d == 1:
            p.dma("sp", yfl[:, t, :], Yf[rows, :], r=[("Yf", ch)], w=[("yfl", t)], sem="yl%d" % t)
            p.dma("sp", zl[:, t, :], Zd[rows, :], r=[("Zd", ch)], w=[("zl", t)], sem="zl%d" % t)
            p.tt("pool", v3(tmpd[:]), v3(Xtm[:, ch, :]), bc3(vec[:, 32:40]), ALU.mult, r=["Xtm", "vec"], w=["tmpd"])
            p.tt("pool", yfl[:, t, :], yfl[:, t, :], tmpd[:], ALU.add, r=["tmpd", ("yfl", t)], w=[("yfl", t)])

    def preB(i, d, ch):
        q, t = i % 2, i % 3
        tri = triU if d == 0 else triL
        lat = ch >= 2
        if lat:
            cbp = bank[7][:, q * 128:(q + 1) * 128]
            p.tt("dve", CBm[:, q, :], cbp, tri, ALU.mult, r=[("cbp", q), "cs"], w=[("CBm", q)])
            p.stt("dve", G[:, q], dm[:, q], 1.0, CBm[:, q, :].unsqueeze(1).broadcast_to([128, 8, 128]), ALU.min, ALU.mult,
                  r=[("dm", q), ("CBm", q)], w=[("G", q)])
            for e in range(8):
                p.mm(bank[4][:, e * 64:(e + 1) * 64], G[:, q, e, :], xdt[:, t, e * 64:(e + 1) * 64], True, True,
                     r=[("G", q), ("xdt", t)], w=[("bank", 4)])
            p.copy("act", yis[:, q, :], bank[4][:, 0:512], r=[("bank", 4)], w=[("yis", q)])
        p.mm(bank[6][:, 0:512], Btm[:, ch, :], xw[:, t, :], True, True, r=["Btm", ("xw", t)], w=[("bank", 6)])
        p.copy("act", sus[:, q, :], bank[6][:, 0:512], r=[("bank", 6)], w=[("sus", q)])

    def post(i, d, ch):
        q, t = i % 2, i % 3
        lat = ch >= 2
        tk = slice(ch * 128, (ch + 1) * 128)
        if lat:
            rows = slice((ch - 2) * 128, (ch - 1) * 128)
            p.mm(bank[5][:, 0:512], CT[:, tk], Sb[:], True, True, r=["CT", "Sb"], w=[("bank", 5)])
        p.tt("dve", v3(S32[:]), v3(S32[:]), bc3(dec[:, d, ch, :]), ALU.mult, r=["S32", "dec"], w=["S32"])
        p.tt("dve", S32[:], S32[:], sus[:, q, :], ALU.add, r=["S32", ("sus", q)], w=["S32"])
        p.copy("act", Sb[:], S32[:], r=["S32"], w=["Sb"])
        if lat:
            p.tt("dve", v3(t2[:]), v3(bank[5][:, 0:512]), bc3(eo[:, d, ch, :]), ALU.mult, r=[("bank", 5), "eo"], w=["t2"])
            p.tt("dve", ysb[:, q, :], t2[:], yis[:, q, :], ALU.add, r=["t2", ("yis", q)], w=[("ysb", q)])
            if d == 0:
                p.dma("sp", Yf[rows, :], ysb[:, q, :], r=[("ysb", q)], w=[("Yf", ch)], sem="yf%d" % q)
            else:
                p.tt("dve", ysb[:, q, :], ysb[:, q, :], yfl[:, t, :], ALU.add, r=[("ysb", q), ("yfl", t)], w=[("ysb", q)])
                p.tt("dve", ysb[:, q, :], ysb[:, q, :], zl[:, t, :], ALU.mult, r=[("ysb", q), ("zl", t)], w=[("ysb", q)])
                p.add("act", lambda e, q=q: e.activation(out=t3[:], in_=ysb[:, q, :], func=AF.Square, accum_out=sm[:, q, 0:1]),
                      r=[("ysb", q)], w=[("sm", q), "t3"])
                p.act(sm[:, q, 1:2], sm[:, q, 0:1], AF.Sqrt, r=[("sm", q)], w=[("sm", q)], bias=epsb[:, 0:1], scale=1.0 / 512)
                p.add("dve", lambda e, q=q: e.reciprocal(out=sm[:, q, 2:3], in_=sm[:, q, 1:2]), r=[("sm", q)], w=[("sm", q)])
                p.stt("dve", ob[:, q, :], ysb[:, q, :], sm[:, q, 2:3], vec[:, 40:552], ALU.mult, ALU.mult,
                      r=[("ysb", q), ("sm", q), "vec"], w=[("ob", q)])
                p.dma("sp", yo[rows, :], ob[:, q, :], r=[("ob", q)], w=[("yo", ch)], sem="yo%d" % q)

    seq = []
    for d in range(2):
        order = list(range(NCH)) if d == 0 else ([1, 0] + list(range(NCH - 1, 1, -1)))
        seq += [(d, ch) for ch in order]
    NS = len(seq)
    preA(0, *seq[0])
    preA(1, *seq[1])
    preB(0, *seq[0])
    for i, (d, ch) in enumerate(seq):
        if i == 0 or seq[i - 1][0] != d:
            p.memset("dve", S32[:], 0.0, w=["S32"])
            p.memset("dve", Sb[:], 0.0, w=["Sb"])
        if i + 2 < NS:
            preA(i + 2, *seq[i + 2])
        if i + 1 < NS:
            preB(i + 1, *seq[i + 1])
        post(i, d, ch)
    return [("yo", ch) for ch in range(2, NCH)]


def build_p2b():
    nc = bass.Bass("TRN2", target_bir_lowering=False)
    p = Prog(nc)
    bank = [p.ps("bank%d" % b, [128, 512]) for b in range(8)]
    keys = emit_p2b(nc, p, bank)
    p.add("sp", lambda e: e.nop(), r=keys)
    p.emit()
    return nc, p


def build_p2():
    nc = bass.Bass("TRN2", target_bir_lowering=False)
    p = Prog(nc)
    bank = [p.ps("bank%d" % b, [128, 512]) for b in range(8)]
    with p.scope("a_"):
        emit_p2a(nc, p, bank, "a_")
    with p.scope("b_"):
        keys = emit_p2b(nc, p, bank, "b_")
        p.add("sp", lambda e: e.nop(), r=keys)
    p.emit()
    return nc, p


def p2b_inputs(inputs, h_all, hc_all, g):
    w_in = inputs["w_in"][0]
    o = 6176
    cols = np.concatenate([o + np.arange(g * 512, (g + 1) * 512), o + 4096 + np.arange(g * 512, (g + 1) * 512),
                           o + 8192 + np.arange(g * 128, (g + 1) * 128), o + 9216 + np.arange(g * 128, (g + 1) * 128),
                           o + 10240 + np.arange(g * 8, (g + 1) * 8), o + 10304 + np.arange(g * 8, (g + 1) * 8)])
    cch = np.concatenate([np.arange(g * 512, (g + 1) * 512), 4096 + np.arange(g * 128, (g + 1) * 128),
                          5120 + np.arange(g * 128, (g + 1) * 128)])
    cwm = inputs["ssm_conv_w"][0][:, cch]
    cb = inputs["ssm_conv_b"][0][cch]
    cw = np.concatenate([cwm.reshape(5, 6, 128).transpose(2, 1, 0).reshape(128, 30), cb.reshape(6, 128).T], axis=1)
    hsl = slice(g * 8, (g + 1) * 8)
    vec = np.concatenate([inputs["ssm_dt_bias"][0][0, hsl], inputs["ssm_dt_bias"][0][1, hsl],
                          inputs["ssm_a_log"][0][0, hsl], inputs["ssm_a_log"][0][1, hsl],
                          inputs["ssm_d"][0][hsl], inputs["ssm_norm_w"][0][g * 512:(g + 1) * 512]])
    hrm = np.concatenate([hc_all, h_all], axis=0).T
    return {"hT": np.ascontiguousarray(hrm), "wss": np.ascontiguousarray(w_in[:, cols]), "cst": consts_tri(),
            "cw": np.ascontiguousarray(cw, dtype=np.float32),
            "vec": np.ascontiguousarray(np.broadcast_to(vec[None, :], (128, 552)), dtype=np.float32)}


def p2_inputs(inputs, h_all, hc_all, j):
    m = {"a_" + k: v for k, v in p2a_inputs(inputs, h_all, hc_all, j).items()}
    m.update({"b_" + k: v for k, v in p2b_inputs(inputs, h_all, hc_all, j).items()})
    return m


def emit_p3a(nc, p, bank, pre="", modd=None):
    T = 1024
    x1T = dram(nc, pre + "x1T", [D, T], F32, "ExternalInput")
    hTd = dram(nc, pre + "hT", [D, T], BF16, "ExternalInput")
    ymlT = dram(nc, pre + "ymlT", [D, T], BF16, "ExternalInput")
    yssT = dram(nc, pre + "yssT", [2 * D, T], BF16, "ExternalInput")
    if modd is None:
        modd = dram(nc, pre + "modfm", [128, 288], F32, "ExternalInput")
    bgd = dram(nc, pre + "bg", [128, 32], F32, "ExternalInput")
    wpm = dram(nc, pre + "wpm", [D, D], F32, "ExternalInput")
    wps = dram(nc, pre + "wps", [2 * D, D], F32, "ExternalInput")
    wgt = dram(nc, pre + "wgt", [D, 2 * D], F32, "ExternalInput")
    wo = dram(nc, pre + "wo", [D, D], F32, "ExternalInput")
    x2T = dram(nc, pre + "x2T", [D, T], F32, "ExternalOutput")
    c = setup_common(p, T, nslot=5, ffn=False, bank=bank)
    modfm = p.sb("modfm_sb", [128, 288], F32)
    bg = p.sb("bg_sb", [128, 32], F32)
    yml = p.sb("yml", [128, KC, 512], BF16)
    yss = p.sb("yss", [128, 2 * KC, 512], BF16)
    hT = p.sb("hTt", [128, KC, 512], BF16)
    mg = p.sb("mg", [128, KC, T], BF16)
    gs = p.sb("gs", [128, 2, 2, 512], F32)
    tu = p.sb("tu", [128, 2, 2, 512], F32)
    xt = p.sb("xt", [128, 2, 512], F32)
    p.dma("sp", modfm[:], modd, w=["modfm"], sem="c0")
    p.dma("sp", bg[:], bgd, w=["bg"], sem="c1")
    gmix = modfm[:, :].rearrange("p (c j) -> p c j", j=2)[:, 5 * KC:6 * KC, 0]
    r3 = lambda w_: w_.rearrange("(k p) n -> p k n", p=128)
    wpm_r, wps_r, wgt_r, wo_r = r3(wpm), r3(wps), r3(wgt), r3(wo)
    it = 0
    for n in range(2):
        ts_ = slice(n * 512, (n + 1) * 512)
        p.dma("sp", yml[:], r3(ymlT)[:, :, ts_], w=["yml"], sem="a0")
        p.dma("sp", yss[:], r3(yssT)[:, :, ts_], w=["yss"], sem="a1")
        p.dma("sp", hT[:], r3(hTd)[:, :, ts_], w=["hTt"], sem="a2")
        for dp in range(KC // 2):
            cs_ = slice(dp * 256, (dp + 1) * 256)
            s1, s2, s3 = next_slot(c), next_slot(c), next_slot(c)
            w1 = c.slot[s1][:, 0:4096].rearrange("p (k n) -> p k n", k=KC)
            w1g = c.slot[s1][:, 4096:8192].rearrange("p (k n) -> p k n", k=KC)
            w2 = c.slot[s2][:, :].rearrange("p (k n) -> p k n", k=2 * KC)
            w3 = c.slot[s3][:, 0:4096].rearrange("p (k n) -> p k n", k=KC)
            p.dma("pool", w1, wpm_r[:, :, cs_], w=[("slot", s1)], sem="slot%d" % s1)
            p.dma("pool", w1g, wgt_r[:, :, cs_], w=[("slot", s1)], sem="slot%d" % s1)
            p.dma("pool", w2, wps_r[:, :, cs_], w=[("slot", s2)], sem="slot%d" % s2)
            p.dma("pool", w3, wgt_r[:, :, D + dp * 256:D + (dp + 1) * 256], w=[("slot", s3)], sem="slot%d" % s3)
            for dd in range(2):
                d = dp * 2 + dd
                q = it % 2
                it += 1
                b0 = q * 4
                dsl = slice(dd * 128, (dd + 1) * 128)
                for k in range(KC):
                    p.mm(c.bank[b0][:, :], w1[:, k, dsl], yml[:, k, :], k == 0, k == KC - 1, r=[("slot", s1), "yml"], w=[("bank", b0)])
                for k in range(2 * KC):
                    p.mm(c.bank[b0 + 1][:, :], w2[:, k, dsl], yss[:, k, :], k == 0, k == 2 * KC - 1, r=[("slot", s2), "yss"], w=[("bank", b0 + 1)])
                for k in range(KC):
                    p.mm(c.bank[b0 + 2][:, :], w1g[:, k, dsl], hT[:, k, :], k == 0, k == KC - 1, r=[("slot", s1), "hTt"], w=[("bank", b0 + 2)])
                for k in range(KC):
                    p.mm(c.bank[b0 + 3][:, :], w3[:, k, dsl], hT[:, k, :], k == 0, k == KC - 1, r=[("slot", s3), "hTt"], w=[("bank", b0 + 3)])
                p.act(gs[:, q, 0, :], c.bank[b0 + 2][:, :], AF.Sigmoid, r=[("bank", b0 + 2), "bg"], w=[("gs", q)], bias=bg[:, d:d + 1])
                p.act(gs[:, q, 1, :], c.bank[b0 + 3][:, :], AF.Sigmoid, r=[("bank", b0 + 3), "bg"], w=[("gs", q)], bias=bg[:, KC + d:KC + d + 1])
                p.tt("dve", tu[:, q, 0, :], gs[:, q, 0, :], c.bank[b0][:, :], ALU.mult, r=[("gs", q), ("bank", b0)], w=[("tu", q)])
                p.tt("dve", tu[:, q, 1, :], gs[:, q, 1, :], c.bank[b0 + 1][:, :], ALU.mult, r=[("gs", q), ("bank", b0 + 1)], w=[("tu", q)])
                p.tt("dve", mg[:, d, ts_], tu[:, q, 0, :], tu[:, q, 1, :], ALU.add, r=[("tu", q)], w=[("mg", n)])
    for n in range(2):
        ts_ = slice(n * 512, (n + 1) * 512)
        for dp in range(KC // 2):
            s1 = next_slot(c)
            w1 = c.slot[s1][:, 0:4096].rearrange("p (k n) -> p k n", k=KC)
            p.dma("pool", w1, wo_r[:, :, dp * 256:(dp + 1) * 256], w=[("slot", s1)], sem="slot%d" % s1)
            for dd in range(2):
                d = dp * 2 + dd
                q = it % 2
                it += 1
                b0 = q
                p.dma("sp", xt[:, q, :], x1T[d * 128:(d + 1) * 128, ts_], w=[("xt", q)], sem="xt%d" % q)
                for k in range(KC):
                    p.mm(c.bank[b0][:, :], w1[:, k, dd * 128:(dd + 1) * 128], mg[:, k, ts_], k == 0, k == KC - 1,
                         r=[("slot", s1), ("mg", n)], w=[("bank", b0)])
                p.stt("dve", xt[:, q, :], c.bank[b0][:, :], gmix[:, d:d + 1], xt[:, q, :], ALU.mult, ALU.add,
                      r=[("bank", b0), ("xt", q), "modfm"], w=[("xt", q)])
                p.dma("sp", x2T[d * 128:(d + 1) * 128, ts_], xt[:, q, :], r=[("xt", q)], w=[("x2", d, n)], sem="xo%d" % q)
    return x2T, [("x2", d, n) for d in range(KC) for n in range(2)]


def build_p3a():
    nc = bass.Bass("TRN2", target_bir_lowering=False)
    p = Prog(nc)
    bank = [p.ps("bank%d" % b, [128, 512]) for b in range(8)]
    _, keys = emit_p3a(nc, p, bank)
    p.add("sp", lambda e: e.nop(), r=keys)
    p.emit()
    return nc, p


def emit_p3b(nc, p, bank, pre="", x2T=None, modd=None):
    T = 1024
    if x2T is None:
        x2T = dram(nc, pre + "x2T", [D, T], F32, "ExternalInput")
    if modd is None:
        modd = dram(nc, pre + "modfm", [128, 288], F32, "ExternalInput")
    nwd = dram(nc, pre + "nw", [128, 4 * KC], F32, "ExternalInput")
    wg = dram(nc, pre + "wg", [D, DFF], F32, "ExternalInput")
    wu = dram(nc, pre + "wu", [D, DFF], F32, "ExternalInput")
    wd = dram(nc, pre + "wd", [DFF, D], F32, "ExternalInput")
    x3T = dram(nc, pre + "x3T", [D, T], F32, "ExternalOutput")
    outT = dram(nc, "outT", [D, T], F32, "ExternalOutput")
    c = setup_common(p, T, bank=bank)
    c.epsb = p.sb("epsb", [128, 1], F32)
    p.memset("dve", c.epsb[:], EPS, w=["epsb"])
    modfm = p.sb("modfm_sb", [128, 288], F32)
    nw = p.sb("nwt", [128, 4, KC], F32)
    ob = p.sb("obuf", [128, 2, T], F32)
    p.dma("sp", modfm[:], modd, w=["modfm"], sem="c0")
    p.dma("sp", nw[:].rearrange("p s k -> p (s k)"), nwd, w=["nw"], sem="c1")
    A, B, G, gt = make_AB(p, c, modfm, nw, 2)
    p.ts("dve", G[:], gt, 0.5, None, ALU.mult, None, r=["modfm"], w=["modAB"])
    Ao = p.sb("Ao", [128, KC, 2], F32)
    for j in range(2):
        p.copy("dve", Ao[:, :, j], nw[:, 3, :], r=["nw"], w=["modAB"])
    tiles = [(0, 512), (512, 1024)]
    segs = [(0, 1024, 0)]
    ffn_stage(p, c, "f2", tiles, segs, x2T, A, B, G, wg, wu, wd, Ao, None, x3T, outT, ob, hout_rot=2)
    return [("f2_hdram", k) for k in range(KC)]


def build_p3b():
    nc = bass.Bass("TRN2", target_bir_lowering=False)
    p = Prog(nc)
    bank = [p.ps("bank%d" % b, [128, 512]) for b in range(8)]
    keys = emit_p3b(nc, p, bank)
    p.add("sp", lambda e: e.nop(), r=keys)
    p.emit()
    return nc, p


def build_p3():
    nc = bass.Bass("TRN2", target_bir_lowering=False)
    p = Prog(nc)
    bank = [p.ps("bank%d" % b, [128, 512]) for b in range(8)]
    modd = dram(nc, "modfm", [128, 288], F32, "ExternalInput")
    with p.scope("a_"):
        x2T, _ = emit_p3a(nc, p, bank, "a_", modd)
    with p.scope("b_"):
        keys = emit_p3b(nc, p, bank, "b_", x2T=x2T, modd=modd)
        p.add("sp", lambda e: e.nop(), r=keys)
    p.emit()
    return nc, p


def _run(nc, in_maps):
    return run_bass_kernel_spmd(nc, in_maps, core_ids=list(range(len(in_maps)))).results


def kernel(**inputs):
    bf = ml_dtypes.bfloat16
    r1 = run_p1(inputs)
    x1T = [np.asarray(r["x1T"]) for r in r1]
    hT2 = [np.asarray(r["hT2"]) for r in r1]
    modfm = np.asarray(r1[0]["modfm"])
    h_all = np.concatenate([h[:, :1024].T for h in hT2], axis=0)
    hc_all = np.concatenate([h[:, 1024:].T for h in hT2], axis=0)
    nc, _ = build_p2()
    r2 = _run(nc, [p2_inputs(inputs, h_all, hc_all, j) for j in range(NCORES)])
    yml = np.concatenate([from_cm(np.asarray(r["yml"])) for r in r2], axis=1)
    yss = np.concatenate([np.asarray(r["yssm"]) for r in r2], axis=1)
    nc, _ = build_p3()
    nw = np.concatenate([_fm(inputs["norm_w"][0, s]) for s in range(3)] + [_fm(inputs["final_norm_w"])], axis=1)
    common = {"modfm": modfm, "a_bg": _fm(inputs["b_gate"][0]),
              "a_wpm": np.ascontiguousarray(inputs["w_proj_ml"][0]), "a_wps": np.ascontiguousarray(inputs["w_proj_ssm"][0]),
              "a_wgt": np.ascontiguousarray(inputs["w_gate"][0]), "a_wo": np.ascontiguousarray(inputs["w_out"][0]),
              "b_nw": np.ascontiguousarray(nw),
              "b_wg": np.ascontiguousarray(inputs["ffn_w_gate"][0, 1]), "b_wu": np.ascontiguousarray(inputs["ffn_w_up"][0, 1]),
              "b_wd": np.ascontiguousarray(inputs["ffn_w_down"][0, 1])}
    maps = []
    for i in range(NCORES):
        sl = slice(1024 * i, 1024 * (i + 1))
        m = dict(common)
        m["a_x1T"] = np.ascontiguousarray(x1T[i][:, :1024])
        m["a_hT"] = np.ascontiguousarray(hT2[i][:, :1024])
        m["a_ymlT"] = np.ascontiguousarray(yml[sl].T)
        m["a_yssT"] = np.ascontiguousarray(yss[sl].T)
        maps.append(m)
    r3b = _run(nc, maps)
    out = np.concatenate([np.asarray(r["outT"]).T for r in r3b], axis=0)
    return np.ascontiguousarray(out[None].astype(np.float32))
```

```python
import contextlib
import numpy as np
import ml_dtypes
import concourse.bass as bass
import concourse.mybir as mybir
from concourse.bass_utils import run_bass_kernel_spmd

F32 = mybir.dt.float32
BF16 = mybir.dt.bfloat16
AF = mybir.ActivationFunctionType
ALU = mybir.AluOpType
AX = mybir.AxisListType

D = 2048
KC = 16
DFF = 5632
FC = 44
EPS = 1e-6
NCORES = 8

ENGS = ("pe", "act", "dve", "pool", "sp")


class Op:
    __slots__ = ("eng", "fn", "deps", "isdma", "sem", "val", "inc")

    def __init__(self, eng, fn, isdma):
        self.eng = eng
        self.fn = fn
        self.deps = []
        self.isdma = isdma
        self.sem = None
        self.val = None
        self.inc = False


class Prog:
    def __init__(self, nc):
        self.nc = nc
        self.ops = {e: [] for e in ENGS}
        self.keys = {}
        self.stack = contextlib.ExitStack()
        self.dram_n = 0
        self.extra_r = []
        self.nbar = 0
        self.prefix = ""
        self.bar_t = self.sb("bar_t", [128, 8], F32)

    def sb(self, name, shape, dtype, stack=None):
        return (stack or self.stack).enter_context(self.nc.sbuf_tensor(self.prefix + name, list(shape), dtype))

    def ps(self, name, shape, dtype=F32, stack=None):
        return (stack or self.stack).enter_context(self.nc.psum_tensor(self.prefix + name, list(shape), dtype))

    @contextlib.contextmanager
    def scope(self, prefix):
        old_stack, old_prefix = self.stack, self.prefix
        sub = contextlib.ExitStack()
        self.stack, self.prefix = sub, prefix
        try:
            yield
        finally:
            self.barrier()
            sub.close()
            self.stack, self.prefix = old_stack, old_prefix

    def _track(self, op, r, w, after=()):
        deps = set(after)
        for k in r:
            st = self.keys.get(k)
            if st is None:
                st = self.keys[k] = [None, {}, []]
            if st[0] is not None:
                deps.add(st[0])
            if op.isdma:
                st[2].append(op)
            else:
                st[1][op.eng] = op
        for k in w:
            st = self.keys.get(k)
            if st is None:
                st = self.keys[k] = [None, {}, []]
            if st[0] is not None:
                deps.add(st[0])
            for rd in st[1].values():
                deps.add(rd)
            for rd in st[2]:
                deps.add(rd)
            st[0] = op
            st[1] = {}
            st[2] = []
        deps.discard(op)
        for d in deps:
            if d.eng == "pe" and op.eng == "pe" and not d.isdma and not op.isdma:
                continue
            op.deps.append(d)
            d.inc = True

    def add(self, eng, fn, r=(), w=(), after=()):
        op = Op(eng, fn, False)
        r = list(r) + self.extra_r
        self._track(op, r, w, after)
        self.ops[eng].append(op)
        return op

    def dma(self, eng, out, in_, r=(), w=(), sem=None, after=(), **kw):
        op = Op(eng, (lambda e, out=out, in_=in_, kw=kw: e.dma_start(out=out, in_=in_, **kw)), True)
        op.sem = ("dma", sem)
        op.inc = True
        r = list(r) + self.extra_r
        self._track(op, r, w, after)
        self.ops[eng].append(op)
        return op

    def coll(self, kind, src, dst, r, w, sem):
        groups = [list(range(NCORES))]
        op = Op("pool", (lambda e: e.collective_compute(kind, ALU.bypass, replica_groups=groups, ins=[src], outs=[dst])), True)
        op.sem = ("dma", sem)
        op.inc = True
        self._track(op, list(r) + self.extra_r, w)
        self.ops["pool"].append(op)
        return op

    def mm(self, out, lhsT, rhs, start, stop, r, w):
        return self.add("pe", lambda e: e.matmul(out, lhsT, rhs, start=start, stop=stop), r=r, w=w)

    def act(self, out, in_, func, r, w, bias=0.0, scale=1.0, eng="act"):
        return self.add(eng, lambda e: e.activation(out=out, in_=in_, func=func, bias=bias, scale=scale), r=r, w=w)

    def tt(self, eng, out, in0, in1, op, r, w):
        return self.add(eng, lambda e: e.tensor_tensor(out=out, in0=in0, in1=in1, op=op), r=r, w=w)

    def ts(self, eng, out, in0, s1, s2, op0, op1, r, w):
        if s2 is None:
            return self.add(eng, lambda e: e.tensor_scalar(out, in0, s1, None, op0), r=r, w=w)
        return self.add(eng, lambda e: e.tensor_scalar(out, in0, s1, s2, op0, op1), r=r, w=w)

    def stt(self, eng, out, in0, scalar, in1, op0, op1, r, w):
        return self.add(eng, lambda e: e.scalar_tensor_tensor(out=out, in0=in0, scalar=scalar, in1=in1, op0=op0, op1=op1), r=r, w=w)

    def copy(self, eng, out, in_, r, w):
        if eng == "act":
            return self.add(eng, lambda e: e.activation(out=out, in_=in_, func=AF.Identity), r=r, w=w)
        return self.add(eng, lambda e: e.tensor_copy(out=out, in_=in_), r=r, w=w)

    def memset(self, eng, ap, val, w):
        return self.add(eng, lambda e: e.memset(ap, val), w=w)

    def barrier(self):
        allkeys = list(self.keys.keys())
        self.extra_r = []
        tok = ("bar", self.nbar)
        i = self.nbar
        self.nbar += 1
        self.add("dve", lambda e: e.memset(self.bar_t[:, i % 8:i % 8 + 1], 0.0), r=allkeys, w=allkeys + [tok])
        self.extra_r = [tok]

    def emit(self):
        nc = self.nc
        counts = {}
        for e in ENGS:
            for op in self.ops[e]:
                if op.isdma:
                    k = op.sem
                    counts[k] = counts.get(k, 0) + 16
                    op.val = counts[k]
                elif op.inc:
                    k = ("eng", e)
                    op.sem = k
                    counts[k] = counts.get(k, 0) + 1
                    op.val = counts[k]
        sems = {}
        st = contextlib.ExitStack()
        for i, k in enumerate(counts.keys()):
            sems[k] = st.enter_context(nc.semaphore("s%d" % i))
        self.maxcount = max(counts.values()) if counts else 0
        self.nsems = len(counts)
        engobj = {"pe": "tensor", "act": "scalar", "dve": "vector", "pool": "gpsimd", "sp": "sync"}

        def run(e, eo):
            waited = {}
            for op in self.ops[e]:
                need = {}
                for d in op.deps:
                    if need.get(d.sem, 0) < d.val:
                        need[d.sem] = d.val
                for k, v in need.items():
                    if waited.get(k, 0) < v:
                        eo.wait_ge(sems[k], v)
                        waited[k] = v
                ins = op.fn(eo)
                if op.inc:
                    ins.then_inc(sems[op.sem], 16 if op.isdma else 1)

        with nc.Block() as block:
            for e in ENGS:
                if not self.ops[e]:
                    continue
                getattr(block, engobj[e])(lambda eo, e=e: run(e, eo))
        st.close()
        self.stack.close()


def dram(nc, name, shape, dtype, kind):
    return nc.dram_tensor(name, list(shape), dtype, kind=kind).ap()


class Ctx:
    pass


def setup_common(p, T, nslot=3, ffn=True, bank=None):
    c = Ctx()
    c.T = T
    c.bank = bank if bank is not None else [p.ps("bank%d" % b, [128, 512]) for b in range(8)]
    c.NSLOT = nslot
    c.slot = [p.sb("slot%d" % s, [128, 8192], BF16) for s in range(c.NSLOT)]
    c.slot_i = 0
    c.ones = p.sb("ones", [128, 128], F32)
    p.memset("dve", c.ones[:], 1.0, w=["ones"])
    if ffn:
        c.xb = p.sb("xb", [128, 2, T], F32)
        c.xb_i = 0
        c.rstd = p.sb("rstd", [128, T], F32)
        c.sg = p.sb("sg", [128, 2, 512], F32)
        c.sq = p.sb("sq", [128, 2, 512], F32)
        c.sq_i = 0
        c.hT = p.sb("hT", [128, KC, T], BF16)
        c.aT = p.sb("aT", [128, FC, T], BF16)
    return c


def next_slot(c):
    s = c.slot_i % c.NSLOT
    c.slot_i += 1
    return s


def ffn_stage(p, c, name, tiles, segs, x_in, A_in, B_in, HG, wg, wu, wd, A_out, B_out,
              x_out, h_out, h_out_sb, interleave=None, hout_rot=0, mid_hook=None):
    T = c.T
    nt = len(tiles)
    ssb = [5, 6, 7]

    def segs_in(n0, n1):
        out = []
        for (c0, c1, j) in segs:
            a, b = max(c0, n0), min(c1, n1)
            if a < b:
                out.append((a, b, j))
        return out

    def load_x(src, k, tag):
        b = c.xb_i % 2
        c.xb_i += 1
        p.dma("sp", c.xb[:, b, :], src[k * 128:(k + 1) * 128, :], r=[(tag, k)], w=[("xb", b)], sem="xb%d" % b)
        return b

    def sumsq_accum(src_ap_fn, srckeys, k, first, last):
        for n, (n0, n1) in enumerate(tiles):
            q = c.sq_i % 2
            c.sq_i += 1
            w_ = n1 - n0
            p.act(c.sq[:, q, 0:w_], src_ap_fn(n0, n1), AF.Square, r=srckeys, w=[("sq", q)])
            p.mm(c.bank[ssb[n]][:, 0:w_], c.ones[:], c.sq[:, q, 0:w_], first, last,
                 r=["ones", ("sq", q)], w=[("bank", ssb[n])])

    def make_rstd():
        for n, (n0, n1) in enumerate(tiles):
            w_ = n1 - n0
            p.act(c.rstd[:, n0:n1], c.bank[ssb[n]][:, 0:w_], AF.Sqrt, r=[("bank", ssb[n])], w=[("rstd", n)],
                  bias=c.epsb[:, 0:1], scale=1.0 / D)
            p.add("dve", lambda e, n0=n0, n1=n1: e.reciprocal(out=c.rstd[:, n0:n1], in_=c.rstd[:, n0:n1]),
                  r=[("rstd", n)], w=[("rstd", n)])

    rstd_keys = [("rstd", n) for n in range(nt)]

    def apply_norm(b, k, A, B, dst, dstkey, kk=None):
        if kk is None:
            kk = k
        p.tt("dve", c.xb[:, b, :], c.xb[:, b, :], c.rstd[:, :], ALU.mult, r=[("xb", b)] + rstd_keys, w=[("xb", b)])
        for (c0, c1, j) in segs:
            bias = B[:, k, j:j + 1] if B is not None else 0.0
            p.act(dst[:, kk, c0:c1], c.xb[:, b, c0:c1], AF.Identity, r=[("xb", b), "modAB"], w=[(dstkey, kk)],
                  bias=bias, scale=A[:, k, j:j + 1])

    for k in range(KC):
        b = load_x(x_in, k, name + "_xin")
        sumsq_accum(lambda n0, n1, b=b: c.xb[:, b, n0:n1], [("xb", b)], k, k == 0, k == KC - 1)
    make_rstd()
    if mid_hook is not None:
        mid_hook()
    for k in range(KC):
        b = load_x(x_in, k, name + "_xin")
        apply_norm(b, k, A_in, B_in, c.hT, "hT")

    wg_r = wg.rearrange("(k p) n -> p k n", p=128)
    wu_r = wu.rearrange("(k p) n -> p k n", p=128)
    gu_i = 0
    for fb in range(FC // 2):
        s = next_slot(c)
        sl = c.slot[s]
        wgs = sl[:, 0:KC * 256].rearrange("p (k n) -> p k n", k=KC)
        wus = sl[:, KC * 256:2 * KC * 256].rearrange("p (k n) -> p k n", k=KC)
        p.dma("pool", wgs, wg_r[:, :, fb * 256:(fb + 1) * 256], w=[("slot", s)], sem="slot%d" % s)
        p.dma("pool", wus, wu_r[:, :, fb * 256:(fb + 1) * 256], w=[("slot", s)], sem="slot%d" % s)
        for ff in range(2):
            f = fb * 2 + ff
            for n, (n0, n1) in enumerate(tiles):
                w_ = n1 - n0
                par = gu_i % 2
                gu_i += 1
                gb, ub = par * 2, par * 2 + 1
                for k in range(KC):
                    p.mm(c.bank[gb][:, 0:w_], wgs[:, k, ff * 128:(ff + 1) * 128], c.hT[:, k, n0:n1], k == 0, k == KC - 1,
                         r=[("slot", s), ("hT", k)], w=[("bank", gb)])
                for k in range(KC):
                    p.mm(c.bank[ub][:, 0:w_], wus[:, k, ff * 128:(ff + 1) * 128], c.hT[:, k, n0:n1], k == 0, k == KC - 1,
                         r=[("slot", s), ("hT", k)], w=[("bank", ub)])
                p.act(c.sg[:, par, 0:w_], c.bank[gb][:, 0:w_], AF.Silu, r=[("bank", gb)], w=[("sg", par)])
                p.tt("dve", c.aT[:, f, n0:n1], c.sg[:, par, 0:w_], c.bank[ub][:, 0:w_], ALU.mult,
                     r=[("sg", par), ("bank", ub)], w=[("aT", f)])
        if interleave is not None:
            interleave(fb)

    wd_r = wd.rearrange("(f p) n -> p f n", p=128)
    HF = FC // 2
    dn_i = 0
    for db in range(KC // 2):
        ss_ = []
        for half in range(2):
            s = next_slot(c)
            ws = c.slot[s][:, 0:HF * 256].rearrange("p (f n) -> p f n", f=HF)
            p.dma("pool", ws, wd_r[:, half * HF:(half + 1) * HF, db * 256:(db + 1) * 256], w=[("slot", s)], sem="slot%d" % s)
            ss_.append((s, ws))
        for dd in range(2):
            d = db * 2 + dd
            b = load_x(x_in, d, name + "_xin")
            for n, (n0, n1) in enumerate(tiles):
                w_ = n1 - n0
                ob = dn_i % 2
                dn_i += 1
                for f in range(FC):
                    s, ws = ss_[f // HF]
                    p.mm(c.bank[ob][:, 0:w_], ws[:, f % HF, dd * 128:(dd + 1) * 128], c.aT[:, f, n0:n1], f == 0, f == FC - 1,
                         r=[("slot", s), ("aT", f)], w=[("bank", ob)])
                for (a0, a1, j) in segs_in(n0, n1):
                    p.stt("dve", c.xb[:, b, a0:a1], c.bank[ob][:, a0 - n0:a1 - n0], HG[:, d, j:j + 1], c.xb[:, b, a0:a1],
                          ALU.mult, ALU.add, r=[("bank", ob), ("xb", b), "modAB"], w=[("xb", b)])
            sumsq_accum(lambda n0, n1, b=b: c.xb[:, b, n0:n1], [("xb", b)], d, d == 0, d == KC - 1)
            p.dma("sp", x_out[d * 128:(d + 1) * 128, :], c.xb[:, b, :], r=[("xb", b)], w=[(name + "_xout", d)], sem=name + "_xo")
    make_rstd()
    for k in range(KC):
        b = load_x(x_out, k, name + "_xout")
        kk = k % hout_rot if hout_rot else k
        apply_norm(b, k, A_out, B_out, h_out_sb, name + "_hout", kk)
        p.dma("sp", h_out[k * 128:(k + 1) * 128, :], h_out_sb[:, kk, :], r=[(name + "_hout", kk)], w=[(name + "_hdram", k)],
              sem=name + "_ho")


def make_AB(p, c, modfm, nw, sub, want_out=None):
    A = p.sb("A%d" % sub, [128, KC, 2], F32)
    B = p.sb("B%d" % sub, [128, KC, 2], F32)
    G = p.sb("G%d" % sub, [128, KC, 2], F32)
    m = modfm[:, :].rearrange("p (c j) -> p c j", j=2)
    sh = m[:, (3 * sub) * KC:(3 * sub + 1) * KC, :]
    sc = m[:, (3 * sub + 1) * KC:(3 * sub + 2) * KC, :]
    gt = m[:, (3 * sub + 2) * KC:(3 * sub + 3) * KC, :]
    for j in range(2):
        p.stt("dve", A[:, :, j], sc[:, :, j], 1.0, nw[:, sub, :], ALU.add, ALU.mult, r=["modfm", "nw"], w=["modAB"])
    p.copy("dve", B[:], sh, r=["modfm"], w=["modAB"])
    return A, B, G, gt


def build_p1():
    nc = bass.Bass("TRN2", target_bir_lowering=False)
    T = 1056
    xT = dram(nc, "xT", [D, T], F32, "ExternalInput")
    c2 = dram(nc, "c2", [128, KC * 2], F32, "ExternalInput")
    w_ada = dram(nc, "w_ada", [D, 9 * D], F32, "ExternalInput")
    b_fm = dram(nc, "b_fm", [128, 144], F32, "ExternalInput")
    nwd = dram(nc, "nw", [128, 3 * KC], F32, "ExternalInput")
    wg = dram(nc, "wg", [D, DFF], F32, "ExternalInput")
    wu = dram(nc, "wu", [D, DFF], F32, "ExternalInput")
    wd = dram(nc, "wd", [DFF, D], F32, "ExternalInput")
    x1T = dram(nc, "x1T", [D, T], F32, "ExternalOutput")
    hT2 = dram(nc, "hT2", [D, T], BF16, "ExternalOutput")
    modo = dram(nc, "modfm", [128, 288], F32, "ExternalOutput")

    p = Prog(nc)
    c = setup_common(p, T)
    c.epsb = p.sb("epsb", [128, 1], F32)
    p.memset("dve", c.epsb[:], EPS, w=["epsb"])
    c2t = p.sb("c2t", [128, KC * 2], F32)
    sc2 = p.sb("sc2", [128, KC, 2], BF16)
    bfm = p.sb("bfm", [128, 144], F32)
    nw = p.sb("nwt", [128, 3, KC], F32)
    modfm = p.sb("modfm_sb", [128, 288], F32)
    p.dma("sp", c2t[:], c2, w=["c2t"], sem="c0")
    p.dma("sp", bfm[:], b_fm, w=["bfm"], sem="c1")
    p.dma("sp", nw[:].rearrange("p s k -> p (s k)"), nwd, w=["nw"], sem="c2")
    p.act(sc2[:].rearrange("p k j -> p (k j)"), c2t[:], AF.Silu, r=["c2t"], w=["sc2"])

    wa_r = w_ada.rearrange("(k p) n -> p k n", p=128)
    modps = c.bank[4]

    def mod_block(blk):
        s = next_slot(c)
        ws = c.slot[s][:, :].rearrange("p (k n) -> p k n", k=KC)
        p.dma("pool", ws, wa_r[:, :, blk * 512:(blk + 1) * 512], w=[("slot", s)], sem="slot%d" % s)
        for cc in range(4):
            ch = blk * 4 + cc
            for k in range(KC):
                p.mm(modps[:, 2 * ch:2 * ch + 2], ws[:, k, cc * 128:(cc + 1) * 128], sc2[:, k, :], k == 0, k == KC - 1,
                     r=[("slot", s), "sc2"], w=[("bank", 4)])

    def mod_finish(c0, c1):
        for j in range(2):
            src = modps[:, 2 * c0:2 * c1].rearrange("p (c j) -> p c j", j=2)[:, :, j]
            dst = modfm[:, 2 * c0:2 * c1].rearrange("p (c j) -> p c j", j=2)[:, :, j]
            p.tt("dve", dst, src, bfm[:, c0:c1], ALU.add, r=[("bank", 4), "bfm"], w=["modfm"])

    A1 = p.sb("A1x", [128, KC, 2], F32)
    B1 = p.sb("B1x", [128, KC, 2], F32)
    G1 = p.sb("G1x", [128, KC, 2], F32)
    m_ = modfm[:, :].rearrange("p (c j) -> p c j", j=2)
    gt1 = m_[:, 2 * KC:3 * KC, :]

    def mid_hook():
        for blk in range(8):
            mod_block(blk)
        mod_finish(0, 32)
        for j in range(2):
            p.stt("dve", A1[:, :, j], m_[:, KC:2 * KC, j], 1.0, nw[:, 0, :], ALU.add, ALU.mult, r=["modfm", "nw"], w=["modAB"])
        p.copy("dve", B1[:], m_[:, 0:KC, :], r=["modfm"], w=["modAB"])

    tiles = [(0, 352), (352, 704), (704, 1056)]
    segs = [(0, 1024, 0), (1024, 1056, 1)]
    A2 = p.sb("A2x", [128, KC, 2], F32)
    B2 = p.sb("B2x", [128, KC, 2], F32)

    pending = list(range(8, 36))

    def inter2(fb):
        if not pending:
            return
        for _ in range(2):
            if pending:
                mod_block(pending.pop(0))
        if not pending:
            mod_finish(32, 144)
            p.ts("dve", G1[:], gt1, 0.5, None, ALU.mult, None, r=["modfm"], w=["modAB"])
            m = modfm[:, :].rearrange("p (c j) -> p c j", j=2)
            for j in range(2):
                p.stt("dve", A2[:, :, j], m[:, 4 * KC:5 * KC, j], 1.0, nw[:, 1, :], ALU.add, ALU.mult,
                      r=["modfm", "nw"], w=["modAB"])
            p.copy("dve", B2[:], m[:, 3 * KC:4 * KC, :], r=["modfm"], w=["modAB"])
            p.dma("sp", modo, modfm[:], r=["modfm"], w=["modo"], sem="modo")

    ffn_stage(p, c, "f1", tiles, segs, xT, A1, B1, G1, wg, wu, wd, A2, B2, x1T, hT2, c.hT, interleave=inter2, mid_hook=mid_hook)
    p.add("sp", lambda e: e.nop(), r=["modo"] + [("f1_hdram", k) for k in range(KC)] + [("f1_xout", k) for k in range(KC)])
    p.emit()
    return nc, p


def _fm(v):
    return np.ascontiguousarray(v.reshape(-1, 128).T)


def run_p1(inputs):
    x = inputs["x"][0]
    ctx = inputs["ctx"][0]
    nc, p = build_p1()
    c2 = np.stack([_fm(inputs["c"][0]), _fm(inputs["c_ctx"])], axis=-1).reshape(128, 32)
    nw = np.stack([_fm(inputs["norm_w"][0, s]) for s in range(3)], axis=1).reshape(128, 48)
    common = {
        "c2": np.ascontiguousarray(c2, dtype=np.float32),
        "w_ada": np.ascontiguousarray(inputs["w_ada"][0]),
        "b_fm": _fm(inputs["b_ada"][0]),
        "nw": np.ascontiguousarray(nw),
        "wg": np.ascontiguousarray(inputs["ffn_w_gate"][0, 0]),
        "wu": np.ascontiguousarray(inputs["ffn_w_up"][0, 0]),
        "wd": np.ascontiguousarray(inputs["ffn_w_down"][0, 0]),
    }
    in_maps = []
    for i in range(NCORES):
        xt = np.concatenate([x[1024 * i:1024 * (i + 1)], ctx[32 * i:32 * (i + 1)]], axis=0).T
        m = dict(common)
        m["xT"] = np.ascontiguousarray(xt)
        in_maps.append(m)
    res = run_bass_kernel_spmd(nc, in_maps, core_ids=list(range(NCORES)))
    return res.results


NTOK = 8448
NCH = 66
UW = 8456


def upos(t):
    return 2 + t if t < 256 else 6 + t


UW2 = 8576


def upos2(t):
    return 32 + t if t < 256 else 96 + t


def emit_p2a(nc, p, bank, pre=""):
    hT = dram(nc, pre + "hT", [D, NTOK], BF16, "ExternalInput")
    wml = dram(nc, pre + "wml", [D, 772], F32, "ExternalInput")
    cst = dram(nc, pre + "cst", [128, 3 * 128], F32, "ExternalInput")
    cw = dram(nc, pre + "cw", [128, 2 * 5 + 2], F32, "ExternalInput")
    gbn = dram(nc, pre + "gbn", [128, 4 + 256], F32, "ExternalInput")
    yo = dram(nc, "yml", [8192, 256], BF16, "ExternalOutput")
    cs = p.sb("cs", [128, 3, 128], F32)
    ident, triU, triL = cs[:, 0, :], cs[:, 1, :], cs[:, 2, :]
    identb = p.sb("identb", [128, 128], BF16)
    cwt = p.sb("cwt", [128, 12], F32)
    gb = p.sb("gb", [128, 260], F32)
    ones = p.sb("ones", [128, 128], F32)
    epsb = p.sb("epsb", [128, 1], F32)
    QT = p.sb("QT", [128, NTOK], BF16)
    KT = p.sb("KT", [128, NTOK], BF16)
    Ktm = p.sb("Ktm", [128, NCH, 128], BF16)
    Va = p.sb("Va", [128, NCH, 257], BF16)
    SO = p.sb("SO", [128, NCH, 256], BF16)
    gat = p.sb("gat", [128, NCH, 4], F32)
    p.dma("sp", cs[:].rearrange("p a b -> p (a b)"), cst, w=["cs"], sem="c0")
    p.dma("sp", cwt[:], cw, w=["cwt"], sem="c1")
    p.dma("sp", gb[:], gbn, w=["gb"], sem="c2")
    p.memset("dve", ones[:], 1.0, w=["ones"])
    p.memset("dve", epsb[:], EPS, w=["epsb"])
    p.copy("dve", identb[:], ident, r=["cs"], w=["identb"])
    p.memset("dve", Va[:, :, 256:257], 1.0, w=["Va"])

    stA = contextlib.ExitStack()
    W = p.sb("W", [128, KC, 772], BF16, stack=stA)
    ht2 = p.sb("ht", [128, 2, KC, 256], BF16, stack=stA)
    U = p.sb("U", [128, 2, UW], BF16, stack=stA)
    dg = p.sb("dg", [128, 2, 5, 128], BF16, stack=stA)
    wr = wml.rearrange("(k p) n -> p k n", p=128)
    for k0 in range(0, KC, 4):
        p.dma("pool", W[:, k0:k0 + 4, :], wr[:, k0:k0 + 4, :], w=["W"], sem="W")
    p.memset("dve", U[:], 0.0, w=["U"])
    for f in range(2):
        for tap in range(5):
            p.ts("dve", dg[:, f, tap, :], ident, cwt[:, f * 5 + tap:f * 5 + tap + 1], None, ALU.mult, None,
                 r=["cs", "cwt"], w=["dg"])
    hr = hT.rearrange("(k p) n -> p k n", p=128)
    bi = 0
    for ti, t0 in enumerate(range(0, NTOK, 256)):
        n = 256
        hq = ti % 2
        ht = ht2[:, hq]
        p.dma("sp", ht[:, :, 0:n], hr[:, :, t0:t0 + n], w=[("ht", hq)], sem="ht%d" % hq)
        for f in range(2):
            b = bi % 4
            bi += 1
            for k in range(KC):
                p.mm(bank[b][:, 0:n], W[:, k, f * 128:(f + 1) * 128], ht[:, k, 0:n], k == 0, k == KC - 1,
                     r=["W", ("ht", hq)], w=[("bank", b)])
            for (a0, a1) in [(0, n)]:
                p0 = upos(t0 + a0)
                p.copy("act", U[:, f, p0:p0 + a1 - a0], bank[b][:, a0:a1], r=[("bank", b)], w=["U"])
        for sub in range(n // 128):
            ch = t0 // 128 + sub
            b = bi % 4
            bi += 1
            b2 = 4 + b
            for k in range(KC):
                p.mm(bank[b][:, 0:260], ht[:, k, sub * 128:(sub + 1) * 128], W[:, k, 256:516], k == 0, k == KC - 1,
                     r=["W", ("ht", hq)], w=[("bank", b)])
            for k in range(KC):
                p.mm(bank[b2][:, 0:256], ht[:, k, sub * 128:(sub + 1) * 128], W[:, k, 516:772], k == 0, k == KC - 1,
                     r=["W", ("ht", hq)], w=[("bank", b2)])
            p.copy("dve", Va[:, ch, 0:256], bank[b][:, 0:256], r=[("bank", b)], w=["Va"])
            p.tt("dve", gat[:, ch, :], bank[b][:, 256:260], gb[:, 0:4], ALU.add, r=[("bank", b), "gb"], w=["gat"])
            p.act(SO[:, ch, :], bank[b2][:, 0:256], AF.Sigmoid, r=[("bank", b2)], w=["SO"])
    for f, dst in ((0, QT), (1, KT)):
        for (s0, s1) in [(0, 256)] + [(a, a + 512) for a in range(256, NTOK, 512)]:
            b = bi % 4
            bi += 1
            n = s1 - s0
            p0 = upos(s0)
            for tap in range(5):
                p.mm(bank[b][:, 0:n], dg[:, f, tap, :], U[:, f, p0 + tap - 2:p0 + tap - 2 + n], tap == 0, tap == 4,
                     r=["dg", "U"], w=[("bank", b)])
            p.act(dst[:, s0:s1], bank[b][:, 0:n], AF.Silu, r=[("bank", b), "cwt"], w=["QK"], bias=cwt[:, 10 + f:11 + f])
    for ch in range(NCH):
        b = bi % 4
        bi += 1
        p.mm(bank[b][:, 0:128], KT[:, ch * 128:(ch + 1) * 128], identb[:], True, True, r=["QK", "identb"], w=[("bank", b)])
        p.copy("act", Ktm[:, ch, :], bank[b][:, 0:128], r=[("bank", b)], w=["Ktm"])
    p.barrier()
    stA.close()

    lf = p.sb("lf", [128, 2, NCH], F32)
    sc = p.sb("sc", [128, 8, NCH], F32)
    tmp = p.sb("tmpg", [128, 2, NCH], F32)
    for d in range(2):
        p.act(tmp[:, d, :], gat[:, :, 2 * d + 1], AF.Exp, r=["gat"], w=["tmpg"], scale=-1.0)
        p.act(tmp[:, d, :], tmp[:, d, :], AF.Ln, r=["tmpg"], w=["tmpg"], bias=1.0)
        p.ts("dve", lf[:, d, :], tmp[:, d, :], -1.0, None, ALU.mult, None, r=["tmpg"], w=["lf"])
    cum = bank[7]
    p.mm(cum[:, 0:NCH], triU, lf[:, 0, :], True, True, r=["cs", "lf"], w=[("bank", 7)])
    p.mm(cum[:, NCH:2 * NCH], triL, lf[:, 1, :], True, True, r=["cs", "lf"], w=[("bank", 7)])
    p.mm(cum[:, 2 * NCH:4 * NCH], ones[:], lf[:].rearrange("p d c -> p (d c)"), True, True, r=["ones", "lf"], w=[("bank", 7)])
    for d in range(2):
        bcol = cum[:, d * NCH:(d + 1) * NCH]
        tot = cum[:, (2 + d) * NCH:(3 + d) * NCH]
        ig = gat[:, :, 2 * d]
        o = 4 * d
        p.tt("dve", tmp[:, d, :], ig, bcol, ALU.subtract, r=["gat", ("bank", 7)], w=["tmpg"])
        p.act(sc[:, o + 0, :], tmp[:, d, :], AF.Exp, r=["tmpg"], w=["sc"])
        p.act(sc[:, o + 1, :], bcol, AF.Exp, r=[("bank", 7)], w=["sc"])
        p.ts("dve", sc[:, o + 1, :], sc[:, o + 1, :], 128.0 ** -0.5, None, ALU.mult, None, r=["sc"], w=["sc"])
        p.act(sc[:, o + 3, :], tot, AF.Exp, r=[("bank", 7)], w=["sc"])
        p.tt("dve", sc[:, o + 2, :], sc[:, o + 0, :], sc[:, o + 3, :], ALU.mult, r=["sc"], w=["sc"])

    C32 = p.sb("C32", [128, 257], F32)
    Cb = p.sb("Cb", [128, 257], BF16)
    PT = p.sb("PT", [128, 3, 128], BF16)
    kw = p.sb("kw", [128, 3, 128], BF16)
    sm = p.sb("sm", [128, 2, 4], F32)
    hraw = p.sb("hraw", [128, 2, 64, 256], BF16)
    den = p.sb("den", [128, 2, 64], F32)
    rr = p.sb("rr", [128, 2, 64], F32)
    ssq = p.sb("ssq", [128, 64], F32)
    rsd = p.sb("rsd", [128, 64], F32)
    hs = p.sb("hs", [128, 2, 256], F32)
    yb = p.sb("yb", [128, 2, 256], BF16)
    def pre(i, d, ch):
        q = i % 3
        o = 4 * d
        mask = triU if d == 0 else triL
        sb_, ub_ = 0 + i % 2, 4 + q
        tk = slice(ch * 128, (ch + 1) * 128)
        if ch >= 2:
            p.mm(bank[sb_][:, 0:128], KT[:, tk], QT[:, tk], True, True, r=["QK"], w=[("bank", sb_)])
            p.stt("dve", PT[:, q, :], bank[sb_][:, 0:128], sc[:, o + 0, ch:ch + 1], mask, ALU.mult, ALU.mult,
                  r=[("bank", sb_), "sc", "cs"], w=[("PT", q)])
        p.ts("dve", kw[:, q, :], Ktm[:, ch, :], sc[:, o + 2, ch:ch + 1], None, ALU.mult, None, r=["Ktm", "sc"], w=[("kw", q)])
        p.mm(bank[ub_][:, 0:257], kw[:, q, :], Va[:, ch, :], True, True, r=[("kw", q), "Va"], w=[("bank", ub_)])

    def post(i, d, ch):
        q3 = i % 3
        q = i % 2
        o = 4 * d
        ab_, ub_ = 2 + q, 4 + q3
        tk = slice(ch * 128, (ch + 1) * 128)
        if ch >= 2:
            p.mm(bank[ab_][:, 0:257], PT[:, q3, :], Va[:, ch, :], True, False, r=[("PT", q3), "Va"], w=[("bank", ab_)])
            p.mm(bank[ab_][:, 0:257], QT[:, tk], Cb[:], False, True, r=["QK", "Cb"], w=[("bank", ab_)])
        p.stt("dve", C32[:], C32[:], sc[:, o + 3, ch:ch + 1], bank[ub_][:, 0:257], ALU.mult, ALU.add,
              r=["C32", "sc", ("bank", ub_)], w=["C32"])
        p.copy("act", Cb[:], C32[:], r=["C32"], w=["Cb"])
        if ch >= 2:
            c_ = ch - 2
            p.act(den[:, d, c_:c_ + 1], bank[ab_][:, 256:257], AF.Abs, r=[("bank", ab_), "sc"], w=[("den", d, c_)],
                  scale=sc[:, o + 1, ch:ch + 1])
            p.copy("act", hraw[:, d, c_, :], bank[ab_][:, 0:256], r=[("bank", ab_)], w=[("hraw", d, c_)])

    seq = []
    for d in range(2):
        order = list(range(NCH)) if d == 0 else ([1, 0] + list(range(NCH - 1, 1, -1)))
        seq += [(d, ch) for ch in order]
    pre(0, *seq[0])
    pre(1, *seq[1])
    for i, (d, ch) in enumerate(seq):
        if i == 0 or seq[i - 1][0] != d:
            p.memset("dve", C32[:], 0.0, w=["C32"])
            p.memset("dve", Cb[:], 0.0, w=["Cb"])
        if i + 2 < len(seq):
            pre(i + 2, *seq[i + 2])
        post(i, d, ch)
    denk = [("den", d, c_) for d in range(2) for c_ in range(64)]
    for d in range(2):
        p.ts("dve", rr[:, d, :], den[:, d, :], 1.0, None, ALU.max, None, r=denk, w=["rr"])
        p.add("dve", lambda e, d=d: e.reciprocal(out=rr[:, d, :], in_=rr[:, d, :]), r=["rr"], w=["rr"])
        p.tt("dve", rr[:, d, :], rr[:, d, :], sc[:, 4 * d + 1, 2:66], ALU.mult, r=["rr", "sc"], w=["rr"])

    def comb(c_, q):
        p.ts("dve", hs[:, q, :], hraw[:, 0, c_, :], rr[:, 0, c_:c_ + 1], None, ALU.mult, None,
             r=[("hraw", 0, c_), "rr"], w=[("hs", q)])
        p.stt("dve", hs[:, q, :], hraw[:, 1, c_, :], rr[:, 1, c_:c_ + 1], hs[:, q, :], ALU.mult, ALU.add,
              r=[("hraw", 1, c_), "rr", ("hs", q)], w=[("hs", q)])

    for c_ in range(64):
        q = c_ % 2
        comb(c_, q)
        p.add("act", lambda e, q=q, c_=c_: e.activation(out=yb[:, q, :], in_=hs[:, q, :], func=AF.Square,
                                                        accum_out=ssq[:, c_:c_ + 1]), r=[("hs", q)], w=[("ssq", c_), ("yb", q)])
    p.act(rsd[:], ssq[:], AF.Sqrt, r=[("ssq", c_) for c_ in range(64)], w=["rsd"], bias=epsb[:, 0:1], scale=1.0 / 256)
    p.add("dve", lambda e: e.reciprocal(out=rsd[:], in_=rsd[:]), r=["rsd"], w=["rsd"])
    for c_ in range(64):
        q = c_ % 2
        comb(c_, q)
        p.stt("dve", hs[:, q, :], hs[:, q, :], rsd[:, c_:c_ + 1], gb[:, 4:260], ALU.mult, ALU.mult,
              r=[("hs", q), "rsd", "gb"], w=[("hs", q)])
        p.tt("dve", yb[:, q, :], hs[:, q, :], SO[:, c_ + 2, :], ALU.mult, r=[("hs", q), "SO"], w=[("yb", q)])
        p.dma("sp", yo[c_ * 128:(c_ + 1) * 128, :], yb[:, q, :], r=[("yb", q)], w=[("yo", c_ + 2)], sem="yo%d" % q)
    return [("yo", ch) for ch in range(2, NCH)]


def build_p2a():
    nc = bass.Bass("TRN2", target_bir_lowering=False)
    p = Prog(nc)
    bank = [p.ps("bank%d" % b, [128, 512]) for b in range(8)]
    keys = emit_p2a(nc, p, bank)
    p.add("sp", lambda e: e.nop(), r=keys)
    p.emit()
    return nc, p


def to_cm(a):
    return a.reshape(128, 64, -1).transpose(1, 0, 2).reshape(8192, -1)


def from_cm(a):
    return a.reshape(64, 128, -1).transpose(1, 0, 2).reshape(8192, -1)


def consts_tri():
    ident = np.eye(128, dtype=np.float32)
    triU = np.triu(np.ones((128, 128), np.float32))
    triL = np.tril(np.ones((128, 128), np.float32))
    return np.ascontiguousarray(np.concatenate([ident, triU, triL], axis=1))


def p2a_inputs(inputs, h_all, hc_all, j):
    w_in = inputs["w_in"][0]
    cols = np.concatenate([np.arange(j * 128, (j + 1) * 128), 1024 + np.arange(j * 128, (j + 1) * 128),
                           2048 + np.arange(j * 256, (j + 1) * 256), 6144 + np.arange(4) * 8 + j,
                           4096 + np.arange(j * 256, (j + 1) * 256)])
    cwm = inputs["ml_conv_w"][0]
    cb = inputs["ml_conv_b"][0]
    cw = np.concatenate([cwm[:, j * 128:(j + 1) * 128].T, cwm[:, 1024 + j * 128:1024 + (j + 1) * 128].T,
                         cb[j * 128:(j + 1) * 128, None], cb[1024 + j * 128:1024 + (j + 1) * 128, None]], axis=1)
    gbn = np.concatenate([inputs["ml_gate_b"][0][:, j], inputs["ml_norm_w"][0][j * 256:(j + 1) * 256]])
    hcm = np.concatenate([hc_all, to_cm(h_all)], axis=0).T
    return {"hT": np.ascontiguousarray(hcm), "wml": np.ascontiguousarray(w_in[:, cols]), "cst": consts_tri(),
            "cw": np.ascontiguousarray(cw, dtype=np.float32),
            "gbn": np.ascontiguousarray(np.broadcast_to(gbn[None, :], (128, 260)), dtype=np.float32)}


def emit_p2b(nc, p, bank, pre=""):
    hT = dram(nc, pre + "hT", [D, NTOK], BF16, "ExternalInput")
    wss = dram(nc, pre + "wss", [D, 1296], F32, "ExternalInput")
    cst = dram(nc, pre + "cst", [128, 3 * 128], F32, "ExternalInput")
    cw = dram(nc, pre + "cw", [128, 36], F32, "ExternalInput")
    vecd = dram(nc, pre + "vec", [128, 552], F32, "ExternalInput")
    yo = dram(nc, "yssm", [8192, 512], BF16, "ExternalOutput")
    Ud = dram(nc, pre + "Ud", [768, UW2], BF16, "ExternalOutput")
    Zd = dram(nc, pre + "Zd", [8192, 512], BF16, "ExternalOutput")
    Yf = dram(nc, pre + "Yf", [8192, 512], F32, "ExternalOutput")
    cs = p.sb("cs", [128, 3, 128], F32)
    ident, triU, triL = cs[:, 0, :], cs[:, 1, :], cs[:, 2, :]
    identb = p.sb("identb", [128, 128], BF16)
    cwt = p.sb("cwt", [128, 36], F32)
    vec = p.sb("vec_sb", [128, 552], F32)
    ones = p.sb("ones", [128, 128], F32)
    epsb = p.sb("epsb", [128, 1], F32)
    Xtm = p.sb("Xtm", [128, NCH, 512], BF16)
    Btm = p.sb("Btm", [128, NCH, 128], BF16)
    BT = p.sb("BT", [128, NTOK], BF16)
    CT = p.sb("CT", [128, NTOK], BF16)
    dt = p.sb("dt", [128, NCH, 16], F32)
    p.dma("sp", cs[:].rearrange("p a b -> p (a b)"), cst, w=["cs"], sem="c0")
    p.dma("sp", cwt[:], cw, w=["cwt"], sem="c1")
    p.dma("sp", vec[:], vecd, w=["vec"], sem="c2")
    p.memset("dve", ones[:], 1.0, w=["ones"])
    p.memset("dve", epsb[:], EPS, w=["epsb"])
    p.copy("dve", identb[:], ident, r=["cs"], w=["identb"])

    stA = contextlib.ExitStack()
    W = p.sb("W", [128, KC, 1296], BF16, stack=stA)
    ht2 = p.sb("ht", [128, 2, KC, 256], BF16, stack=stA)
    ev = p.sb("ev", [128, 2, 512], BF16, stack=stA)
    zb = p.sb("zb", [128, 2, 512], BF16, stack=stA)
    zt = p.sb("zt", [128, 64], BF16, stack=stA)
    dg = p.sb("dg", [128, 6, 5, 128], BF16, stack=stA)
    ut = p.sb("ut", [128, 2, 576], BF16, stack=stA)
    post = p.sb("post", [128, 2, 512], BF16, stack=stA)
    wr = wss.rearrange("(k p) n -> p k n", p=128)
    for k0 in range(0, KC, 4):
        p.dma("pool", W[:, k0:k0 + 4, :], wr[:, k0:k0 + 4, :], w=["W"], sem="W")
    p.memset("dve", zt[:], 0.0, w=["zt"])
    for f in range(6):
        for (a, n_) in ((0, 32), (288, 64), (8544, 32)):
            p.dma("sp", Ud[f * 128:(f + 1) * 128, a:a + n_], zt[:, 0:n_], r=["zt"], w=[("Ud", f)], sem="udz")
        for tap in range(5):
            p.ts("dve", dg[:, f, tap, :], ident, cwt[:, f * 5 + tap:f * 5 + tap + 1], None, ALU.mult, None,
                 r=["cs", "cwt"], w=["dg"])
    hr = hT.rearrange("(k p) n -> p k n", p=128)
    bi = 0
    ei = 0
    for ti, t0 in enumerate(range(0, NTOK, 256)):
        n = 256
        hq = ti % 2
        ht = ht2[:, hq]
        p.dma("sp", ht[:, :, 0:n], hr[:, :, t0:t0 + n], w=[("ht", hq)], sem="ht%d" % hq)
        for f in range(6):
            b = bi % 4
            bi += 1
            q = ei % 2
            ei += 1
            for k in range(KC):
                p.mm(bank[b][:, 0:n], W[:, k, 512 + f * 128:512 + (f + 1) * 128], ht[:, k, 0:n], k == 0, k == KC - 1,
                     r=["W", ("ht", hq)], w=[("bank", b)])
            p.copy("act", ev[:, q, 0:n], bank[b][:, 0:n], r=[("bank", b)], w=[("ev", q)])
            for (a0, a1) in [(0, n)]:
                p0 = upos2(t0 + a0)
                p.dma("sp", Ud[f * 128:(f + 1) * 128, p0:p0 + a1 - a0], ev[:, q, a0:a1], r=[("ev", q)], w=[("Ud", f)], sem="ev%d" % q)
        for sub in range(n // 128):
            ch = t0 // 128 + sub
            b = bi % 4
            bi += 1
            b2 = 4 + b
            if ch >= 2:
                q = ei % 2
                ei += 1
                for k in range(KC):
                    p.mm(bank[b][:, 0:512], ht[:, k, sub * 128:(sub + 1) * 128], W[:, k, 0:512], k == 0, k == KC - 1,
                         r=["W", ("ht", hq)], w=[("bank", b)])
                p.act(zb[:, q, :], bank[b][:, 0:512], AF.Silu, r=[("bank", b)], w=[("zb", q)])
                p.dma("sp", Zd[(ch - 2) * 128:(ch - 1) * 128, :], zb[:, q, :], r=[("zb", q)], w=[("Zd", ch)], sem="zb%d" % q)
            for k in range(KC):
                p.mm(bank[b2][:, 0:16], ht[:, k, sub * 128:(sub + 1) * 128], W[:, k, 1280:1296], k == 0, k == KC - 1,
                     r=["W", ("ht", hq)], w=[("bank", b2)])
            p.tt("dve", dt[:, ch, :], bank[b2][:, 0:16], vec[:, 0:16], ALU.add, r=[("bank", b2), "vec"], w=["dt"])
    dtf = dt[:].rearrange("p c e -> p (c e)")
    p.act(dtf, dtf, AF.Exp, r=["dt"], w=["dt"])
    p.act(dtf, dtf, AF.Ln, r=["dt"], w=["dt"], bias=1.0)
    ui = 0
    for (s0, s1) in [(0, 256)] + [(a, a + 512) for a in range(256, NTOK, 512)]:
        n = s1 - s0
        p0 = upos2(s0)
        for f in range(6):
            q = ui % 2
            ui += 1
            b = bi % 4
            bi += 1
            p.dma("sp", ut[:, q, 0:n + 64], Ud[f * 128:(f + 1) * 128, p0 - 32:p0 + n + 32], r=[("Ud", f)], w=[("ut", q)], sem="ut%d" % q)
            for tap in range(5):
                p.mm(bank[b][:, 0:n], dg[:, f, tap, :], ut[:, q, 30 + tap:30 + tap + n], tap == 0, tap == 4, r=["dg", ("ut", q)], w=[("bank", b)])
            if f == 5:
                p.act(CT[:, s0:s1], bank[b][:, 0:n], AF.Silu, r=[("bank", b), "cwt"], w=["CT"], bias=cwt[:, 30 + f:31 + f])
                continue
            dst = BT[:, s0:s1] if f == 4 else post[:, q, 0:n]
            dkey = "BT" if f == 4 else ("post", q)
            p.act(dst, bank[b][:, 0:n], AF.Silu, r=[("bank", b), "cwt"], w=[dkey], bias=cwt[:, 30 + f:31 + f])
            for sub in range(n // 128):
                ch = s0 // 128 + sub
                b2 = 4 + (bi % 4)
                bi += 1
                p.mm(bank[b2][:, 0:128], dst[:, sub * 128:(sub + 1) * 128], identb[:], True, True, r=[dkey, "identb"], w=[("bank", b2)])
                if f == 4:
                    p.copy("dve", Btm[:, ch, :], bank[b2][:, 0:128], r=[("bank", b2)], w=["Btm"])
                else:
                    p.copy("dve", Xtm[:, ch, f * 128:(f + 1) * 128], bank[b2][:, 0:128], r=[("bank", b2)], w=["Xtm"])
    p.barrier()
    stA.close()

    Aneg = p.sb("Aneg", [128, 16], F32)
    dtA = p.sb("dtA", [128, NCH, 16], F32)
    bsb = p.sb("bsb", [128, 2, NCH, 8], F32)
    tot = p.sb("tot", [128, 2, NCH, 8], F32)
    eo = p.sb("eo", [128, 2, NCH, 8], F32)
    ws = p.sb("ws", [128, 2, NCH, 8], F32)
    dec = p.sb("dec", [128, 2, NCH, 8], F32)
    p.act(Aneg[:], vec[:, 16:32], AF.Exp, r=["vec"], w=["Aneg"])
    p.ts("dve", Aneg[:], Aneg[:], -1.0, None, ALU.mult, None, r=["Aneg"], w=["Aneg"])
    p.tt("dve", dtA[:], dt[:], Aneg[:].unsqueeze(1).broadcast_to([128, NCH, 16]), ALU.mult, r=["dt", "Aneg"], w=["dtA"])
    H = NCH // 2
    for d in range(2):
        tri = triU if d == 0 else triL
        for hf_ in range(2):
            src = dtA[:, hf_ * H:(hf_ + 1) * H, d * 8:(d + 1) * 8]
            b0, b1 = (d * 2 + hf_) % 4, 4 + (d * 2 + hf_) % 4
            p.mm(bank[b0][:, 0:H * 8].rearrange("p (c e) -> p c e", e=8), tri, src, True, True, r=["cs", "dtA"], w=[("bank", b0)])
            p.mm(bank[b1][:, 0:H * 8].rearrange("p (c e) -> p c e", e=8), ones[:], src, True, True, r=["ones", "dtA"], w=[("bank", b1)])
            p.copy("dve", bsb[:, d, hf_ * H:(hf_ + 1) * H, :], bank[b0][:, 0:H * 8].rearrange("p (c e) -> p c e", e=8),
                   r=[("bank", b0)], w=["bsb"])
            p.copy("dve", tot[:, d, hf_ * H:(hf_ + 1) * H, :], bank[b1][:, 0:H * 8].rearrange("p (c e) -> p c e", e=8),
                   r=[("bank", b1)], w=["tot"])
    fl = lambda t: t[:].rearrange("p d c e -> p (d c e)")
    p.act(fl(eo), fl(bsb), AF.Exp, r=["bsb"], w=["eo"])
    p.tt("dve", fl(ws), fl(tot), fl(bsb), ALU.subtract, r=["tot", "bsb"], w=["ws"])
    p.act(fl(ws), fl(ws), AF.Exp, r=["ws"], w=["ws"])
    p.act(fl(dec), fl(tot), AF.Exp, r=["tot"], w=["dec"])

    negb = p.sb("negb", [128, 2, NCH, 8], F32)
    p.ts("dve", fl(negb), fl(bsb), -1.0, None, ALU.mult, None, r=["bsb"], w=["negb"])
    S32 = p.sb("S32", [128, 512], F32)
    Sb = p.sb("Sb", [128, 512], BF16)
    CBm = p.sb("CBm", [128, 2, 128], F32)
    diagb = p.sb("diagb", [128, 2, 8, 128], F32)
    dm = p.sb("dm", [128, 2, 8, 128], BF16)
    G = p.sb("G", [128, 2, 8, 128], BF16)
    xdt = p.sb("xdt", [128, 3, 512], BF16)
    xw = p.sb("xw", [128, 3, 512], BF16)
    yis = p.sb("yis", [128, 2, 512], F32)
    sus = p.sb("sus", [128, 2, 512], F32)
    tmpd = p.sb("tmpd", [128, 512], F32)
    t2 = p.sb("t2", [128, 512], F32)
    t3 = p.sb("t3", [128, 512], BF16)
    ysb = p.sb("ysb", [128, 2, 512], F32)
    yfl = p.sb("yfl", [128, 3, 512], F32)
    zl = p.sb("zl", [128, 3, 512], BF16)
    ob = p.sb("ob", [128, 2, 512], BF16)
    sm = p.sb("sm", [128, 2, 4], F32)
    bc3 = lambda ap: ap.unsqueeze(2).broadcast_to([128, 8, 64])
    v3 = lambda ap: ap.rearrange("p (e c) -> p e c", e=8)

    def preA(i, d, ch):
        q, t = i % 2, i % 3
        lat = ch >= 2
        tk = slice(ch * 128, (ch + 1) * 128)
        dsl = slice(d * 8, (d + 1) * 8)
        p.tt("pool", v3(xdt[:, t, :]), v3(Xtm[:, ch, :]), bc3(dt[:, ch, dsl]), ALU.mult, r=["Xtm", "dt"], w=[("xdt", t)])
        p.tt("pool", v3(xw[:, t, :]), v3(xdt[:, t, :]), bc3(ws[:, d, ch, :]), ALU.mult, r=[("xdt", t), "ws"], w=[("xw", t)])
        if not lat:
            return
        rows = slice((ch - 2) * 128, (ch - 1) * 128)
        p.tt("pool", diagb[:, q], ident.unsqueeze(1).broadcast_to([128, 8, 128]),
             bsb[:, d, ch, :].unsqueeze(2).broadcast_to([128, 8, 128]), ALU.mult, r=["cs", "bsb"], w=[("diagb", q)])
        cbp = bank[7][:, q * 128:(q + 1) * 128]
        p.mm(cbp, BT[:, tk], CT[:, tk], True, True, r=["BT", "CT"], w=[("cbp", q)])
        for h2 in range(2):
            bb = 2 * q + h2
            p.mm(bank[bb][:, 0:512], ones[:], diagb[:, q, 4 * h2:4 * h2 + 4, :].rearrange("p e i -> p (e i)"), True, True,
                 r=["ones", ("diagb", q)], w=[("bank", bb)])
            for e4 in range(4):
                e = 4 * h2 + e4
                p.act(dm[:, q, e, :], bank[bb][:, e4 * 128:(e4 + 1) * 128], AF.Exp, r=[("bank", bb), "negb"], w=[("dm", q)],
                      bias=negb[:, d, ch, e:e + 1])
        if d == 1:
            p.dma("sp", yfl[:, t, :], Yf[rows, :], r=[("Yf", ch)], w=[("yfl", t)], sem="yl%d" % t)
            p.dma("sp", zl[:, t, :], Zd[rows, :], r=[("Zd", ch)], w=[("zl", t)], sem="zl%d" % t)
            p.tt("pool", v3(tmpd[:]), v3(Xtm[:, ch, :]), bc3(vec[:, 32:40]), ALU.mult, r=["Xtm", "vec"], w=["tmpd"])
            p.tt("pool", yfl[:, t, :], yfl[:, t, :], tmpd[:], ALU.add, r=["tmpd", ("yfl", t)], w=[("yfl", t)])

    def preB(i, d, ch):
        q, t = i % 2, i % 3
        tri = triU if d == 0 else triL
        lat = ch >= 2
        if lat:
            cbp = bank[7][:, q * 128:(q + 1) * 128]
            p.tt("dve", CBm[:, q, :], cbp, tri, ALU.mult, r=[("cbp", q), "cs"], w=[("CBm", q)])
            p.stt("dve", G[:, q], dm[:, q], 1.0, CBm[:, q, :].unsqueeze(1).broadcast_to([128, 8, 128]), ALU.min, ALU.mult,
                  r=[("dm", q), ("CBm", q)], w=[("G", q)])
            for e in range(8):
                p.mm(bank[4][:, e * 64:(e + 1) * 64], G[:, q, e, :], xdt[:, t, e * 64:(e + 1) * 64], True, True,
                     r=[("G", q), ("xdt", t)], w=[("bank", 4)])
            p.copy("act", yis[:, q, :], bank[4][:, 0:512], r=[("bank", 4)], w=[("yis", q)])
        p.mm(bank[6][:, 0:512], Btm[:, ch, :], xw[:, t, :], True, True, r=["Btm", ("xw", t)], w=[("bank", 6)])
        p.copy("act", sus[:, q, :], bank[6][:, 0:512], r=[("bank", 6)], w=[("sus", q)])

    def post(i, d, ch):
        q, t = i % 2, i % 3
        lat = ch >= 2
        tk = slice(ch * 128, (ch + 1) * 128)
        if lat:
            rows = slice((ch - 2) * 128, (ch - 1) * 128)
            p.mm(bank[5][:, 0:512], CT[:, tk], Sb[:], True, True, r=["CT", "Sb"], w=[("bank", 5)])
        p.tt("dve", v3(S32[:]), v3(S32[:]), bc3(dec[:, d, ch, :]), ALU.mult, r=["S32", "dec"], w=["S32"])
        p.tt("dve", S32[:], S32[:], sus[:, q, :], ALU.add, r=["S32", ("sus", q)], w=["S32"])
        p.copy("act", Sb[:], S32[:], r=["S32"], w=["Sb"])
        if lat:
            p.tt("dve", v3(t2[:]), v3(bank[5][:, 0:512]), bc3(eo[:, d, ch, :]), ALU.mult, r=[("bank", 5), "eo"], w=["t2"])
            p.tt("dve", ysb[:, q, :], t2[:], yis[:, q, :], ALU.add, r=["t2", ("yis", q)], w=[("ysb", q)])
            if d == 0:
                p.dma("sp", Yf[rows, :], ysb[:, q, :], r=[("ysb", q)], w=[("Yf", ch)], sem="yf%d" % q)
            else:
                p.tt("dve", ysb[:, q, :], ysb[:, q, :], yfl[:, t, :], ALU.add, r=[("ysb", q), ("yfl", t)], w=[("ysb", q)])
                p.tt("dve", ysb[:, q, :], ysb[:, q, :], zl[:, t, :], ALU.mult, r=[("ysb", q), ("zl", t)], w=[("ysb", q)])
                p.add("act", lambda e, q=q: e.activation(out=t3[:], in_=ysb[:, q, :], func=AF.Square, accum_out=sm[:, q, 0:1]),
                      r=[("ysb", q)], w=[("sm", q), "t3"])
                p.act(sm[:, q, 1:2], sm[:, q, 0:1], AF.Sqrt, r=[("sm", q)], w=[("sm", q)], bias=epsb[:, 0:1], scale=1.0 / 512)
                p.add("dve", lambda e, q=q: e.reciprocal(out=sm[:, q, 2:3], in_=sm[:, q, 1:2]), r=[("sm", q)], w=[("sm", q)])
                p.stt("dve", ob[:, q, :], ysb[:, q, :], sm[:, q, 2:3], vec[:, 40:552], ALU.mult, ALU.mult,
                      r=[("ysb", q), ("sm", q), "vec"], w=[("ob", q)])
                p.dma("sp", yo[rows, :], ob[:, q, :], r=[("ob", q)], w=[("yo", ch)], sem="yo%d" % q)

    seq = []
    for d in range(2):
        order = list(range(NCH)) if d == 0 else ([1, 0] + list(range(NCH - 1, 1, -1)))
        seq += [(d, ch) for ch in order]
    NS = len(seq)
    preA(0, *seq[0])
    preA(1, *seq[1])
    preB(0, *seq[0])
    for i, (d, ch) in enumerate(seq):
        if i == 0 or seq[i - 1][0] != d:
            p.memset("dve", S32[:], 0.0, w=["S32"])
            p.memset("dve", Sb[:], 0.0, w=["Sb"])
        if i + 2 < NS:
            preA(i + 2, *seq[i + 2])
        if i + 1 < NS:
            preB(i + 1, *seq[i + 1])
        post(i, d, ch)
    return [("yo", ch) for ch in range(2, NCH)]


def build_p2b():
    nc = bass.Bass("TRN2", target_bir_lowering=False)
    p = Prog(nc)
    bank = [p.ps("bank%d" % b, [128, 512]) for b in range(8)]
    keys = emit_p2b(nc, p, bank)
    p.add("sp", lambda e: e.nop(), r=keys)
    p.emit()
    return nc, p


def build_p2():
    nc = bass.Bass("TRN2", target_bir_lowering=False)
    p = Prog(nc)
    bank = [p.ps("bank%d" % b, [128, 512]) for b in range(8)]
    with p.scope("a_"):
        emit_p2a(nc, p, bank, "a_")
    with p.scope("b_"):
        keys = emit_p2b(nc, p, bank, "b_")
        p.add("sp", lambda e: e.nop(), r=keys)
    p.emit()
    return nc, p


def p2b_inputs(inputs, h_all, hc_all, g):
    w_in = inputs["w_in"][0]
    o = 6176
    cols = np.concatenate([o + np.arange(g * 512, (g + 1) * 512), o + 4096 + np.arange(g * 512, (g + 1) * 512),
                           o + 8192 + np.arange(g * 128, (g + 1) * 128), o + 9216 + np.arange(g * 128, (g + 1) * 128),
                           o + 10240 + np.arange(g * 8, (g + 1) * 8), o + 10304 + np.arange(g * 8, (g + 1) * 8)])
    cch = np.concatenate([np.arange(g * 512, (g + 1) * 512), 4096 + np.arange(g * 128, (g + 1) * 128),
                          5120 + np.arange(g * 128, (g + 1) * 128)])
    cwm = inputs["ssm_conv_w"][0][:, cch]
    cb = inputs["ssm_conv_b"][0][cch]
    cw = np.concatenate([cwm.reshape(5, 6, 128).transpose(2, 1, 0).reshape(128, 30), cb.reshape(6, 128).T], axis=1)
    hsl = slice(g * 8, (g + 1) * 8)
    vec = np.concatenate([inputs["ssm_dt_bias"][0][0, hsl], inputs["ssm_dt_bias"][0][1, hsl],
                          inputs["ssm_a_log"][0][0, hsl], inputs["ssm_a_log"][0][1, hsl],
                          inputs["ssm_d"][0][hsl], inputs["ssm_norm_w"][0][g * 512:(g + 1) * 512]])
    hrm = np.concatenate([hc_all, h_all], axis=0).T
    return {"hT": np.ascontiguousarray(hrm), "wss": np.ascontiguousarray(w_in[:, cols]), "cst": consts_tri(),
            "cw": np.ascontiguousarray(cw, dtype=np.float32),
            "vec": np.ascontiguousarray(np.broadcast_to(vec[None, :], (128, 552)), dtype=np.float32)}


def p2_inputs(inputs, h_all, hc_all, j):
    m = {"a_" + k: v for k, v in p2a_inputs(inputs, h_all, hc_all, j).items()}
    m.update({"b_" + k: v for k, v in p2b_inputs(inputs, h_all, hc_all, j).items()})
    return m


def emit_p3a(nc, p, bank, pre="", modd=None):
    T = 1024
    x1T = dram(nc, pre + "x1T", [D, T], F32, "ExternalInput")
    hTd = dram(nc, pre + "hT", [D, T], BF16, "ExternalInput")
    ymlT = dram(nc, pre + "ymlT", [D, T], BF16, "ExternalInput")
    yssT = dram(nc, pre + "yssT", [2 * D, T], BF16, "ExternalInput")
    if modd is None:
        modd = dram(nc, pre + "modfm", [128, 288], F32, "ExternalInput")
    bgd = dram(nc, pre + "bg", [128, 32], F32, "ExternalInput")
    wpm = dram(nc, pre + "wpm", [D, D], F32, "ExternalInput")
    wps = dram(nc, pre + "wps", [2 * D, D], F32, "ExternalInput")
    wgt = dram(nc, pre + "wgt", [D, 2 * D], F32, "ExternalInput")
    wo = dram(nc, pre + "wo", [D, D], F32, "ExternalInput")
    x2T = dram(nc, pre + "x2T", [D, T], F32, "ExternalOutput")
    c = setup_common(p, T, nslot=5, ffn=False, bank=bank)
    modfm = p.sb("modfm_sb", [128, 288], F32)
    bg = p.sb("bg_sb", [128, 32], F32)
    yml = p.sb("yml", [128, KC, 512], BF16)
    yss = p.sb("yss", [128, 2 * KC, 512], BF16)
    hT = p.sb("hTt", [128, KC, 512], BF16)
    mg = p.sb("mg", [128, KC, T], BF16)
    gs = p.sb("gs", [128, 2, 2, 512], F32)
    tu = p.sb("tu", [128, 2, 2, 512], F32)
    xt = p.sb("xt", [128, 2, 512], F32)
    p.dma("sp", modfm[:], modd, w=["modfm"], sem="c0")
    p.dma("sp", bg[:], bgd, w=["bg"], sem="c1")
    gmix = modfm[:, :].rearrange("p (c j) -> p c j", j=2)[:, 5 * KC:6 * KC, 0]
    r3 = lambda w_: w_.rearrange("(k p) n -> p k n", p=128)
    wpm_r, wps_r, wgt_r, wo_r = r3(wpm), r3(wps), r3(wgt), r3(wo)
    it = 0
    for n in range(2):
        ts_ = slice(n * 512, (n + 1) * 512)
        p.dma("sp", yml[:], r3(ymlT)[:, :, ts_], w=["yml"], sem="a0")
        p.dma("sp", yss[:], r3(yssT)[:, :, ts_], w=["yss"], sem="a1")
        p.dma("sp", hT[:], r3(hTd)[:, :, ts_], w=["hTt"], sem="a2")
        for dp in range(KC // 2):
            cs_ = slice(dp * 256, (dp + 1) * 256)
            s1, s2, s3 = next_slot(c), next_slot(c), next_slot(c)
            w1 = c.slot[s1][:, 0:4096].rearrange("p (k n) -> p k n", k=KC)
            w1g = c.slot[s1][:, 4096:8192].rearrange("p (k n) -> p k n", k=KC)
            w2 = c.slot[s2][:, :].rearrange("p (k n) -> p k n", k=2 * KC)
            w3 = c.slot[s3][:, 0:4096].rearrange("p (k n) -> p k n", k=KC)
            p.dma("pool", w1, wpm_r[:, :, cs_], w=[("slot", s1)], sem="slot%d" % s1)
            p.dma("pool", w1g, wgt_r[:, :, cs_], w=[("slot", s1)], sem="slot%d" % s1)
            p.dma("pool", w2, wps_r[:, :, cs_], w=[("slot", s2)], sem="slot%d" % s2)
            p.dma("pool", w3, wgt_r[:, :, D + dp * 256:D + (dp + 1) * 256], w=[("slot", s3)], sem="slot%d" % s3)
            for dd in range(2):
                d = dp * 2 + dd
                q = it % 2
                it += 1
                b0 = q * 4
                dsl = slice(dd * 128, (dd + 1) * 128)
                for k in range(KC):
                    p.mm(c.bank[b0][:, :], w1[:, k, dsl], yml[:, k, :], k == 0, k == KC - 1, r=[("slot", s1), "yml"], w=[("bank", b0)])
                for k in range(2 * KC):
                    p.mm(c.bank[b0 + 1][:, :], w2[:, k, dsl], yss[:, k, :], k == 0, k == 2 * KC - 1, r=[("slot", s2), "yss"], w=[("bank", b0 + 1)])
                for k in range(KC):
                    p.mm(c.bank[b0 + 2][:, :], w1g[:, k, dsl], hT[:, k, :], k == 0, k == KC - 1, r=[("slot", s1), "hTt"], w=[("bank", b0 + 2)])
                for k in range(KC):
                    p.mm(c.bank[b0 + 3][:, :], w3[:, k, dsl], hT[:, k, :], k == 0, k == KC - 1, r=[("slot", s3), "hTt"], w=[("bank", b0 + 3)])
                p.act(gs[:, q, 0, :], c.bank[b0 + 2][:, :], AF.Sigmoid, r=[("bank", b0 + 2), "bg"], w=[("gs", q)], bias=bg[:, d:d + 1])
                p.act(gs[:, q, 1, :], c.bank[b0 + 3][:, :], AF.Sigmoid, r=[("bank", b0 + 3), "bg"], w=[("gs", q)], bias=bg[:, KC + d:KC + d + 1])
                p.tt("dve", tu[:, q, 0, :], gs[:, q, 0, :], c.bank[b0][:, :], ALU.mult, r=[("gs", q), ("bank", b0)], w=[("tu", q)])
                p.tt("dve", tu[:, q, 1, :], gs[:, q, 1, :], c.bank[b0 + 1][:, :], ALU.mult, r=[("gs", q), ("bank", b0 + 1)], w=[("tu", q)])
                p.tt("dve", mg[:, d, ts_], tu[:, q, 0, :], tu[:, q, 1, :], ALU.add, r=[("tu", q)], w=[("mg", n)])
    for n in range(2):
        ts_ = slice(n * 512, (n + 1) * 512)
        for dp in range(KC // 2):
            s1 = next_slot(c)
            w1 = c.slot[s1][:, 0:4096].rearrange("p (k n) -> p k n", k=KC)
            p.dma("pool", w1, wo_r[:, :, dp * 256:(dp + 1) * 256], w=[("slot", s1)], sem="slot%d" % s1)
            for dd in range(2):
                d = dp * 2 + dd
                q = it % 2
                it += 1
                b0 = q
                p.dma("sp", xt[:, q, :], x1T[d * 128:(d + 1) * 128, ts_], w=[("xt", q)], sem="xt%d" % q)
                for k in range(KC):
                    p.mm(c.bank[b0][:, :], w1[:, k, dd * 128:(dd + 1) * 128], mg[:, k, ts_], k == 0, k == KC - 1,
                         r=[("slot", s1), ("mg", n)], w=[("bank", b0)])
                p.stt("dve", xt[:, q, :], c.bank[b0][:, :], gmix[:, d:d + 1], xt[:, q, :], ALU.mult, ALU.add,
                      r=[("bank", b0), ("xt", q), "modfm"], w=[("xt", q)])
                p.dma("sp", x2T[d * 128:(d + 1) * 128, ts_], xt[:, q, :], r=[("xt", q)], w=[("x2", d, n)], sem="xo%d" % q)
    return x2T, [("x2", d, n) for d in range(KC) for n in range(2)]


def build_p3a():
    nc = bass.Bass("TRN2", target_bir_lowering=False)
    p = Prog(nc)
    bank = [p.ps("bank%d" % b, [128, 512]) for b in range(8)]
    _, keys = emit_p3a(nc, p, bank)
    p.add("sp", lambda e: e.nop(), r=keys)
    p.emit()
    return nc, p


def emit_p3b(nc, p, bank, pre="", x2T=None, modd=None):
    T = 1024
    if x2T is None:
        x2T = dram(nc, pre + "x2T", [D, T], F32, "ExternalInput")
    if modd is None:
        modd = dram(nc, pre + "modfm", [128, 288], F32, "ExternalInput")
    nwd = dram(nc, pre + "nw", [128, 4 * KC], F32, "ExternalInput")
    wg = dram(nc, pre + "wg", [D, DFF], F32, "ExternalInput")
    wu = dram(nc, pre + "wu", [D, DFF], F32, "ExternalInput")
    wd = dram(nc, pre + "wd", [DFF, D], F32, "ExternalInput")
    x3T = dram(nc, pre + "x3T", [D, T], F32, "ExternalOutput")
    outT = dram(nc, "outT", [D, T], F32, "ExternalOutput")
    c = setup_common(p, T, bank=bank)
    c.epsb = p.sb("epsb", [128, 1], F32)
    p.memset("dve", c.epsb[:], EPS, w=["epsb"])
    modfm = p.sb("modfm_sb", [128, 288], F32)
    nw = p.sb("nwt", [128, 4, KC], F32)
    ob = p.sb("obuf", [128, 2, T], F32)
    p.dma("sp", modfm[:], modd, w=["modfm"], sem="c0")
    p.dma("sp", nw[:].rearrange("p s k -> p (s k)"), nwd, w=["nw"], sem="c1")
    A, B, G, gt = make_AB(p, c, modfm, nw, 2)
    p.ts("dve", G[:], gt, 0.5, None, ALU.mult, None, r=["modfm"], w=["modAB"])
    Ao = p.sb("Ao", [128, KC, 2], F32)
    for j in range(2):
        p.copy("dve", Ao[:, :, j], nw[:, 3, :], r=["nw"], w=["modAB"])
    tiles = [(0, 512), (512, 1024)]
    segs = [(0, 1024, 0)]
    ffn_stage(p, c, "f2", tiles, segs, x2T, A, B, G, wg, wu, wd, Ao, None, x3T, outT, ob, hout_rot=2)
    return [("f2_hdram", k) for k in range(KC)]


def build_p3b():
    nc = bass.Bass("TRN2", target_bir_lowering=False)
    p = Prog(nc)
    bank = [p.ps("bank%d" % b, [128, 512]) for b in range(8)]
    keys = emit_p3b(nc, p, bank)
    p.add("sp", lambda e: e.nop(), r=keys)
    p.emit()
    return nc, p


def build_p3():
    nc = bass.Bass("TRN2", target_bir_lowering=False)
    p = Prog(nc)
    bank = [p.ps("bank%d" % b, [128, 512]) for b in range(8)]
    modd = dram(nc, "modfm", [128, 288], F32, "ExternalInput")
    with p.scope("a_"):
        x2T, _ = emit_p3a(nc, p, bank, "a_", modd)
    with p.scope("b_"):
        keys = emit_p3b(nc, p, bank, "b_", x2T=x2T, modd=modd)
        p.add("sp", lambda e: e.nop(), r=keys)
    p.emit()
    return nc, p


def _run(nc, in_maps):
    return run_bass_kernel_spmd(nc, in_maps, core_ids=list(range(len(in_maps)))).results


def kernel(**inputs):
    bf = ml_dtypes.bfloat16
    r1 = run_p1(inputs)
    x1T = [np.asarray(r["x1T"]) for r in r1]
    hT2 = [np.asarray(r["hT2"]) for r in r1]
    modfm = np.asarray(r1[0]["modfm"])
    h_all = np.concatenate([h[:, :1024].T for h in hT2], axis=0)
    hc_all = np.concatenate([h[:, 1024:].T for h in hT2], axis=0)
    nc, _ = build_p2()
    r2 = _run(nc, [p2_inputs(inputs, h_all, hc_all, j) for j in range(NCORES)])
    yml = np.concatenate([from_cm(np.asarray(r["yml"])) for r in r2], axis=1)
    yss = np.concatenate([np.asarray(r["yssm"]) for r in r2], axis=1)
    nc, _ = build_p3()
    nw = np.concatenate([_fm(inputs["norm_w"][0, s]) for s in range(3)] + [_fm(inputs["final_norm_w"])], axis=1)
    common = {"modfm": modfm, "a_bg": _fm(inputs["b_gate"][0]),
              "a_wpm": np.ascontiguousarray(inputs["w_proj_ml"][0]), "a_wps": np.ascontiguousarray(inputs["w_proj_ssm"][0]),
              "a_wgt": np.ascontiguousarray(inputs["w_gate"][0]), "a_wo": np.ascontiguousarray(inputs["w_out"][0]),
              "b_nw": np.ascontiguousarray(nw),
              "b_wg": np.ascontiguousarray(inputs["ffn_w_gate"][0, 1]), "b_wu": np.ascontiguousarray(inputs["ffn_w_up"][0, 1]),
              "b_wd": np.ascontiguousarray(inputs["ffn_w_down"][0, 1])}
    maps = []
    for i in range(NCORES):
        sl = slice(1024 * i, 1024 * (i + 1))
        m = dict(common)
        m["a_x1T"] = np.ascontiguousarray(x1T[i][:, :1024])
        m["a_hT"] = np.ascontiguousarray(hT2[i][:, :1024])
        m["a_ymlT"] = np.ascontiguousarray(yml[sl].T)
        m["a_yssT"] = np.ascontiguousarray(yss[sl].T)
        maps.append(m)
    r3b = _run(nc, maps)
    out = np.concatenate([np.asarray(r["outT"]).T for r in r3b], axis=0)
    return np.ascontiguousarray(out[None].astype(np.float32))
```

```python
import contextlib
import numpy as np
import ml_dtypes
import concourse.bass as bass
import concourse.mybir as mybir
from concourse.bass_utils import run_bass_kernel_spmd

F32 = mybir.dt.float32
BF16 = mybir.dt.bfloat16
AF = mybir.ActivationFunctionType
ALU = mybir.AluOpType
AX = mybir.AxisListType

D = 2048
KC = 16
DFF = 5632
FC = 44
EPS = 1e-6
NCORES = 8

ENGS = ("pe", "act", "dve", "pool", "sp")


class Op:
    __slots__ = ("eng", "fn", "deps", "isdma", "sem", "val", "inc")

    def __init__(self, eng, fn, isdma):
        self.eng = eng
        self.fn = fn
        self.deps = []
        self.isdma = isdma
        self.sem = None
        self.val = None
        self.inc = False


class Prog:
    def __init__(self, nc):
        self.nc = nc
        self.ops = {e: [] for e in ENGS}
        self.keys = {}
        self.stack = contextlib.ExitStack()
        self.dram_n = 0
        self.extra_r = []
        self.nbar = 0
        self.prefix = ""
        self.bar_t = self.sb("bar_t", [128, 8], F32)

    def sb(self, name, shape, dtype, stack=None):
        return (stack or self.stack).enter_context(self.nc.sbuf_tensor(self.prefix + name, list(shape), dtype))

    def ps(self, name, shape, dtype=F32, stack=None):
        return (stack or self.stack).enter_context(self.nc.psum_tensor(self.prefix + name, list(shape), dtype))

    @contextlib.contextmanager
    def scope(self, prefix):
        old_stack, old_prefix = self.stack, self.prefix
        sub = contextlib.ExitStack()
        self.stack, self.prefix = sub, prefix
        try:
            yield
        finally:
            self.barrier()
            sub.close()
            self.stack, self.prefix = old_stack, old_prefix

    def _track(self, op, r, w, after=()):
        deps = set(after)
        for k in r:
            st = self.keys.get(k)
            if st is None:
                st = self.keys[k] = [None, {}, []]
            if st[0] is not None:
                deps.add(st[0])
            if op.isdma:
                st[2].append(op)
            else:
                st[1][op.eng] = op
        for k in w:
            st = self.keys.get(k)
            if st is None:
                st = self.keys[k] = [None, {}, []]
            if st[0] is not None:
                deps.add(st[0])
            for rd in st[1].values():
                deps.add(rd)
            for rd in st[2]:
                deps.add(rd)
            st[0] = op
            st[1] = {}
            st[2] = []
        deps.discard(op)
        for d in deps:
            if d.eng == "pe" and op.eng == "pe" and not d.isdma and not op.isdma:
                continue
            op.deps.append(d)
            d.inc = True

    def add(self, eng, fn, r=(), w=(), after=()):
        op = Op(eng, fn, False)
        r = list(r) + self.extra_r
        self._track(op, r, w, after)
        self.ops[eng].append(op)
        return op

    def dma(self, eng, out, in_, r=(), w=(), sem=None, after=(), **kw):
        op = Op(eng, (lambda e, out=out, in_=in_, kw=kw: e.dma_start(out=out, in_=in_, **kw)), True)
        op.sem = ("dma", sem)
        op.inc = True
        r = list(r) + self.extra_r
        self._track(op, r, w, after)
        self.ops[eng].append(op)
        return op

    def coll(self, kind, src, dst, r, w, sem):
        groups = [list(range(NCORES))]
        op = Op("pool", (lambda e: e.collective_compute(kind, ALU.bypass, replica_groups=groups, ins=[src], outs=[dst])), True)
        op.sem = ("dma", sem)
        op.inc = True
        self._track(op, list(r) + self.extra_r, w)
        self.ops["pool"].append(op)
        return op

    def mm(self, out, lhsT, rhs, start, stop, r, w):
        return self.add("pe", lambda e: e.matmul(out, lhsT, rhs, start=start, stop=stop), r=r, w=w)

    def act(self, out, in_, func, r, w, bias=0.0, scale=1.0, eng="act"):
        return self.add(eng, lambda e: e.activation(out=out, in_=in_, func=func, bias=bias, scale=scale), r=r, w=w)

    def tt(self, eng, out, in0, in1, op, r, w):
        return self.add(eng, lambda e: e.tensor_tensor(out=out, in0=in0, in1=in1, op=op), r=r, w=w)

    def ts(self, eng, out, in0, s1, s2, op0, op1, r, w):
        if s2 is None:
            return self.add(eng, lambda e: e.tensor_scalar(out, in0, s1, None, op0), r=r, w=w)
        return self.add(eng, lambda e: e.tensor_scalar(out, in0, s1, s2, op0, op1), r=r, w=w)

    def stt(self, eng, out, in0, scalar, in1, op0, op1, r, w):
        return self.add(eng, lambda e: e.scalar_tensor_tensor(out=out, in0=in0, scalar=scalar, in1=in1, op0=op0, op1=op1), r=r, w=w)

    def copy(self, eng, out, in_, r, w):
        if eng == "act":
            return self.add(eng, lambda e: e.activation(out=out, in_=in_, func=AF.Identity), r=r, w=w)
        return self.add(eng, lambda e: e.tensor_copy(out=out, in_=in_), r=r, w=w)

    def memset(self, eng, ap, val, w):
        return self.add(eng, lambda e: e.memset(ap, val), w=w)

    def barrier(self):
        allkeys = list(self.keys.keys())
        self.extra_r = []
        tok = ("bar", self.nbar)
        i = self.nbar
        self.nbar += 1
        self.add("dve", lambda e: e.memset(self.bar_t[:, i % 8:i % 8 + 1], 0.0), r=allkeys, w=allkeys + [tok])
        self.extra_r = [tok]

    def emit(self):
        nc = self.nc
        counts = {}
        for e in ENGS:
            for op in self.ops[e]:
                if op.isdma:
                    k = op.sem
                    counts[k] = counts.get(k, 0) + 16
                    op.val = counts[k]
                elif op.inc:
                    k = ("eng", e)
                    op.sem = k
                    counts[k] = counts.get(k, 0) + 1
                    op.val = counts[k]
        sems = {}
        st = contextlib.ExitStack()
        for i, k in enumerate(counts.keys()):
            sems[k] = st.enter_context(nc.semaphore("s%d" % i))
        self.maxcount = max(counts.values()) if counts else 0
        self.nsems = len(counts)
        engobj = {"pe": "tensor", "act": "scalar", "dve": "vector", "pool": "gpsimd", "sp": "sync"}

        def run(e, eo):
            waited = {}
            for op in self.ops[e]:
                need = {}
                for d in op.deps:
                    if need.get(d.sem, 0) < d.val:
                        need[d.sem] = d.val
                for k, v in need.items():
                    if waited.get(k, 0) < v:
                        eo.wait_ge(sems[k], v)
                        waited[k] = v
                ins = op.fn(eo)
                if op.inc:
                    ins.then_inc(sems[op.sem], 16 if op.isdma else 1)

        with nc.Block() as block:
            for e in ENGS:
                if not self.ops[e]:
                    continue
                getattr(block, engobj[e])(lambda eo, e=e: run(e, eo))
        st.close()
        self.stack.close()


def dram(nc, name, shape, dtype, kind):
    return nc.dram_tensor(name, list(shape), dtype, kind=kind).ap()


class Ctx:
    pass


def setup_common(p, T, nslot=3, ffn=True, bank=None):
    c = Ctx()
    c.T = T
    c.bank = bank if bank is not None else [p.ps("bank%d" % b, [128, 512]) for b in range(8)]
    c.NSLOT = nslot
    c.slot = [p.sb("slot%d" % s, [128, 8192], BF16) for s in range(c.NSLOT)]
    c.slot_i = 0
    c.ones = p.sb("ones", [128, 128], F32)
    p.memset("dve", c.ones[:], 1.0, w=["ones"])
    if ffn:
        c.xb = p.sb("xb", [128, 2, T], F32)
        c.xb_i = 0
        c.rstd = p.sb("rstd", [128, T], F32)
        c.sg = p.sb("sg", [128, 2, 512], F32)
        c.sq = p.sb("sq", [128, 2, 512], F32)
        c.sq_i = 0
        c.hT = p.sb("hT", [128, KC, T], BF16)
        c.aT = p.sb("aT", [128, FC, T], BF16)
    return c


def next_slot(c):
    s = c.slot_i % c.NSLOT
    c.slot_i += 1
    return s


def ffn_stage(p, c, name, tiles, segs, x_in, A_in, B_in, HG, wg, wu, wd, A_out, B_out,
              x_out, h_out, h_out_sb, interleave=None, hout_rot=0, mid_hook=None):
    T = c.T
    nt = len(tiles)
    ssb = [5, 6, 7]

    def segs_in(n0, n1):
        out = []
        for (c0, c1, j) in segs:
            a, b = max(c0, n0), min(c1, n1)
            if a < b:
                out.append((a, b, j))
        return out

    def load_x(src, k, tag):
        b = c.xb_i % 2
        c.xb_i += 1
        p.dma("sp", c.xb[:, b, :], src[k * 128:(k + 1) * 128, :], r=[(tag, k)], w=[("xb", b)], sem="xb%d" % b)
        return b

    def sumsq_accum(src_ap_fn, srckeys, k, first, last):
        for n, (n0, n1) in enumerate(tiles):
            q = c.sq_i % 2
            c.sq_i += 1
            w_ = n1 - n0
            p.act(c.sq[:, q, 0:w_], src_ap_fn(n0, n1), AF.Square, r=srckeys, w=[("sq", q)])
            p.mm(c.bank[ssb[n]][:, 0:w_], c.ones[:], c.sq[:, q, 0:w_], first, last,
                 r=["ones", ("sq", q)], w=[("bank", ssb[n])])

    def make_rstd():
        for n, (n0, n1) in enumerate(tiles):
            w_ = n1 - n0
            p.act(c.rstd[:, n0:n1], c.bank[ssb[n]][:, 0:w_], AF.Sqrt, r=[("bank", ssb[n])], w=[("rstd", n)],
                  bias=c.epsb[:, 0:1], scale=1.0 / D)
            p.add("dve", lambda e, n0=n0, n1=n1: e.reciprocal(out=c.rstd[:, n0:n1], in_=c.rstd[:, n0:n1]),
                  r=[("rstd", n)], w=[("rstd", n)])

    rstd_keys = [("rstd", n) for n in range(nt)]

    def apply_norm(b, k, A, B, dst, dstkey, kk=None):
        if kk is None:
            kk = k
        p.tt("dve", c.xb[:, b, :], c.xb[:, b, :], c.rstd[:, :], ALU.mult, r=[("xb", b)] + rstd_keys, w=[("xb", b)])
        for (c0, c1, j) in segs:
            bias = B[:, k, j:j + 1] if B is not None else 0.0
            p.act(dst[:, kk, c0:c1], c.xb[:, b, c0:c1], AF.Identity, r=[("xb", b), "modAB"], w=[(dstkey, kk)],
                  bias=bias, scale=A[:, k, j:j + 1])

    for k in range(KC):
        b = load_x(x_in, k, name + "_xin")
        sumsq_accum(lambda n0, n1, b=b: c.xb[:, b, n0:n1], [("xb", b)], k, k == 0, k == KC - 1)
    make_rstd()
    if mid_hook is not None:
        mid_hook()
    for k in range(KC):
        b = load_x(x_in, k, name + "_xin")
        apply_norm(b, k, A_in, B_in, c.hT, "hT")

    wg_r = wg.rearrange("(k p) n -> p k n", p=128)
    wu_r = wu.rearrange("(k p) n -> p k n", p=128)
    gu_i = 0
    for fb in range(FC // 2):
        s = next_slot(c)
        sl = c.slot[s]
        wgs = sl[:, 0:KC * 256].rearrange("p (k n) -> p k n", k=KC)
        wus = sl[:, KC * 256:2 * KC * 256].rearrange("p (k n) -> p k n", k=KC)
        p.dma("pool", wgs, wg_r[:, :, fb * 256:(fb + 1) * 256], w=[("slot", s)], sem="slot%d" % s)
        p.dma("pool", wus, wu_r[:, :, fb * 256:(fb + 1) * 256], w=[("slot", s)], sem="slot%d" % s)
        for ff in range(2):
            f = fb * 2 + ff
            for n, (n0, n1) in enumerate(tiles):
                w_ = n1 - n0
                par = gu_i % 2
                gu_i += 1
                gb, ub = par * 2, par * 2 + 1
                for k in range(KC):
                    p.mm(c.bank[gb][:, 0:w_], wgs[:, k, ff * 128:(ff + 1) * 128], c.hT[:, k, n0:n1], k == 0, k == KC - 1,
                         r=[("slot", s), ("hT", k)], w=[("bank", gb)])
                for k in range(KC):
                    p.mm(c.bank[ub][:, 0:w_], wus[:, k, ff * 128:(ff + 1) * 128], c.hT[:, k, n0:n1], k == 0, k == KC - 1,
                         r=[("slot", s), ("hT", k)], w=[("bank", ub)])
                p.act(c.sg[:, par, 0:w_], c.bank[gb][:, 0:w_], AF.Silu, r=[("bank", gb)], w=[("sg", par)])
                p.tt("dve", c.aT[:, f, n0:n1], c.sg[:, par, 0:w_], c.bank[ub][:, 0:w_], ALU.mult,
                     r=[("sg", par), ("bank", ub)], w=[("aT", f)])
        if interleave is not None:
            interleave(fb)

    wd_r = wd.rearrange("(f p) n -> p f n", p=128)
    HF = FC // 2
    dn_i = 0
    for db in range(KC // 2):
        ss_ = []
        for half in range(2):
            s = next_slot(c)
            ws = c.slot[s][:, 0:HF * 256].rearrange("p (f n) -> p f n", f=HF)
            p.dma("pool", ws, wd_r[:, half * HF:(half + 1) * HF, db * 256:(db + 1) * 256], w=[("slot", s)], sem="slot%d" % s)
            ss_.append((s, ws))
        for dd in range(2):
            d = db * 2 + dd
            b = load_x(x_in, d, name + "_xin")
            for n, (n0, n1) in enumerate(tiles):
                w_ = n1 - n0
                ob = dn_i % 2
                dn_i += 1
                for f in range(FC):
                    s, ws = ss_[f // HF]
                    p.mm(c.bank[ob][:, 0:w_], ws[:, f % HF, dd * 128:(dd + 1) * 128], c.aT[:, f, n0:n1], f == 0, f == FC - 1,
                         r=[("slot", s), ("aT", f)], w=[("bank", ob)])
                for (a0, a1, j) in segs_in(n0, n1):
                    p.stt("dve", c.xb[:, b, a0:a1], c.bank[ob][:, a0 - n0:a1 - n0], HG[:, d, j:j + 1], c.xb[:, b, a0:a1],
                          ALU.mult, ALU.add, r=[("bank", ob), ("xb", b), "modAB"], w=[("xb", b)])
            sumsq_accum(lambda n0, n1, b=b: c.xb[:, b, n0:n1], [("xb", b)], d, d == 0, d == KC - 1)
            p.dma("sp", x_out[d * 128:(d + 1) * 128, :], c.xb[:, b, :], r=[("xb", b)], w=[(name + "_xout", d)], sem=name + "_xo")
    make_rstd()
    for k in range(KC):
        b = load_x(x_out, k, name + "_xout")
        kk = k % hout_rot if hout_rot else k
        apply_norm(b, k, A_out, B_out, h_out_sb, name + "_hout", kk)
        p.dma("sp", h_out[k * 128:(k + 1) * 128, :], h_out_sb[:, kk, :], r=[(name + "_hout", kk)], w=[(name + "_hdram", k)],
              sem=name + "_ho")


def make_AB(p, c, modfm, nw, sub, want_out=None):
    A = p.sb("A%d" % sub, [128, KC, 2], F32)
    B = p.sb("B%d" % sub, [128, KC, 2], F32)
    G = p.sb("G%d" % sub, [128, KC, 2], F32)
    m = modfm[:, :].rearrange("p (c j) -> p c j", j=2)
    sh = m[:, (3 * sub) * KC:(3 * sub + 1) * KC, :]
    sc = m[:, (3 * sub + 1) * KC:(3 * sub + 2) * KC, :]
    gt = m[:, (3 * sub + 2) * KC:(3 * sub + 3) * KC, :]
    for j in range(2):
        p.stt("dve", A[:, :, j], sc[:, :, j], 1.0, nw[:, sub, :], ALU.add, ALU.mult, r=["modfm", "nw"], w=["modAB"])
    p.copy("dve", B[:], sh, r=["modfm"], w=["modAB"])
    return A, B, G, gt


def build_p1():
    nc = bass.Bass("TRN2", target_bir_lowering=False)
    T = 1056
    xT = dram(nc, "xT", [D, T], F32, "ExternalInput")
    c2 = dram(nc, "c2", [128, KC * 2], F32, "ExternalInput")
    w_ada = dram(nc, "w_ada", [D, 9 * D], F32, "ExternalInput")
    b_fm = dram(nc, "b_fm", [128, 144], F32, "ExternalInput")
    nwd = dram(nc, "nw", [128, 3 * KC], F32, "ExternalInput")
    wg = dram(nc, "wg", [D, DFF], F32, "ExternalInput")
    wu = dram(nc, "wu", [D, DFF], F32, "ExternalInput")
    wd = dram(nc, "wd", [DFF, D], F32, "ExternalInput")
    x1T = dram(nc, "x1T", [D, T], F32, "ExternalOutput")
    hT2 = dram(nc, "hT2", [D, T], BF16, "ExternalOutput")
    modo = dram(nc, "modfm", [128, 288], F32, "ExternalOutput")

    p = Prog(nc)
    c = setup_common(p, T)
    c.epsb = p.sb("epsb", [128, 1], F32)
    p.memset("dve", c.epsb[:], EPS, w=["epsb"])
    c2t = p.sb("c2t", [128, KC * 2], F32)
    sc2 = p.sb("sc2", [128, KC, 2], BF16)
    bfm = p.sb("bfm", [128, 144], F32)
    nw = p.sb("nwt", [128, 3, KC], F32)
    modfm = p.sb("modfm_sb", [128, 288], F32)
    p.dma("sp", c2t[:], c2, w=["c2t"], sem="c0")
    p.dma("sp", bfm[:], b_fm, w=["bfm"], sem="c1")
    p.dma("sp", nw[:].rearrange("p s k -> p (s k)"), nwd, w=["nw"], sem="c2")
    p.act(sc2[:].rearrange("p k j -> p (k j)"), c2t[:], AF.Silu, r=["c2t"], w=["sc2"])

    wa_r = w_ada.rearrange("(k p) n -> p k n", p=128)
    modps = c.bank[4]

    def mod_block(blk):
        s = next_slot(c)
        ws = c.slot[s][:, :].rearrange("p (k n) -> p k n", k=KC)
        p.dma("pool", ws, wa_r[:, :, blk * 512:(blk + 1) * 512], w=[("slot", s)], sem="slot%d" % s)
        for cc in range(4):
            ch = blk * 4 + cc
            for k in range(KC):
                p.mm(modps[:, 2 * ch:2 * ch + 2], ws[:, k, cc * 128:(cc + 1) * 128], sc2[:, k, :], k == 0, k == KC - 1,
                     r=[("slot", s), "sc2"], w=[("bank", 4)])

    def mod_finish(c0, c1):
        for j in range(2):
            src = modps[:, 2 * c0:2 * c1].rearrange("p (c j) -> p c j", j=2)[:, :, j]
            dst = modfm[:, 2 * c0:2 * c1].rearrange("p (c j) -> p c j", j=2)[:, :, j]
            p.tt("dve", dst, src, bfm[:, c0:c1], ALU.add, r=[("bank", 4), "bfm"], w=["modfm"])

    A1 = p.sb("A1x", [128, KC, 2], F32)
    B1 = p.sb("B1x", [128, KC, 2], F32)
    G1 = p.sb("G1x", [128, KC, 2], F32)
    m_ = modfm[:, :].rearrange("p (c j) -> p c j", j=2)
    gt1 = m_[:, 2 * KC:3 * KC, :]

    def mid_hook():
        for blk in range(8):
            mod_block(blk)
        mod_finish(0, 32)
        for j in range(2):
            p.stt("dve", A1[:, :, j], m_[:, KC:2 * KC, j], 1.0, nw[:, 0, :], ALU.add, ALU.mult, r=["modfm", "nw"], w=["modAB"])
        p.copy("dve", B1[:], m_[:, 0:KC, :], r=["modfm"], w=["modAB"])

    tiles = [(0, 352), (352, 704), (704, 1056)]
    segs = [(0, 1024, 0), (1024, 1056, 1)]
    A2 = p.sb("A2x", [128, KC, 2], F32)
    B2 = p.sb("B2x", [128, KC, 2], F32)

    pending = list(range(8, 36))

    def inter2(fb):
        if not pending:
            return
        for _ in range(2):
            if pending:
                mod_block(pending.pop(0))
        if not pending:
            mod_finish(32, 144)
            p.ts("dve", G1[:], gt1, 0.5, None, ALU.mult, None, r=["modfm"], w=["modAB"])
            m = modfm[:, :].rearrange("p (c j) -> p c j", j=2)
            for j in range(2):
                p.stt("dve", A2[:, :, j], m[:, 4 * KC:5 * KC, j], 1.0, nw[:, 1, :], ALU.add, ALU.mult,
                      r=["modfm", "nw"], w=["modAB"])
            p.copy("dve", B2[:], m[:, 3 * KC:4 * KC, :], r=["modfm"], w=["modAB"])
            p.dma("sp", modo, modfm[:], r=["modfm"], w=["modo"], sem="modo")

    ffn_stage(p, c, "f1", tiles, segs, xT, A1, B1, G1, wg, wu, wd, A2, B2, x1T, hT2, c.hT, interleave=inter2, mid_hook=mid_hook)
    p.add("sp", lambda e: e.nop(), r=["modo"] + [("f1_hdram", k) for k in range(KC)] + [("f1_xout", k) for k in range(KC)])
    p.emit()
    return nc, p


def _fm(v):
    return np.ascontiguousarray(v.reshape(-1, 128).T)


def run_p1(inputs):
    x = inputs["x"][0]
    ctx = inputs["ctx"][0]
    nc, p = build_p1()
    c2 = np.stack([_fm(inputs["c"][0]), _fm(inputs["c_ctx"])], axis=-1).reshape(128, 32)
    nw = np.stack([_fm(inputs["norm_w"][0, s]) for s in range(3)], axis=1).reshape(128, 48)
    common = {
        "c2": np.ascontiguousarray(c2, dtype=np.float32),
        "w_ada": np.ascontiguousarray(inputs["w_ada"][0]),
        "b_fm": _fm(inputs["b_ada"][0]),
        "nw": np.ascontiguousarray(nw),
        "wg": np.ascontiguousarray(inputs["ffn_w_gate"][0, 0]),
        "wu": np.ascontiguousarray(inputs["ffn_w_up"][0, 0]),
        "wd": np.ascontiguousarray(inputs["ffn_w_down"][0, 0]),
    }
    in_maps = []
    for i in range(NCORES):
        xt = np.concatenate([x[1024 * i:1024 * (i + 1)], ctx[32 * i:32 * (i + 1)]], axis=0).T
        m = dict(common)
        m["xT"] = np.ascontiguousarray(xt)
        in_maps.append(m)
    res = run_bass_kernel_spmd(nc, in_maps, core_ids=list(range(NCORES)))
    return res.results


NTOK = 8448
NCH = 66
UW = 8456


def upos(t):
    return 2 + t if t < 256 else 6 + t


UW2 = 8576


def upos2(t):
    return 32 + t if t < 256 else 96 + t


def emit_p2a(nc, p, bank, pre=""):
    hT = dram(nc, pre + "hT", [D, NTOK], BF16, "ExternalInput")
    wml = dram(nc, pre + "wml", [D, 772], F32, "ExternalInput")
    cst = dram(nc, pre + "cst", [128, 3 * 128], F32, "ExternalInput")
    cw = dram(nc, pre + "cw", [128, 2 * 5 + 2], F32, "ExternalInput")
    gbn = dram(nc, pre + "gbn", [128, 4 + 256], F32, "ExternalInput")
    yo = dram(nc, "yml", [8192, 256], BF16, "ExternalOutput")
    cs = p.sb("cs", [128, 3, 128], F32)
    ident, triU, triL = cs[:, 0, :], cs[:, 1, :], cs[:, 2, :]
    identb = p.sb("identb", [128, 128], BF16)
    cwt = p.sb("cwt", [128, 12], F32)
    gb = p.sb("gb", [128, 260], F32)
    ones = p.sb("ones", [128, 128], F32)
    epsb = p.sb("epsb", [128, 1], F32)
    QT = p.sb("QT", [128, NTOK], BF16)
    KT = p.sb("KT", [128, NTOK], BF16)
    Ktm = p.sb("Ktm", [128, NCH, 128], BF16)
    Va = p.sb("Va", [128, NCH, 257], BF16)
    SO = p.sb("SO", [128, NCH, 256], BF16)
    gat = p.sb("gat", [128, NCH, 4], F32)
    p.dma("sp", cs[:].rearrange("p a b -> p (a b)"), cst, w=["cs"], sem="c0")
    p.dma("sp", cwt[:], cw, w=["cwt"], sem="c1")
    p.dma("sp", gb[:], gbn, w=["gb"], sem="c2")
    p.memset("dve", ones[:], 1.0, w=["ones"])
    p.memset("dve", epsb[:], EPS, w=["epsb"])
    p.copy("dve", identb[:], ident, r=["cs"], w=["identb"])
    p.memset("dve", Va[:, :, 256:257], 1.0, w=["Va"])

    stA = contextlib.ExitStack()
    W = p.sb("W", [128, KC, 772], BF16, stack=stA)
    ht2 = p.sb("ht", [128, 2, KC, 256], BF16, stack=stA)
    U = p.sb("U", [128, 2, UW], BF16, stack=stA)
    dg = p.sb("dg", [128, 2, 5, 128], BF16, stack=stA)
    wr = wml.rearrange("(k p) n -> p k n", p=128)
    for k0 in range(0, KC, 4):
        p.dma("pool", W[:, k0:k0 + 4, :], wr[:, k0:k0 + 4, :], w=["W"], sem="W")
    p.memset("dve", U[:], 0.0, w=["U"])
    for f in range(2):
        for tap in range(5):
            p.ts("dve", dg[:, f, tap, :], ident, cwt[:, f * 5 + tap:f * 5 + tap + 1], None, ALU.mult, None,
                 r=["cs", "cwt"], w=["dg"])
    hr = hT.rearrange("(k p) n -> p k n", p=128)
    bi = 0
    for ti, t0 in enumerate(range(0, NTOK, 256)):
        n = 256
        hq = ti % 2
        ht = ht2[:, hq]
        p.dma("sp", ht[:, :, 0:n], hr[:, :, t0:t0 + n], w=[("ht", hq)], sem="ht%d" % hq)
        for f in range(2):
            b = bi % 4
            bi += 1
            for k in range(KC):
                p.mm(bank[b][:, 0:n], W[:, k, f * 128:(f + 1) * 128], ht[:, k, 0:n], k == 0, k == KC - 1,
                     r=["W", ("ht", hq)], w=[("bank", b)])
            for (a0, a1) in [(0, n)]:
                p0 = upos(t0 + a0)
                p.copy("act", U[:, f, p0:p0 + a1 - a0], bank[b][:, a0:a1], r=[("bank", b)], w=["U"])
        for sub in range(n // 128):
            ch = t0 // 128 + sub
            b = bi % 4
            bi += 1
            b2 = 4 + b
            for k in range(KC):
                p.mm(bank[b][:, 0:260], ht[:, k, sub * 128:(sub + 1) * 128], W[:, k, 256:516], k == 0, k == KC - 1,
                     r=["W", ("ht", hq)], w=[("bank", b)])
            for k in range(KC):
                p.mm(bank[b2][:, 0:256], ht[:, k, sub * 128:(sub + 1) * 128], W[:, k, 516:772], k == 0, k == KC - 1,
                     r=["W", ("ht", hq)], w=[("bank", b2)])
            p.copy("dve", Va[:, ch, 0:256], bank[b][:, 0:256], r=[("bank", b)], w=["Va"])
            p.tt("dve", gat[:, ch, :], bank[b][:, 256:260], gb[:, 0:4], ALU.add, r=[("bank", b), "gb"], w=["gat"])
            p.act(SO[:, ch, :], bank[b2][:, 0:256], AF.Sigmoid, r=[("bank", b2)], w=["SO"])
    for f, dst in ((0, QT), (1, KT)):
        for (s0, s1) in [(0, 256)] + [(a, a + 512) for a in range(256, NTOK, 512)]:
            b = bi % 4
            bi += 1
            n = s1 - s0
            p0 = upos(s0)
            for tap in range(5):
                p.mm(bank[b][:, 0:n], dg[:, f, tap, :], U[:, f, p0 + tap - 2:p0 + tap - 2 + n], tap == 0, tap == 4,
                     r=["dg", "U"], w=[("bank", b)])
            p.act(dst[:, s0:s1], bank[b][:, 0:n], AF.Silu, r=[("bank", b), "cwt"], w=["QK"], bias=cwt[:, 10 + f:11 + f])
    for ch in range(NCH):
        b = bi % 4
        bi += 1
        p.mm(bank[b][:, 0:128], KT[:, ch * 128:(ch + 1) * 128], identb[:], True, True, r=["QK", "identb"], w=[("bank", b)])
        p.copy("act", Ktm[:, ch, :], bank[b][:, 0:128], r=[("bank", b)], w=["Ktm"])
    p.barrier()
    stA.close()

    lf = p.sb("lf", [128, 2, NCH], F32)
    sc = p.sb("sc", [128, 8, NCH], F32)
    tmp = p.sb("tmpg", [128, 2, NCH], F32)
    for d in range(2):
        p.act(tmp[:, d, :], gat[:, :, 2 * d + 1], AF.Exp, r=["gat"], w=["tmpg"], scale=-1.0)
        p.act(tmp[:, d, :], tmp[:, d, :], AF.Ln, r=["tmpg"], w=["tmpg"], bias=1.0)
        p.ts("dve", lf[:, d, :], tmp[:, d, :], -1.0, None, ALU.mult, None, r=["tmpg"], w=["lf"])
    cum = bank[7]
    p.mm(cum[:, 0:NCH], triU, lf[:, 0, :], True, True, r=["cs", "lf"], w=[("bank", 7)])
    p.mm(cum[:, NCH:2 * NCH], triL, lf[:, 1, :], True, True, r=["cs", "lf"], w=[("bank", 7)])
    p.mm(cum[:, 2 * NCH:4 * NCH], ones[:], lf[:].rearrange("p d c -> p (d c)"), True, True, r=["ones", "lf"], w=[("bank", 7)])
    for d in range(2):
        bcol = cum[:, d * NCH:(d + 1) * NCH]
        tot = cum[:, (2 + d) * NCH:(3 + d) * NCH]
        ig = gat[:, :, 2 * d]
        o = 4 * d
        p.tt("dve", tmp[:, d, :], ig, bcol, ALU.subtract, r=["gat", ("bank", 7)], w=["tmpg"])
        p.act(sc[:, o + 0, :], tmp[:, d, :], AF.Exp, r=["tmpg"], w=["sc"])
        p.act(sc[:, o + 1, :], bcol, AF.Exp, r=[("bank", 7)], w=["sc"])
        p.ts("dve", sc[:, o + 1, :], sc[:, o + 1, :], 128.0 ** -0.5, None, ALU.mult, None, r=["sc"], w=["sc"])
        p.act(sc[:, o + 3, :], tot, AF.Exp, r=[("bank", 7)], w=["sc"])
        p.tt("dve", sc[:, o + 2, :], sc[:, o + 0, :], sc[:, o + 3, :], ALU.mult, r=["sc"], w=["sc"])

    C32 = p.sb("C32", [128, 257], F32)
    Cb = p.sb("Cb", [128, 257], BF16)
    PT = p.sb("PT", [128, 3, 128], BF16)
    kw = p.sb("kw", [128, 3, 128], BF16)
    sm = p.sb("sm", [128, 2, 4], F32)
    hraw = p.sb("hraw", [128, 2, 64, 256], BF16)
    den = p.sb("den", [128, 2, 64], F32)
    rr = p.sb("rr", [128, 2, 64], F32)
    ssq = p.sb("ssq", [128, 64], F32)
    rsd = p.sb("rsd", [128, 64], F32)
    hs = p.sb("hs", [128, 2, 256], F32)
    yb = p.sb("yb", [128, 2, 256], BF16)
    def pre(i, d, ch):
        q = i % 3
        o = 4 * d
        mask = triU if d == 0 else triL
        sb_, ub_ = 0 + i % 2, 4 + q
        tk = slice(ch * 128, (ch + 1) * 128)
        if ch >= 2:
            p.mm(bank[sb_][:, 0:128], KT[:, tk], QT[:, tk], True, True, r=["QK"], w=[("bank", sb_)])
            p.stt("dve", PT[:, q, :], bank[sb_][:, 0:128], sc[:, o + 0, ch:ch + 1], mask, ALU.mult, ALU.mult,
                  r=[("bank", sb_), "sc", "cs"], w=[("PT", q)])
        p.ts("dve", kw[:, q, :], Ktm[:, ch, :], sc[:, o + 2, ch:ch + 1], None, ALU.mult, None, r=["Ktm", "sc"], w=[("kw", q)])
        p.mm(bank[ub_][:, 0:257], kw[:, q, :], Va[:, ch, :], True, True, r=[("kw", q), "Va"], w=[("bank", ub_)])

    def post(i, d, ch):
        q3 = i % 3
        q = i % 2
        o = 4 * d
        ab_, ub_ = 2 + q, 4 + q3
        tk = slice(ch * 128, (ch + 1) * 128)
        if ch >= 2:
            p.mm(bank[ab_][:, 0:257], PT[:, q3, :], Va[:, ch, :], True, False, r=[("PT", q3), "Va"], w=[("bank", ab_)])
            p.mm(bank[ab_][:, 0:257], QT[:, tk], Cb[:], False, True, r=["QK", "Cb"], w=[("bank", ab_)])
        p.stt("dve", C32[:], C32[:], sc[:, o + 3, ch:ch + 1], bank[ub_][:, 0:257], ALU.mult, ALU.add,
              r=["C32", "sc", ("bank", ub_)], w=["C32"])
        p.copy("act", Cb[:], C32[:], r=["C32"], w=["Cb"])
        if ch >= 2:
            c_ = ch - 2
            p.act(den[:, d, c_:c_ + 1], bank[ab_][:, 256:257], AF.Abs, r=[("bank", ab_), "sc"], w=[("den", d, c_)],
                  scale=sc[:, o + 1, ch:ch + 1])
            p.copy("act", hraw[:, d, c_, :], bank[ab_][:, 0:256], r=[("bank", ab_)], w=[("hraw", d, c_)])

    seq = []
    for d in range(2):
        order = list(range(NCH)) if d == 0 else ([1, 0] + list(range(NCH - 1, 1, -1)))
        seq += [(d, ch) for ch in order]
    pre(0, *seq[0])
    pre(1, *seq[1])
    for i, (d, ch) in enumerate(seq):
        if i == 0 or seq[i - 1][0] != d:
            p.memset("dve", C32[:], 0.0, w=["C32"])
            p.memset("dve", Cb[:], 0.0, w=["Cb"])
        if i + 2 < len(seq):
            pre(i + 2, *seq[i + 2])
        post(i, d, ch)
    denk = [("den", d, c_) for d in range(2) for c_ in range(64)]
    for d in range(2):
        p.ts("dve", rr[:, d, :], den[:, d, :], 1.0, None, ALU.max, None, r=denk, w=["rr"])
        p.add("dve", lambda e, d=d: e.reciprocal(out=rr[:, d, :], in_=rr[:, d, :]), r=["rr"], w=["rr"])
        p.tt("dve", rr[:, d, :], rr[:, d, :], sc[:, 4 * d + 1, 2:66], ALU.mult, r=["rr", "sc"], w=["rr"])

    def comb(c_, q):
        p.ts("dve", hs[:, q, :], hraw[:, 0, c_, :], rr[:, 0, c_:c_ + 1], None, ALU.mult, None,
             r=[("hraw", 0, c_), "rr"], w=[("hs", q)])
        p.stt("dve", hs[:, q, :], hraw[:, 1, c_, :], rr[:, 1, c_:c_ + 1], hs[:, q, :], ALU.mult, ALU.add,
              r=[("hraw", 1, c_), "rr", ("hs", q)], w=[("hs", q)])

    for c_ in range(64):
        q = c_ % 2
        comb(c_, q)
        p.add("act", lambda e, q=q, c_=c_: e.activation(out=yb[:, q, :], in_=hs[:, q, :], func=AF.Square,
                                                        accum_out=ssq[:, c_:c_ + 1]), r=[("hs", q)], w=[("ssq", c_), ("yb", q)])
    p.act(rsd[:], ssq[:], AF.Sqrt, r=[("ssq", c_) for c_ in range(64)], w=["rsd"], bias=epsb[:, 0:1], scale=1.0 / 256)
    p.add("dve", lambda e: e.reciprocal(out=rsd[:], in_=rsd[:]), r=["rsd"], w=["rsd"])
    for c_ in range(64):
        q = c_ % 2
        comb(c_, q)
        p.stt("dve", hs[:, q, :], hs[:, q, :], rsd[:, c_:c_ + 1], gb[:, 4:260], ALU.mult, ALU.mult,
              r=[("hs", q), "rsd", "gb"], w=[("hs", q)])
        p.tt("dve", yb[:, q, :], hs[:, q, :], SO[:, c_ + 2, :], ALU.mult, r=[("hs", q), "SO"], w=[("yb", q)])
        p.dma("sp", yo[c_ * 128:(c_ + 1) * 128, :], yb[:, q, :], r=[("yb", q)], w=[("yo", c_ + 2)], sem="yo%d" % q)
    return [("yo", ch) for ch in range(2, NCH)]


def build_p2a():
    nc = bass.Bass("TRN2", target_bir_lowering=False)
    p = Prog(nc)
    bank = [p.ps("bank%d" % b, [128, 512]) for b in range(8)]
    keys = emit_p2a(nc, p, bank)
    p.add("sp", lambda e: e.nop(), r=keys)
    p.emit()
    return nc, p


def to_cm(a):
    return a.reshape(128, 64, -1).transpose(1, 0, 2).reshape(8192, -1)


def from_cm(a):
    return a.reshape(64, 128, -1).transpose(1, 0, 2).reshape(8192, -1)


def consts_tri():
    ident = np.eye(128, dtype=np.float32)
    triU = np.triu(np.ones((128, 128), np.float32))
    triL = np.tril(np.ones((128, 128), np.float32))
    return np.ascontiguousarray(np.concatenate([ident, triU, triL], axis=1))


def p2a_inputs(inputs, h_all, hc_all, j):
    w_in = inputs["w_in"][0]
    cols = np.concatenate([np.arange(j * 128, (j + 1) * 128), 1024 + np.arange(j * 128, (j + 1) * 128),
                           2048 + np.arange(j * 256, (j + 1) * 256), 6144 + np.arange(4) * 8 + j,
                           4096 + np.arange(j * 256, (j + 1) * 256)])
    cwm = inputs["ml_conv_w"][0]
    cb = inputs["ml_conv_b"][0]
    cw = np.concatenate([cwm[:, j * 128:(j + 1) * 128].T, cwm[:, 1024 + j * 128:1024 + (j + 1) * 128].T,
                         cb[j * 128:(j + 1) * 128, None], cb[1024 + j * 128:1024 + (j + 1) * 128, None]], axis=1)
    gbn = np.concatenate([inputs["ml_gate_b"][0][:, j], inputs["ml_norm_w"][0][j * 256:(j + 1) * 256]])
    hcm = np.concatenate([hc_all, to_cm(h_all)], axis=0).T
    return {"hT": np.ascontiguousarray(hcm), "wml": np.ascontiguousarray(w_in[:, cols]), "cst": consts_tri(),
            "cw": np.ascontiguousarray(cw, dtype=np.float32),
            "gbn": np.ascontiguousarray(np.broadcast_to(gbn[None, :], (128, 260)), dtype=np.float32)}


def emit_p2b(nc, p, bank, pre=""):
    hT = dram(nc, pre + "hT", [D, NTOK], BF16, "ExternalInput")
    wss = dram(nc, pre + "wss", [D, 1296], F32, "ExternalInput")
    cst = dram(nc, pre + "cst", [128, 3 * 128], F32, "ExternalInput")
    cw = dram(nc, pre + "cw", [128, 36], F32, "ExternalInput")
    vecd = dram(nc, pre + "vec", [128, 552], F32, "ExternalInput")
    yo = dram(nc, "yssm", [8192, 512], BF16, "ExternalOutput")
    Ud = dram(nc, pre + "Ud", [768, UW2], BF16, "ExternalOutput")
    Zd = dram(nc, pre + "Zd", [8192, 512], BF16, "ExternalOutput")
    Yf = dram(nc, pre + "Yf", [8192, 512], F32, "ExternalOutput")
    cs = p.sb("cs", [128, 3, 128], F32)
    ident, triU, triL = cs[:, 0, :], cs[:, 1, :], cs[:, 2, :]
    identb = p.sb("identb", [128, 128], BF16)
    cwt = p.sb("cwt", [128, 36], F32)
    vec = p.sb("vec_sb", [128, 552], F32)
    ones = p.sb("ones", [128, 128], F32)
    epsb = p.sb("epsb", [128, 1], F32)
    Xtm = p.sb("Xtm", [128, NCH, 512], BF16)
    Btm = p.sb("Btm", [128, NCH, 128], BF16)
    BT = p.sb("BT", [128, NTOK], BF16)
    CT = p.sb("CT", [128, NTOK], BF16)
    dt = p.sb("dt", [128, NCH, 16], F32)
    p.dma("sp", cs[:].rearrange("p a b -> p (a b)"), cst, w=["cs"], sem="c0")
    p.dma("sp", cwt[:], cw, w=["cwt"], sem="c1")
    p.dma("sp", vec[:], vecd, w=["vec"], sem="c2")
    p.memset("dve", ones[:], 1.0, w=["ones"])
    p.memset("dve", epsb[:], EPS, w=["epsb"])
    p.copy("dve", identb[:], ident, r=["cs"], w=["identb"])

    stA = contextlib.ExitStack()
    W = p.sb("W", [128, KC, 1296], BF16, stack=stA)
    ht2 = p.sb("ht", [128, 2, KC, 256], BF16, stack=stA)
    ev = p.sb("ev", [128, 2, 512], BF16, stack=stA)
    zb = p.sb("zb", [128, 2, 512], BF16, stack=stA)
    zt = p.sb("zt", [128, 64], BF16, stack=stA)
    dg = p.sb("dg", [128, 6, 5, 128], BF16, stack=stA)
    ut = p.sb("ut", [128, 2, 576], BF16, stack=stA)
    post = p.sb("post", [128, 2, 512], BF16, stack=stA)
    wr = wss.rearrange("(k p) n -> p k n", p=128)
    for k0 in range(0, KC, 4):
        p.dma("pool", W[:, k0:k0 + 4, :], wr[:, k0:k0 + 4, :], w=["W"], sem="W")
    p.memset("dve", zt[:], 0.0, w=["zt"])
    for f in range(6):
        for (a, n_) in ((0, 32), (288, 64), (8544, 32)):
            p.dma("sp", Ud[f * 128:(f + 1) * 128, a:a + n_], zt[:, 0:n_], r=["zt"], w=[("Ud", f)], sem="udz")
        for tap in range(5):
            p.ts("dve", dg[:, f, tap, :], ident, cwt[:, f * 5 + tap:f * 5 + tap + 1], None, ALU.mult, None,
                 r=["cs", "cwt"], w=["dg"])
    hr = hT.rearrange("(k p) n -> p k n", p=128)
    bi = 0
    ei = 0
    for ti, t0 in enumerate(range(0, NTOK, 256)):
        n = 256
        hq = ti % 2
        ht = ht2[:, hq]
        p.dma("sp", ht[:, :, 0:n], hr[:, :, t0:t0 + n], w=[("ht", hq)], sem="ht%d" % hq)
        for f in range(6):
            b = bi % 4
            bi += 1
            q = ei % 2
            ei += 1
            for k in range(KC):
                p.mm(bank[b][:, 0:n], W[:, k, 512 + f * 128:512 + (f + 1) * 128], ht[:, k, 0:n], k == 0, k == KC - 1,
                     r=["W", ("ht", hq)], w=[("bank", b)])
            p.copy("act", ev[:, q, 0:n], bank[b][:, 0:n], r=[("bank", b)], w=[("ev", q)])
            for (a0, a1) in [(0, n)]:
                p0 = upos2(t0 + a0)
                p.dma("sp", Ud[f * 128:(f + 1) * 128, p0:p0 + a1 - a0], ev[:, q, a0:a1], r=[("ev", q)], w=[("Ud", f)], sem="ev%d" % q)
        for sub in range(n // 128):
            ch = t0 // 128 + sub
            b = bi % 4
            bi += 1
            b2 = 4 + b
            if ch >= 2:
                q = ei % 2
                ei += 1
                for k in range(KC):
                    p.mm(bank[b][:, 0:512], ht[:, k, sub * 128:(sub + 1) * 128], W[:, k, 0:512], k == 0, k == KC - 1,
                         r=["W", ("ht", hq)], w=[("bank", b)])
                p.act(zb[:, q, :], bank[b][:, 0:512], AF.Silu, r=[("bank", b)], w=[("zb", q)])
                p.dma("sp", Zd[(ch - 2) * 128:(ch - 1) * 128, :], zb[:, q, :], r=[("zb", q)], w=[("Zd", ch)], sem="zb%d" % q)
            for k in range(KC):
                p.mm(bank[b2][:, 0:16], ht[:, k, sub * 128:(sub + 1) * 128], W[:, k, 1280:1296], k == 0, k == KC - 1,
                     r=["W", ("ht", hq)], w=[("bank", b2)])
            p.tt("dve", dt[:, ch, :], bank[b2][:, 0:16], vec[:, 0:16], ALU.add, r=[("bank", b2), "vec"], w=["dt"])
    dtf = dt[:].rearrange("p c e -> p (c e)")
    p.act(dtf, dtf, AF.Exp, r=["dt"], w=["dt"])
    p.act(dtf, dtf, AF.Ln, r=["dt"], w=["dt"], bias=1.0)
    ui = 0
    for (s0, s1) in [(0, 256)] + [(a, a + 512) for a in range(256, NTOK, 512)]:
        n = s1 - s0
        p0 = upos2(s0)
        for f in range(6):
            q = ui % 2
            ui += 1
            b = bi % 4
            bi += 1
            p.dma("sp", ut[:, q, 0:n + 64], Ud[f * 128:(f + 1) * 128, p0 - 32:p0 + n + 32], r=[("Ud", f)], w=[("ut", q)], sem="ut%d" % q)
            for tap in range(5):
                p.mm(bank[b][:, 0:n], dg[:, f, tap, :], ut[:, q, 30 + tap:30 + tap + n], tap == 0, tap == 4, r=["dg", ("ut", q)], w=[("bank", b)])
            if f == 5:
                p.act(CT[:, s0:s1], bank[b][:, 0:n], AF.Silu, r=[("bank", b), "cwt"], w=["CT"], bias=cwt[:, 30 + f:31 + f])
                continue
            dst = BT[:, s0:s1] if f == 4 else post[:, q, 0:n]
            dkey = "BT" if f == 4 else ("post", q)
            p.act(dst, bank[b][:, 0:n], AF.Silu, r=[("bank", b), "cwt"], w=[dkey], bias=cwt[:, 30 + f:31 + f])
            for sub in range(n // 128):
                ch = s0 // 128 + sub
                b2 = 4 + (bi % 4)
                bi += 1
                p.mm(bank[b2][:, 0:128], dst[:, sub * 128:(sub + 1) * 128], identb[:], True, True, r=[dkey, "identb"], w=[("bank", b2)])
                if f == 4:
                    p.copy("dve", Btm[:, ch, :], bank[b2][:, 0:128], r=[("bank", b2)], w=["Btm"])
                else:
                    p.copy("dve", Xtm[:, ch, f * 128:(f + 1) * 128], bank[b2][:, 0:128], r=[("bank", b2)], w=["Xtm"])
    p.barrier()
    stA.close()

    Aneg = p.sb("Aneg", [128, 16], F32)
    dtA = p.sb("dtA", [128, NCH, 16], F32)
    bsb = p.sb("bsb", [128, 2, NCH, 8], F32)
    tot = p.sb("tot", [128, 2, NCH, 8], F32)
    eo = p.sb("eo", [128, 2, NCH, 8], F32)
    ws = p.sb("ws", [128, 2, NCH, 8], F32)
    dec = p.sb("dec", [128, 2, NCH, 8], F32)
    p.act(Aneg[:], vec[:, 16:32], AF.Exp, r=["vec"], w=["Aneg"])
    p.ts("dve", Aneg[:], Aneg[:], -1.0, None, ALU.mult, None, r=["Aneg"], w=["Aneg"])
    p.tt("dve", dtA[:], dt[:], Aneg[:].unsqueeze(1).broadcast_to([128, NCH, 16]), ALU.mult, r=["dt", "Aneg"], w=["dtA"])
    H = NCH // 2
    for d in range(2):
        tri = triU if d == 0 else triL
        for hf_ in range(2):
            src = dtA[:, hf_ * H:(hf_ + 1) * H, d * 8:(d + 1) * 8]
            b0, b1 = (d * 2 + hf_) % 4, 4 + (d * 2 + hf_) % 4
            p.mm(bank[b0][:, 0:H * 8].rearrange("p (c e) -> p c e", e=8), tri, src, True, True, r=["cs", "dtA"], w=[("bank", b0)])
            p.mm(bank[b1][:, 0:H * 8].rearrange("p (c e) -> p c e", e=8), ones[:], src, True, True, r=["ones", "dtA"], w=[("bank", b1)])
            p.copy("dve", bsb[:, d, hf_ * H:(hf_ + 1) * H, :], bank[b0][:, 0:H * 8].rearrange("p (c e) -> p c e", e=8),
                   r=[("bank", b0)], w=["bsb"])
            p.copy("dve", tot[:, d, hf_ * H:(hf_ + 1) * H, :], bank[b1][:, 0:H * 8].rearrange("p (c e) -> p c e", e=8),
                   r=[("bank", b1)], w=["tot"])
    fl = lambda t: t[:].rearrange("p d c e -> p (d c e)")
    p.act(fl(eo), fl(bsb), AF.Exp, r=["bsb"], w=["eo"])
    p.tt("dve", fl(ws), fl(tot), fl(bsb), ALU.subtract, r=["tot", "bsb"], w=["ws"])
    p.act(fl(ws), fl(ws), AF.Exp, r=["ws"], w=["ws"])
    p.act(fl(dec), fl(tot), AF.Exp, r=["tot"], w=["dec"])

    negb = p.sb("negb", [128, 2, NCH, 8], F32)
    p.ts("dve", fl(negb), fl(bsb), -1.0, None, ALU.mult, None, r=["bsb"], w=["negb"])
    S32 = p.sb("S32", [128, 512], F32)
    Sb = p.sb("Sb", [128, 512], BF16)
    CBm = p.sb("CBm", [128, 2, 128], F32)
    diagb = p.sb("diagb", [128, 2, 8, 128], F32)
    dm = p.sb("dm", [128, 2, 8, 128], BF16)
    G = p.sb("G", [128, 2, 8, 128], BF16)
    xdt = p.sb("xdt", [128, 3, 512], BF16)
    xw = p.sb("xw", [128, 3, 512], BF16)
    yis = p.sb("yis", [128, 2, 512], F32)
    sus = p.sb("sus", [128, 2, 512], F32)
    tmpd = p.sb("tmpd", [128, 512], F32)
    t2 = p.sb("t2", [128, 512], F32)
    t3 = p.sb("t3", [128, 512], BF16)
    ysb = p.sb("ysb", [128, 2, 512], F32)
    yfl = p.sb("yfl", [128, 3, 512], F32)
    zl = p.sb("zl", [128, 3, 512], BF16)
    ob = p.sb("ob", [128, 2, 512], BF16)
    sm = p.sb("sm", [128, 2, 4], F32)
    bc3 = lambda ap: ap.unsqueeze(2).broadcast_to([128, 8, 64])
    v3 = lambda ap: ap.rearrange("p (e c) -> p e c", e=8)

    def preA(i, d, ch):
        q, t = i % 2, i % 3
        lat = ch >= 2
        tk = slice(ch * 128, (ch + 1) * 128)
        dsl = slice(d * 8, (d + 1) * 8)
        p.tt("pool", v3(xdt[:, t, :]), v3(Xtm[:, ch, :]), bc3(dt[:, ch, dsl]), ALU.mult, r=["Xtm", "dt"], w=[("xdt", t)])
        p.tt("pool", v3(xw[:, t, :]), v3(xdt[:, t, :]), bc3(ws[:, d, ch, :]), ALU.mult, r=[("xdt", t), "ws"], w=[("xw", t)])
        if not lat:
            return
        rows = slice((ch - 2) * 128, (ch - 1) * 128)
        p.tt("pool", diagb[:, q], ident.unsqueeze(1).broadcast_to([128, 8, 128]),
             bsb[:, d, ch, :].unsqueeze(2).broadcast_to([128, 8, 128]), ALU.mult, r=["cs", "bsb"], w=[("diagb", q)])
        cbp = bank[7][:, q * 128:(q + 1) * 128]
        p.mm(cbp, BT[:, tk], CT[:, tk], True, True, r=["BT", "CT"], w=[("cbp", q)])
        for h2 in range(2):
            bb = 2 * q + h2
            p.mm(bank[bb][:, 0:512], ones[:], diagb[:, q, 4 * h2:4 * h2 + 4, :].rearrange("p e i -> p (e i)"), True, True,
                 r=["ones", ("diagb", q)], w=[("bank", bb)])
            for e4 in range(4):
                e = 4 * h2 + e4
                p.act(dm[:, q, e, :], bank[bb][:, e4 * 128:(e4 + 1) * 128], AF.Exp, r=[("bank", bb), "negb"], w=[("dm", q)],
                      bias=negb[:, d, ch, e:e + 1])
        if d == 1:
            p.dma("sp", yfl[:, t, :], Yf[rows, :], r=[("Yf", ch)], w=[("yfl", t)], sem="yl%d" % t)
            p.dma("sp", zl[:, t, :], Zd[rows, :], r=[("Zd", ch)], w=[("zl", t)], sem="zl%d" % t)
            p.tt("pool", v3(tmpd[:]), v3(Xtm[:, ch, :]), bc3(vec[:, 32:40]), ALU.mult, r=["Xtm", "vec"], w=["tmpd"])
            p.tt("pool", yfl[:, t, :], yfl[:, t, :], tmpd[:], ALU.add, r=["tmpd", ("yfl", t)], w=[("yfl", t)])

    def preB(i, d, ch):
        q, t = i % 2, i % 3
        tri = triU if d == 0 else triL
        lat = ch >= 2
        if lat:
            cbp = bank[7][:, q * 128:(q + 1) * 128]
            p.tt("dve", CBm[:, q, :], cbp, tri, ALU.mult, r=[("cbp", q), "cs"], w=[("CBm", q)])
            p.stt("dve", G[:, q], dm[:, q], 1.0, CBm[:, q, :].unsqueeze(1).broadcast_to([128, 8, 128]), ALU.min, ALU.mult,
                  r=[("dm", q), ("CBm", q)], w=[("G", q)])
            for e in range(8):
                p.mm(bank[4][:, e * 64:(e + 1) * 64], G[:, q, e, :], xdt[:, t, e * 64:(e + 1) * 64], True, True,
                     r=[("G", q), ("xdt", t)], w=[("bank", 4)])
            p.copy("act", yis[:, q, :], bank[4][:, 0:512], r=[("bank", 4)], w=[("yis", q)])
        p.mm(bank[6][:, 0:512], Btm[:, ch, :], xw[:, t, :], True, True, r=["Btm", ("xw", t)], w=[("bank", 6)])
        p.copy("act", sus[:, q, :], bank[6][:, 0:512], r=[("bank", 6)], w=[("sus", q)])

    def post(i, d, ch):
        q, t = i % 2, i % 3
        lat = ch >= 2
        tk = slice(ch * 128, (ch + 1) * 128)
        if lat:
            rows = slice((ch - 2) * 128, (ch - 1) * 128)
            p.mm(bank[5][:, 0:512], CT[:, tk], Sb[:], True, True, r=["CT", "Sb"], w=[("bank", 5)])
        p.tt("dve", v3(S32[:]), v3(S32[:]), bc3(dec[:, d, ch, :]), ALU.mult, r=["S32", "dec"], w=["S32"])
        p.tt("dve", S32[:], S32[:], sus[:, q, :], ALU.add, r=["S32", ("sus", q)], w=["S32"])
        p.copy("act", Sb[:], S32[:], r=["S32"], w=["Sb"])
        if lat:
            p.tt("dve", v3(t2[:]), v3(bank[5][:, 0:512]), bc3(eo[:, d, ch, :]), ALU.mult, r=[("bank", 5), "eo"], w=["t2"])
            p.tt("dve", ysb[:, q, :], t2[:], yis[:, q, :], ALU.add, r=["t2", ("yis", q)], w=[("ysb", q)])
            if d == 0:
                p.dma("sp", Yf[rows, :], ysb[:, q, :], r=[("ysb", q)], w=[("Yf", ch)], sem="yf%d" % q)
            else:
                p.tt("dve", ysb[:, q, :], ysb[:, q, :], yfl[:, t, :], ALU.add, r=[("ysb", q), ("yfl", t)], w=[("ysb", q)])
                p.tt("dve", ysb[:, q, :], ysb[:, q, :], zl[:, t, :], ALU.mult, r=[("ysb", q), ("zl", t)], w=[("ysb", q)])
                p.add("act", lambda e, q=q: e.activation(out=t3[:], in_=ysb[:, q, :], func=AF.Square, accum_out=sm[:, q, 0:1]),
                      r=[("ysb", q)], w=[("sm", q), "t3"])
                p.act(sm[:, q, 1:2], sm[:, q, 0:1], AF.Sqrt, r=[("sm", q)], w=[("sm", q)], bias=epsb[:, 0:1], scale=1.0 / 512)
                p.add("dve", lambda e, q=q: e.reciprocal(out=sm[:, q, 2:3], in_=sm[:, q, 1:2]), r=[("sm", q)], w=[("sm", q)])
                p.stt("dve", ob[:, q, :], ysb[:, q, :], sm[:, q, 2:3], vec[:, 40:552], ALU.mult, ALU.mult,
                      r=[("ysb", q), ("sm", q), "vec"], w=[("ob", q)])
                p.dma("sp", yo[rows, :], ob[:, q, :], r=[("ob", q)], w=[("yo", ch)], sem="yo%d" % q)

    seq = []
    for d in range(2):
        order = list(range(NCH)) if d == 0 else ([1, 0] + list(range(NCH - 1, 1, -1)))
        seq += [(d, ch) for ch in order]
    NS = len(seq)
    preA(0, *seq[0])
    preA(1, *seq[1])
    preB(0, *seq[0])
    for i, (d, ch) in enumerate(seq):
        if i == 0 or seq[i - 1][0] != d:
            p.memset("dve", S32[:], 0.0, w=["S32"])
            p.memset("dve", Sb[:], 0.0, w=["Sb"])
        if i + 2 < NS:
            preA(i + 2, *seq[i + 2])
        if i + 1 < NS:
            preB(i + 1, *seq[i + 1])
        post(i, d, ch)
    return [("yo", ch) for ch in range(2, NCH)]


def build_p2b():
    nc = bass.Bass("TRN2", target_bir_lowering=False)
    p = Prog(nc)
    bank = [p.ps("bank%d" % b, [128, 512]) for b in range(8)]
    keys = emit_p2b(nc, p, bank)
    p.add("sp", lambda e: e.nop(), r=keys)
    p.emit()
    return nc, p


def build_p2():
    nc = bass.Bass("TRN2", target_bir_lowering=False)
    p = Prog(nc)
    bank = [p.ps("bank%d" % b, [128, 512]) for b in range(8)]
    with p.scope("a_"):
        emit_p2a(nc, p, bank, "a_")
    with p.scope("b_"):
        keys = emit_p2b(nc, p, bank, "b_")
        p.add("sp", lambda e: e.nop(), r=keys)
    p.emit()
    return nc, p


def p2b_inputs(inputs, h_all, hc_all, g):
    w_in = inputs["w_in"][0]
    o = 6176
    cols = np.concatenate([o + np.arange(g * 512, (g + 1) * 512), o + 4096 + np.arange(g * 512, (g + 1) * 512),
                           o + 8192 + np.arange(g * 128, (g + 1) * 128), o + 9216 + np.arange(g * 128, (g + 1) * 128),
                           o + 10240 + np.arange(g * 8, (g + 1) * 8), o + 10304 + np.arange(g * 8, (g + 1) * 8)])
    cch = np.concatenate([np.arange(g * 512, (g + 1) * 512), 4096 + np.arange(g * 128, (g + 1) * 128),
                          5120 + np.arange(g * 128, (g + 1) * 128)])
    cwm = inputs["ssm_conv_w"][0][:, cch]
    cb = inputs["ssm_conv_b"][0][cch]
    cw = np.concatenate([cwm.reshape(5, 6, 128).transpose(2, 1, 0).reshape(128, 30), cb.reshape(6, 128).T], axis=1)
    hsl = slice(g * 8, (g + 1) * 8)
    vec = np.concatenate([inputs["ssm_dt_bias"][0][0, hsl], inputs["ssm_dt_bias"][0][1, hsl],
                          inputs["ssm_a_log"][0][0, hsl], inputs["ssm_a_log"][0][1, hsl],
                          inputs["ssm_d"][0][hsl], inputs["ssm_norm_w"][0][g * 512:(g + 1) * 512]])
    hrm = np.concatenate([hc_all, h_all], axis=0).T
    return {"hT": np.ascontiguousarray(hrm), "wss": np.ascontiguousarray(w_in[:, cols]), "cst": consts_tri(),
            "cw": np.ascontiguousarray(cw, dtype=np.float32),
            "vec": np.ascontiguousarray(np.broadcast_to(vec[None, :], (128, 552)), dtype=np.float32)}


def p2_inputs(inputs, h_all, hc_all, j):
    m = {"a_" + k: v for k, v in p2a_inputs(inputs, h_all, hc_all, j).items()}
    m.update({"b_" + k: v for k, v in p2b_inputs(inputs, h_all, hc_all, j).items()})
    return m


def emit_p3a(nc, p, bank, pre="", modd=None):
    T = 1024
    x1T = dram(nc, pre + "x1T", [D, T], F32, "ExternalInput")
    hTd = dram(nc, pre + "hT", [D, T], BF16, "ExternalInput")
    ymlT = dram(nc, pre + "ymlT", [D, T], BF16, "ExternalInput")
    yssT = dram(nc, pre + "yssT", [2 * D, T], BF16, "ExternalInput")
    if modd is None:
        modd = dram(nc, pre + "modfm", [128, 288], F32, "ExternalInput")
    bgd = dram(nc, pre + "bg", [128, 32], F32, "ExternalInput")
    wpm = dram(nc, pre + "wpm", [D, D], F32, "ExternalInput")
    wps = dram(nc, pre + "wps", [2 * D, D], F32, "ExternalInput")
    wgt = dram(nc, pre + "wgt", [D, 2 * D], F32, "ExternalInput")
    wo = dram(nc, pre + "wo", [D, D], F32, "ExternalInput")
    x2T = dram(nc, pre + "x2T", [D, T], F32, "ExternalOutput")
    c = setup_common(p, T, nslot=3, ffn=False, bank=bank)
    modfm = p.sb("modfm_sb", [128, 288], F32)
    bg = p.sb("bg_sb", [128, 32], F32)
    pss = p.sb("pss", [128, KC, T], BF16)
    p.dma("sp", modfm[:], modd, w=["modfm"], sem="c0")
    p.dma("sp", bg[:], bgd, w=["bg"], sem="c1")
    gmix = modfm[:, :].rearrange("p (c j) -> p c j", j=2)[:, 5 * KC:6 * KC, 0]
    r3 = lambda w_: w_.rearrange("(k p) n -> p k n", p=128)
    wpm_r, wps_r, wgt_r, wo_r = r3(wpm), r3(wps), r3(wgt), r3(wo)
    it = 0
    with p.scope(pre + "A_"):
        yss = p.sb("yss", [128, 2 * KC, T], BF16)
        for n in range(2):
            ts_ = slice(n * 512, (n + 1) * 512)
            p.dma("sp", yss[:, :, ts_], r3(yssT)[:, :, ts_], w=[("yss", n)], sem="a1")
        for dp in range(KC // 2):
            s2 = next_slot(c)
            w2 = c.slot[s2][:, :].rearrange("p (k n) -> p k n", k=2 * KC)
            p.dma("pool", w2, wps_r[:, :, dp * 256:(dp + 1) * 256], w=[("slot", s2)], sem="slot%d" % s2)
            for dd in range(2):
                d = dp * 2 + dd
                for n in range(2):
                    ts_ = slice(n * 512, (n + 1) * 512)
                    q = it % 4
                    it += 1
                    for k in range(2 * KC):
                        p.mm(c.bank[q][:, :], w2[:, k, dd * 128:(dd + 1) * 128], yss[:, k, ts_], k == 0, k == 2 * KC - 1,
                             r=[("slot", s2), ("yss", n)], w=[("bank", q)])
                    p.copy("act", pss[:, d, ts_], c.bank[q][:, :], r=[("bank", q)], w=[("pss", d, n)])
    with p.scope(pre + "B_"):
        yml = p.sb("yml", [128, KC, T], BF16)
        hT = p.sb("hTt", [128, KC, T], BF16)
        mg = p.sb("mg", [128, KC, T], BF16)
        gs = p.sb("gs", [128, 2, 2, 512], F32)
        tu = p.sb("tu", [128, 2, 2, 512], F32)
        xt = p.sb("xt", [128, 2, 512], F32)
        for n in range(2):
            ts_ = slice(n * 512, (n + 1) * 512)
            p.dma("sp", yml[:, :, ts_], r3(ymlT)[:, :, ts_], w=[("yml", n)], sem="a0")
            p.dma("sp", hT[:, :, ts_], r3(hTd)[:, :, ts_], w=[("hTt", n)], sem="a2")
        it = 0
        for dp in range(KC // 2):
            cs_ = slice(dp * 256, (dp + 1) * 256)
            s1, s3 = next_slot(c), next_slot(c)
            w1 = c.slot[s1][:, 0:4096].rearrange("p (k n) -> p k n", k=KC)
            w1g = c.slot[s1][:, 4096:8192].rearrange("p (k n) -> p k n", k=KC)
            w3 = c.slot[s3][:, 0:4096].rearrange("p (k n) -> p k n", k=KC)
            p.dma("pool", w1, wpm_r[:, :, cs_], w=[("slot", s1)], sem="slot%d" % s1)
            p.dma("pool", w1g, wgt_r[:, :, cs_], w=[("slot", s1)], sem="slot%d" % s1)
            p.dma("pool", w3, wgt_r[:, :, D + dp * 256:D + (dp + 1) * 256], w=[("slot", s3)], sem="slot%d" % s3)
            for dd in range(2):
                d = dp * 2 + dd
                dsl = slice(dd * 128, (dd + 1) * 128)
                for n in range(2):
                    ts_ = slice(n * 512, (n + 1) * 512)
                    q = it % 2
                    it += 1
                    b0 = q * 3
                    for k in range(KC):
                        p.mm(c.bank[b0][:, :], w1[:, k, dsl], yml[:, k, ts_], k == 0, k == KC - 1, r=[("slot", s1), ("yml", n)], w=[("bank", b0)])
                    for k in range(KC):
                        p.mm(c.bank[b0 + 1][:, :], w1g[:, k, dsl], hT[:, k, ts_], k == 0, k == KC - 1, r=[("slot", s1), ("hTt", n)], w=[("bank", b0 + 1)])
                    for k in range(KC):
                        p.mm(c.bank[b0 + 2][:, :], w3[:, k, dsl], hT[:, k, ts_], k == 0, k == KC - 1, r=[("slot", s3), ("hTt", n)], w=[("bank", b0 + 2)])
                    p.act(gs[:, q, 0, :], c.bank[b0 + 1][:, :], AF.Sigmoid, r=[("bank", b0 + 1), "bg"], w=[("gs", q)], bias=bg[:, d:d + 1])
                    p.act(gs[:, q, 1, :], c.bank[b0 + 2][:, :], AF.Sigmoid, r=[("bank", b0 + 2), "bg"], w=[("gs", q)], bias=bg[:, KC + d:KC + d + 1])
                    p.tt("dve", tu[:, q, 0, :], gs[:, q, 0, :], c.bank[b0][:, :], ALU.mult, r=[("gs", q), ("bank", b0)], w=[("tu", q)])
                    p.tt("dve", tu[:, q, 1, :], gs[:, q, 1, :], pss[:, d, ts_], ALU.mult, r=[("gs", q), ("pss", d, n)], w=[("tu", q)])
                    p.tt("dve", mg[:, d, ts_], tu[:, q, 0, :], tu[:, q, 1, :], ALU.add, r=[("tu", q)], w=[("mg", n)])
        for dp in range(KC // 2):
            s1 = next_slot(c)
            w1 = c.slot[s1][:, 0:4096].rearrange("p (k n) -> p k n", k=KC)
            p.dma("pool", w1, wo_r[:, :, dp * 256:(dp + 1) * 256], w=[("slot", s1)], sem="slot%d" % s1)
            for dd in range(2):
                d = dp * 2 + dd
                for n in range(2):
                    ts_ = slice(n * 512, (n + 1) * 512)
                    q = it % 2
                    it += 1
                    b0 = 6 + q
                    p.dma("sp", xt[:, q, :], x1T[d * 128:(d + 1) * 128, ts_], w=[("xt", q)], sem="xt%d" % q)
                    for k in range(KC):
                        p.mm(c.bank[b0][:, :], w1[:, k, dd * 128:(dd + 1) * 128], mg[:, k, ts_], k == 0, k == KC - 1,
                             r=[("slot", s1), ("mg", 0), ("mg", 1)], w=[("bank", b0)])
                    p.stt("dve", xt[:, q, :], c.bank[b0][:, :], gmix[:, d:d + 1], xt[:, q, :], ALU.mult, ALU.add,
                          r=[("bank", b0), ("xt", q), "modfm"], w=[("xt", q)])
                    p.dma("sp", x2T[d * 128:(d + 1) * 128, ts_], xt[:, q, :], r=[("xt", q)], w=[("x2", d, n)], sem="xo%d" % q)
    return x2T, [("x2", d, n) for d in range(KC) for n in range(2)]


def build_p3a():
    nc = bass.Bass("TRN2", target_bir_lowering=False)
    p = Prog(nc)
    bank = [p.ps("bank%d" % b, [128, 512]) for b in range(8)]
    _, keys = emit_p3a(nc, p, bank)
    p.add("sp", lambda e: e.nop(), r=keys)
    p.emit()
    return nc, p


def emit_p3b(nc, p, bank, pre="", x2T=None, modd=None):
    T = 1024
    if x2T is None:
        x2T = dram(nc, pre + "x2T", [D, T], F32, "ExternalInput")
    if modd is None:
        modd = dram(nc, pre + "modfm", [128, 288], F32, "ExternalInput")
    nwd = dram(nc, pre + "nw", [128, 4 * KC], F32, "ExternalInput")
    wg = dram(nc, pre + "wg", [D, DFF], F32, "ExternalInput")
    wu = dram(nc, pre + "wu", [D, DFF], F32, "ExternalInput")
    wd = dram(nc, pre + "wd", [DFF, D], F32, "ExternalInput")
    x3T = dram(nc, pre + "x3T", [D, T], F32, "ExternalOutput")
    outT = dram(nc, "outT", [D, T], F32, "ExternalOutput")
    c = setup_common(p, T, bank=bank)
    c.epsb = p.sb("epsb", [128, 1], F32)
    p.memset("dve", c.epsb[:], EPS, w=["epsb"])
    modfm = p.sb("modfm_sb", [128, 288], F32)
    nw = p.sb("nwt", [128, 4, KC], F32)
    ob = p.sb("obuf", [128, 2, T], F32)
    p.dma("sp", modfm[:], modd, w=["modfm"], sem="c0")
    p.dma("sp", nw[:].rearrange("p s k -> p (s k)"), nwd, w=["nw"], sem="c1")
    A, B, G, gt = make_AB(p, c, modfm, nw, 2)
    p.ts("dve", G[:], gt, 0.5, None, ALU.mult, None, r=["modfm"], w=["modAB"])
    Ao = p.sb("Ao", [128, KC, 2], F32)
    for j in range(2):
        p.copy("dve", Ao[:, :, j], nw[:, 3, :], r=["nw"], w=["modAB"])
    tiles = [(0, 512), (512, 1024)]
    segs = [(0, 1024, 0)]
    ffn_stage(p, c, "f2", tiles, segs, x2T, A, B, G, wg, wu, wd, Ao, None, x3T, outT, ob, hout_rot=2)
    return [("f2_hdram", k) for k in range(KC)]


def build_p3b():
    nc = bass.Bass("TRN2", target_bir_lowering=False)
    p = Prog(nc)
    bank = [p.ps("bank%d" % b, [128, 512]) for b in range(8)]
    keys = emit_p3b(nc, p, bank)
    p.add("sp", lambda e: e.nop(), r=keys)
    p.emit()
    return nc, p


def build_p3():
    nc = bass.Bass("TRN2", target_bir_lowering=False)
    p = Prog(nc)
    bank = [p.ps("bank%d" % b, [128, 512]) for b in range(8)]
    modd = dram(nc, "modfm", [128, 288], F32, "ExternalInput")
    with p.scope("a_"):
        x2T, _ = emit_p3a(nc, p, bank, "a_", modd)
    with p.scope("b_"):
        keys = emit_p3b(nc, p, bank, "b_", x2T=x2T, modd=modd)
        p.add("sp", lambda e: e.nop(), r=keys)
    p.emit()
    return nc, p


def _run(nc, in_maps):
    return run_bass_kernel_spmd(nc, in_maps, core_ids=list(range(len(in_maps)))).results


def kernel(**inputs):
    bf = ml_dtypes.bfloat16
    r1 = run_p1(inputs)
    x1T = [np.asarray(r["x1T"]) for r in r1]
    hT2 = [np.asarray(r["hT2"]) for r in r1]
    modfm = np.asarray(r1[0]["modfm"])
    h_all = np.concatenate([h[:, :1024].T for h in hT2], axis=0)
    hc_all = np.concatenate([h[:, 1024:].T for h in hT2], axis=0)
    nc, _ = build_p2()
    r2 = _run(nc, [p2_inputs(inputs, h_all, hc_all, j) for j in range(NCORES)])
    yml = np.concatenate([from_cm(np.asarray(r["yml"])) for r in r2], axis=1)
    yss = np.concatenate([np.asarray(r["yssm"]) for r in r2], axis=1)
    nc, _ = build_p3()
    nw = np.concatenate([_fm(inputs["norm_w"][0, s]) for s in range(3)] + [_fm(inputs["final_norm_w"])], axis=1)
    common = {"modfm": modfm, "a_bg": _fm(inputs["b_gate"][0]),
              "a_wpm": np.ascontiguousarray(inputs["w_proj_ml"][0]), "a_wps": np.ascontiguousarray(inputs["w_proj_ssm"][0]),
              "a_wgt": np.ascontiguousarray(inputs["w_gate"][0]), "a_wo": np.ascontiguousarray(inputs["w_out"][0]),
              "b_nw": np.ascontiguousarray(nw),
              "b_wg": np.ascontiguousarray(inputs["ffn_w_gate"][0, 1]), "b_wu": np.ascontiguousarray(inputs["ffn_w_up"][0, 1]),
              "b_wd": np.ascontiguousarray(inputs["ffn_w_down"][0, 1])}
    maps = []
    for i in range(NCORES):
        sl = slice(1024 * i, 1024 * (i + 1))
        m = dict(common)
        m["a_x1T"] = np.ascontiguousarray(x1T[i][:, :1024])
        m["a_hT"] = np.ascontiguousarray(hT2[i][:, :1024])
        m["a_ymlT"] = np.ascontiguousarray(yml[sl].T)
        m["a_yssT"] = np.ascontiguousarray(yss[sl].T)
        maps.append(m)
    r3b = _run(nc, maps)
    out = np.concatenate([np.asarray(r["outT"]).T for r in r3b], axis=0)
    return np.ascontiguousarray(out[None].astype(np.float32))
```

```python
import contextlib
import numpy as np
import ml_dtypes
import concourse.bass as bass
import concourse.mybir as mybir
from concourse.bass_utils import run_bass_kernel_spmd

F32 = mybir.dt.float32
BF16 = mybir.dt.bfloat16
AF = mybir.ActivationFunctionType
ALU = mybir.AluOpType
AX = mybir.AxisListType

D = 2048
KC = 16
DFF = 5632
FC = 44
EPS = 1e-6
NCORES = 8

ENGS = ("pe", "act", "dve", "pool", "sp")


class Op:
    __slots__ = ("eng", "fn", "deps", "isdma", "sem", "val", "inc")

    def __init__(self, eng, fn, isdma):
        self.eng = eng
        self.fn = fn
        self.deps = []
        self.isdma = isdma
        self.sem = None
        self.val = None
        self.inc = False


class Prog:
    def __init__(self, nc):
        self.nc = nc
        self.ops = {e: [] for e in ENGS}
        self.keys = {}
        self.stack = contextlib.ExitStack()
        self.dram_n = 0
        self.extra_r = []
        self.nbar = 0
        self.prefix = ""
        self.bar_t = self.sb("bar_t", [128, 8], F32)

    def sb(self, name, shape, dtype, stack=None):
        return (stack or self.stack).enter_context(self.nc.sbuf_tensor(self.prefix + name, list(shape), dtype))

    def ps(self, name, shape, dtype=F32, stack=None):
        return (stack or self.stack).enter_context(self.nc.psum_tensor(self.prefix + name, list(shape), dtype))

    @contextlib.contextmanager
    def scope(self, prefix):
        old_stack, old_prefix = self.stack, self.prefix
        sub = contextlib.ExitStack()
        self.stack, self.prefix = sub, prefix
        try:
            yield
        finally:
            self.barrier()
            sub.close()
            self.stack, self.prefix = old_stack, old_prefix

    def _track(self, op, r, w, after=()):
        deps = set(after)
        for k in r:
            st = self.keys.get(k)
            if st is None:
                st = self.keys[k] = [None, {}, []]
            if st[0] is not None:
                deps.add(st[0])
            if op.isdma:
                st[2].append(op)
            else:
                st[1][op.eng] = op
        for k in w:
            st = self.keys.get(k)
            if st is None:
                st = self.keys[k] = [None, {}, []]
            if st[0] is not None:
                deps.add(st[0])
            for rd in st[1].values():
                deps.add(rd)
            for rd in st[2]:
                deps.add(rd)
            st[0] = op
            st[1] = {}
            st[2] = []
        deps.discard(op)
        for d in deps:
            if d.eng == "pe" and op.eng == "pe" and not d.isdma and not op.isdma:
                continue
            op.deps.append(d)
            d.inc = True

    def add(self, eng, fn, r=(), w=(), after=()):
        op = Op(eng, fn, False)
        r = list(r) + self.extra_r
        self._track(op, r, w, after)
        self.ops[eng].append(op)
        return op

    def dma(self, eng, out, in_, r=(), w=(), sem=None, after=(), **kw):
        op = Op(eng, (lambda e, out=out, in_=in_, kw=kw: e.dma_start(out=out, in_=in_, **kw)), True)
        op.sem = ("dma", sem)
        op.inc = True
        r = list(r) + self.extra_r
        self._track(op, r, w, after)
        self.ops[eng].append(op)
        return op

    def coll(self, kind, src, dst, r, w, sem):
        groups = [list(range(NCORES))]
        op = Op("pool", (lambda e: e.collective_compute(kind, ALU.bypass, replica_groups=groups, ins=[src], outs=[dst])), True)
        op.sem = ("dma", sem)
        op.inc = True
        self._track(op, list(r) + self.extra_r, w)
        self.ops["pool"].append(op)
        return op

    def mm(self, out, lhsT, rhs, start, stop, r, w):
        return self.add("pe", lambda e: e.matmul(out, lhsT, rhs, start=start, stop=stop), r=r, w=w)

    def act(self, out, in_, func, r, w, bias=0.0, scale=1.0, eng="act"):
        return self.add(eng, lambda e: e.activation(out=out, in_=in_, func=func, bias=bias, scale=scale), r=r, w=w)

    def tt(self, eng, out, in0, in1, op, r, w):
        return self.add(eng, lambda e: e.tensor_tensor(out=out, in0=in0, in1=in1, op=op), r=r, w=w)

    def ts(self, eng, out, in0, s1, s2, op0, op1, r, w):
        if s2 is None:
            return self.add(eng, lambda e: e.tensor_scalar(out, in0, s1, None, op0), r=r, w=w)
        return self.add(eng, lambda e: e.tensor_scalar(out, in0, s1, s2, op0, op1), r=r, w=w)

    def stt(self, eng, out, in0, scalar, in1, op0, op1, r, w):
        return self.add(eng, lambda e: e.scalar_tensor_tensor(out=out, in0=in0, scalar=scalar, in1=in1, op0=op0, op1=op1), r=r, w=w)

    def copy(self, eng, out, in_, r, w):
        if eng == "act":
            return self.add(eng, lambda e: e.activation(out=out, in_=in_, func=AF.Identity), r=r, w=w)
        return self.add(eng, lambda e: e.tensor_copy(out=out, in_=in_), r=r, w=w)

    def memset(self, eng, ap, val, w):
        return self.add(eng, lambda e: e.memset(ap, val), w=w)

    def barrier(self):
        allkeys = list(self.keys.keys())
        self.extra_r = []
        tok = ("bar", self.nbar)
        i = self.nbar
        self.nbar += 1
        self.add("dve", lambda e: e.memset(self.bar_t[:, i % 8:i % 8 + 1], 0.0), r=allkeys, w=allkeys + [tok])
        self.extra_r = [tok]

    def emit(self):
        nc = self.nc
        counts = {}
        for e in ENGS:
            for op in self.ops[e]:
                if op.isdma:
                    k = op.sem
                    counts[k] = counts.get(k, 0) + 16
                    op.val = counts[k]
                elif op.inc:
                    k = ("eng", e)
                    op.sem = k
                    counts[k] = counts.get(k, 0) + 1
                    op.val = counts[k]
        sems = {}
        st = contextlib.ExitStack()
        for i, k in enumerate(counts.keys()):
            sems[k] = st.enter_context(nc.semaphore("s%d" % i))
        self.maxcount = max(counts.values()) if counts else 0
        self.nsems = len(counts)
        engobj = {"pe": "tensor", "act": "scalar", "dve": "vector", "pool": "gpsimd", "sp": "sync"}

        def run(e, eo):
            waited = {}
            for op in self.ops[e]:
                need = {}
                for d in op.deps:
                    if need.get(d.sem, 0) < d.val:
                        need[d.sem] = d.val
                for k, v in need.items():
                    if waited.get(k, 0) < v:
                        eo.wait_ge(sems[k], v)
                        waited[k] = v
                ins = op.fn(eo)
                if op.inc:
                    ins.then_inc(sems[op.sem], 16 if op.isdma else 1)

        with nc.Block() as block:
            for e in ENGS:
                if not self.ops[e]:
                    continue
                getattr(block, engobj[e])(lambda eo, e=e: run(e, eo))
        st.close()
        self.stack.close()


def dram(nc, name, shape, dtype, kind):
    return nc.dram_tensor(name, list(shape), dtype, kind=kind).ap()


class Ctx:
    pass


def setup_common(p, T, nslot=3, ffn=True, bank=None):
    c = Ctx()
    c.T = T
    c.bank = bank if bank is not None else [p.ps("bank%d" % b, [128, 512]) for b in range(8)]
    c.NSLOT = nslot
    c.slot = [p.sb("slot%d" % s, [128, 8192], BF16) for s in range(c.NSLOT)]
    c.slot_i = 0
    c.ones = p.sb("ones", [128, 128], F32)
    p.memset("dve", c.ones[:], 1.0, w=["ones"])
    if ffn:
        c.xb = p.sb("xb", [128, 3, T], F32)
        c.xb_i = 0
        c.rstd = p.sb("rstd", [128, T], F32)
        c.sg = p.sb("sg", [128, 2, 512], F32)
        c.sq = p.sb("sq", [128, 2, 512], F32)
        c.sq_i = 0
        c.hT = p.sb("hT", [128, KC, T], BF16)
        c.aT = p.sb("aT", [128, FC, T], BF16)
    return c


def next_slot(c):
    s = c.slot_i % c.NSLOT
    c.slot_i += 1
    return s


def ffn_stage(p, c, name, tiles, segs, x_in, A_in, B_in, HG, wg, wu, wd, A_out, B_out,
              x_out, h_out, h_out_sb, interleave=None, hout_rot=0, mid_hook=None):
    T = c.T
    nt = len(tiles)
    ssb = [5, 6, 7]

    def segs_in(n0, n1):
        out = []
        for (c0, c1, j) in segs:
            a, b = max(c0, n0), min(c1, n1)
            if a < b:
                out.append((a, b, j))
        return out

    def load_x(src, k, tag):
        b = c.xb_i % 3
        c.xb_i += 1
        p.dma("sp", c.xb[:, b, :], src[k * 128:(k + 1) * 128, :], r=[(tag, k)], w=[("xb", b)], sem="xb%d" % b)
        return b

    def sumsq_accum(src_ap_fn, srckeys, k, first, last):
        for n, (n0, n1) in enumerate(tiles):
            q = c.sq_i % 2
            c.sq_i += 1
            w_ = n1 - n0
            p.act(c.sq[:, q, 0:w_], src_ap_fn(n0, n1), AF.Square, r=srckeys, w=[("sq", q)])
            p.mm(c.bank[ssb[n]][:, 0:w_], c.ones[:], c.sq[:, q, 0:w_], first, last,
                 r=["ones", ("sq", q)], w=[("bank", ssb[n])])

    def make_rstd():
        for n, (n0, n1) in enumerate(tiles):
            w_ = n1 - n0
            p.act(c.rstd[:, n0:n1], c.bank[ssb[n]][:, 0:w_], AF.Sqrt, r=[("bank", ssb[n])], w=[("rstd", n)],
                  bias=c.epsb[:, 0:1], scale=1.0 / D)
            p.add("dve", lambda e, n0=n0, n1=n1: e.reciprocal(out=c.rstd[:, n0:n1], in_=c.rstd[:, n0:n1]),
                  r=[("rstd", n)], w=[("rstd", n)])

    rstd_keys = [("rstd", n) for n in range(nt)]

    def apply_norm(b, k, A, B, dst, dstkey, kk=None):
        if kk is None:
            kk = k
        p.tt("dve", c.xb[:, b, :], c.xb[:, b, :], c.rstd[:, :], ALU.mult, r=[("xb", b)] + rstd_keys, w=[("xb", b)])
        for (c0, c1, j) in segs:
            bias = B[:, k, j:j + 1] if B is not None else 0.0
            p.act(dst[:, kk, c0:c1], c.xb[:, b, c0:c1], AF.Identity, r=[("xb", b), "modAB"], w=[(dstkey, kk)],
                  bias=bias, scale=A[:, k, j:j + 1])

    for k in range(KC):
        b = load_x(x_in, k, name + "_xin")
        sumsq_accum(lambda n0, n1, b=b: c.xb[:, b, n0:n1], [("xb", b)], k, k == 0, k == KC - 1)
    make_rstd()
    if mid_hook is not None:
        mid_hook()
    for k in range(KC):
        b = load_x(x_in, k, name + "_xin")
        apply_norm(b, k, A_in, B_in, c.hT, "hT")

    wg_r = wg.rearrange("(k p) n -> p k n", p=128)
    wu_r = wu.rearrange("(k p) n -> p k n", p=128)
    gu_i = 0
    for fb in range(FC // 2):
        s = next_slot(c)
        sl = c.slot[s]
        wgs = sl[:, 0:KC * 256].rearrange("p (k n) -> p k n", k=KC)
        wus = sl[:, KC * 256:2 * KC * 256].rearrange("p (k n) -> p k n", k=KC)
        p.dma("pool", wgs, wg_r[:, :, fb * 256:(fb + 1) * 256], w=[("slot", s)], sem="slot%d" % s)
        p.dma("pool", wus, wu_r[:, :, fb * 256:(fb + 1) * 256], w=[("slot", s)], sem="slot%d" % s)
        for ff in range(2):
            f = fb * 2 + ff
            for n, (n0, n1) in enumerate(tiles):
                w_ = n1 - n0
                par = gu_i % 2
                gu_i += 1
                gb, ub = par * 2, par * 2 + 1
                for k in range(KC):
                    p.mm(c.bank[gb][:, 0:w_], wgs[:, k, ff * 128:(ff + 1) * 128], c.hT[:, k, n0:n1], k == 0, k == KC - 1,
                         r=[("slot", s), ("hT", k)], w=[("bank", gb)])
                for k in range(KC):
                    p.mm(c.bank[ub][:, 0:w_], wus[:, k, ff * 128:(ff + 1) * 128], c.hT[:, k, n0:n1], k == 0, k == KC - 1,
                         r=[("slot", s), ("hT", k)], w=[("bank", ub)])
                p.act(c.sg[:, par, 0:w_], c.bank[gb][:, 0:w_], AF.Silu, r=[("bank", gb)], w=[("sg", par)])
                p.tt("dve", c.aT[:, f, n0:n1], c.sg[:, par, 0:w_], c.bank[ub][:, 0:w_], ALU.mult,
                     r=[("sg", par), ("bank", ub)], w=[("aT", f)])
        if interleave is not None:
            interleave(fb)

    wd_r = wd.rearrange("(f p) n -> p f n", p=128)
    HF = FC // 2
    dn_i = 0
    for db in range(KC // 2):
        ss_ = []
        for half in range(2):
            s = next_slot(c)
            ws = c.slot[s][:, 0:HF * 256].rearrange("p (f n) -> p f n", f=HF)
            p.dma("pool", ws, wd_r[:, half * HF:(half + 1) * HF, db * 256:(db + 1) * 256], w=[("slot", s)], sem="slot%d" % s)
            ss_.append((s, ws))
        for dd in range(2):
            d = db * 2 + dd
            b = load_x(x_in, d, name + "_xin")
            for n, (n0, n1) in enumerate(tiles):
                w_ = n1 - n0
                ob = dn_i % 2
                dn_i += 1
                for f in range(FC):
                    s, ws = ss_[f // HF]
                    p.mm(c.bank[ob][:, 0:w_], ws[:, f % HF, dd * 128:(dd + 1) * 128], c.aT[:, f, n0:n1], f == 0, f == FC - 1,
                         r=[("slot", s), ("aT", f)], w=[("bank", ob)])
                for (a0, a1, j) in segs_in(n0, n1):
                    p.stt("dve", c.xb[:, b, a0:a1], c.bank[ob][:, a0 - n0:a1 - n0], HG[:, d, j:j + 1], c.xb[:, b, a0:a1],
                          ALU.mult, ALU.add, r=[("bank", ob), ("xb", b), "modAB"], w=[("xb", b)])
            sumsq_accum(lambda n0, n1, b=b: c.xb[:, b, n0:n1], [("xb", b)], d, d == 0, d == KC - 1)
            p.dma("sp", x_out[d * 128:(d + 1) * 128, :], c.xb[:, b, :], r=[("xb", b)], w=[(name + "_xout", d)], sem=name + "_xo")
    make_rstd()
    for k in range(KC):
        b = load_x(x_out, k, name + "_xout")
        kk = k % hout_rot if hout_rot else k
        apply_norm(b, k, A_out, B_out, h_out_sb, name + "_hout", kk)
        p.dma("sp", h_out[k * 128:(k + 1) * 128, :], h_out_sb[:, kk, :], r=[(name + "_hout", kk)], w=[(name + "_hdram", k)],
              sem=name + "_ho")


def make_AB(p, c, modfm, nw, sub, want_out=None):
    A = p.sb("A%d" % sub, [128, KC, 2], F32)
    B = p.sb("B%d" % sub, [128, KC, 2], F32)
    G = p.sb("G%d" % sub, [128, KC, 2], F32)
    m = modfm[:, :].rearrange("p (c j) -> p c j", j=2)
    sh = m[:, (3 * sub) * KC:(3 * sub + 1) * KC, :]
    sc = m[:, (3 * sub + 1) * KC:(3 * sub + 2) * KC, :]
    gt = m[:, (3 * sub + 2) * KC:(3 * sub + 3) * KC, :]
    for j in range(2):
        p.stt("dve", A[:, :, j], sc[:, :, j], 1.0, nw[:, sub, :], ALU.add, ALU.mult, r=["modfm", "nw"], w=["modAB"])
    p.copy("dve", B[:], sh, r=["modfm"], w=["modAB"])
    return A, B, G, gt


def build_p1():
    nc = bass.Bass("TRN2", target_bir_lowering=False)
    T = 1056
    xT = dram(nc, "xT", [D, T], F32, "ExternalInput")
    c2 = dram(nc, "c2", [128, KC * 2], F32, "ExternalInput")
    w_ada = dram(nc, "w_ada", [D, 9 * D], F32, "ExternalInput")
    b_fm = dram(nc, "b_fm", [128, 144], F32, "ExternalInput")
    nwd = dram(nc, "nw", [128, 3 * KC], F32, "ExternalInput")
    wg = dram(nc, "wg", [D, DFF], F32, "ExternalInput")
    wu = dram(nc, "wu", [D, DFF], F32, "ExternalInput")
    wd = dram(nc, "wd", [DFF, D], F32, "ExternalInput")
    x1T = dram(nc, "x1T", [D, T], F32, "ExternalOutput")
    hT2 = dram(nc, "hT2", [D, T], BF16, "ExternalOutput")
    modo = dram(nc, "modfm", [128, 288], F32, "ExternalOutput")

    p = Prog(nc)
    c = setup_common(p, T)
    c.epsb = p.sb("epsb", [128, 1], F32)
    p.memset("dve", c.epsb[:], EPS, w=["epsb"])
    c2t = p.sb("c2t", [128, KC * 2], F32)
    sc2 = p.sb("sc2", [128, KC, 2], BF16)
    bfm = p.sb("bfm", [128, 144], F32)
    nw = p.sb("nwt", [128, 3, KC], F32)
    modfm = p.sb("modfm_sb", [128, 288], F32)
    p.dma("sp", c2t[:], c2, w=["c2t"], sem="c0")
    p.dma("sp", bfm[:], b_fm, w=["bfm"], sem="c1")
    p.dma("sp", nw[:].rearrange("p s k -> p (s k)"), nwd, w=["nw"], sem="c2")
    p.act(sc2[:].rearrange("p k j -> p (k j)"), c2t[:], AF.Silu, r=["c2t"], w=["sc2"])

    wa_r = w_ada.rearrange("(k p) n -> p k n", p=128)
    modps = c.bank[4]

    def mod_block(blk):
        s = next_slot(c)
        ws = c.slot[s][:, :].rearrange("p (k n) -> p k n", k=KC)
        p.dma("pool", ws, wa_r[:, :, blk * 512:(blk + 1) * 512], w=[("slot", s)], sem="slot%d" % s)
        for cc in range(4):
            ch = blk * 4 + cc
            for k in range(KC):
                p.mm(modps[:, 2 * ch:2 * ch + 2], ws[:, k, cc * 128:(cc + 1) * 128], sc2[:, k, :], k == 0, k == KC - 1,
                     r=[("slot", s), "sc2"], w=[("bank", 4)])

    def mod_finish(c0, c1):
        for j in range(2):
            src = modps[:, 2 * c0:2 * c1].rearrange("p (c j) -> p c j", j=2)[:, :, j]
            dst = modfm[:, 2 * c0:2 * c1].rearrange("p (c j) -> p c j", j=2)[:, :, j]
            p.tt("dve", dst, src, bfm[:, c0:c1], ALU.add, r=[("bank", 4), "bfm"], w=["modfm"])

    A1 = p.sb("A1x", [128, KC, 2], F32)
    B1 = p.sb("B1x", [128, KC, 2], F32)
    G1 = p.sb("G1x", [128, KC, 2], F32)
    m_ = modfm[:, :].rearrange("p (c j) -> p c j", j=2)
    gt1 = m_[:, 2 * KC:3 * KC, :]

    def mid_hook():
        for blk in range(8):
            mod_block(blk)
        mod_finish(0, 32)
        for j in range(2):
            p.stt("dve", A1[:, :, j], m_[:, KC:2 * KC, j], 1.0, nw[:, 0, :], ALU.add, ALU.mult, r=["modfm", "nw"], w=["modAB"])
        p.copy("dve", B1[:], m_[:, 0:KC, :], r=["modfm"], w=["modAB"])

    tiles = [(0, 352), (352, 704), (704, 1056)]
    segs = [(0, 1024, 0), (1024, 1056, 1)]
    A2 = p.sb("A2x", [128, KC, 2], F32)
    B2 = p.sb("B2x", [128, KC, 2], F32)

    pending = list(range(8, 36))

    def inter2(fb):
        if not pending:
            return
        for _ in range(2):
            if pending:
                mod_block(pending.pop(0))
        if not pending:
            mod_finish(32, 144)
            p.ts("dve", G1[:], gt1, 0.5, None, ALU.mult, None, r=["modfm"], w=["modAB"])
            m = modfm[:, :].rearrange("p (c j) -> p c j", j=2)
            for j in range(2):
                p.stt("dve", A2[:, :, j], m[:, 4 * KC:5 * KC, j], 1.0, nw[:, 1, :], ALU.add, ALU.mult,
                      r=["modfm", "nw"], w=["modAB"])
            p.copy("dve", B2[:], m[:, 3 * KC:4 * KC, :], r=["modfm"], w=["modAB"])
            p.dma("sp", modo, modfm[:], r=["modfm"], w=["modo"], sem="modo")

    ffn_stage(p, c, "f1", tiles, segs, xT, A1, B1, G1, wg, wu, wd, A2, B2, x1T, hT2, c.hT, interleave=inter2, mid_hook=mid_hook)
    p.add("sp", lambda e: e.nop(), r=["modo"] + [("f1_hdram", k) for k in range(KC)] + [("f1_xout", k) for k in range(KC)])
    p.emit()
    return nc, p


def _fm(v):
    return np.ascontiguousarray(v.reshape(-1, 128).T)


def run_p1(inputs):
    x = inputs["x"][0]
    ctx = inputs["ctx"][0]
    nc, p = build_p1()
    c2 = np.stack([_fm(inputs["c"][0]), _fm(inputs["c_ctx"])], axis=-1).reshape(128, 32)
    nw = np.stack([_fm(inputs["norm_w"][0, s]) for s in range(3)], axis=1).reshape(128, 48)
    common = {
        "c2": np.ascontiguousarray(c2, dtype=np.float32),
        "w_ada": np.ascontiguousarray(inputs["w_ada"][0]),
        "b_fm": _fm(inputs["b_ada"][0]),
        "nw": np.ascontiguousarray(nw),
        "wg": np.ascontiguousarray(inputs["ffn_w_gate"][0, 0]),
        "wu": np.ascontiguousarray(inputs["ffn_w_up"][0, 0]),
        "wd": np.ascontiguousarray(inputs["ffn_w_down"][0, 0]),
    }
    in_maps = []
    for i in range(NCORES):
        xt = np.concatenate([x[1024 * i:1024 * (i + 1)], ctx[32 * i:32 * (i + 1)]], axis=0).T
        m = dict(common)
        m["xT"] = np.ascontiguousarray(xt)
        in_maps.append(m)
    res = run_bass_kernel_spmd(nc, in_maps, core_ids=list(range(NCORES)))
    return res.results


NTOK = 8448
NCH = 66
UW = 8456


def upos(t):
    return 2 + t if t < 256 else 6 + t


UW2 = 8576


def upos2(t):
    return 32 + t if t < 256 else 96 + t


def emit_p2a(nc, p, bank, pre=""):
    hT = dram(nc, pre + "hT", [D, NTOK], BF16, "ExternalInput")
    wml = dram(nc, pre + "wml", [D, 772], F32, "ExternalInput")
    cst = dram(nc, pre + "cst", [128, 3 * 128], F32, "ExternalInput")
    cw = dram(nc, pre + "cw", [128, 2 * 5 + 2], F32, "ExternalInput")
    gbn = dram(nc, pre + "gbn", [128, 4 + 256], F32, "ExternalInput")
    yo = dram(nc, "yml", [8192, 256], BF16, "ExternalOutput")
    cs = p.sb("cs", [128, 3, 128], F32)
    ident, triU, triL = cs[:, 0, :], cs[:, 1, :], cs[:, 2, :]
    identb = p.sb("identb", [128, 128], BF16)
    cwt = p.sb("cwt", [128, 12], F32)
    gb = p.sb("gb", [128, 260], F32)
    ones = p.sb("ones", [128, 128], F32)
    epsb = p.sb("epsb", [128, 1], F32)
    QT = p.sb("QT", [128, NTOK], BF16)
    KT = p.sb("KT", [128, NTOK], BF16)
    Ktm = p.sb("Ktm", [128, NCH, 128], BF16)
    Va = p.sb("Va", [128, NCH, 257], BF16)
    SO = p.sb("SO", [128, NCH, 256], BF16)
    gat = p.sb("gat", [128, NCH, 4], F32)
    p.dma("sp", cs[:].rearrange("p a b -> p (a b)"), cst, w=["cs"], sem="c0")
    p.dma("sp", cwt[:], cw, w=["cwt"], sem="c1")
    p.dma("sp", gb[:], gbn, w=["gb"], sem="c2")
    p.memset("dve", ones[:], 1.0, w=["ones"])
    p.memset("dve", epsb[:], EPS, w=["epsb"])
    p.copy("dve", identb[:], ident, r=["cs"], w=["identb"])
    p.memset("dve", Va[:, :, 256:257], 1.0, w=["Va"])

    stA = contextlib.ExitStack()
    W = p.sb("W", [128, KC, 772], BF16, stack=stA)
    ht2 = p.sb("ht", [128, 2, KC, 256], BF16, stack=stA)
    U = p.sb("U", [128, 2, UW], BF16, stack=stA)
    dg = p.sb("dg", [128, 2, 5, 128], BF16, stack=stA)
    wr = wml.rearrange("(k p) n -> p k n", p=128)
    for k0 in range(0, KC, 4):
        p.dma("pool", W[:, k0:k0 + 4, :], wr[:, k0:k0 + 4, :], w=["W"], sem="W")
    p.memset("dve", U[:], 0.0, w=["U"])
    for f in range(2):
        for tap in range(5):
            p.ts("dve", dg[:, f, tap, :], ident, cwt[:, f * 5 + tap:f * 5 + tap + 1], None, ALU.mult, None,
                 r=["cs", "cwt"], w=["dg"])
    hr = hT.rearrange("(k p) n -> p k n", p=128)
    bi = 0
    for ti, t0 in enumerate(range(0, NTOK, 256)):
        n = 256
        hq = ti % 2
        ht = ht2[:, hq]
        p.dma("sp", ht[:, :, 0:n], hr[:, :, t0:t0 + n], w=[("ht", hq)], sem="ht%d" % hq)
        for f in range(2):
            b = bi % 4
            bi += 1
            for k in range(KC):
                p.mm(bank[b][:, 0:n], W[:, k, f * 128:(f + 1) * 128], ht[:, k, 0:n], k == 0, k == KC - 1,
                     r=["W", ("ht", hq)], w=[("bank", b)])
            for (a0, a1) in [(0, n)]:
                p0 = upos(t0 + a0)
                p.copy("act", U[:, f, p0:p0 + a1 - a0], bank[b][:, a0:a1], r=[("bank", b)], w=["U"])
        for sub in range(n // 128):
            ch = t0 // 128 + sub
            b = bi % 4
            bi += 1
            b2 = 4 + b
            for k in range(KC):
                p.mm(bank[b][:, 0:260], ht[:, k, sub * 128:(sub + 1) * 128], W[:, k, 256:516], k == 0, k == KC - 1,
                     r=["W", ("ht", hq)], w=[("bank", b)])
            for k in range(KC):
                p.mm(bank[b2][:, 0:256], ht[:, k, sub * 128:(sub + 1) * 128], W[:, k, 516:772], k == 0, k == KC - 1,
                     r=["W", ("ht", hq)], w=[("bank", b2)])
            p.copy("dve", Va[:, ch, 0:256], bank[b][:, 0:256], r=[("bank", b)], w=["Va"])
            p.tt("dve", gat[:, ch, :], bank[b][:, 256:260], gb[:, 0:4], ALU.add, r=[("bank", b), "gb"], w=["gat"])
            p.act(SO[:, ch, :], bank[b2][:, 0:256], AF.Sigmoid, r=[("bank", b2)], w=["SO"])
    for f, dst in ((0, QT), (1, KT)):
        for (s0, s1) in [(0, 256)] + [(a, a + 512) for a in range(256, NTOK, 512)]:
            b = bi % 4
            bi += 1
            n = s1 - s0
            p0 = upos(s0)
            for tap in range(5):
                p.mm(bank[b][:, 0:n], dg[:, f, tap, :], U[:, f, p0 + tap - 2:p0 + tap - 2 + n], tap == 0, tap == 4,
                     r=["dg", "U"], w=[("bank", b)])
            p.act(dst[:, s0:s1], bank[b][:, 0:n], AF.Silu, r=[("bank", b), "cwt"], w=["QK"], bias=cwt[:, 10 + f:11 + f])
    for ch in range(NCH):
        b = bi % 4
        bi += 1
        p.mm(bank[b][:, 0:128], KT[:, ch * 128:(ch + 1) * 128], identb[:], True, True, r=["QK", "identb"], w=[("bank", b)])
        p.copy("act", Ktm[:, ch, :], bank[b][:, 0:128], r=[("bank", b)], w=["Ktm"])
    p.barrier()
    stA.close()

    lf = p.sb("lf", [128, 2, NCH], F32)
    sc = p.sb("sc", [128, 8, NCH], F32)
    tmp = p.sb("tmpg", [128, 2, NCH], F32)
    for d in range(2):
        p.act(tmp[:, d, :], gat[:, :, 2 * d + 1], AF.Exp, r=["gat"], w=["tmpg"], scale=-1.0)
        p.act(tmp[:, d, :], tmp[:, d, :], AF.Ln, r=["tmpg"], w=["tmpg"], bias=1.0)
        p.ts("dve", lf[:, d, :], tmp[:, d, :], -1.0, None, ALU.mult, None, r=["tmpg"], w=["lf"])
    cum = bank[7]
    p.mm(cum[:, 0:NCH], triU, lf[:, 0, :], True, True, r=["cs", "lf"], w=[("bank", 7)])
    p.mm(cum[:, NCH:2 * NCH], triL, lf[:, 1, :], True, True, r=["cs", "lf"], w=[("bank", 7)])
    p.mm(cum[:, 2 * NCH:4 * NCH], ones[:], lf[:].rearrange("p d c -> p (d c)"), True, True, r=["ones", "lf"], w=[("bank", 7)])
    for d in range(2):
        bcol = cum[:, d * NCH:(d + 1) * NCH]
        tot = cum[:, (2 + d) * NCH:(3 + d) * NCH]
        ig = gat[:, :, 2 * d]
        o = 4 * d
        p.tt("dve", tmp[:, d, :], ig, bcol, ALU.subtract, r=["gat", ("bank", 7)], w=["tmpg"])
        p.act(sc[:, o + 0, :], tmp[:, d, :], AF.Exp, r=["tmpg"], w=["sc"])
        p.act(sc[:, o + 1, :], bcol, AF.Exp, r=[("bank", 7)], w=["sc"])
        p.ts("dve", sc[:, o + 1, :], sc[:, o + 1, :], 128.0 ** -0.5, None, ALU.mult, None, r=["sc"], w=["sc"])
        p.act(sc[:, o + 3, :], tot, AF.Exp, r=[("bank", 7)], w=["sc"])
        p.tt("dve", sc[:, o + 2, :], sc[:, o + 0, :], sc[:, o + 3, :], ALU.mult, r=["sc"], w=["sc"])

    C32 = p.sb("C32", [128, 257], F32)
    Cb = p.sb("Cb", [128, 257], BF16)
    PT = p.sb("PT", [128, 3, 128], BF16)
    kw = p.sb("kw", [128, 3, 128], BF16)
    sm = p.sb("sm", [128, 2, 4], F32)
    hraw = p.sb("hraw", [128, 2, 64, 256], BF16)
    den = p.sb("den", [128, 2, 64], F32)
    rr = p.sb("rr", [128, 2, 64], F32)
    ssq = p.sb("ssq", [128, 64], F32)
    rsd = p.sb("rsd", [128, 64], F32)
    hs = p.sb("hs", [128, 2, 256], F32)
    yb = p.sb("yb", [128, 2, 256], BF16)
    def pre(i, d, ch):
        q = i % 3
        o = 4 * d
        mask = triU if d == 0 else triL
        sb_, ub_ = 0 + i % 2, 4 + q
        tk = slice(ch * 128, (ch + 1) * 128)
        if ch >= 2:
            p.mm(bank[sb_][:, 0:128], KT[:, tk], QT[:, tk], True, True, r=["QK"], w=[("bank", sb_)])
            p.stt("dve", PT[:, q, :], bank[sb_][:, 0:128], sc[:, o + 0, ch:ch + 1], mask, ALU.mult, ALU.mult,
                  r=[("bank", sb_), "sc", "cs"], w=[("PT", q)])
        p.ts("dve", kw[:, q, :], Ktm[:, ch, :], sc[:, o + 2, ch:ch + 1], None, ALU.mult, None, r=["Ktm", "sc"], w=[("kw", q)])
        p.mm(bank[ub_][:, 0:257], kw[:, q, :], Va[:, ch, :], True, True, r=[("kw", q), "Va"], w=[("bank", ub_)])

    def post(i, d, ch):
        q3 = i % 3
        q = i % 2
        o = 4 * d
        ab_, ub_ = 2 + q, 4 + q3
        tk = slice(ch * 128, (ch + 1) * 128)
        if ch >= 2:
            p.mm(bank[ab_][:, 0:257], PT[:, q3, :], Va[:, ch, :], True, False, r=[("PT", q3), "Va"], w=[("bank", ab_)])
            p.mm(bank[ab_][:, 0:257], QT[:, tk], Cb[:], False, True, r=["QK", "Cb"], w=[("bank", ab_)])
        p.stt("dve", C32[:], C32[:], sc[:, o + 3, ch:ch + 1], bank[ub_][:, 0:257], ALU.mult, ALU.add,
              r=["C32", "sc", ("bank", ub_)], w=["C32"])
        p.copy("act", Cb[:], C32[:], r=["C32"], w=["Cb"])
        if ch >= 2:
            c_ = ch - 2
            p.act(den[:, d, c_:c_ + 1], bank[ab_][:, 256:257], AF.Abs, r=[("bank", ab_), "sc"], w=[("den", d, c_)],
                  scale=sc[:, o + 1, ch:ch + 1])
            p.copy("act", hraw[:, d, c_, :], bank[ab_][:, 0:256], r=[("bank", ab_)], w=[("hraw", d, c_)])

    seq = []
    for d in range(2):
        order = list(range(NCH)) if d == 0 else ([1, 0] + list(range(NCH - 1, 1, -1)))
        seq += [(d, ch) for ch in order]
    pre(0, *seq[0])
    pre(1, *seq[1])
    for i, (d, ch) in enumerate(seq):
        if i == 0 or seq[i - 1][0] != d:
            p.memset("dve", C32[:], 0.0, w=["C32"])
            p.memset("dve", Cb[:], 0.0, w=["Cb"])
        if i + 2 < len(seq):
            pre(i + 2, *seq[i + 2])
        post(i, d, ch)
    denk = [("den", d, c_) for d in range(2) for c_ in range(64)]
    for d in range(2):
        p.ts("dve", rr[:, d, :], den[:, d, :], 1.0, None, ALU.max, None, r=denk, w=["rr"])
        p.add("dve", lambda e, d=d: e.reciprocal(out=rr[:, d, :], in_=rr[:, d, :]), r=["rr"], w=["rr"])
        p.tt("dve", rr[:, d, :], rr[:, d, :], sc[:, 4 * d + 1, 2:66], ALU.mult, r=["rr", "sc"], w=["rr"])

    def comb(c_, q):
        p.ts("dve", hs[:, q, :], hraw[:, 0, c_, :], rr[:, 0, c_:c_ + 1], None, ALU.mult, None,
             r=[("hraw", 0, c_), "rr"], w=[("hs", q)])
        p.stt("dve", hs[:, q, :], hraw[:, 1, c_, :], rr[:, 1, c_:c_ + 1], hs[:, q, :], ALU.mult, ALU.add,
              r=[("hraw", 1, c_), "rr", ("hs", q)], w=[("hs", q)])

    for c_ in range(64):
        q = c_ % 2
        comb(c_, q)
        p.add("act", lambda e, q=q, c_=c_: e.activation(out=yb[:, q, :], in_=hs[:, q, :], func=AF.Square,
                                                        accum_out=ssq[:, c_:c_ + 1]), r=[("hs", q)], w=[("ssq", c_), ("yb", q)])
    p.act(rsd[:], ssq[:], AF.Sqrt, r=[("ssq", c_) for c_ in range(64)], w=["rsd"], bias=epsb[:, 0:1], scale=1.0 / 256)
    p.add("dve", lambda e: e.reciprocal(out=rsd[:], in_=rsd[:]), r=["rsd"], w=["rsd"])
    for c_ in range(64):
        q = c_ % 2
        comb(c_, q)
        p.stt("dve", hs[:, q, :], hs[:, q, :], rsd[:, c_:c_ + 1], gb[:, 4:260], ALU.mult, ALU.mult,
              r=[("hs", q), "rsd", "gb"], w=[("hs", q)])
        p.tt("dve", yb[:, q, :], hs[:, q, :], SO[:, c_ + 2, :], ALU.mult, r=[("hs", q), "SO"], w=[("yb", q)])
        p.dma("sp", yo[c_ * 128:(c_ + 1) * 128, :], yb[:, q, :], r=[("yb", q)], w=[("yo", c_ + 2)], sem="yo%d" % q)
    return [("yo", ch) for ch in range(2, NCH)]


def build_p2a():
    nc = bass.Bass("TRN2", target_bir_lowering=False)
    p = Prog(nc)
    bank = [p.ps("bank%d" % b, [128, 512]) for b in range(8)]
    keys = emit_p2a(nc, p, bank)
    p.add("sp", lambda e: e.nop(), r=keys)
    p.emit()
    return nc, p


def to_cm(a):
    return a.reshape(128, 64, -1).transpose(1, 0, 2).reshape(8192, -1)


def from_cm(a):
    return a.reshape(64, 128, -1).transpose(1, 0, 2).reshape(8192, -1)


def consts_tri():
    ident = np.eye(128, dtype=np.float32)
    triU = np.triu(np.ones((128, 128), np.float32))
    triL = np.tril(np.ones((128, 128), np.float32))
    return np.ascontiguousarray(np.concatenate([ident, triU, triL], axis=1))


def p2a_inputs(inputs, h_all, hc_all, j):
    w_in = inputs["w_in"][0]
    cols = np.concatenate([np.arange(j * 128, (j + 1) * 128), 1024 + np.arange(j * 128, (j + 1) * 128),
                           2048 + np.arange(j * 256, (j + 1) * 256), 6144 + np.arange(4) * 8 + j,
                           4096 + np.arange(j * 256, (j + 1) * 256)])
    cwm = inputs["ml_conv_w"][0]
    cb = inputs["ml_conv_b"][0]
    cw = np.concatenate([cwm[:, j * 128:(j + 1) * 128].T, cwm[:, 1024 + j * 128:1024 + (j + 1) * 128].T,
                         cb[j * 128:(j + 1) * 128, None], cb[1024 + j * 128:1024 + (j + 1) * 128, None]], axis=1)
    gbn = np.concatenate([inputs["ml_gate_b"][0][:, j], inputs["ml_norm_w"][0][j * 256:(j + 1) * 256]])
    hcm = np.concatenate([hc_all, to_cm(h_all)], axis=0).T
    return {"hT": np.ascontiguousarray(hcm), "wml": np.ascontiguousarray(w_in[:, cols]), "cst": consts_tri(),
            "cw": np.ascontiguousarray(cw, dtype=np.float32),
            "gbn": np.ascontiguousarray(np.broadcast_to(gbn[None, :], (128, 260)), dtype=np.float32)}


def emit_p2b(nc, p, bank, pre=""):
    hT = dram(nc, pre + "hT", [D, NTOK], BF16, "ExternalInput")
    wss = dram(nc, pre + "wss", [D, 1296], F32, "ExternalInput")
    cst = dram(nc, pre + "cst", [128, 3 * 128], F32, "ExternalInput")
    cw = dram(nc, pre + "cw", [128, 36], F32, "ExternalInput")
    vecd = dram(nc, pre + "vec", [128, 552], F32, "ExternalInput")
    yo = dram(nc, "yssm", [8192, 512], BF16, "ExternalOutput")
    Ud = dram(nc, pre + "Ud", [768, UW2], BF16, "ExternalOutput")
    Zd = dram(nc, pre + "Zd", [8192, 512], BF16, "ExternalOutput")
    Yf = dram(nc, pre + "Yf", [8192, 512], F32, "ExternalOutput")
    cs = p.sb("cs", [128, 3, 128], F32)
    ident, triU, triL = cs[:, 0, :], cs[:, 1, :], cs[:, 2, :]
    identb = p.sb("identb", [128, 128], BF16)
    cwt = p.sb("cwt", [128, 36], F32)
    vec = p.sb("vec_sb", [128, 552], F32)
    ones = p.sb("ones", [128, 128], F32)
    epsb = p.sb("epsb", [128, 1], F32)
    Xtm = p.sb("Xtm", [128, NCH, 512], BF16)
    Btm = p.sb("Btm", [128, NCH, 128], BF16)
    BT = p.sb("BT", [128, NTOK], BF16)
    CT = p.sb("CT", [128, NTOK], BF16)
    dt = p.sb("dt", [128, NCH, 16], F32)
    p.dma("sp", cs[:].rearrange("p a b -> p (a b)"), cst, w=["cs"], sem="c0")
    p.dma("sp", cwt[:], cw, w=["cwt"], sem="c1")
    p.dma("sp", vec[:], vecd, w=["vec"], sem="c2")
    p.memset("dve", ones[:], 1.0, w=["ones"])
    p.memset("dve", epsb[:], EPS, w=["epsb"])
    p.copy("dve", identb[:], ident, r=["cs"], w=["identb"])

    stA = contextlib.ExitStack()
    W = p.sb("W", [128, KC, 1296], BF16, stack=stA)
    ht2 = p.sb("ht", [128, 2, KC, 256], BF16, stack=stA)
    ev = p.sb("ev", [128, 2, 512], BF16, stack=stA)
    zb = p.sb("zb", [128, 2, 512], BF16, stack=stA)
    zt = p.sb("zt", [128, 64], BF16, stack=stA)
    dg = p.sb("dg", [128, 6, 5, 128], BF16, stack=stA)
    ut = p.sb("ut", [128, 2, 576], BF16, stack=stA)
    post = p.sb("post", [128, 2, 512], BF16, stack=stA)
    wr = wss.rearrange("(k p) n -> p k n", p=128)
    for k0 in range(0, KC, 4):
        p.dma("pool", W[:, k0:k0 + 4, :], wr[:, k0:k0 + 4, :], w=["W"], sem="W")
    p.memset("dve", zt[:], 0.0, w=["zt"])
    for f in range(6):
        for (a, n_) in ((0, 32), (288, 64), (8544, 32)):
            p.dma("sp", Ud[f * 128:(f + 1) * 128, a:a + n_], zt[:, 0:n_], r=["zt"], w=[("Ud", f)], sem="udz")
        for tap in range(5):
            p.ts("dve", dg[:, f, tap, :], ident, cwt[:, f * 5 + tap:f * 5 + tap + 1], None, ALU.mult, None,
                 r=["cs", "cwt"], w=["dg"])
    hr = hT.rearrange("(k p) n -> p k n", p=128)
    bi = 0
    ei = 0
    for ti, t0 in enumerate(range(0, NTOK, 256)):
        n = 256
        hq = ti % 2
        ht = ht2[:, hq]
        p.dma("sp", ht[:, :, 0:n], hr[:, :, t0:t0 + n], w=[("ht", hq)], sem="ht%d" % hq)
        for f in range(6):
            b = bi % 4
            bi += 1
            q = ei % 2
            ei += 1
            for k in range(KC):
                p.mm(bank[b][:, 0:n], W[:, k, 512 + f * 128:512 + (f + 1) * 128], ht[:, k, 0:n], k == 0, k == KC - 1,
                     r=["W", ("ht", hq)], w=[("bank", b)])
            p.copy("act", ev[:, q, 0:n], bank[b][:, 0:n], r=[("bank", b)], w=[("ev", q)])
            for (a0, a1) in [(0, n)]:
                p0 = upos2(t0 + a0)
                p.dma("sp", Ud[f * 128:(f + 1) * 128, p0:p0 + a1 - a0], ev[:, q, a0:a1], r=[("ev", q)], w=[("Ud", f)], sem="ev%d" % q)
        for sub in range(n // 128):
            ch = t0 // 128 + sub
            b = bi % 4
            bi += 1
            b2 = 4 + b
            if ch >= 2:
                q = ei % 2
                ei += 1
                for k in range(KC):
                    p.mm(bank[b][:, 0:512], ht[:, k, sub * 128:(sub + 1) * 128], W[:, k, 0:512], k == 0, k == KC - 1,
                         r=["W", ("ht", hq)], w=[("bank", b)])
                p.act(zb[:, q, :], bank[b][:, 0:512], AF.Silu, r=[("bank", b)], w=[("zb", q)])
                p.dma("sp", Zd[(ch - 2) * 128:(ch - 1) * 128, :], zb[:, q, :], r=[("zb", q)], w=[("Zd", ch)], sem="zb%d" % q)
            for k in range(KC):
                p.mm(bank[b2][:, 0:16], ht[:, k, sub * 128:(sub + 1) * 128], W[:, k, 1280:1296], k == 0, k == KC - 1,
                     r=["W", ("ht", hq)], w=[("bank", b2)])
            p.tt("dve", dt[:, ch, :], bank[b2][:, 0:16], vec[:, 0:16], ALU.add, r=[("bank", b2), "vec"], w=["dt"])
    dtf = dt[:].rearrange("p c e -> p (c e)")
    p.act(dtf, dtf, AF.Exp, r=["dt"], w=["dt"])
    p.act(dtf, dtf, AF.Ln, r=["dt"], w=["dt"], bias=1.0)
    ui = 0
    for (s0, s1) in [(0, 256)] + [(a, a + 512) for a in range(256, NTOK, 512)]:
        n = s1 - s0
        p0 = upos2(s0)
        for f in range(6):
            q = ui % 2
            ui += 1
            b = bi % 4
            bi += 1
            p.dma("sp", ut[:, q, 0:n + 64], Ud[f * 128:(f + 1) * 128, p0 - 32:p0 + n + 32], r=[("Ud", f)], w=[("ut", q)], sem="ut%d" % q)
            for tap in range(5):
                p.mm(bank[b][:, 0:n], dg[:, f, tap, :], ut[:, q, 30 + tap:30 + tap + n], tap == 0, tap == 4, r=["dg", ("ut", q)], w=[("bank", b)])
            if f == 5:
                p.act(CT[:, s0:s1], bank[b][:, 0:n], AF.Silu, r=[("bank", b), "cwt"], w=["CT"], bias=cwt[:, 30 + f:31 + f])
                continue
            dst = BT[:, s0:s1] if f == 4 else post[:, q, 0:n]
            dkey = "BT" if f == 4 else ("post", q)
            p.act(dst, bank[b][:, 0:n], AF.Silu, r=[("bank", b), "cwt"], w=[dkey], bias=cwt[:, 30 + f:31 + f])
            for sub in range(n // 128):
                ch = s0 // 128 + sub
                b2 = 4 + (bi % 4)
                bi += 1
                p.mm(bank[b2][:, 0:128], dst[:, sub * 128:(sub + 1) * 128], identb[:], True, True, r=[dkey, "identb"], w=[("bank", b2)])
                if f == 4:
                    p.copy("dve", Btm[:, ch, :], bank[b2][:, 0:128], r=[("bank", b2)], w=["Btm"])
                else:
                    p.copy("dve", Xtm[:, ch, f * 128:(f + 1) * 128], bank[b2][:, 0:128], r=[("bank", b2)], w=["Xtm"])
    p.barrier()
    stA.close()

    Aneg = p.sb("Aneg", [128, 16], F32)
    dtA = p.sb("dtA", [128, NCH, 16], F32)
    bsb = p.sb("bsb", [128, 2, NCH, 8], F32)
    tot = p.sb("tot", [128, 2, NCH, 8], F32)
    eo = p.sb("eo", [128, 2, NCH, 8], F32)
    ws = p.sb("ws", [128, 2, NCH, 8], F32)
    dec = p.sb("dec", [128, 2, NCH, 8], F32)
    p.act(Aneg[:], vec[:, 16:32], AF.Exp, r=["vec"], w=["Aneg"])
    p.ts("dve", Aneg[:], Aneg[:], -1.0, None, ALU.mult, None, r=["Aneg"], w=["Aneg"])
    p.tt("dve", dtA[:], dt[:], Aneg[:].unsqueeze(1).broadcast_to([128, NCH, 16]), ALU.mult, r=["dt", "Aneg"], w=["dtA"])
    H = NCH // 2
    for d in range(2):
        tri = triU if d == 0 else triL
        for hf_ in range(2):
            src = dtA[:, hf_ * H:(hf_ + 1) * H, d * 8:(d + 1) * 8]
            b0, b1 = (d * 2 + hf_) % 4, 4 + (d * 2 + hf_) % 4
            p.mm(bank[b0][:, 0:H * 8].rearrange("p (c e) -> p c e", e=8), tri, src, True, True, r=["cs", "dtA"], w=[("bank", b0)])
            p.mm(bank[b1][:, 0:H * 8].rearrange("p (c e) -> p c e", e=8), ones[:], src, True, True, r=["ones", "dtA"], w=[("bank", b1)])
            p.copy("dve", bsb[:, d, hf_ * H:(hf_ + 1) * H, :], bank[b0][:, 0:H * 8].rearrange("p (c e) -> p c e", e=8),
                   r=[("bank", b0)], w=["bsb"])
            p.copy("dve", tot[:, d, hf_ * H:(hf_ + 1) * H, :], bank[b1][:, 0:H * 8].rearrange("p (c e) -> p c e", e=8),
                   r=[("bank", b1)], w=["tot"])
    fl = lambda t: t[:].rearrange("p d c e -> p (d c e)")
    p.act(fl(eo), fl(bsb), AF.Exp, r=["bsb"], w=["eo"])
    p.tt("dve", fl(ws), fl(tot), fl(bsb), ALU.subtract, r=["tot", "bsb"], w=["ws"])
    p.act(fl(ws), fl(ws), AF.Exp, r=["ws"], w=["ws"])
    p.act(fl(dec), fl(tot), AF.Exp, r=["tot"], w=["dec"])

    negb = p.sb("negb", [128, 2, NCH, 8], F32)
    p.ts("dve", fl(negb), fl(bsb), -1.0, None, ALU.mult, None, r=["bsb"], w=["negb"])
    S32 = p.sb("S32", [128, 512], F32)
    Sb = p.sb("Sb", [128, 512], BF16)
    CBm = p.sb("CBm", [128, 2, 128], F32)
    diagb = p.sb("diagb", [128, 2, 8, 128], F32)
    dm = p.sb("dm", [128, 2, 8, 128], BF16)
    G = p.sb("G", [128, 2, 8, 128], BF16)
    xdt = p.sb("xdt", [128, 3, 512], BF16)
    xw = p.sb("xw", [128, 3, 512], BF16)
    yis = p.sb("yis", [128, 2, 512], F32)
    sus = p.sb("sus", [128, 2, 512], F32)
    tmpd = p.sb("tmpd", [128, 512], F32)
    t2 = p.sb("t2", [128, 512], F32)
    t3 = p.sb("t3", [128, 512], BF16)
    ysb = p.sb("ysb", [128, 2, 512], F32)
    yfl = p.sb("yfl", [128, 3, 512], F32)
    zl = p.sb("zl", [128, 3, 512], BF16)
    ob = p.sb("ob", [128, 2, 512], BF16)
    sm = p.sb("sm", [128, 2, 4], F32)
    bc3 = lambda ap: ap.unsqueeze(2).broadcast_to([128, 8, 64])
    v3 = lambda ap: ap.rearrange("p (e c) -> p e c", e=8)

    def preA(i, d, ch):
        q, t = i % 2, i % 3
        lat = ch >= 2
        tk = slice(ch * 128, (ch + 1) * 128)
        dsl = slice(d * 8, (d + 1) * 8)
        p.tt("pool", v3(xdt[:, t, :]), v3(Xtm[:, ch, :]), bc3(dt[:, ch, dsl]), ALU.mult, r=["Xtm", "dt"], w=[("xdt", t)])
        p.tt("pool", v3(xw[:, t, :]), v3(xdt[:, t, :]), bc3(ws[:, d, ch, :]), ALU.mult, r=[("xdt", t), "ws"], w=[("xw", t)])
        if not lat:
            return
        rows = slice((ch - 2) * 128, (ch - 1) * 128)
        p.tt("pool", diagb[:, q], ident.unsqueeze(1).broadcast_to([128, 8, 128]),
             bsb[:, d, ch, :].unsqueeze(2).broadcast_to([128, 8, 128]), ALU.mult, r=["cs", "bsb"], w=[("diagb", q)])
        cbp = bank[7][:, q * 128:(q + 1) * 128]
        p.mm(cbp, BT[:, tk], CT[:, tk], True, True, r=["BT", "CT"], w=[("cbp", q)])
        for h2 in range(2):
            bb = 2 * q + h2
            p.mm(bank[bb][:, 0:512], ones[:], diagb[:, q, 4 * h2:4 * h2 + 4, :].rearrange("p e i -> p (e i)"), True, True,
                 r=["ones", ("diagb", q)], w=[("bank", bb)])
            for e4 in range(4):
                e = 4 * h2 + e4
                p.act(dm[:, q, e, :], bank[bb][:, e4 * 128:(e4 + 1) * 128], AF.Exp, r=[("bank", bb), "negb"], w=[("dm", q)],
                      bias=negb[:, d, ch, e:e + 1])
        if d == 1:
            p.dma("sp", yfl[:, t, :], Yf[rows, :], r=[("Yf", ch)], w=[("yfl", t)], sem="yl%d" % t)
            p.dma("sp", zl[:, t, :], Zd[rows, :], r=[("Zd", ch)], w=[("zl", t)], sem="zl%d" % t)
            p.tt("pool", v3(tmpd[:]), v3(Xtm[:, ch, :]), bc3(vec[:, 32:40]), ALU.mult, r=["Xtm", "vec"], w=["tmpd"])
            p.tt("pool", yfl[:, t, :], yfl[:, t, :], tmpd[:], ALU.add, r=["tmpd", ("yfl", t)], w=[("yfl", t)])

    def preB(i, d, ch):
        q, t = i % 2, i % 3
        tri = triU if d == 0 else triL
        lat = ch >= 2
        if lat:
            cbp = bank[7][:, q * 128:(q + 1) * 128]
            p.tt("dve", CBm[:, q, :], cbp, tri, ALU.mult, r=[("cbp", q), "cs"], w=[("CBm", q)])
            p.stt("dve", G[:, q], dm[:, q], 1.0, CBm[:, q, :].unsqueeze(1).broadcast_to([128, 8, 128]), ALU.min, ALU.mult,
                  r=[("dm", q), ("CBm", q)], w=[("G", q)])
            for e in range(8):
                p.mm(bank[4][:, e * 64:(e + 1) * 64], G[:, q, e, :], xdt[:, t, e * 64:(e + 1) * 64], True, True,
                     r=[("G", q), ("xdt", t)], w=[("bank", 4)])
            p.copy("act", yis[:, q, :], bank[4][:, 0:512], r=[("bank", 4)], w=[("yis", q)])
        p.mm(bank[6][:, 0:512], Btm[:, ch, :], xw[:, t, :], True, True, r=["Btm", ("xw", t)], w=[("bank", 6)])
        p.copy("act", sus[:, q, :], bank[6][:, 0:512], r=[("bank", 6)], w=[("sus", q)])

    def post(i, d, ch):
        q, t = i % 2, i % 3
        lat = ch >= 2
        tk = slice(ch * 128, (ch + 1) * 128)
        if lat:
            rows = slice((ch - 2) * 128, (ch - 1) * 128)
            p.mm(bank[5][:, 0:512], CT[:, tk], Sb[:], True, True, r=["CT", "Sb"], w=[("bank", 5)])
        p.tt("dve", v3(S32[:]), v3(S32[:]), bc3(dec[:, d, ch, :]), ALU.mult, r=["S32", "dec"], w=["S32"])
        p.tt("dve", S32[:], S32[:], sus[:, q, :], ALU.add, r=["S32", ("sus", q)], w=["S32"])
        p.copy("act", Sb[:], S32[:], r=["S32"], w=["Sb"])
        if lat:
            p.tt("dve", v3(t2[:]), v3(bank[5][:, 0:512]), bc3(eo[:, d, ch, :]), ALU.mult, r=[("bank", 5), "eo"], w=["t2"])
            p.tt("dve", ysb[:, q, :], t2[:], yis[:, q, :], ALU.add, r=["t2", ("yis", q)], w=[("ysb", q)])
            if d == 0:
                p.dma("sp", Yf[rows, :], ysb[:, q, :], r=[("ysb", q)], w=[("Yf", ch)], sem="yf%d" % q)
            else:
                p.tt("dve", ysb[:, q, :], ysb[:, q, :], yfl[:, t, :], ALU.add, r=[("ysb", q), ("yfl", t)], w=[("ysb", q)])
                p.tt("dve", ysb[:, q, :], ysb[:, q, :], zl[:, t, :], ALU.mult, r=[("ysb", q), ("zl", t)], w=[("ysb", q)])
                p.add("act", lambda e, q=q: e.activation(out=t3[:], in_=ysb[:, q, :], func=AF.Square, accum_out=sm[:, q, 0:1]),
                      r=[("ysb", q)], w=[("sm", q), "t3"])
                p.act(sm[:, q, 1:2], sm[:, q, 0:1], AF.Sqrt, r=[("sm", q)], w=[("sm", q)], bias=epsb[:, 0:1], scale=1.0 / 512)
                p.add("dve", lambda e, q=q: e.reciprocal(out=sm[:, q, 2:3], in_=sm[:, q, 1:2]), r=[("sm", q)], w=[("sm", q)])
                p.stt("dve", ob[:, q, :], ysb[:, q, :], sm[:, q, 2:3], vec[:, 40:552], ALU.mult, ALU.mult,
                      r=[("ysb", q), ("sm", q), "vec"], w=[("ob", q)])
                p.dma("sp", yo[rows, :], ob[:, q, :], r=[("ob", q)], w=[("yo", ch)], sem="yo%d" % q)

    seq = []
    for d in range(2):
        order = list(range(NCH)) if d == 0 else ([1, 0] + list(range(NCH - 1, 1, -1)))
        seq += [(d, ch) for ch in order]
    NS = len(seq)
    preA(0, *seq[0])
    preA(1, *seq[1])
    preB(0, *seq[0])
    for i, (d, ch) in enumerate(seq):
        if i == 0 or seq[i - 1][0] != d:
            p.memset("dve", S32[:], 0.0, w=["S32"])
            p.memset("dve", Sb[:], 0.0, w=["Sb"])
        if i + 2 < NS:
            preA(i + 2, *seq[i + 2])
        if i + 1 < NS:
            preB(i + 1, *seq[i + 1])
        post(i, d, ch)
    return [("yo", ch) for ch in range(2, NCH)]


def build_p2b():
    nc = bass.Bass("TRN2", target_bir_lowering=False)
    p = Prog(nc)
    bank = [p.ps("bank%d" % b, [128, 512]) for b in range(8)]
    keys = emit_p2b(nc, p, bank)
    p.add("sp", lambda e: e.nop(), r=keys)
    p.emit()
    return nc, p


def build_p2():
    nc = bass.Bass("TRN2", target_bir_lowering=False)
    p = Prog(nc)
    bank = [p.ps("bank%d" % b, [128, 512]) for b in range(8)]
    with p.scope("a_"):
        emit_p2a(nc, p, bank, "a_")
    with p.scope("b_"):
        keys = emit_p2b(nc, p, bank, "b_")
        p.add("sp", lambda e: e.nop(), r=keys)
    p.emit()
    return nc, p


def p2b_inputs(inputs, h_all, hc_all, g):
    w_in = inputs["w_in"][0]
    o = 6176
    cols = np.concatenate([o + np.arange(g * 512, (g + 1) * 512), o + 4096 + np.arange(g * 512, (g + 1) * 512),
                           o + 8192 + np.arange(g * 128, (g + 1) * 128), o + 9216 + np.arange(g * 128, (g + 1) * 128),
                           o + 10240 + np.arange(g * 8, (g + 1) * 8), o + 10304 + np.arange(g * 8, (g + 1) * 8)])
    cch = np.concatenate([np.arange(g * 512, (g + 1) * 512), 4096 + np.arange(g * 128, (g + 1) * 128),
                          5120 + np.arange(g * 128, (g + 1) * 128)])
    cwm = inputs["ssm_conv_w"][0][:, cch]
    cb = inputs["ssm_conv_b"][0][cch]
    cw = np.concatenate([cwm.reshape(5, 6, 128).transpose(2, 1, 0).reshape(128, 30), cb.reshape(6, 128).T], axis=1)
    hsl = slice(g * 8, (g + 1) * 8)
    vec = np.concatenate([inputs["ssm_dt_bias"][0][0, hsl], inputs["ssm_dt_bias"][0][1, hsl],
                          inputs["ssm_a_log"][0][0, hsl], inputs["ssm_a_log"][0][1, hsl],
                          inputs["ssm_d"][0][hsl], inputs["ssm_norm_w"][0][g * 512:(g + 1) * 512]])
    hrm = np.concatenate([hc_all, h_all], axis=0).T
    return {"hT": np.ascontiguousarray(hrm), "wss": np.ascontiguousarray(w_in[:, cols]), "cst": consts_tri(),
            "cw": np.ascontiguousarray(cw, dtype=np.float32),
            "vec": np.ascontiguousarray(np.broadcast_to(vec[None, :], (128, 552)), dtype=np.float32)}


def p2_inputs(inputs, h_all, hc_all, j):
    m = {"a_" + k: v for k, v in p2a_inputs(inputs, h_all, hc_all, j).items()}
    m.update({"b_" + k: v for k, v in p2b_inputs(inputs, h_all, hc_all, j).items()})
    return m


def emit_p3a(nc, p, bank, pre="", modd=None):
    T = 1024
    x1T = dram(nc, pre + "x1T", [D, T], F32, "ExternalInput")
    hTd = dram(nc, pre + "hT", [D, T], BF16, "ExternalInput")
    ymlT = dram(nc, pre + "ymlT", [D, T], BF16, "ExternalInput")
    yssT = dram(nc, pre + "yssT", [2 * D, T], BF16, "ExternalInput")
    if modd is None:
        modd = dram(nc, pre + "modfm", [128, 288], F32, "ExternalInput")
    bgd = dram(nc, pre + "bg", [128, 32], F32, "ExternalInput")
    wpm = dram(nc, pre + "wpm", [D, D], F32, "ExternalInput")
    wps = dram(nc, pre + "wps", [2 * D, D], F32, "ExternalInput")
    wgt = dram(nc, pre + "wgt", [D, 2 * D], F32, "ExternalInput")
    wo = dram(nc, pre + "wo", [D, D], F32, "ExternalInput")
    x2T = dram(nc, pre + "x2T", [D, T], F32, "ExternalOutput")
    c = setup_common(p, T, nslot=3, ffn=False, bank=bank)
    modfm = p.sb("modfm_sb", [128, 288], F32)
    bg = p.sb("bg_sb", [128, 32], F32)
    pss = p.sb("pss", [128, KC, T], BF16)
    p.dma("sp", modfm[:], modd, w=["modfm"], sem="c0")
    p.dma("sp", bg[:], bgd, w=["bg"], sem="c1")
    gmix = modfm[:, :].rearrange("p (c j) -> p c j", j=2)[:, 5 * KC:6 * KC, 0]
    r3 = lambda w_: w_.rearrange("(k p) n -> p k n", p=128)
    wpm_r, wps_r, wgt_r, wo_r = r3(wpm), r3(wps), r3(wgt), r3(wo)
    it = 0
    with p.scope(pre + "A_"):
        yss = p.sb("yss", [128, 2 * KC, T], BF16)
        for n in range(2):
            ts_ = slice(n * 512, (n + 1) * 512)
            p.dma("sp", yss[:, :, ts_], r3(yssT)[:, :, ts_], w=[("yss", n)], sem="a1")
        for dp in range(KC // 2):
            s2 = next_slot(c)
            w2 = c.slot[s2][:, :].rearrange("p (k n) -> p k n", k=2 * KC)
            p.dma("pool", w2, wps_r[:, :, dp * 256:(dp + 1) * 256], w=[("slot", s2)], sem="slot%d" % s2)
            for dd in range(2):
                d = dp * 2 + dd
                for n in range(2):
                    ts_ = slice(n * 512, (n + 1) * 512)
                    q = it % 4
                    it += 1
                    for k in range(2 * KC):
                        p.mm(c.bank[q][:, :], w2[:, k, dd * 128:(dd + 1) * 128], yss[:, k, ts_], k == 0, k == 2 * KC - 1,
                             r=[("slot", s2), ("yss", n)], w=[("bank", q)])
                    p.copy("act", pss[:, d, ts_], c.bank[q][:, :], r=[("bank", q)], w=[("pss", d, n)])
    with p.scope(pre + "B_"):
        yml = p.sb("yml", [128, KC, T], BF16)
        hT = p.sb("hTt", [128, KC, T], BF16)
        mg = p.sb("mg", [128, KC, T], BF16)
        gs = p.sb("gs", [128, 2, 2, 512], F32)
        tu = p.sb("tu", [128, 2, 2, 512], F32)
        xt = p.sb("xt", [128, 4, 512], F32)
        for n in range(2):
            ts_ = slice(n * 512, (n + 1) * 512)
            p.dma("sp", yml[:, :, ts_], r3(ymlT)[:, :, ts_], w=[("yml", n)], sem="a0")
            p.dma("sp", hT[:, :, ts_], r3(hTd)[:, :, ts_], w=[("hTt", n)], sem="a2")
        it = 0
        for dp in range(KC // 2):
            cs_ = slice(dp * 256, (dp + 1) * 256)
            s1, s3 = next_slot(c), next_slot(c)
            w1 = c.slot[s1][:, 0:4096].rearrange("p (k n) -> p k n", k=KC)
            w1g = c.slot[s1][:, 4096:8192].rearrange("p (k n) -> p k n", k=KC)
            w3 = c.slot[s3][:, 0:4096].rearrange("p (k n) -> p k n", k=KC)
            p.dma("pool", w1, wpm_r[:, :, cs_], w=[("slot", s1)], sem="slot%d" % s1)
            p.dma("pool", w1g, wgt_r[:, :, cs_], w=[("slot", s1)], sem="slot%d" % s1)
            p.dma("pool", w3, wgt_r[:, :, D + dp * 256:D + (dp + 1) * 256], w=[("slot", s3)], sem="slot%d" % s3)
            for dd in range(2):
                d = dp * 2 + dd
                dsl = slice(dd * 128, (dd + 1) * 128)
                for n in range(2):
                    ts_ = slice(n * 512, (n + 1) * 512)
                    q = it % 2
                    it += 1
                    b0 = q * 3
                    for k in range(KC):
                        p.mm(c.bank[b0][:, :], w1[:, k, dsl], yml[:, k, ts_], k == 0, k == KC - 1, r=[("slot", s1), ("yml", n)], w=[("bank", b0)])
                    for k in range(KC):
                        p.mm(c.bank[b0 + 1][:, :], w1g[:, k, dsl], hT[:, k, ts_], k == 0, k == KC - 1, r=[("slot", s1), ("hTt", n)], w=[("bank", b0 + 1)])
                    for k in range(KC):
                        p.mm(c.bank[b0 + 2][:, :], w3[:, k, dsl], hT[:, k, ts_], k == 0, k == KC - 1, r=[("slot", s3), ("hTt", n)], w=[("bank", b0 + 2)])
                    p.act(gs[:, q, 0, :], c.bank[b0 + 1][:, :], AF.Sigmoid, r=[("bank", b0 + 1), "bg"], w=[("gs", q)], bias=bg[:, d:d + 1])
                    p.act(gs[:, q, 1, :], c.bank[b0 + 2][:, :], AF.Sigmoid, r=[("bank", b0 + 2), "bg"], w=[("gs", q)], bias=bg[:, KC + d:KC + d + 1])
                    p.tt("dve", tu[:, q, 0, :], gs[:, q, 0, :], c.bank[b0][:, :], ALU.mult, r=[("gs", q), ("bank", b0)], w=[("tu", q)])
                    p.tt("dve", tu[:, q, 1, :], gs[:, q, 1, :], pss[:, d, ts_], ALU.mult, r=[("gs", q), ("pss", d, n)], w=[("tu", q)])
                    p.tt("dve", mg[:, d, ts_], tu[:, q, 0, :], tu[:, q, 1, :], ALU.add, r=[("tu", q)], w=[("mg", n)])
        for dp in range(KC // 2):
            s1 = next_slot(c)
            w1 = c.slot[s1][:, 0:4096].rearrange("p (k n) -> p k n", k=KC)
            p.dma("pool", w1, wo_r[:, :, dp * 256:(dp + 1) * 256], w=[("slot", s1)], sem="slot%d" % s1)
            for dd in range(2):
                d = dp * 2 + dd
                for n in range(2):
                    ts_ = slice(n * 512, (n + 1) * 512)
                    q = it % 4
                    it += 1
                    b0 = (6, 7, 2, 5)[q]
                    p.dma("sp", xt[:, q, :], x1T[d * 128:(d + 1) * 128, ts_], w=[("xt", q)], sem="xt%d" % q)
                    for k in range(KC):
                        p.mm(c.bank[b0][:, :], w1[:, k, dd * 128:(dd + 1) * 128], mg[:, k, ts_], k == 0, k == KC - 1,
                             r=[("slot", s1), ("mg", 0), ("mg", 1)], w=[("bank", b0)])
                    p.stt("dve", xt[:, q, :], c.bank[b0][:, :], gmix[:, d:d + 1], xt[:, q, :], ALU.mult, ALU.add,
                          r=[("bank", b0), ("xt", q), "modfm"], w=[("xt", q)])
                    p.dma("sp", x2T[d * 128:(d + 1) * 128, ts_], xt[:, q, :], r=[("xt", q)], w=[("x2", d, n)], sem="xo%d" % q)
    return x2T, [("x2", d, n) for d in range(KC) for n in range(2)]


def build_p3a():
    nc = bass.Bass("TRN2", target_bir_lowering=False)
    p = Prog(nc)
    bank = [p.ps("bank%d" % b, [128, 512]) for b in range(8)]
    _, keys = emit_p3a(nc, p, bank)
    p.add("sp", lambda e: e.nop(), r=keys)
    p.emit()
    return nc, p


def emit_p3b(nc, p, bank, pre="", x2T=None, modd=None):
    T = 1024
    if x2T is None:
        x2T = dram(nc, pre + "x2T", [D, T], F32, "ExternalInput")
    if modd is None:
        modd = dram(nc, pre + "modfm", [128, 288], F32, "ExternalInput")
    nwd = dram(nc, pre + "nw", [128, 4 * KC], F32, "ExternalInput")
    wg = dram(nc, pre + "wg", [D, DFF], F32, "ExternalInput")
    wu = dram(nc, pre + "wu", [D, DFF], F32, "ExternalInput")
    wd = dram(nc, pre + "wd", [DFF, D], F32, "ExternalInput")
    x3T = dram(nc, pre + "x3T", [D, T], F32, "ExternalOutput")
    outT = dram(nc, "outT", [D, T], F32, "ExternalOutput")
    c = setup_common(p, T, bank=bank)
    c.epsb = p.sb("epsb", [128, 1], F32)
    p.memset("dve", c.epsb[:], EPS, w=["epsb"])
    modfm = p.sb("modfm_sb", [128, 288], F32)
    nw = p.sb("nwt", [128, 4, KC], F32)
    ob = p.sb("obuf", [128, 2, T], F32)
    p.dma("sp", modfm[:], modd, w=["modfm"], sem="c0")
    p.dma("sp", nw[:].rearrange("p s k -> p (s k)"), nwd, w=["nw"], sem="c1")
    A, B, G, gt = make_AB(p, c, modfm, nw, 2)
    p.ts("dve", G[:], gt, 0.5, None, ALU.mult, None, r=["modfm"], w=["modAB"])
    Ao = p.sb("Ao", [128, KC, 2], F32)
    for j in range(2):
        p.copy("dve", Ao[:, :, j], nw[:, 3, :], r=["nw"], w=["modAB"])
    tiles = [(0, 512), (512, 1024)]
    segs = [(0, 1024, 0)]
    ffn_stage(p, c, "f2", tiles, segs, x2T, A, B, G, wg, wu, wd, Ao, None, x3T, outT, ob, hout_rot=2)
    return [("f2_hdram", k) for k in range(KC)]


def build_p3b():
    nc = bass.Bass("TRN2", target_bir_lowering=False)
    p = Prog(nc)
    bank = [p.ps("bank%d" % b, [128, 512]) for b in range(8)]
    keys = emit_p3b(nc, p, bank)
    p.add("sp", lambda e: e.nop(), r=keys)
    p.emit()
    return nc, p


def build_p3():
    nc = bass.Bass("TRN2", target_bir_lowering=False)
    p = Prog(nc)
    bank = [p.ps("bank%d" % b, [128, 512]) for b in range(8)]
    modd = dram(nc, "modfm", [128, 288], F32, "ExternalInput")
    with p.scope("a_"):
        x2T, _ = emit_p3a(nc, p, bank, "a_", modd)
    with p.scope("b_"):
        keys = emit_p3b(nc, p, bank, "b_", x2T=x2T, modd=modd)
        p.add("sp", lambda e: e.nop(), r=keys)
    p.emit()
    return nc, p


def _run(nc, in_maps):
    return run_bass_kernel_spmd(nc, in_maps, core_ids=list(range(len(in_maps)))).results


def kernel(**inputs):
    bf = ml_dtypes.bfloat16
    r1 = run_p1(inputs)
    x1T = [np.asarray(r["x1T"]) for r in r1]
    hT2 = [np.asarray(r["hT2"]) for r in r1]
    modfm = np.asarray(r1[0]["modfm"])
    h_all = np.concatenate([h[:, :1024].T for h in hT2], axis=0)
    hc_all = np.concatenate([h[:, 1024:].T for h in hT2], axis=0)
    nc, _ = build_p2()
    r2 = _run(nc, [p2_inputs(inputs, h_all, hc_all, j) for j in range(NCORES)])
    yml = np.concatenate([from_cm(np.asarray(r["yml"])) for r in r2], axis=1)
    yss = np.concatenate([np.asarray(r["yssm"]) for r in r2], axis=1)
    nc, _ = build_p3()
    nw = np.concatenate([_fm(inputs["norm_w"][0, s]) for s in range(3)] + [_fm(inputs["final_norm_w"])], axis=1)
    common = {"modfm": modfm, "a_bg": _fm(inputs["b_gate"][0]),
              "a_wpm": np.ascontiguousarray(inputs["w_proj_ml"][0]), "a_wps": np.ascontiguousarray(inputs["w_proj_ssm"][0]),
              "a_wgt": np.ascontiguousarray(inputs["w_gate"][0]), "a_wo": np.ascontiguousarray(inputs["w_out"][0]),
              "b_nw": np.ascontiguousarray(nw),
              "b_wg": np.ascontiguousarray(inputs["ffn_w_gate"][0, 1]), "b_wu": np.ascontiguousarray(inputs["ffn_w_up"][0, 1]),
              "b_wd": np.ascontiguousarray(inputs["ffn_w_down"][0, 1])}
    maps = []
    for i in range(NCORES):
        sl = slice(1024 * i, 1024 * (i + 1))
        m = dict(common)
        m["a_x1T"] = np.ascontiguousarray(x1T[i][:, :1024])
        m["a_hT"] = np.ascontiguousarray(hT2[i][:, :1024])
        m["a_ymlT"] = np.ascontiguousarray(yml[sl].T)
        m["a_yssT"] = np.ascontiguousarray(yss[sl].T)
        maps.append(m)
    r3b = _run(nc, maps)
    out = np.concatenate([np.asarray(r["outT"]).T for r in r3b], axis=0)
    return np.ascontiguousarray(out[None].astype(np.float32))
```

```python
import contextlib
import numpy as np
import ml_dtypes
import concourse.bass as bass
import concourse.mybir as mybir
from concourse.bass_utils import run_bass_kernel_spmd

F32 = mybir.dt.float32
BF16 = mybir.dt.bfloat16
AF = mybir.ActivationFunctionType
ALU = mybir.AluOpType
AX = mybir.AxisListType

D = 2048
KC = 16
DFF = 5632
FC = 44
EPS = 1e-6
NCORES = 8

ENGS = ("pe", "act", "dve", "pool", "sp")


class Op:
    __slots__ = ("eng", "fn", "deps", "isdma", "sem", "val", "inc")

    def __init__(self, eng, fn, isdma):
        self.eng = eng
        self.fn = fn
        self.deps = []
        self.isdma = isdma
        self.sem = None
        self.val = None
        self.inc = False


class Prog:
    def __init__(self, nc):
        self.nc = nc
        self.ops = {e: [] for e in ENGS}
        self.keys = {}
        self.stack = contextlib.ExitStack()
        self.dram_n = 0
        self.extra_r = []
        self.nbar = 0
        self.prefix = ""
        self.bar_t = self.sb("bar_t", [128, 8], F32)

    def sb(self, name, shape, dtype, stack=None):
        return (stack or self.stack).enter_context(self.nc.sbuf_tensor(self.prefix + name, list(shape), dtype))

    def ps(self, name, shape, dtype=F32, stack=None):
        return (stack or self.stack).enter_context(self.nc.psum_tensor(self.prefix + name, list(shape), dtype))

    @contextlib.contextmanager
    def scope(self, prefix):
        old_stack, old_prefix = self.stack, self.prefix
        sub = contextlib.ExitStack()
        self.stack, self.prefix = sub, prefix
        try:
            yield
        finally:
            self.barrier()
            sub.close()
            self.stack, self.prefix = old_stack, old_prefix

    def _track(self, op, r, w, after=()):
        deps = set(after)
        for k in r:
            st = self.keys.get(k)
            if st is None:
                st = self.keys[k] = [None, {}, []]
            if st[0] is not None:
                deps.add(st[0])
            if op.isdma:
                st[2].append(op)
            else:
                st[1][op.eng] = op
        for k in w:
            st = self.keys.get(k)
            if st is None:
                st = self.keys[k] = [None, {}, []]
            if st[0] is not None:
                deps.add(st[0])
            for rd in st[1].values():
                deps.add(rd)
            for rd in st[2]:
                deps.add(rd)
            st[0] = op
            st[1] = {}
            st[2] = []
        deps.discard(op)
        for d in deps:
            if d.eng == "pe" and op.eng == "pe" and not d.isdma and not op.isdma:
                continue
            op.deps.append(d)
            d.inc = True

    def add(self, eng, fn, r=(), w=(), after=()):
        op = Op(eng, fn, False)
        r = list(r) + self.extra_r
        self._track(op, r, w, after)
        self.ops[eng].append(op)
        return op

    def dma(self, eng, out, in_, r=(), w=(), sem=None, after=(), **kw):
        op = Op(eng, (lambda e, out=out, in_=in_, kw=kw: e.dma_start(out=out, in_=in_, **kw)), True)
        op.sem = ("dma", sem)
        op.inc = True
        r = list(r) + self.extra_r
        self._track(op, r, w, after)
        self.ops[eng].append(op)
        return op

    def coll(self, kind, src, dst, r, w, sem):
        groups = [list(range(NCORES))]
        op = Op("pool", (lambda e: e.collective_compute(kind, ALU.bypass, replica_groups=groups, ins=[src], outs=[dst])), True)
        op.sem = ("dma", sem)
        op.inc = True
        self._track(op, list(r) + self.extra_r, w)
        self.ops["pool"].append(op)
        return op

    def mm(self, out, lhsT, rhs, start, stop, r, w):
        return self.add("pe", lambda e: e.matmul(out, lhsT, rhs, start=start, stop=stop), r=r, w=w)

    def act(self, out, in_, func, r, w, bias=0.0, scale=1.0, eng="act"):
        return self.add(eng, lambda e: e.activation(out=out, in_=in_, func=func, bias=bias, scale=scale), r=r, w=w)

    def tt(self, eng, out, in0, in1, op, r, w):
        return self.add(eng, lambda e: e.tensor_tensor(out=out, in0=in0, in1=in1, op=op), r=r, w=w)

    def ts(self, eng, out, in0, s1, s2, op0, op1, r, w):
        if s2 is None:
            return self.add(eng, lambda e: e.tensor_scalar(out, in0, s1, None, op0), r=r, w=w)
        return self.add(eng, lambda e: e.tensor_scalar(out, in0, s1, s2, op0, op1), r=r, w=w)

    def stt(self, eng, out, in0, scalar, in1, op0, op1, r, w):
        return self.add(eng, lambda e: e.scalar_tensor_tensor(out=out, in0=in0, scalar=scalar, in1=in1, op0=op0, op1=op1), r=r, w=w)

    def copy(self, eng, out, in_, r, w):
        if eng == "act":
            return self.add(eng, lambda e: e.activation(out=out, in_=in_, func=AF.Identity), r=r, w=w)
        return self.add(eng, lambda e: e.tensor_copy(out=out, in_=in_), r=r, w=w)

    def memset(self, eng, ap, val, w):
        return self.add(eng, lambda e: e.memset(ap, val), w=w)

    def barrier(self):
        allkeys = list(self.keys.keys())
        self.extra_r = []
        tok = ("bar", self.nbar)
        i = self.nbar
        self.nbar += 1
        self.add("dve", lambda e: e.memset(self.bar_t[:, i % 8:i % 8 + 1], 0.0), r=allkeys, w=allkeys + [tok])
        self.extra_r = [tok]

    def emit(self):
        nc = self.nc
        counts = {}
        for e in ENGS:
            for op in self.ops[e]:
                if op.isdma:
                    k = op.sem
                    counts[k] = counts.get(k, 0) + 16
                    op.val = counts[k]
                elif op.inc:
                    k = ("eng", e)
                    op.sem = k
                    counts[k] = counts.get(k, 0) + 1
                    op.val = counts[k]
        sems = {}
        st = contextlib.ExitStack()
        for i, k in enumerate(counts.keys()):
            sems[k] = st.enter_context(nc.semaphore("s%d" % i))
        self.maxcount = max(counts.values()) if counts else 0
        self.nsems = len(counts)
        engobj = {"pe": "tensor", "act": "scalar", "dve": "vector", "pool": "gpsimd", "sp": "sync"}

        def run(e, eo):
            waited = {}
            for op in self.ops[e]:
                need = {}
                for d in op.deps:
                    if need.get(d.sem, 0) < d.val:
                        need[d.sem] = d.val
                for k, v in need.items():
                    if waited.get(k, 0) < v:
                        eo.wait_ge(sems[k], v)
                        waited[k] = v
                ins = op.fn(eo)
                if op.inc:
                    ins.then_inc(sems[op.sem], 16 if op.isdma else 1)

        with nc.Block() as block:
            for e in ENGS:
                if not self.ops[e]:
                    continue
                getattr(block, engobj[e])(lambda eo, e=e: run(e, eo))
        st.close()
        self.stack.close()


def dram(nc, name, shape, dtype, kind):
    return nc.dram_tensor(name, list(shape), dtype, kind=kind).ap()


class Ctx:
    pass


def setup_common(p, T, nslot=3, ffn=True, bank=None):
    c = Ctx()
    c.T = T
    c.bank = bank if bank is not None else [p.ps("bank%d" % b, [128, 512]) for b in range(8)]
    c.NSLOT = nslot
    c.slot = [p.sb("slot%d" % s, [128, 8192], BF16) for s in range(c.NSLOT)]
    c.slot_i = 0
    c.ones = p.sb("ones", [128, 128], F32)
    p.memset("dve", c.ones[:], 1.0, w=["ones"])
    if ffn:
        c.xb = p.sb("xb", [128, 3, T], F32)
        c.xb_i = 0
        c.rstd = p.sb("rstd", [128, T], F32)
        c.sg = p.sb("sg", [128, 2, 512], F32)
        c.sq = p.sb("sq", [128, 2, 512], F32)
        c.sq_i = 0
        c.hT = p.sb("hT", [128, KC, T], BF16)
        c.aT = p.sb("aT", [128, FC, T], BF16)
    return c


def next_slot(c):
    s = c.slot_i % c.NSLOT
    c.slot_i += 1
    return s


def ffn_stage(p, c, name, tiles, segs, x_in, A_in, B_in, HG, wg, wu, wd, A_out, B_out,
              x_out, h_out, h_out_sb, interleave=None, hout_rot=0, mid_hook=None):
    T = c.T
    nt = len(tiles)
    ssb = [5, 6, 7]

    def segs_in(n0, n1):
        out = []
        for (c0, c1, j) in segs:
            a, b = max(c0, n0), min(c1, n1)
            if a < b:
                out.append((a, b, j))
        return out

    def load_x(src, k, tag, allk=False):
        b = c.xb_i % 3
        c.xb_i += 1
        rk = [(tag, kk) for kk in range(KC)] if allk else [(tag, k)]
        p.dma("sp", c.xb[:, b, :], src[k * 128:(k + 1) * 128, :], r=rk, w=[("xb", b)], sem="xb%d" % b)
        return b

    def sumsq_accum(src_ap_fn, srckeys, k, first, last):
        for n, (n0, n1) in enumerate(tiles):
            q = c.sq_i % 2
            c.sq_i += 1
            w_ = n1 - n0
            p.act(c.sq[:, q, 0:w_], src_ap_fn(n0, n1), AF.Square, r=srckeys, w=[("sq", q)])
            p.mm(c.bank[ssb[n]][:, 0:w_], c.ones[:], c.sq[:, q, 0:w_], first, last,
                 r=["ones", ("sq", q)], w=[("bank", ssb[n])])

    def make_rstd():
        for n, (n0, n1) in enumerate(tiles):
            w_ = n1 - n0
            p.act(c.rstd[:, n0:n1], c.bank[ssb[n]][:, 0:w_], AF.Sqrt, r=[("bank", ssb[n]), "epsb"], w=[("rstd", n)],
                  bias=c.epsb[:, 0:1], scale=1.0 / D)
            p.add("dve", lambda e, n0=n0, n1=n1: e.reciprocal(out=c.rstd[:, n0:n1], in_=c.rstd[:, n0:n1]),
                  r=[("rstd", n)], w=[("rstd", n)])

    rstd_keys = [("rstd", n) for n in range(nt)]

    def apply_norm(b, k, A, B, dst, dstkey, kk=None):
        if kk is None:
            kk = k
        p.tt("dve", c.xb[:, b, :], c.xb[:, b, :], c.rstd[:, :], ALU.mult, r=[("xb", b)] + rstd_keys, w=[("xb", b)])
        for (c0, c1, j) in segs:
            bias = B[:, k, j:j + 1] if B is not None else 0.0
            p.act(dst[:, kk, c0:c1], c.xb[:, b, c0:c1], AF.Identity, r=[("xb", b), "modAB"], w=[(dstkey, kk)],
                  bias=bias, scale=A[:, k, j:j + 1])

    for k in range(KC):
        b = load_x(x_in, k, name + "_xin")
        sumsq_accum(lambda n0, n1, b=b: c.xb[:, b, n0:n1], [("xb", b)], k, k == 0, k == KC - 1)
    make_rstd()
    if mid_hook is not None:
        mid_hook()
    for k in range(KC):
        b = load_x(x_in, k, name + "_xin")
        apply_norm(b, k, A_in, B_in, c.hT, "hT")

    wg_r = wg.rearrange("(k p) n -> p k n", p=128)
    wu_r = wu.rearrange("(k p) n -> p k n", p=128)
    gu_i = 0
    for fb in range(FC // 2):
        s = next_slot(c)
        sl = c.slot[s]
        wgs = sl[:, 0:KC * 256].rearrange("p (k n) -> p k n", k=KC)
        wus = sl[:, KC * 256:2 * KC * 256].rearrange("p (k n) -> p k n", k=KC)
        p.dma("pool", wgs, wg_r[:, :, fb * 256:(fb + 1) * 256], w=[("slot", s)], sem="slot%d" % s)
        p.dma("pool", wus, wu_r[:, :, fb * 256:(fb + 1) * 256], w=[("slot", s)], sem="slot%d" % s)
        for ff in range(2):
            f = fb * 2 + ff
            for n, (n0, n1) in enumerate(tiles):
                w_ = n1 - n0
                par = gu_i % 2
                gu_i += 1
                gb, ub = par * 2, par * 2 + 1
                for k in range(KC):
                    p.mm(c.bank[gb][:, 0:w_], wgs[:, k, ff * 128:(ff + 1) * 128], c.hT[:, k, n0:n1], k == 0, k == KC - 1,
                         r=[("slot", s), ("hT", k)], w=[("bank", gb)])
                for k in range(KC):
                    p.mm(c.bank[ub][:, 0:w_], wus[:, k, ff * 128:(ff + 1) * 128], c.hT[:, k, n0:n1], k == 0, k == KC - 1,
                         r=[("slot", s), ("hT", k)], w=[("bank", ub)])
                p.act(c.sg[:, par, 0:w_], c.bank[gb][:, 0:w_], AF.Silu, r=[("bank", gb)], w=[("sg", par)])
                p.tt("dve", c.aT[:, f, n0:n1], c.sg[:, par, 0:w_], c.bank[ub][:, 0:w_], ALU.mult,
                     r=[("sg", par), ("bank", ub)], w=[("aT", f)])
        if interleave is not None:
            interleave(fb)

    wd_r = wd.rearrange("(f p) n -> p f n", p=128)
    HF = FC // 2
    dn_i = 0
    for db in range(KC // 2):
        ss_ = []
        for half in range(2):
            s = next_slot(c)
            ws = c.slot[s][:, 0:HF * 256].rearrange("p (f n) -> p f n", f=HF)
            p.dma("pool", ws, wd_r[:, half * HF:(half + 1) * HF, db * 256:(db + 1) * 256], w=[("slot", s)], sem="slot%d" % s)
            ss_.append((s, ws))
        for dd in range(2):
            d = db * 2 + dd
            b = load_x(x_in, d, name + "_xin")
            for n, (n0, n1) in enumerate(tiles):
                w_ = n1 - n0
                ob = dn_i % 2
                dn_i += 1
                for f in range(FC):
                    s, ws = ss_[f // HF]
                    p.mm(c.bank[ob][:, 0:w_], ws[:, f % HF, dd * 128:(dd + 1) * 128], c.aT[:, f, n0:n1], f == 0, f == FC - 1,
                         r=[("slot", s), ("aT", f)], w=[("bank", ob)])
                for (a0, a1, j) in segs_in(n0, n1):
                    p.stt("dve", c.xb[:, b, a0:a1], c.bank[ob][:, a0 - n0:a1 - n0], HG[:, d, j:j + 1], c.xb[:, b, a0:a1],
                          ALU.mult, ALU.add, r=[("bank", ob), ("xb", b), "modAB"], w=[("xb", b)])
            sumsq_accum(lambda n0, n1, b=b: c.xb[:, b, n0:n1], [("xb", b)], d, d == 0, d == KC - 1)
            p.dma("sp", x_out[d * 128:(d + 1) * 128, :], c.xb[:, b, :], r=[("xb", b)], w=[(name + "_xout", d)], sem=name + "_xo%d" % b)
    make_rstd()
    for k in range(KC):
        b = load_x(x_out, k, name + "_xout", allk=True)
        kk = k % hout_rot if hout_rot else k
        apply_norm(b, k, A_out, B_out, h_out_sb, name + "_hout", kk)
        p.dma("sp", h_out[k * 128:(k + 1) * 128, :], h_out_sb[:, kk, :], r=[(name + "_hout", kk)], w=[(name + "_hdram", k)],
              sem=name + "_ho%d" % (kk % 4))


def make_AB(p, c, modfm, nw, sub, want_out=None):
    A = p.sb("A%d" % sub, [128, KC, 2], F32)
    B = p.sb("B%d" % sub, [128, KC, 2], F32)
    G = p.sb("G%d" % sub, [128, KC, 2], F32)
    m = modfm[:, :].rearrange("p (c j) -> p c j", j=2)
    sh = m[:, (3 * sub) * KC:(3 * sub + 1) * KC, :]
    sc = m[:, (3 * sub + 1) * KC:(3 * sub + 2) * KC, :]
    gt = m[:, (3 * sub + 2) * KC:(3 * sub + 3) * KC, :]
    for j in range(2):
        p.stt("dve", A[:, :, j], sc[:, :, j], 1.0, nw[:, sub, :], ALU.add, ALU.mult, r=["modfm", "nw"], w=["modAB"])
    p.copy("dve", B[:], sh, r=["modfm"], w=["modAB"])
    return A, B, G, gt


def build_p1():
    nc = bass.Bass("TRN2", target_bir_lowering=False)
    T = 1056
    xT = dram(nc, "xT", [D, T], F32, "ExternalInput")
    c2 = dram(nc, "c2", [128, KC * 2], F32, "ExternalInput")
    w_ada = dram(nc, "w_ada", [D, 9 * D], F32, "ExternalInput")
    b_fm = dram(nc, "b_fm", [128, 144], F32, "ExternalInput")
    nwd = dram(nc, "nw", [128, 3 * KC], F32, "ExternalInput")
    wg = dram(nc, "wg", [D, DFF], F32, "ExternalInput")
    wu = dram(nc, "wu", [D, DFF], F32, "ExternalInput")
    wd = dram(nc, "wd", [DFF, D], F32, "ExternalInput")
    x1T = dram(nc, "x1T", [D, T], F32, "ExternalOutput")
    hT2 = dram(nc, "hT2", [D, T], BF16, "ExternalOutput")
    modo = dram(nc, "modfm", [128, 288], F32, "ExternalOutput")

    p = Prog(nc)
    c = setup_common(p, T)
    c.epsb = p.sb("epsb", [128, 1], F32)
    p.memset("dve", c.epsb[:], EPS, w=["epsb"])
    c2t = p.sb("c2t", [128, KC * 2], F32)
    sc2 = p.sb("sc2", [128, KC, 2], BF16)
    bfm = p.sb("bfm", [128, 144], F32)
    nw = p.sb("nwt", [128, 3, KC], F32)
    modfm = p.sb("modfm_sb", [128, 288], F32)
    p.dma("sp", c2t[:], c2, w=["c2t"], sem="c0")
    p.dma("sp", bfm[:], b_fm, w=["bfm"], sem="c1")
    p.dma("sp", nw[:].rearrange("p s k -> p (s k)"), nwd, w=["nw"], sem="c2")
    p.act(sc2[:].rearrange("p k j -> p (k j)"), c2t[:], AF.Silu, r=["c2t"], w=["sc2"])

    wa_r = w_ada.rearrange("(k p) n -> p k n", p=128)
    modps = c.bank[4]

    def mod_block(blk):
        s = next_slot(c)
        ws = c.slot[s][:, :].rearrange("p (k n) -> p k n", k=KC)
        p.dma("pool", ws, wa_r[:, :, blk * 512:(blk + 1) * 512], w=[("slot", s)], sem="slot%d" % s)
        for cc in range(4):
            ch = blk * 4 + cc
            for k in range(KC):
                p.mm(modps[:, 2 * ch:2 * ch + 2], ws[:, k, cc * 128:(cc + 1) * 128], sc2[:, k, :], k == 0, k == KC - 1,
                     r=[("slot", s), "sc2"], w=[("bank", 4)])

    def mod_finish(c0, c1):
        for j in range(2):
            src = modps[:, 2 * c0:2 * c1].rearrange("p (c j) -> p c j", j=2)[:, :, j]
            dst = modfm[:, 2 * c0:2 * c1].rearrange("p (c j) -> p c j", j=2)[:, :, j]
            p.tt("dve", dst, src, bfm[:, c0:c1], ALU.add, r=[("bank", 4), "bfm"], w=["modfm"])

    A1 = p.sb("A1x", [128, KC, 2], F32)
    B1 = p.sb("B1x", [128, KC, 2], F32)
    G1 = p.sb("G1x", [128, KC, 2], F32)
    m_ = modfm[:, :].rearrange("p (c j) -> p c j", j=2)
    gt1 = m_[:, 2 * KC:3 * KC, :]

    def mid_hook():
        for blk in range(8):
            mod_block(blk)
        mod_finish(0, 32)
        for j in range(2):
            p.stt("dve", A1[:, :, j], m_[:, KC:2 * KC, j], 1.0, nw[:, 0, :], ALU.add, ALU.mult, r=["modfm", "nw"], w=["modAB"])
        p.copy("dve", B1[:], m_[:, 0:KC, :], r=["modfm"], w=["modAB"])

    tiles = [(0, 352), (352, 704), (704, 1056)]
    segs = [(0, 1024, 0), (1024, 1056, 1)]
    A2 = p.sb("A2x", [128, KC, 2], F32)
    B2 = p.sb("B2x", [128, KC, 2], F32)

    pending = list(range(8, 36))

    def inter2(fb):
        if not pending:
            return
        for _ in range(2):
            if pending:
                mod_block(pending.pop(0))
        if not pending:
            mod_finish(32, 144)
            p.ts("dve", G1[:], gt1, 0.5, None, ALU.mult, None, r=["modfm"], w=["modAB"])
            m = modfm[:, :].rearrange("p (c j) -> p c j", j=2)
            for j in range(2):
                p.stt("dve", A2[:, :, j], m[:, 4 * KC:5 * KC, j], 1.0, nw[:, 1, :], ALU.add, ALU.mult,
                      r=["modfm", "nw"], w=["modAB"])
            p.copy("dve", B2[:], m[:, 3 * KC:4 * KC, :], r=["modfm"], w=["modAB"])
            p.dma("sp", modo, modfm[:], r=["modfm"], w=["modo"], sem="modo")

    ffn_stage(p, c, "f1", tiles, segs, xT, A1, B1, G1, wg, wu, wd, A2, B2, x1T, hT2, c.hT, interleave=inter2, mid_hook=mid_hook)
    p.add("sp", lambda e: e.nop(), r=["modo"] + [("f1_hdram", k) for k in range(KC)] + [("f1_xout", k) for k in range(KC)])
    p.emit()
    return nc, p


def _fm(v):
    return np.ascontiguousarray(v.reshape(-1, 128).T)


def run_p1(inputs):
    x = inputs["x"][0]
    ctx = inputs["ctx"][0]
    nc, p = build_p1()
    c2 = np.stack([_fm(inputs["c"][0]), _fm(inputs["c_ctx"])], axis=-1).reshape(128, 32)
    nw = np.stack([_fm(inputs["norm_w"][0, s]) for s in range(3)], axis=1).reshape(128, 48)
    common = {
        "c2": np.ascontiguousarray(c2, dtype=np.float32),
        "w_ada": np.ascontiguousarray(inputs["w_ada"][0]),
        "b_fm": _fm(inputs["b_ada"][0]),
        "nw": np.ascontiguousarray(nw),
        "wg": np.ascontiguousarray(inputs["ffn_w_gate"][0, 0]),
        "wu": np.ascontiguousarray(inputs["ffn_w_up"][0, 0]),
        "wd": np.ascontiguousarray(inputs["ffn_w_down"][0, 0]),
    }
    in_maps = []
    for i in range(NCORES):
        xt = np.concatenate([x[1024 * i:1024 * (i + 1)], ctx[32 * i:32 * (i + 1)]], axis=0).T
        m = dict(common)
        m["xT"] = np.ascontiguousarray(xt)
        in_maps.append(m)
    res = run_bass_kernel_spmd(nc, in_maps, core_ids=list(range(NCORES)))
    return res.results


NTOK = 8448
NCH = 66
UW = 8456


def upos(t):
    return 2 + t if t < 256 else 6 + t


UW2 = 8576


def upos2(t):
    return 32 + t if t < 256 else 96 + t


def emit_p2a(nc, p, bank, pre=""):
    hT = dram(nc, pre + "hT", [D, NTOK], BF16, "ExternalInput")
    wml = dram(nc, pre + "wml", [D, 772], F32, "ExternalInput")
    cst = dram(nc, pre + "cst", [128, 3 * 128], F32, "ExternalInput")
    cw = dram(nc, pre + "cw", [128, 2 * 5 + 2], F32, "ExternalInput")
    gbn = dram(nc, pre + "gbn", [128, 4 + 256], F32, "ExternalInput")
    yo = dram(nc, "yml", [8192, 256], BF16, "ExternalOutput")
    cs = p.sb("cs", [128, 3, 128], F32)
    ident, triU, triL = cs[:, 0, :], cs[:, 1, :], cs[:, 2, :]
    identb = p.sb("identb", [128, 128], BF16)
    cwt = p.sb("cwt", [128, 12], F32)
    gb = p.sb("gb", [128, 260], F32)
    ones = p.sb("ones", [128, 128], F32)
    epsb = p.sb("epsb", [128, 1], F32)
    QT = p.sb("QT", [128, NTOK], BF16)
    KT = p.sb("KT", [128, NTOK], BF16)
    Ktm = p.sb("Ktm", [128, NCH, 128], BF16)
    Va = p.sb("Va", [128, NCH, 257], BF16)
    SO = p.sb("SO", [128, NCH, 256], BF16)
    gat = p.sb("gat", [128, NCH, 4], F32)
    p.dma("sp", cs[:].rearrange("p a b -> p (a b)"), cst, w=["cs"], sem="c0")
    p.dma("sp", cwt[:], cw, w=["cwt"], sem="c1")
    p.dma("sp", gb[:], gbn, w=["gb"], sem="c2")
    p.memset("dve", ones[:], 1.0, w=["ones"])
    p.memset("dve", epsb[:], EPS, w=["epsb"])
    p.copy("dve", identb[:], ident, r=["cs"], w=["identb"])
    p.memset("dve", Va[:, :, 256:257], 1.0, w=["Va"])

    stA = contextlib.ExitStack()
    W = p.sb("W", [128, KC, 772], BF16, stack=stA)
    ht2 = p.sb("ht", [128, 2, KC, 256], BF16, stack=stA)
    U = p.sb("U", [128, 2, UW], BF16, stack=stA)
    dg = p.sb("dg", [128, 2, 5, 128], BF16, stack=stA)
    wr = wml.rearrange("(k p) n -> p k n", p=128)
    for k0 in range(0, KC, 4):
        p.dma("pool", W[:, k0:k0 + 4, :], wr[:, k0:k0 + 4, :], w=["W"], sem="W")
    p.memset("dve", U[:], 0.0, w=["U"])
    for f in range(2):
        for tap in range(5):
            p.ts("dve", dg[:, f, tap, :], ident, cwt[:, f * 5 + tap:f * 5 + tap + 1], None, ALU.mult, None,
                 r=["cs", "cwt"], w=["dg"])
    hr = hT.rearrange("(k p) n -> p k n", p=128)
    bi = 0
    for ti, t0 in enumerate(range(0, NTOK, 256)):
        n = 256
        hq = ti % 2
        ht = ht2[:, hq]
        p.dma("sp", ht[:, :, 0:n], hr[:, :, t0:t0 + n], w=[("ht", hq)], sem="ht%d" % hq)
        for f in range(2):
            b = bi % 4
            bi += 1
            for k in range(KC):
                p.mm(bank[b][:, 0:n], W[:, k, f * 128:(f + 1) * 128], ht[:, k, 0:n], k == 0, k == KC - 1,
                     r=["W", ("ht", hq)], w=[("bank", b)])
            for (a0, a1) in [(0, n)]:
                p0 = upos(t0 + a0)
                p.copy("act", U[:, f, p0:p0 + a1 - a0], bank[b][:, a0:a1], r=[("bank", b)], w=["U"])
        for sub in range(n // 128):
            ch = t0 // 128 + sub
            b = bi % 4
            bi += 1
            b2 = 4 + b
            for k in range(KC):
                p.mm(bank[b][:, 0:260], ht[:, k, sub * 128:(sub + 1) * 128], W[:, k, 256:516], k == 0, k == KC - 1,
                     r=["W", ("ht", hq)], w=[("bank", b)])
            for k in range(KC):
                p.mm(bank[b2][:, 0:256], ht[:, k, sub * 128:(sub + 1) * 128], W[:, k, 516:772], k == 0, k == KC - 1,
                     r=["W", ("ht", hq)], w=[("bank", b2)])
            p.copy("dve", Va[:, ch, 0:256], bank[b][:, 0:256], r=[("bank", b)], w=["Va"])
            p.tt("dve", gat[:, ch, :], bank[b][:, 256:260], gb[:, 0:4], ALU.add, r=[("bank", b), "gb"], w=["gat"])
            p.act(SO[:, ch, :], bank[b2][:, 0:256], AF.Sigmoid, r=[("bank", b2)], w=["SO"])
    for f, dst in ((0, QT), (1, KT)):
        for (s0, s1) in [(0, 256)] + [(a, a + 512) for a in range(256, NTOK, 512)]:
            b = bi % 4
            bi += 1
            n = s1 - s0
            p0 = upos(s0)
            for tap in range(5):
                p.mm(bank[b][:, 0:n], dg[:, f, tap, :], U[:, f, p0 + tap - 2:p0 + tap - 2 + n], tap == 0, tap == 4,
                     r=["dg", "U"], w=[("bank", b)])
            p.act(dst[:, s0:s1], bank[b][:, 0:n], AF.Silu, r=[("bank", b), "cwt"], w=["QK"], bias=cwt[:, 10 + f:11 + f])
    for ch in range(NCH):
        b = bi % 4
        bi += 1
        p.mm(bank[b][:, 0:128], KT[:, ch * 128:(ch + 1) * 128], identb[:], True, True, r=["QK", "identb"], w=[("bank", b)])
        p.copy("act", Ktm[:, ch, :], bank[b][:, 0:128], r=[("bank", b)], w=["Ktm"])
    p.barrier()
    stA.close()

    lf = p.sb("lf", [128, 2, NCH], F32)
    sc = p.sb("sc", [128, 8, NCH], F32)
    tmp = p.sb("tmpg", [128, 2, NCH], F32)
    for d in range(2):
        p.act(tmp[:, d, :], gat[:, :, 2 * d + 1], AF.Exp, r=["gat"], w=["tmpg"], scale=-1.0)
        p.act(tmp[:, d, :], tmp[:, d, :], AF.Ln, r=["tmpg"], w=["tmpg"], bias=1.0)
        p.ts("dve", lf[:, d, :], tmp[:, d, :], -1.0, None, ALU.mult, None, r=["tmpg"], w=["lf"])
    cum = bank[7]
    p.mm(cum[:, 0:NCH], triU, lf[:, 0, :], True, True, r=["cs", "lf"], w=[("bank", 7)])
    p.mm(cum[:, NCH:2 * NCH], triL, lf[:, 1, :], True, True, r=["cs", "lf"], w=[("bank", 7)])
    p.mm(cum[:, 2 * NCH:4 * NCH], ones[:], lf[:].rearrange("p d c -> p (d c)"), True, True, r=["ones", "lf"], w=[("bank", 7)])
    for d in range(2):
        bcol = cum[:, d * NCH:(d + 1) * NCH]
        tot = cum[:, (2 + d) * NCH:(3 + d) * NCH]
        ig = gat[:, :, 2 * d]
        o = 4 * d
        p.tt("dve", tmp[:, d, :], ig, bcol, ALU.subtract, r=["gat", ("bank", 7)], w=["tmpg"])
        p.act(sc[:, o + 0, :], tmp[:, d, :], AF.Exp, r=["tmpg"], w=["sc"])
        p.act(sc[:, o + 1, :], bcol, AF.Exp, r=[("bank", 7)], w=["sc"])
        p.ts("dve", sc[:, o + 1, :], sc[:, o + 1, :], 128.0 ** -0.5, None, ALU.mult, None, r=["sc"], w=["sc"])
        p.act(sc[:, o + 3, :], tot, AF.Exp, r=[("bank", 7)], w=["sc"])
        p.tt("dve", sc[:, o + 2, :], sc[:, o + 0, :], sc[:, o + 3, :], ALU.mult, r=["sc"], w=["sc"])

    C32 = p.sb("C32", [128, 257], F32)
    Cb = p.sb("Cb", [128, 257], BF16)
    PT = p.sb("PT", [128, 3, 128], BF16)
    kw = p.sb("kw", [128, 3, 128], BF16)
    sm = p.sb("sm", [128, 2, 4], F32)
    hraw = p.sb("hraw", [128, 2, 64, 256], BF16)
    den = p.sb("den", [128, 2, 64], F32)
    rr = p.sb("rr", [128, 2, 64], F32)
    ssq = p.sb("ssq", [128, 64], F32)
    rsd = p.sb("rsd", [128, 64], F32)
    hs = p.sb("hs", [128, 2, 256], F32)
    yb = p.sb("yb", [128, 2, 256], BF16)
    def pre(i, d, ch):
        q = i % 3
        o = 4 * d
        mask = triU if d == 0 else triL
        sb_, ub_ = 0 + i % 2, 4 + q
        tk = slice(ch * 128, (ch + 1) * 128)
        if ch >= 2:
            p.mm(bank[sb_][:, 0:128], KT[:, tk], QT[:, tk], True, True, r=["QK"], w=[("bank", sb_)])
            p.stt("dve", PT[:, q, :], bank[sb_][:, 0:128], sc[:, o + 0, ch:ch + 1], mask, ALU.mult, ALU.mult,
                  r=[("bank", sb_), "sc", "cs"], w=[("PT", q)])
        p.ts("dve", kw[:, q, :], Ktm[:, ch, :], sc[:, o + 2, ch:ch + 1], None, ALU.mult, None, r=["Ktm", "sc"], w=[("kw", q)])
        p.mm(bank[ub_][:, 0:257], kw[:, q, :], Va[:, ch, :], True, True, r=[("kw", q), "Va"], w=[("bank", ub_)])

    def post(i, d, ch):
        q3 = i % 3
        q = i % 2
        o = 4 * d
        ab_, ub_ = 2 + q, 4 + q3
        tk = slice(ch * 128, (ch + 1) * 128)
        if ch >= 2:
            p.mm(bank[ab_][:, 0:257], PT[:, q3, :], Va[:, ch, :], True, False, r=[("PT", q3), "Va"], w=[("bank", ab_)])
            p.mm(bank[ab_][:, 0:257], QT[:, tk], Cb[:], False, True, r=["QK", "Cb"], w=[("bank", ab_)])
        p.stt("dve", C32[:], C32[:], sc[:, o + 3, ch:ch + 1], bank[ub_][:, 0:257], ALU.mult, ALU.add,
              r=["C32", "sc", ("bank", ub_)], w=["C32"])
        p.copy("act", Cb[:], C32[:], r=["C32"], w=["Cb"])
        if ch >= 2:
            c_ = ch - 2
            p.act(den[:, d, c_:c_ + 1], bank[ab_][:, 256:257], AF.Abs, r=[("bank", ab_), "sc"], w=[("den", d, c_)],
                  scale=sc[:, o + 1, ch:ch + 1])
            p.copy("act", hraw[:, d, c_, :], bank[ab_][:, 0:256], r=[("bank", ab_)], w=[("hraw", d, c_)])

    seq = []
    for d in range(2):
        order = list(range(NCH)) if d == 0 else ([1, 0] + list(range(NCH - 1, 1, -1)))
        seq += [(d, ch) for ch in order]
    pre(0, *seq[0])
    pre(1, *seq[1])
    for i, (d, ch) in enumerate(seq):
        if i == 0 or seq[i - 1][0] != d:
            p.memset("dve", C32[:], 0.0, w=["C32"])
            p.memset("dve", Cb[:], 0.0, w=["Cb"])
        if i + 2 < len(seq):
            pre(i + 2, *seq[i + 2])
        post(i, d, ch)
    denk = [("den", d, c_) for d in range(2) for c_ in range(64)]
    for d in range(2):
        p.ts("dve", rr[:, d, :], den[:, d, :], 1.0, None, ALU.max, None, r=denk, w=["rr"])
        p.add("dve", lambda e, d=d: e.reciprocal(out=rr[:, d, :], in_=rr[:, d, :]), r=["rr"], w=["rr"])
        p.tt("dve", rr[:, d, :], rr[:, d, :], sc[:, 4 * d + 1, 2:66], ALU.mult, r=["rr", "sc"], w=["rr"])

    def comb(c_, q):
        p.ts("dve", hs[:, q, :], hraw[:, 0, c_, :], rr[:, 0, c_:c_ + 1], None, ALU.mult, None,
             r=[("hraw", 0, c_), "rr"], w=[("hs", q)])
        p.stt("dve", hs[:, q, :], hraw[:, 1, c_, :], rr[:, 1, c_:c_ + 1], hs[:, q, :], ALU.mult, ALU.add,
              r=[("hraw", 1, c_), "rr", ("hs", q)], w=[("hs", q)])

    for c_ in range(64):
        q = c_ % 2
        comb(c_, q)
        p.add("act", lambda e, q=q, c_=c_: e.activation(out=yb[:, q, :], in_=hs[:, q, :], func=AF.Square,
                                                        accum_out=ssq[:, c_:c_ + 1]), r=[("hs", q)], w=[("ssq", c_), ("yb", q)])
    p.act(rsd[:], ssq[:], AF.Sqrt, r=[("ssq", c_) for c_ in range(64)] + ["epsb"], w=["rsd"], bias=epsb[:, 0:1], scale=1.0 / 256)
    p.add("dve", lambda e: e.reciprocal(out=rsd[:], in_=rsd[:]), r=["rsd"], w=["rsd"])
    for c_ in range(64):
        q = c_ % 2
        comb(c_, q)
        p.stt("dve", hs[:, q, :], hs[:, q, :], rsd[:, c_:c_ + 1], gb[:, 4:260], ALU.mult, ALU.mult,
              r=[("hs", q), "rsd", "gb"], w=[("hs", q)])
        p.tt("dve", yb[:, q, :], hs[:, q, :], SO[:, c_ + 2, :], ALU.mult, r=[("hs", q), "SO"], w=[("yb", q)])
        p.dma("sp", yo[c_ * 128:(c_ + 1) * 128, :], yb[:, q, :], r=[("yb", q)], w=[("yo", c_ + 2)], sem="yo%d" % q)
    return [("yo", ch) for ch in range(2, NCH)]


def build_p2a():
    nc = bass.Bass("TRN2", target_bir_lowering=False)
    p = Prog(nc)
    bank = [p.ps("bank%d" % b, [128, 512]) for b in range(8)]
    keys = emit_p2a(nc, p, bank)
    p.add("sp", lambda e: e.nop(), r=keys)
    p.emit()
    return nc, p


def to_cm(a):
    return a.reshape(128, 64, -1).transpose(1, 0, 2).reshape(8192, -1)


def from_cm(a):
    return a.reshape(64, 128, -1).transpose(1, 0, 2).reshape(8192, -1)


def consts_tri():
    ident = np.eye(128, dtype=np.float32)
    triU = np.triu(np.ones((128, 128), np.float32))
    triL = np.tril(np.ones((128, 128), np.float32))
    return np.ascontiguousarray(np.concatenate([ident, triU, triL], axis=1))


def p2a_inputs(inputs, h_all, hc_all, j):
    w_in = inputs["w_in"][0]
    cols = np.concatenate([np.arange(j * 128, (j + 1) * 128), 1024 + np.arange(j * 128, (j + 1) * 128),
                           2048 + np.arange(j * 256, (j + 1) * 256), 6144 + np.arange(4) * 8 + j,
                           4096 + np.arange(j * 256, (j + 1) * 256)])
    cwm = inputs["ml_conv_w"][0]
    cb = inputs["ml_conv_b"][0]
    cw = np.concatenate([cwm[:, j * 128:(j + 1) * 128].T, cwm[:, 1024 + j * 128:1024 + (j + 1) * 128].T,
                         cb[j * 128:(j + 1) * 128, None], cb[1024 + j * 128:1024 + (j + 1) * 128, None]], axis=1)
    gbn = np.concatenate([inputs["ml_gate_b"][0][:, j], inputs["ml_norm_w"][0][j * 256:(j + 1) * 256]])
    hcm = np.concatenate([hc_all, to_cm(h_all)], axis=0).T
    return {"hT": np.ascontiguousarray(hcm), "wml": np.ascontiguousarray(w_in[:, cols]), "cst": consts_tri(),
            "cw": np.ascontiguousarray(cw, dtype=np.float32),
            "gbn": np.ascontiguousarray(np.broadcast_to(gbn[None, :], (128, 260)), dtype=np.float32)}


def emit_p2b(nc, p, bank, pre=""):
    hT = dram(nc, pre + "hT", [D, NTOK], BF16, "ExternalInput")
    wss = dram(nc, pre + "wss", [D, 1296], F32, "ExternalInput")
    cst = dram(nc, pre + "cst", [128, 3 * 128], F32, "ExternalInput")
    cw = dram(nc, pre + "cw", [128, 36], F32, "ExternalInput")
    vecd = dram(nc, pre + "vec", [128, 552], F32, "ExternalInput")
    yo = dram(nc, "yssm", [8192, 512], BF16, "ExternalOutput")
    Ud = dram(nc, pre + "Ud", [768, UW2], BF16, "ExternalOutput")
    Zd = dram(nc, pre + "Zd", [8192, 512], BF16, "ExternalOutput")
    Yf = dram(nc, pre + "Yf", [8192, 512], F32, "ExternalOutput")
    cs = p.sb("cs", [128, 3, 128], F32)
    ident, triU, triL = cs[:, 0, :], cs[:, 1, :], cs[:, 2, :]
    identb = p.sb("identb", [128, 128], BF16)
    cwt = p.sb("cwt", [128, 36], F32)
    vec = p.sb("vec_sb", [128, 552], F32)
    ones = p.sb("ones", [128, 128], F32)
    epsb = p.sb("epsb", [128, 1], F32)
    Xtm = p.sb("Xtm", [128, NCH, 512], BF16)
    Btm = p.sb("Btm", [128, NCH, 128], BF16)
    BT = p.sb("BT", [128, NTOK], BF16)
    CT = p.sb("CT", [128, NTOK], BF16)
    dt = p.sb("dt", [128, NCH, 16], F32)
    p.dma("sp", cs[:].rearrange("p a b -> p (a b)"), cst, w=["cs"], sem="c0")
    p.dma("sp", cwt[:], cw, w=["cwt"], sem="c1")
    p.dma("sp", vec[:], vecd, w=["vec"], sem="c2")
    p.memset("dve", ones[:], 1.0, w=["ones"])
    p.memset("dve", epsb[:], EPS, w=["epsb"])
    p.copy("dve", identb[:], ident, r=["cs"], w=["identb"])

    stA = contextlib.ExitStack()
    W = p.sb("W", [128, KC, 1296], BF16, stack=stA)
    ht2 = p.sb("ht", [128, 2, KC, 256], BF16, stack=stA)
    ev = p.sb("ev", [128, 2, 512], BF16, stack=stA)
    zb = p.sb("zb", [128, 2, 512], BF16, stack=stA)
    zt = p.sb("zt", [128, 64], BF16, stack=stA)
    dg = p.sb("dg", [128, 6, 5, 128], BF16, stack=stA)
    ut = p.sb("ut", [128, 2, 576], BF16, stack=stA)
    post = p.sb("post", [128, 2, 512], BF16, stack=stA)
    wr = wss.rearrange("(k p) n -> p k n", p=128)
    for k0 in range(0, KC, 4):
        p.dma("pool", W[:, k0:k0 + 4, :], wr[:, k0:k0 + 4, :], w=["W"], sem="W")
    p.memset("dve", zt[:], 0.0, w=["zt"])
    for f in range(6):
        for (a, n_) in ((0, 32), (288, 64), (8544, 32)):
            p.dma("sp", Ud[f * 128:(f + 1) * 128, a:a + n_], zt[:, 0:n_], r=["zt"], w=[("Ud", f)], sem="udz%d" % f)
        for tap in range(5):
            p.ts("dve", dg[:, f, tap, :], ident, cwt[:, f * 5 + tap:f * 5 + tap + 1], None, ALU.mult, None,
                 r=["cs", "cwt"], w=["dg"])
    hr = hT.rearrange("(k p) n -> p k n", p=128)
    bi = 0
    ei = 0
    for ti, t0 in enumerate(range(0, NTOK, 256)):
        n = 256
        hq = ti % 2
        ht = ht2[:, hq]
        p.dma("sp", ht[:, :, 0:n], hr[:, :, t0:t0 + n], w=[("ht", hq)], sem="ht%d" % hq)
        for f in range(6):
            b = bi % 4
            bi += 1
            q = ei % 2
            ei += 1
            for k in range(KC):
                p.mm(bank[b][:, 0:n], W[:, k, 512 + f * 128:512 + (f + 1) * 128], ht[:, k, 0:n], k == 0, k == KC - 1,
                     r=["W", ("ht", hq)], w=[("bank", b)])
            p.copy("act", ev[:, q, 0:n], bank[b][:, 0:n], r=[("bank", b)], w=[("ev", q)])
            for (a0, a1) in [(0, n)]:
                p0 = upos2(t0 + a0)
                p.dma("sp", Ud[f * 128:(f + 1) * 128, p0:p0 + a1 - a0], ev[:, q, a0:a1], r=[("ev", q)], w=[("Ud", f)], sem="ev%d" % q)
        for sub in range(n // 128):
            ch = t0 // 128 + sub
            b = bi % 4
            bi += 1
            b2 = 4 + b
            if ch >= 2:
                q = ei % 2
                ei += 1
                for k in range(KC):
                    p.mm(bank[b][:, 0:512], ht[:, k, sub * 128:(sub + 1) * 128], W[:, k, 0:512], k == 0, k == KC - 1,
                         r=["W", ("ht", hq)], w=[("bank", b)])
                p.act(zb[:, q, :], bank[b][:, 0:512], AF.Silu, r=[("bank", b)], w=[("zb", q)])
                p.dma("sp", Zd[(ch - 2) * 128:(ch - 1) * 128, :], zb[:, q, :], r=[("zb", q)], w=[("Zd", ch)], sem="zb%d" % q)
            for k in range(KC):
                p.mm(bank[b2][:, 0:16], ht[:, k, sub * 128:(sub + 1) * 128], W[:, k, 1280:1296], k == 0, k == KC - 1,
                     r=["W", ("ht", hq)], w=[("bank", b2)])
            p.tt("dve", dt[:, ch, :], bank[b2][:, 0:16], vec[:, 0:16], ALU.add, r=[("bank", b2), "vec"], w=["dt"])
    dtf = dt[:].rearrange("p c e -> p (c e)")
    p.act(dtf, dtf, AF.Exp, r=["dt"], w=["dt"])
    p.act(dtf, dtf, AF.Ln, r=["dt"], w=["dt"], bias=1.0)
    ui = 0
    for (s0, s1) in [(0, 256)] + [(a, a + 512) for a in range(256, NTOK, 512)]:
        n = s1 - s0
        p0 = upos2(s0)
        for f in range(6):
            q = ui % 2
            ui += 1
            b = bi % 4
            bi += 1
            p.dma("sp", ut[:, q, 0:n + 64], Ud[f * 128:(f + 1) * 128, p0 - 32:p0 + n + 32], r=[("Ud", f)], w=[("ut", q)], sem="ut%d" % q)
            for tap in range(5):
                p.mm(bank[b][:, 0:n], dg[:, f, tap, :], ut[:, q, 30 + tap:30 + tap + n], tap == 0, tap == 4, r=["dg", ("ut", q)], w=[("bank", b)])
            if f == 5:
                p.act(CT[:, s0:s1], bank[b][:, 0:n], AF.Silu, r=[("bank", b), "cwt"], w=["CT"], bias=cwt[:, 30 + f:31 + f])
                continue
            dst = BT[:, s0:s1] if f == 4 else post[:, q, 0:n]
            dkey = "BT" if f == 4 else ("post", q)
            p.act(dst, bank[b][:, 0:n], AF.Silu, r=[("bank", b), "cwt"], w=[dkey], bias=cwt[:, 30 + f:31 + f])
            for sub in range(n // 128):
                ch = s0 // 128 + sub
                b2 = 4 + (bi % 4)
                bi += 1
                p.mm(bank[b2][:, 0:128], dst[:, sub * 128:(sub + 1) * 128], identb[:], True, True, r=[dkey, "identb"], w=[("bank", b2)])
                if f == 4:
                    p.copy("dve", Btm[:, ch, :], bank[b2][:, 0:128], r=[("bank", b2)], w=["Btm"])
                else:
                    p.copy("dve", Xtm[:, ch, f * 128:(f + 1) * 128], bank[b2][:, 0:128], r=[("bank", b2)], w=["Xtm"])
    p.barrier()
    stA.close()

    Aneg = p.sb("Aneg", [128, 16], F32)
    dtA = p.sb("dtA", [128, NCH, 16], F32)
    bsb = p.sb("bsb", [128, 2, NCH, 8], F32)
    tot = p.sb("tot", [128, 2, NCH, 8], F32)
    eo = p.sb("eo", [128, 2, NCH, 8], F32)
    ws = p.sb("ws", [128, 2, NCH, 8], F32)
    dec = p.sb("dec", [128, 2, NCH, 8], F32)
    p.act(Aneg[:], vec[:, 16:32], AF.Exp, r=["vec"], w=["Aneg"])
    p.ts("dve", Aneg[:], Aneg[:], -1.0, None, ALU.mult, None, r=["Aneg"], w=["Aneg"])
    p.tt("dve", dtA[:], dt[:], Aneg[:].unsqueeze(1).broadcast_to([128, NCH, 16]), ALU.mult, r=["dt", "Aneg"], w=["dtA"])
    H = NCH // 2
    for d in range(2):
        tri = triU if d == 0 else triL
        for hf_ in range(2):
            src = dtA[:, hf_ * H:(hf_ + 1) * H, d * 8:(d + 1) * 8]
            b0, b1 = (d * 2 + hf_) % 4, 4 + (d * 2 + hf_) % 4
            p.mm(bank[b0][:, 0:H * 8].rearrange("p (c e) -> p c e", e=8), tri, src, True, True, r=["cs", "dtA"], w=[("bank", b0)])
            p.mm(bank[b1][:, 0:H * 8].rearrange("p (c e) -> p c e", e=8), ones[:], src, True, True, r=["ones", "dtA"], w=[("bank", b1)])
            p.copy("dve", bsb[:, d, hf_ * H:(hf_ + 1) * H, :], bank[b0][:, 0:H * 8].rearrange("p (c e) -> p c e", e=8),
                   r=[("bank", b0)], w=["bsb"])
            p.copy("dve", tot[:, d, hf_ * H:(hf_ + 1) * H, :], bank[b1][:, 0:H * 8].rearrange("p (c e) -> p c e", e=8),
                   r=[("bank", b1)], w=["tot"])
    fl = lambda t: t[:].rearrange("p d c e -> p (d c e)")
    p.act(fl(eo), fl(bsb), AF.Exp, r=["bsb"], w=["eo"])
    p.tt("dve", fl(ws), fl(tot), fl(bsb), ALU.subtract, r=["tot", "bsb"], w=["ws"])
    p.act(fl(ws), fl(ws), AF.Exp, r=["ws"], w=["ws"])
    p.act(fl(dec), fl(tot), AF.Exp, r=["tot"], w=["dec"])

    negb = p.sb("negb", [128, 2, NCH, 8], F32)
    p.ts("dve", fl(negb), fl(bsb), -1.0, None, ALU.mult, None, r=["bsb"], w=["negb"])
    S32 = p.sb("S32", [128, 512], F32)
    Sb = p.sb("Sb", [128, 512], BF16)
    CBm = p.sb("CBm", [128, 2, 128], F32)
    diagb = p.sb("diagb", [128, 2, 8, 128], F32)
    dm = p.sb("dm", [128, 2, 8, 128], BF16)
    G = p.sb("G", [128, 2, 8, 128], BF16)
    xdt = p.sb("xdt", [128, 3, 512], BF16)
    xw = p.sb("xw", [128, 3, 512], BF16)
    yis = p.sb("yis", [128, 2, 512], F32)
    sus = p.sb("sus", [128, 2, 512], F32)
    tmpd = p.sb("tmpd", [128, 512], F32)
    t2 = p.sb("t2", [128, 512], F32)
    t3 = p.sb("t3", [128, 512], BF16)
    ysb = p.sb("ysb", [128, 2, 512], F32)
    yfl = p.sb("yfl", [128, 3, 512], F32)
    zl = p.sb("zl", [128, 3, 512], BF16)
    ob = p.sb("ob", [128, 2, 512], BF16)
    sm = p.sb("sm", [128, 2, 4], F32)
    bc3 = lambda ap: ap.unsqueeze(2).broadcast_to([128, 8, 64])
    v3 = lambda ap: ap.rearrange("p (e c) -> p e c", e=8)

    def preA(i, d, ch):
        q, t = i % 2, i % 3
        lat = ch >= 2
        tk = slice(ch * 128, (ch + 1) * 128)
        dsl = slice(d * 8, (d + 1) * 8)
        p.tt("pool", v3(xdt[:, t, :]), v3(Xtm[:, ch, :]), bc3(dt[:, ch, dsl]), ALU.mult, r=["Xtm", "dt"], w=[("xdt", t)])
        p.tt("pool", v3(xw[:, t, :]), v3(xdt[:, t, :]), bc3(ws[:, d, ch, :]), ALU.mult, r=[("xdt", t), "ws"], w=[("xw", t)])
        if not lat:
            return
        rows = slice((ch - 2) * 128, (ch - 1) * 128)
        p.tt("pool", diagb[:, q], ident.unsqueeze(1).broadcast_to([128, 8, 128]),
             bsb[:, d, ch, :].unsqueeze(2).broadcast_to([128, 8, 128]), ALU.mult, r=["cs", "bsb"], w=[("diagb", q)])
        cbp = bank[7][:, q * 128:(q + 1) * 128]
        p.mm(cbp, BT[:, tk], CT[:, tk], True, True, r=["BT", "CT"], w=[("cbp", q)])
        for h2 in range(2):
            bb = 2 * q + h2
            p.mm(bank[bb][:, 0:512], ones[:], diagb[:, q, 4 * h2:4 * h2 + 4, :].rearrange("p e i -> p (e i)"), True, True,
                 r=["ones", ("diagb", q)], w=[("bank", bb)])
            for e4 in range(4):
                e = 4 * h2 + e4
                p.act(dm[:, q, e, :], bank[bb][:, e4 * 128:(e4 + 1) * 128], AF.Exp, r=[("bank", bb), "negb"], w=[("dm", q)],
                      bias=negb[:, d, ch, e:e + 1])
        if d == 1:
            p.dma("sp", yfl[:, t, :], Yf[rows, :], r=[("Yf", ch)], w=[("yfl", t)], sem="yl%d" % t)
            p.dma("sp", zl[:, t, :], Zd[rows, :], r=[("Zd", ch)], w=[("zl", t)], sem="zl%d" % t)
            p.tt("pool", v3(tmpd[:]), v3(Xtm[:, ch, :]), bc3(vec[:, 32:40]), ALU.mult, r=["Xtm", "vec"], w=["tmpd"])
            p.tt("pool", yfl[:, t, :], yfl[:, t, :], tmpd[:], ALU.add, r=["tmpd", ("yfl", t)], w=[("yfl", t)])

    def preB(i, d, ch):
        q, t = i % 2, i % 3
        tri = triU if d == 0 else triL
        lat = ch >= 2
        if lat:
            cbp = bank[7][:, q * 128:(q + 1) * 128]
            p.tt("dve", CBm[:, q, :], cbp, tri, ALU.mult, r=[("cbp", q), "cs"], w=[("CBm", q)])
            p.stt("dve", G[:, q], dm[:, q], 1.0, CBm[:, q, :].unsqueeze(1).broadcast_to([128, 8, 128]), ALU.min, ALU.mult,
                  r=[("dm", q), ("CBm", q)], w=[("G", q)])
            for e in range(8):
                p.mm(bank[4][:, e * 64:(e + 1) * 64], G[:, q, e, :], xdt[:, t, e * 64:(e + 1) * 64], True, True,
                     r=[("G", q), ("xdt", t)], w=[("bank", 4)])
            p.copy("act", yis[:, q, :], bank[4][:, 0:512], r=[("bank", 4)], w=[("yis", q)])
        p.mm(bank[6][:, 0:512], Btm[:, ch, :], xw[:, t, :], True, True, r=["Btm", ("xw", t)], w=[("bank", 6)])
        p.copy("act", sus[:, q, :], bank[6][:, 0:512], r=[("bank", 6)], w=[("sus", q)])

    def post(i, d, ch):
        q, t = i % 2, i % 3
        lat = ch >= 2
        tk = slice(ch * 128, (ch + 1) * 128)
        if lat:
            rows = slice((ch - 2) * 128, (ch - 1) * 128)
            p.mm(bank[5][:, 0:512], CT[:, tk], Sb[:], True, True, r=["CT", "Sb"], w=[("bank", 5)])
        p.tt("dve", v3(S32[:]), v3(S32[:]), bc3(dec[:, d, ch, :]), ALU.mult, r=["S32", "dec"], w=["S32"])
        p.tt("dve", S32[:], S32[:], sus[:, q, :], ALU.add, r=["S32", ("sus", q)], w=["S32"])
        p.copy("act", Sb[:], S32[:], r=["S32"], w=["Sb"])
        if lat:
            p.tt("dve", v3(t2[:]), v3(bank[5][:, 0:512]), bc3(eo[:, d, ch, :]), ALU.mult, r=[("bank", 5), "eo"], w=["t2"])
            p.tt("dve", ysb[:, q, :], t2[:], yis[:, q, :], ALU.add, r=["t2", ("yis", q)], w=[("ysb", q)])
            if d == 0:
                p.dma("sp", Yf[rows, :], ysb[:, q, :], r=[("ysb", q)], w=[("Yf", ch)], sem="yf%d" % q)
            else:
                p.tt("dve", ysb[:, q, :], ysb[:, q, :], yfl[:, t, :], ALU.add, r=[("ysb", q), ("yfl", t)], w=[("ysb", q)])
                p.tt("dve", ysb[:, q, :], ysb[:, q, :], zl[:, t, :], ALU.mult, r=[("ysb", q), ("zl", t)], w=[("ysb", q)])
                p.add("act", lambda e, q=q: e.activation(out=t3[:], in_=ysb[:, q, :], func=AF.Square, accum_out=sm[:, q, 0:1]),
                      r=[("ysb", q)], w=[("sm", q), "t3"])
                p.act(sm[:, q, 1:2], sm[:, q, 0:1], AF.Sqrt, r=[("sm", q), "epsb"], w=[("sm", q)], bias=epsb[:, 0:1], scale=1.0 / 512)
                p.add("dve", lambda e, q=q: e.reciprocal(out=sm[:, q, 2:3], in_=sm[:, q, 1:2]), r=[("sm", q)], w=[("sm", q)])
                p.stt("dve", ob[:, q, :], ysb[:, q, :], sm[:, q, 2:3], vec[:, 40:552], ALU.mult, ALU.mult,
                      r=[("ysb", q), ("sm", q), "vec"], w=[("ob", q)])
                p.dma("sp", yo[rows, :], ob[:, q, :], r=[("ob", q)], w=[("yo", ch)], sem="yo%d" % q)

    seq = []
    for d in range(2):
        order = list(range(NCH)) if d == 0 else ([1, 0] + list(range(NCH - 1, 1, -1)))
        seq += [(d, ch) for ch in order]
    NS = len(seq)
    preA(0, *seq[0])
    preA(1, *seq[1])
    preB(0, *seq[0])
    for i, (d, ch) in enumerate(seq):
        if i == 0 or seq[i - 1][0] != d:
            p.memset("dve", S32[:], 0.0, w=["S32"])
            p.memset("dve", Sb[:], 0.0, w=["Sb"])
        if i + 2 < NS:
            preA(i + 2, *seq[i + 2])
        if i + 1 < NS:
            preB(i + 1, *seq[i + 1])
        post(i, d, ch)
    return [("yo", ch) for ch in range(2, NCH)]


def build_p2b():
    nc = bass.Bass("TRN2", target_bir_lowering=False)
    p = Prog(nc)
    bank = [p.ps("bank%d" % b, [128, 512]) for b in range(8)]
    keys = emit_p2b(nc, p, bank)
    p.add("sp", lambda e: e.nop(), r=keys)
    p.emit()
    return nc, p


def build_p2():
    nc = bass.Bass("TRN2", target_bir_lowering=False)
    p = Prog(nc)
    bank = [p.ps("bank%d" % b, [128, 512]) for b in range(8)]
    with p.scope("a_"):
        emit_p2a(nc, p, bank, "a_")
    with p.scope("b_"):
        keys = emit_p2b(nc, p, bank, "b_")
        p.add("sp", lambda e: e.nop(), r=keys)
    p.emit()
    return nc, p


def p2b_inputs(inputs, h_all, hc_all, g):
    w_in = inputs["w_in"][0]
    o = 6176
    cols = np.concatenate([o + np.arange(g * 512, (g + 1) * 512), o + 4096 + np.arange(g * 512, (g + 1) * 512),
                           o + 8192 + np.arange(g * 128, (g + 1) * 128), o + 9216 + np.arange(g * 128, (g + 1) * 128),
                           o + 10240 + np.arange(g * 8, (g + 1) * 8), o + 10304 + np.arange(g * 8, (g + 1) * 8)])
    cch = np.concatenate([np.arange(g * 512, (g + 1) * 512), 4096 + np.arange(g * 128, (g + 1) * 128),
                          5120 + np.arange(g * 128, (g + 1) * 128)])
    cwm = inputs["ssm_conv_w"][0][:, cch]
    cb = inputs["ssm_conv_b"][0][cch]
    cw = np.concatenate([cwm.reshape(5, 6, 128).transpose(2, 1, 0).reshape(128, 30), cb.reshape(6, 128).T], axis=1)
    hsl = slice(g * 8, (g + 1) * 8)
    vec = np.concatenate([inputs["ssm_dt_bias"][0][0, hsl], inputs["ssm_dt_bias"][0][1, hsl],
                          inputs["ssm_a_log"][0][0, hsl], inputs["ssm_a_log"][0][1, hsl],
                          inputs["ssm_d"][0][hsl], inputs["ssm_norm_w"][0][g * 512:(g + 1) * 512]])
    hrm = np.concatenate([hc_all, h_all], axis=0).T
    return {"hT": np.ascontiguousarray(hrm), "wss": np.ascontiguousarray(w_in[:, cols]), "cst": consts_tri(),
            "cw": np.ascontiguousarray(cw, dtype=np.float32),
            "vec": np.ascontiguousarray(np.broadcast_to(vec[None, :], (128, 552)), dtype=np.float32)}


def p2_inputs(inputs, h_all, hc_all, j):
    m = {"a_" + k: v for k, v in p2a_inputs(inputs, h_all, hc_all, j).items()}
    m.update({"b_" + k: v for k, v in p2b_inputs(inputs, h_all, hc_all, j).items()})
    return m


def emit_p3a(nc, p, bank, pre="", modd=None):
    T = 1024
    x1T = dram(nc, pre + "x1T", [D, T], F32, "ExternalInput")
    hTd = dram(nc, pre + "hT", [D, T], BF16, "ExternalInput")
    ymlT = dram(nc, pre + "ymlT", [D, T], BF16, "ExternalInput")
    yssT = dram(nc, pre + "yssT", [2 * D, T], BF16, "ExternalInput")
    if modd is None:
        modd = dram(nc, pre + "modfm", [128, 288], F32, "ExternalInput")
    bgd = dram(nc, pre + "bg", [128, 32], F32, "ExternalInput")
    wpm = dram(nc, pre + "wpm", [D, D], F32, "ExternalInput")
    wps = dram(nc, pre + "wps", [2 * D, D], F32, "ExternalInput")
    wgt = dram(nc, pre + "wgt", [D, 2 * D], F32, "ExternalInput")
    wo = dram(nc, pre + "wo", [D, D], F32, "ExternalInput")
    x2T = dram(nc, pre + "x2T", [D, T], F32, "ExternalOutput")
    c = setup_common(p, T, nslot=3, ffn=False, bank=bank)
    modfm = p.sb("modfm_sb", [128, 288], F32)
    bg = p.sb("bg_sb", [128, 32], F32)
    pss = p.sb("pss", [128, KC, T], BF16)
    p.dma("sp", modfm[:], modd, w=["modfm"], sem="c0")
    p.dma("sp", bg[:], bgd, w=["bg"], sem="c1")
    gmix = modfm[:, :].rearrange("p (c j) -> p c j", j=2)[:, 5 * KC:6 * KC, 0]
    r3 = lambda w_: w_.rearrange("(k p) n -> p k n", p=128)
    wpm_r, wps_r, wgt_r, wo_r = r3(wpm), r3(wps), r3(wgt), r3(wo)
    it = 0
    with p.scope(pre + "A_"):
        yss = p.sb("yss", [128, 2 * KC, T], BF16)
        for n in range(2):
            ts_ = slice(n * 512, (n + 1) * 512)
            p.dma("sp", yss[:, :, ts_], r3(yssT)[:, :, ts_], w=[("yss", n)], sem="a1_%d" % n)
        for dp in range(KC // 2):
            s2 = next_slot(c)
            w2 = c.slot[s2][:, :].rearrange("p (k n) -> p k n", k=2 * KC)
            p.dma("pool", w2, wps_r[:, :, dp * 256:(dp + 1) * 256], w=[("slot", s2)], sem="slot%d" % s2)
            for dd in range(2):
                d = dp * 2 + dd
                for n in range(2):
                    ts_ = slice(n * 512, (n + 1) * 512)
                    q = it % 4
                    it += 1
                    for k in range(2 * KC):
                        p.mm(c.bank[q][:, :], w2[:, k, dd * 128:(dd + 1) * 128], yss[:, k, ts_], k == 0, k == 2 * KC - 1,
                             r=[("slot", s2), ("yss", n)], w=[("bank", q)])
                    p.copy("act", pss[:, d, ts_], c.bank[q][:, :], r=[("bank", q)], w=[("pss", d, n)])
    with p.scope(pre + "B_"):
        yml = p.sb("yml", [128, KC, T], BF16)
        hT = p.sb("hTt", [128, KC, T], BF16)
        mg = p.sb("mg", [128, KC, T], BF16)
        gs = p.sb("gs", [128, 2, 2, 512], F32)
        tu = p.sb("tu", [128, 2, 2, 512], F32)
        xt = p.sb("xt", [128, 4, 512], F32)
        for n in range(2):
            ts_ = slice(n * 512, (n + 1) * 512)
            p.dma("sp", yml[:, :, ts_], r3(ymlT)[:, :, ts_], w=[("yml", n)], sem="a0_%d" % n)
            p.dma("sp", hT[:, :, ts_], r3(hTd)[:, :, ts_], w=[("hTt", n)], sem="a2_%d" % n)
        it = 0
        for dp in range(KC // 2):
            cs_ = slice(dp * 256, (dp + 1) * 256)
            s1, s3 = next_slot(c), next_slot(c)
            w1 = c.slot[s1][:, 0:4096].rearrange("p (k n) -> p k n", k=KC)
            w1g = c.slot[s1][:, 4096:8192].rearrange("p (k n) -> p k n", k=KC)
            w3 = c.slot[s3][:, 0:4096].rearrange("p (k n) -> p k n", k=KC)
            p.dma("pool", w1, wpm_r[:, :, cs_], w=[("slot", s1)], sem="slot%d" % s1)
            p.dma("pool", w1g, wgt_r[:, :, cs_], w=[("slot", s1)], sem="slot%d" % s1)
            p.dma("pool", w3, wgt_r[:, :, D + dp * 256:D + (dp + 1) * 256], w=[("slot", s3)], sem="slot%d" % s3)
            for dd in range(2):
                d = dp * 2 + dd
                dsl = slice(dd * 128, (dd + 1) * 128)
                for n in range(2):
                    ts_ = slice(n * 512, (n + 1) * 512)
                    q = it % 2
                    it += 1
                    b0 = q * 3
                    for k in range(KC):
                        p.mm(c.bank[b0][:, :], w1[:, k, dsl], yml[:, k, ts_], k == 0, k == KC - 1, r=[("slot", s1), ("yml", n)], w=[("bank", b0)])
                    for k in range(KC):
                        p.mm(c.bank[b0 + 1][:, :], w1g[:, k, dsl], hT[:, k, ts_], k == 0, k == KC - 1, r=[("slot", s1), ("hTt", n)], w=[("bank", b0 + 1)])
                    for k in range(KC):
                        p.mm(c.bank[b0 + 2][:, :], w3[:, k, dsl], hT[:, k, ts_], k == 0, k == KC - 1, r=[("slot", s3), ("hTt", n)], w=[("bank", b0 + 2)])
                    p.act(gs[:, q, 0, :], c.bank[b0 + 1][:, :], AF.Sigmoid, r=[("bank", b0 + 1), "bg"], w=[("gs", q)], bias=bg[:, d:d + 1])
                    p.act(gs[:, q, 1, :], c.bank[b0 + 2][:, :], AF.Sigmoid, r=[("bank", b0 + 2), "bg"], w=[("gs", q)], bias=bg[:, KC + d:KC + d + 1])
                    p.tt("dve", tu[:, q, 0, :], gs[:, q, 0, :], c.bank[b0][:, :], ALU.mult, r=[("gs", q), ("bank", b0)], w=[("tu", q)])
                    p.tt("dve", tu[:, q, 1, :], gs[:, q, 1, :], pss[:, d, ts_], ALU.mult, r=[("gs", q), ("pss", d, n)], w=[("tu", q)])
                    p.tt("dve", mg[:, d, ts_], tu[:, q, 0, :], tu[:, q, 1, :], ALU.add, r=[("tu", q)], w=[("mg", n)])
        for dp in range(KC // 2):
            s1 = next_slot(c)
            w1 = c.slot[s1][:, 0:4096].rearrange("p (k n) -> p k n", k=KC)
            p.dma("pool", w1, wo_r[:, :, dp * 256:(dp + 1) * 256], w=[("slot", s1)], sem="slot%d" % s1)
            for dd in range(2):
                d = dp * 2 + dd
                for n in range(2):
                    ts_ = slice(n * 512, (n + 1) * 512)
                    q = it % 4
                    it += 1
                    b0 = (6, 7, 2, 5)[q]
                    p.dma("sp", xt[:, q, :], x1T[d * 128:(d + 1) * 128, ts_], w=[("xt", q)], sem="xt%d" % q)
                    for k in range(KC):
                        p.mm(c.bank[b0][:, :], w1[:, k, dd * 128:(dd + 1) * 128], mg[:, k, ts_], k == 0, k == KC - 1,
                             r=[("slot", s1), ("mg", 0), ("mg", 1)], w=[("bank", b0)])
                    p.stt("dve", xt[:, q, :], c.bank[b0][:, :], gmix[:, d:d + 1], xt[:, q, :], ALU.mult, ALU.add,
                          r=[("bank", b0), ("xt", q), "modfm"], w=[("xt", q)])
                    p.dma("sp", x2T[d * 128:(d + 1) * 128, ts_], xt[:, q, :], r=[("xt", q)], w=[("x2", d, n)], sem="xo%d" % q)
    return x2T, [("x2", d, n) for d in range(KC) for n in range(2)]


def build_p3a():
    nc = bass.Bass("TRN2", target_bir_lowering=False)
    p = Prog(nc)
    bank = [p.ps("bank%d" % b, [128, 512]) for b in range(8)]
    _, keys = emit_p3a(nc, p, bank)
    p.add("sp", lambda e: e.nop(), r=keys)
    p.emit()
    return nc, p


def emit_p3b(nc, p, bank, pre="", x2T=None, modd=None):
    T = 1024
    if x2T is None:
        x2T = dram(nc, pre + "x2T", [D, T], F32, "ExternalInput")
    if modd is None:
        modd = dram(nc, pre + "modfm", [128, 288], F32, "ExternalInput")
    nwd = dram(nc, pre + "nw", [128, 4 * KC], F32, "ExternalInput")
    wg = dram(nc, pre + "wg", [D, DFF], F32, "ExternalInput")
    wu = dram(nc, pre + "wu", [D, DFF], F32, "ExternalInput")
    wd = dram(nc, pre + "wd", [DFF, D], F32, "ExternalInput")
    x3T = dram(nc, pre + "x3T", [D, T], F32, "ExternalOutput")
    outT = dram(nc, "outT", [D, T], F32, "ExternalOutput")
    c = setup_common(p, T, bank=bank)
    c.epsb = p.sb("epsb", [128, 1], F32)
    p.memset("dve", c.epsb[:], EPS, w=["epsb"])
    modfm = p.sb("modfm_sb", [128, 288], F32)
    nw = p.sb("nwt", [128, 4, KC], F32)
    ob = p.sb("obuf", [128, 2, T], F32)
    p.dma("sp", modfm[:], modd, w=["modfm"], sem="c0")
    p.dma("sp", nw[:].rearrange("p s k -> p (s k)"), nwd, w=["nw"], sem="c1")
    A, B, G, gt = make_AB(p, c, modfm, nw, 2)
    p.ts("dve", G[:], gt, 0.5, None, ALU.mult, None, r=["modfm"], w=["modAB"])
    Ao = p.sb("Ao", [128, KC, 2], F32)
    for j in range(2):
        p.copy("dve", Ao[:, :, j], nw[:, 3, :], r=["nw"], w=["modAB"])
    tiles = [(0, 512), (512, 1024)]
    segs = [(0, 1024, 0)]
    ffn_stage(p, c, "f2", tiles, segs, x2T, A, B, G, wg, wu, wd, Ao, None, x3T, outT, ob, hout_rot=2)
    return [("f2_hdram", k) for k in range(KC)]


def build_p3b():
    nc = bass.Bass("TRN2", target_bir_lowering=False)
    p = Prog(nc)
    bank = [p.ps("bank%d" % b, [128, 512]) for b in range(8)]
    keys = emit_p3b(nc, p, bank)
    p.add("sp", lambda e: e.nop(), r=keys)
    p.emit()
    return nc, p


def build_p3():
    nc = bass.Bass("TRN2", target_bir_lowering=False)
    p = Prog(nc)
    bank = [p.ps("bank%d" % b, [128, 512]) for b in range(8)]
    modd = dram(nc, "modfm", [128, 288], F32, "ExternalInput")
    with p.scope("a_"):
        x2T, _ = emit_p3a(nc, p, bank, "a_", modd)
    with p.scope("b_"):
        keys = emit_p3b(nc, p, bank, "b_", x2T=x2T, modd=modd)
        p.add("sp", lambda e: e.nop(), r=keys)
    p.emit()
    return nc, p


def _run(nc, in_maps):
    return run_bass_kernel_spmd(nc, in_maps, core_ids=list(range(len(in_maps)))).results


def kernel(**inputs):
    bf = ml_dtypes.bfloat16
    r1 = run_p1(inputs)
    x1T = [np.asarray(r["x1T"]) for r in r1]
    hT2 = [np.asarray(r["hT2"]) for r in r1]
    modfm = np.asarray(r1[0]["modfm"])
    h_all = np.concatenate([h[:, :1024].T for h in hT2], axis=0)
    hc_all = np.concatenate([h[:, 1024:].T for h in hT2], axis=0)
    nc, _ = build_p2()
    r2 = _run(nc, [p2_inputs(inputs, h_all, hc_all, j) for j in range(NCORES)])
    yml = np.concatenate([from_cm(np.asarray(r["yml"])) for r in r2], axis=1)
    yss = np.concatenate([np.asarray(r["yssm"]) for r in r2], axis=1)
    nc, _ = build_p3()
    nw = np.concatenate([_fm(inputs["norm_w"][0, s]) for s in range(3)] + [_fm(inputs["final_norm_w"])], axis=1)
    common = {"modfm": modfm, "a_bg": _fm(inputs["b_gate"][0]),
              "a_wpm": np.ascontiguousarray(inputs["w_proj_ml"][0]), "a_wps": np.ascontiguousarray(inputs["w_proj_ssm"][0]),
              "a_wgt": np.ascontiguousarray(inputs["w_gate"][0]), "a_wo": np.ascontiguousarray(inputs["w_out"][0]),
              "b_nw": np.ascontiguousarray(nw),
              "b_wg": np.ascontiguousarray(inputs["ffn_w_gate"][0, 1]), "b_wu": np.ascontiguousarray(inputs["ffn_w_up"][0, 1]),
              "b_wd": np.ascontiguousarray(inputs["ffn_w_down"][0, 1])}
    maps = []
    for i in range(NCORES):
        sl = slice(1024 * i, 1024 * (i + 1))
        m = dict(common)
        m["a_x1T"] = np.ascontiguousarray(x1T[i][:, :1024])
        m["a_hT"] = np.ascontiguousarray(hT2[i][:, :1024])
        m["a_ymlT"] = np.ascontiguousarray(yml[sl].T)
        m["a_yssT"] = np.ascontiguousarray(yss[sl].T)
        maps.append(m)
    r3b = _run(nc, maps)
    out = np.concatenate([np.asarray(r["outT"]).T for r in r3b], axis=0)
    return np.ascontiguousarray(out[None].astype(np.float32))
```
